# Optimizing a Trainium2 kernel written in Bass

```python
import math
import jax, jax.numpy as jnp
from jax import lax
import numpy as np

D_MODEL = 1024
BATCH = 2
SEQ = 8192
DEPTH = 2

N_MIXERS = 2
N_A_LAYERS = (DEPTH + 1) // 2
N_B_LAYERS = DEPTH // 2

HGRN_EXPAND = 128
HGRN_HEADS = D_MODEL // HGRN_EXPAND
HGRN_FDIM = HGRN_HEADS * HGRN_EXPAND
HGRN_IDIM = D_MODEL // HGRN_HEADS
HGRN_CHUNK = 64

ATTN_HEADS = 8
ATTN_HEAD_DIM = D_MODEL // ATTN_HEADS
MOBA_BLOCK = 256
MOBA_TOPK = 3
MOBA_QCHUNK = 32

REL_BUCKETS = 32
REL_MAX_DISTANCE = 1024

D_FF = -(-(8 * D_MODEL) // (3 * 256)) * 256
RMS_EPS = 1e-6

kernel_name = "hgrn2_moba_interleaved_hybrid"


def rms_norm(x, g):
    xf = x.astype(jnp.float32)
    y = xf * lax.rsqrt(jnp.mean(xf * xf, axis=-1, keepdims=True) + RMS_EPS)
    return (y * g.astype(jnp.float32)).astype(x.dtype)


def t5_bucket(dist):
    n = jnp.maximum(dist, 0)
    max_exact = REL_BUCKETS // 2
    nf = jnp.maximum(n, max_exact).astype(jnp.float32)
    large = max_exact + (jnp.log(nf / max_exact) / math.log(REL_MAX_DISTANCE / max_exact)
                         * (REL_BUCKETS - max_exact)).astype(jnp.int32)
    large = jnp.minimum(large, REL_BUCKETS - 1)
    return jnp.where(n < max_exact, n, large)


def hgrn2_mixer(h, w_in, w_out, lb, out_norm):
    b_, s_, _ = h.shape
    f32 = jnp.float32
    proj = h @ w_in
    q, fz, inp, og = jnp.split(proj, [HGRN_FDIM, 2 * HGRN_FDIM, 2 * HGRN_FDIM + D_MODEL], axis=-1)
    q = jax.nn.silu(q.astype(f32))
    lbf = lb.astype(f32)
    f = lbf + (1.0 - lbf) * jax.nn.sigmoid(fz.astype(f32))
    k = 1.0 - f
    logf = jnp.log(f)
    nc = s_ // HGRN_CHUNK

    def to_chunks(t, d):
        return t.reshape(b_, nc, HGRN_CHUNK, HGRN_HEADS, d).transpose(1, 0, 3, 2, 4)

    qc = to_chunks(q, HGRN_EXPAND)
    kc = to_chunks(k, HGRN_EXPAND)
    gc = to_chunks(logf, HGRN_EXPAND)
    ic = to_chunks(inp.astype(f32), HGRN_IDIM)
    causal = jnp.tril(jnp.ones((HGRN_CHUNK, HGRN_CHUNK), dtype=bool))

    def step(state, xs):
        qb, kb, gb, ib = xs
        cum = jnp.cumsum(gb, axis=2)
        diff = cum[:, :, :, None, :] - cum[:, :, None, :, :]
        decay = jnp.exp(jnp.where(causal[:, :, None], diff, -jnp.inf))
        scores = jnp.einsum('bhtn,bhsn,bhtsn->bhts', qb, kb, decay)
        o = (jnp.einsum('bhts,bhsd->bhtd', scores, ib)
             + jnp.einsum('bhtn,bhnd->bhtd', qb * jnp.exp(cum), state))
        last = cum[:, :, -1:, :]
        state = (state * jnp.exp(last[:, :, 0, :, None])
                 + jnp.einsum('bhsn,bhsd->bhnd', kb * jnp.exp(last - cum), ib))
        return state, o

    s0 = jnp.zeros((b_, HGRN_HEADS, HGRN_EXPAND, HGRN_IDIM), f32)
    _, o = lax.scan(step, s0, (qc, kc, gc, ic))
    o = o.transpose(1, 0, 3, 2, 4).reshape(b_, s_, D_MODEL)
    o = rms_norm(o, out_norm) * jax.nn.sigmoid(og.astype(f32))
    return (o.astype(h.dtype) @ w_out).astype(h.dtype)


def moba_mixer(h, w_in, w_out, rel_table):
    b_, s_, _ = h.shape
    f32 = jnp.float32
    nh, dh = ATTN_HEADS, ATTN_HEAD_DIM
    q, k, v = jnp.split(h @ w_in, 3, axis=-1)
    heads = lambda t: t.reshape(b_, s_, nh, dh).transpose(0, 2, 1, 3)
    q, k, v = heads(q), heads(k), heads(v)
    nb = -(-s_ // MOBA_BLOCK)
    pad = nb * MOBA_BLOCK - s_
    kp = jnp.pad(k, ((0, 0), (0, 0), (0, pad), (0, 0)))
    vp = jnp.pad(v, ((0, 0), (0, 0), (0, pad), (0, 0)))
    kb = kp.reshape(b_, nh, nb, MOBA_BLOCK, dh)
    vb = vp.reshape(b_, nh, nb, MOBA_BLOCK, dh)
    kmean = jnp.mean(kb.astype(f32), axis=3)
    scale = dh ** -0.5
    nqc = s_ // MOBA_QCHUNK
    qch = q.reshape(b_, nh, nqc, MOBA_QCHUNK, dh).transpose(2, 0, 1, 3, 4)
    bi = jnp.arange(b_)[:, None, None, None]
    hi = jnp.arange(nh)[None, :, None, None]
    table_t = rel_table.T
    topk = min(MOBA_TOPK, nb)
    kk = topk * MOBA_BLOCK

    def chunk(args):
        c, qb = args
        q0 = c * MOBA_QCHUNK
        qpos = q0 + jnp.arange(MOBA_QCHUNK)
        j = q0 // MOBA_BLOCK
        gate = jnp.einsum('bhqd,bhnd->bhqn', qb.astype(f32), kmean)
        gate = jnp.where(jnp.arange(nb) < j, gate, -jnp.inf)
        _, sel = lax.top_k(gate, topk)
        sel_ok = jnp.arange(topk) < j
        ks = kb[bi, hi, sel]
        vs = vb[bi, hi, sel]
        s_sel = jnp.einsum('bhqd,bhqkpd->bhqkp', qb, ks).astype(f32) * scale
        kpos_sel = sel[..., None] * MOBA_BLOCK + jnp.arange(MOBA_BLOCK)
        bias_sel = table_t[hi[..., None], t5_bucket(qpos[:, None, None] - kpos_sel)]
        s_sel = jnp.where(sel_ok[:, None], s_sel + bias_sel.astype(f32), -jnp.inf)
        kown = lax.dynamic_slice_in_dim(kp, j * MOBA_BLOCK, MOBA_BLOCK, axis=2)
        vown = lax.dynamic_slice_in_dim(vp, j * MOBA_BLOCK, MOBA_BLOCK, axis=2)
        dist_own = qpos[:, None] - (j * MOBA_BLOCK + jnp.arange(MOBA_BLOCK))[None, :]
        bias_own = rel_table[t5_bucket(dist_own)].transpose(2, 0, 1).astype(f32)
        s_own = jnp.einsum('bhqd,bhpd->bhqp', qb, kown).astype(f32) * scale + bias_own
        s_own = jnp.where(dist_own >= 0, s_own, -jnp.inf)
        logits = jnp.concatenate([s_sel.reshape(b_, nh, MOBA_QCHUNK, kk), s_own], axis=-1)
        p = jax.nn.softmax(logits, axis=-1).astype(vb.dtype)
        p_sel = p[..., :kk].reshape(b_, nh, MOBA_QCHUNK, topk, MOBA_BLOCK)
        p_own = p[..., kk:]
        return (jnp.einsum('bhqkp,bhqkpd->bhqd', p_sel, vs)
                + jnp.einsum('bhqp,bhpd->bhqd', p_own, vown))

    o = lax.map(chunk, (jnp.arange(nqc), qch))
    o = o.transpose(1, 0, 3, 2, 4).reshape(b_, s_, D_MODEL)
    return (o @ w_out).astype(h.dtype)


def swiglu(h, w13, w2):
    g, u = jnp.split(h @ w13, 2, axis=-1)
    return ((jax.nn.silu(g) * u) @ w2).astype(h.dtype)


def setup_inputs(seed: int = 0) -> dict:
    key = jax.random.key(seed)
    ks = jax.random.split(key, 14)
    nrm = lambda k, shape, fan_in: jax.random.normal(k, shape, jnp.float32) * (fan_in ** -0.5)
    gain = lambda k, shape: 1.0 + 0.02 * jax.random.normal(k, shape, jnp.float32)
    return {
        "x": jax.random.normal(ks[0], (BATCH, SEQ, D_MODEL), jnp.float32),
        "norm_mix": gain(ks[1], (DEPTH, D_MODEL)),
        "norm_ffn": gain(ks[2], (DEPTH, D_MODEL)),
        "hgrn_w_in": nrm(ks[3], (N_A_LAYERS, D_MODEL, 2 * HGRN_FDIM + 2 * D_MODEL), D_MODEL),
        "hgrn_lb_logits": 1.0 + 0.1 * jax.random.normal(ks[4], (N_A_LAYERS + 1, HGRN_FDIM), jnp.float32),
        "hgrn_out_norm": gain(ks[5], (N_A_LAYERS, D_MODEL)),
        "hgrn_w_out": nrm(ks[6], (N_A_LAYERS, D_MODEL, D_MODEL), D_MODEL),
        "moba_w_in": nrm(ks[7], (N_B_LAYERS, D_MODEL, 3 * D_MODEL), D_MODEL),
        "moba_w_out": nrm(ks[8], (N_B_LAYERS, D_MODEL, D_MODEL), D_MODEL),
        "rel_bias_table": 0.5 * jax.random.normal(ks[9], (REL_BUCKETS, ATTN_HEADS), jnp.float32),
        "ffn_w13": nrm(ks[10], (DEPTH, D_MODEL, 2 * D_FF), D_MODEL),
        "ffn_w2": nrm(ks[11], (DEPTH, D_FF, D_MODEL), D_FF),
        "final_norm": gain(ks[12], (D_MODEL,)),
    }


def reference(x, norm_mix, norm_ffn, hgrn_w_in, hgrn_lb_logits, hgrn_out_norm, hgrn_w_out,
              moba_w_in, moba_w_out, rel_bias_table, ffn_w13, ffn_w2, final_norm):
    lb_all = jnp.cumsum(jax.nn.softmax(hgrn_lb_logits.astype(jnp.float32), axis=0), axis=0)[:N_A_LAYERS]
    h = x
    for layer in range(DEPTH):
        y = rms_norm(h, norm_mix[layer])
        idx = layer // N_MIXERS
        if layer % N_MIXERS == 0:
            y = hgrn2_mixer(y, hgrn_w_in[idx], hgrn_w_out[idx], lb_all[idx], hgrn_out_norm[idx])
        else:
            y = moba_mixer(y, moba_w_in[idx], moba_w_out[idx], rel_bias_table)
        h = h + y
        h = h + swiglu(rms_norm(h, norm_ffn[layer]), ffn_w13[layer], ffn_w2[layer])
    return rms_norm(h, final_norm)
```

```python
from contextlib import ExitStack

import numpy as np
import concourse.bass as bass
import concourse.mybir as mybir
from concourse.bass_utils import run_bass_kernel_spmd

F32 = mybir.dt.float32
BF16 = mybir.dt.bfloat16
ALU = mybir.AluOpType
AF = mybir.ActivationFunctionType
AX = mybir.AxisListType

D = 1024
DFF = 2816
SEQ = 8192
EPS = 1e-6
NEG = -1.0e30


class Prog:
    def __init__(self):
        self.nc = bass.Bass("TRN2", target_bir_lowering=False)
        nc = self.nc
        self.es = ExitStack()
        self.E = {"pe": nc.tensor, "act": nc.scalar, "dve": nc.vector, "pool": nc.gpsimd, "sp": nc.sync}
        self.semh = {}
        self.cnt = {}
        self.NCH = 8
        self.dn = {"sp": 0, "pool": 0}
        for k in ["pe", "act", "dve", "pool"] + [f"dq_{q}_{i}" for q in ("sp", "pool") for i in range(self.NCH)]:
            self.semh[k] = self.es.enter_context(nc.semaphore(k))
            self.cnt[k] = 0
        self.lastw = {}
        self.readers = {}
        self.waited = {}
        self._uid = 0

    def sb(self, name, shape, dt):
        return self.es.enter_context(self.nc.sbuf_tensor(name, list(shape), dt))

    def ps(self, name, shape, dt):
        return self.es.enter_context(self.nc.psum_tensor(name, list(shape), dt))

    def dram(self, name, shape, dt, kind):
        return self.nc.dram_tensor(name, list(shape), dt, kind=kind).ap()

    def _deps(self, reads, writes):
        deps = {}

        def add(k, v):
            if deps.get(k, 0) < v:
                deps[k] = v

        for r in reads:
            if r in self.lastw:
                add(*self.lastw[r])
        for w in writes:
            if w in self.lastw:
                add(*self.lastw[w])
            for k, v in self.readers.get(w, {}).items():
                add(k, v)
        return deps

    def _wait(self, e, deps, skip=None):
        eng = self.E[e]
        for k, v in deps.items():
            if k == skip:
                continue
            if self.waited.get((e, k), 0) >= v:
                continue
            eng.wait_ge(self.semh[k], v)
            self.waited[(e, k)] = v

    def _commit(self, key, val, reads, writes):
        for r in reads:
            d = self.readers.setdefault(r, {})
            if d.get(key, 0) < val:
                d[key] = val
        for w in writes:
            self.lastw[w] = (key, val)
            self.readers[w] = {}

    def op(self, e, fn, reads=(), writes=()):
        deps = self._deps(reads, writes)
        self._wait(e, deps, skip=("pe" if e == "pe" else None))
        ins = fn(self.E[e])
        self.cnt[e] += 1
        ins.then_inc(self.semh[e], 1)
        self._commit(e, self.cnt[e], reads, writes)

    def dma(self, q, out, in_, reads=(), writes=()):
        key = f"dq_{q}_{self.dn[q] % self.NCH}"
        self.dn[q] += 1
        deps = self._deps(reads, writes)
        if self.cnt[key] > deps.get(key, 0):
            deps[key] = self.cnt[key]
        self._wait(q, deps)
        ins = self.E[q].dma_start(out=out, in_=in_)
        self.cnt[key] += 16
        ins.then_inc(self.semh[key], 16)
        self._commit(key, self.cnt[key], reads, writes)

    def finish(self):
        for k, v in self.cnt.items():
            if v > 0:
                self.E["sp"].wait_ge(self.semh[k], v)
        self.es.close()
        return self.nc


def make_ident(P, name="ident"):
    ones = P.sb(name + "_ones", [128, 128], BF16)
    ident = P.sb(name, [128, 128], BF16)
    P.op("pool", lambda e: e.memset(ones[:], 1.0), writes=[name + "_ones"])
    P.op("pool", lambda e: e.affine_select(out=ident[:], in_=ones[:], pattern=[[-1, 128]],
                                            compare_op=ALU.is_equal, fill=0.0, base=0,
                                            channel_multiplier=1),
         reads=[name + "_ones"], writes=[name])
    return ident


class Normer:
    def __init__(self, P, ident, nbuf=2):
        self.P = P
        self.ident = ident
        self.junk = P.sb("nm_junk", [128, D], BF16)
        self.ss = [P.sb(f"nm_ss{i}", [128, 1], F32) for i in range(nbuf)]
        self.rs = [P.sb(f"nm_rs{i}", [128, 1], F32) for i in range(nbuf)]
        self.yb = [P.sb(f"nm_yb{i}", [128, D], BF16) for i in range(nbuf)]
        self.pT = [P.ps(f"nm_pT{i}", [128, 8, 128], BF16) for i in range(nbuf)]
        self.n = 0
        self.nbuf = nbuf

    def rstd(self, x_ap, xres, i):
        P = self.P
        ss, rs = self.ss[i], self.rs[i]
        P.op("dve", lambda e: e.memset(ss[:], 0.0), writes=[f"nm_ss{i}"])
        P.op("act", lambda e: e.activation(out=self.junk[:], in_=x_ap, func=AF.Square, accum_out=ss[:]),
             reads=[xres, f"nm_ss{i}"], writes=["nm_junk", f"nm_ss{i}"])
        P.op("dve", lambda e: e.tensor_scalar(out=ss[:], in0=ss[:], scalar1=1.0 / D, scalar2=EPS,
                                              op0=ALU.mult, op1=ALU.add),
             reads=[f"nm_ss{i}"], writes=[f"nm_ss{i}"])
        P.op("act", lambda e: e.activation(out=ss[:], in_=ss[:], func=AF.Sqrt),
             reads=[f"nm_ss{i}"], writes=[f"nm_ss{i}"])
        P.op("dve", lambda e: e.reciprocal(out=rs[:], in_=ss[:]),
             reads=[f"nm_ss{i}"], writes=[f"nm_rs{i}"])
        return rs

    def transpose(self, src_bf, srcres, dst_ap, dstres, i, evac="act"):
        P = self.P
        pT = self.pT[i]
        for k in range(8):
            P.op("pe", lambda e, k=k: e.transpose(out=pT[:, k, :], in_=src_bf[:, k * 128:(k + 1) * 128],
                                                   identity=self.ident[:]),
                 reads=[srcres, "ident"], writes=[f"nm_pT{i}"])
        if evac == "act":
            P.op("act", lambda e: e.copy(out=dst_ap, in_=pT[:]), reads=[f"nm_pT{i}"], writes=[dstres])
        else:
            P.op(evac, lambda e: e.tensor_copy(out=dst_ap, in_=pT[:]), reads=[f"nm_pT{i}"], writes=[dstres])

    def norm_T(self, x_ap, xres, gB_ap, dst_ap, dstres, evac="act"):
        P = self.P
        i = self.n % self.nbuf
        self.n += 1
        rs = self.rstd(x_ap, xres, i)
        yb = self.yb[i]
        P.op("dve", lambda e: e.scalar_tensor_tensor(out=yb[:], in0=x_ap, scalar=rs[:, 0:1], in1=gB_ap,
                                                     op0=ALU.mult, op1=ALU.mult),
             reads=[xres, f"nm_rs{i}", "gB"], writes=[f"nm_yb{i}"])
        self.transpose(yb, f"nm_yb{i}", dst_ap, dstres, i, evac)


class WLoader:
    def __init__(self, P, nstage=2, stage_elems=2048, cast_engines=("pool", "dve")):
        self.P = P
        self.stage = [P.sb(f"wst{i}", [128, stage_elems], F32) for i in range(nstage)]
        self.n = 0
        self.stage_elems = stage_elems
        self.cast_engines = cast_engines
        self.queues = ("sp", "pool")

    def load(self, w_ap, r0, nk, c0, ncols, dst, dstres, dk0=0, dc0=0):
        P = self.P
        kmax = max(1, self.stage_elems // ncols)
        k = 0
        while k < nk:
            kk = min(kmax, nk - k)
            i = self.n % len(self.stage)
            q = self.queues[self.n % len(self.queues)]
            ce = self.cast_engines[self.n % len(self.cast_engines)]
            self.n += 1
            st = self.stage[i]
            stv = st[:, 0:kk * ncols].rearrange("p (k c) -> p k c", k=kk)
            src = w_ap[r0 + k * 128:r0 + (k + kk) * 128, c0:c0 + ncols].rearrange("(k p) c -> p k c", p=128)
            P.dma(q, stv, src, writes=[f"wst{i}"])
            dv = dst[:, dk0 + k:dk0 + k + kk, dc0:dc0 + ncols]
            P.op(ce, lambda e, dv=dv, stv=stv: e.tensor_copy(out=dv, in_=stv), reads=[f"wst{i}"], writes=[dstres])
            k += kk


def load_gains(P, gains_ap, n):
    gB = P.sb("gB", [128, n, D], F32)
    for i in range(n):
        P.dma("sp", gB[:, i:i + 1, :], gains_ap[i:i + 1, :].partition_broadcast(128), writes=["gB"])
    return gB


def build_tl(mode, final):
    P = Prog()
    T = 2048
    NP = 2
    TP = T // NP
    NT = TP // 128
    hin = P.dram("hin", [T, D], F32, "ExternalInput")
    oin = P.dram("oin", [T, D], F32, "ExternalInput")
    gains = P.dram("gains", [4, D], F32, "ExternalInput")
    w_out = P.dram("w_out", [D, D], F32, "ExternalInput")
    w13 = P.dram("w13", [D, 2 * DFF], F32, "ExternalInput")
    w2 = P.dram("w2", [DFF, D], F32, "ExternalInput")
    if mode == "hgrn":
        w_og = P.dram("w_og", [D, D], F32, "ExternalInput")
    hout = P.dram("hout", [T, D], F32, "ExternalOutput")

    ident = make_ident(P)
    gB = load_gains(P, gains, 4)
    nm = Normer(P, ident)
    wl = WLoader(P)

    h = P.sb("h", [128, NT, D], F32)
    xT = P.sb("xT", [128, 8, TP], BF16)
    oT = P.sb("oT", [128, 8, TP], BF16)
    aT = P.sb("aT", [128, 11, TP], BF16)
    wsl = [P.sb(f"wsl{i}", [128, 8, 512], BF16) for i in range(2)]
    w2s = [P.sb(f"w2s{i}", [128, 11, 512], BF16) for i in range(2)]
    ot = [P.sb(f"ot{i}", [128, D], F32) for i in range(2)]
    sg = P.sb("sg", [128, D], F32)
    tmpf = P.sb("tmpf", [128, D], F32)
    onb = [P.sb(f"onb{i}", [128, D], BF16) for i in range(2)]
    sgl = [P.sb(f"sgl{i}", [128, 512], F32) for i in range(2)]
    pacc = [P.ps(f"pacc{i}", [128, 512], F32) for i in range(6)]
    nacc = [0]

    def acc():
        i = nacc[0] % 6
        nacc[0] += 1
        return pacc[i], f"pacc{i}"

    nsl = [0]

    def next_wsl():
        i = nsl[0] % 2
        nsl[0] += 1
        return wsl[i], f"wsl{i}"

    for ps_ in range(NP):
        t0 = ps_ * TP
        for t in range(NT):
            P.dma("sp", h[:, t, :], hin[t0 + t * 128:t0 + (t + 1) * 128, :], writes=[("h", t)])
        if mode == "hgrn":
            for t in range(NT):
                nm.norm_T(h[:, t, :], ("h", t), gB[:, 0, :], xT[:, :, t * 128:(t + 1) * 128], ("xT", t))
        wog_sl = [None, None]
        if mode == "hgrn":
            for cg in range(2):
                s, sres = next_wsl()
                wl.load(w_og, 0, 8, cg * 512, 512, s, sres)
                wog_sl[cg] = (s, sres)
        for t in range(NT):
            b = t % 2
            P.dma("pool", ot[b][:], oin[t0 + t * 128:t0 + (t + 1) * 128, :], writes=[f"ot{b}"])
            if mode == "hgrn":
                for cg in range(2):
                    s, sres = wog_sl[cg]
                    pa, pres = acc()
                    for k in range(8):
                        P.op("pe", lambda e, k=k, pa=pa, s=s: e.matmul(pa[:], lhsT=xT[:, k, t * 128:(t + 1) * 128],
                                                                      rhs=s[:, k, :], start=(k == 0), stop=(k == 7)),
                             reads=[("xT", t), sres], writes=[pres])
                    P.op("act", lambda e, pa=pa, cg=cg: e.activation(out=sg[:, cg * 512:(cg + 1) * 512], in_=pa[:],
                                                                     func=AF.Sigmoid),
                         reads=[pres], writes=["sg"])
                i = nm.n % nm.nbuf
                nm.n += 1
                rs = nm.rstd(ot[b][:], f"ot{b}", i)
                P.op("dve", lambda e, rs=rs: e.scalar_tensor_tensor(out=tmpf[:], in0=ot[b][:], scalar=rs[:, 0:1],
                                                                    in1=gB[:, 1, :], op0=ALU.mult, op1=ALU.mult),
                     reads=[f"ot{b}", f"nm_rs{i}", "gB"], writes=["tmpf"])
                P.op("dve", lambda e: e.tensor_tensor(out=onb[b][:], in0=tmpf[:], in1=sg[:], op=ALU.mult),
                     reads=["tmpf", "sg"], writes=[f"onb{b}"])
            else:
                i = nm.n % nm.nbuf
                nm.n += 1
                P.op("act", lambda e: e.copy(out=onb[b][:], in_=ot[b][:]), reads=[f"ot{b}"], writes=[f"onb{b}"])
            nm.transpose(onb[b], f"onb{b}", oT[:, :, t * 128:(t + 1) * 128], ("oT", t), i, evac="act")
        for cg in range(2):
            s, sres = next_wsl()
            wl.load(w_out, 0, 8, cg * 512, 512, s, sres)
            for t in range(NT):
                pa, pres = acc()
                for k in range(8):
                    P.op("pe", lambda e, k=k, pa=pa, s=s: e.matmul(pa[:], lhsT=oT[:, k, t * 128:(t + 1) * 128],
                                                                  rhs=s[:, k, :], start=(k == 0), stop=(k == 7)),
                         reads=[("oT", t), sres], writes=[pres])
                hv = h[:, t, cg * 512:(cg + 1) * 512]
                P.op("dve", lambda e, pa=pa, hv=hv: e.tensor_tensor(out=hv, in0=hv, in1=pa[:], op=ALU.add),
                     reads=[pres, ("h", t)], writes=[("h", t)])
        for t in range(NT):
            nm.norm_T(h[:, t, :], ("h", t), gB[:, 2, :], xT[:, :, t * 128:(t + 1) * 128], ("xT", t))
        xT_all = [("xT", t) for t in range(NT)]
        for half in range(2):
            c0 = half * 11
            ci = 0
            while ci < 11:
                ncg = min(2, 11 - ci)
                s, sres = next_wsl()
                wl.load(w13, 0, 8, (c0 + ci) * 128, ncg * 128, s, sres, dc0=0)
                wl.load(w13, 0, 8, DFF + (c0 + ci) * 128, ncg * 128, s, sres, dc0=256)
                for fc in range(ncg):
                    for tg in range(TP // 512):
                        pg, pgres = acc()
                        pu, pures = acc()
                        for k in range(8):
                            P.op("pe", lambda e, k=k, pg=pg, s=s, fc=fc, tg=tg: e.matmul(
                                pg[:], lhsT=s[:, k, fc * 128:(fc + 1) * 128], rhs=xT[:, k, tg * 512:(tg + 1) * 512],
                                start=(k == 0), stop=(k == 7)), reads=xT_all[tg * 4:tg * 4 + 4] + [sres], writes=[pgres])
                        for k in range(8):
                            P.op("pe", lambda e, k=k, pu=pu, s=s, fc=fc, tg=tg: e.matmul(
                                pu[:], lhsT=s[:, k, 256 + fc * 128:256 + (fc + 1) * 128],
                                rhs=xT[:, k, tg * 512:(tg + 1) * 512],
                                start=(k == 0), stop=(k == 7)), reads=xT_all[tg * 4:tg * 4 + 4] + [sres], writes=[pures])
                        sb_ = nacc[0] % 2
                        P.op("act", lambda e, pg=pg, sb_=sb_: e.activation(out=sgl[sb_][:], in_=pg[:], func=AF.Silu),
                             reads=[pgres], writes=[f"sgl{sb_}"])
                        av = aT[:, ci + fc, tg * 512:(tg + 1) * 512]
                        P.op("dve", lambda e, pu=pu, sb_=sb_, av=av: e.tensor_tensor(out=av, in0=sgl[sb_][:], in1=pu[:],
                                                                                     op=ALU.mult),
                             reads=[pures, f"sgl{sb_}"], writes=[("aT", ci + fc, tg)])
                ci += ncg
            aT_all = [("aT", c, tg) for c in range(11) for tg in range(TP // 512)]
            for cg in range(2):
                wi = (half * 2 + cg) % 2
                wl.load(w2, c0 * 128, 11, cg * 512, 512, w2s[wi], f"w2s{wi}")
                for t in range(NT):
                    pa, pres = acc()
                    for c in range(11):
                        P.op("pe", lambda e, c=c, pa=pa, wi=wi: e.matmul(pa[:], lhsT=aT[:, c, t * 128:(t + 1) * 128],
                                                                        rhs=w2s[wi][:, c, :], start=(c == 0), stop=(c == 10)),
                             reads=[("aT", c, t // 4) for c in range(11)] + [f"w2s{wi}"], writes=[pres])
                    hv = h[:, t, cg * 512:(cg + 1) * 512]
                    P.op("dve", lambda e, pa=pa, hv=hv: e.tensor_tensor(out=hv, in0=hv, in1=pa[:], op=ALU.add),
                         reads=[pres, ("h", t)], writes=[("h", t)])
        for t in range(NT):
            if final:
                i = nm.n % nm.nbuf
                nm.n += 1
                rs = nm.rstd(h[:, t, :], ("h", t), i)
                b = t % 2
                P.op("dve", lambda e, rs=rs, b=b: e.scalar_tensor_tensor(out=ot[b][:], in0=h[:, t, :], scalar=rs[:, 0:1],
                                                                         in1=gB[:, 3, :], op0=ALU.mult, op1=ALU.mult),
                     reads=[("h", t), f"nm_rs{i}", "gB"], writes=[f"ot{b}"])
                P.dma("sp", hout[t0 + t * 128:t0 + (t + 1) * 128, :], ot[b][:], reads=[f"ot{b}"], writes=[("hout", ps_, t)])
            else:
                P.dma("sp", hout[t0 + t * 128:t0 + (t + 1) * 128, :], h[:, t, :], reads=[("h", t)],
                      writes=[("hout", ps_, t)])
    return P.finish()


_CACHE = {}


def _get(name, builder):
    if name not in _CACHE:
        _CACHE[name] = builder()
    return _CACHE[name]


def run_tl(mode, final, hin, oin, gains, w_out, w13, w2, w_og=None):
    nc = _get(("tl", mode, final), lambda: build_tl(mode, final))
    maps = []
    for c in range(8):
        m = {"hin": np.ascontiguousarray(hin[c * 2048:(c + 1) * 2048]),
             "oin": np.ascontiguousarray(oin[c * 2048:(c + 1) * 2048]),
             "gains": gains, "w_out": w_out, "w13": w13, "w2": w2}
        if mode == "hgrn":
            m["w_og"] = w_og
        maps.append(m)
    res = run_bass_kernel_spmd(nc, maps, core_ids=list(range(8)))
    return np.concatenate([res.results[c]["hout"] for c in range(8)], axis=0)


def build_hgrn(NTILE=SEQ // 128):
    P = Prog()
    x = P.dram("x", [SEQ, D], F32, "ExternalInput")
    gains = P.dram("gains", [1, D], F32, "ExternalInput")
    wq = P.dram("wq", [D, 256], F32, "ExternalInput")
    wf = P.dram("wf", [D, 256], F32, "ExternalInput")
    wi = P.dram("wi", [D, 256], F32, "ExternalInput")
    lbl = P.dram("lbl", [128, 4], F32, "ExternalInput")
    oout = P.dram("o", [SEQ, 256], F32, "ExternalOutput")

    ident = make_ident(P)
    gB = load_gains(P, gains, 1)
    nm = Normer(P, ident)
    wl = WLoader(P)
    wq_b = P.sb("wq_b", [128, 8, 256], BF16)
    wf_b = P.sb("wf_b", [128, 8, 256], BF16)
    wi_b = P.sb("wi_b", [128, 8, 256], BF16)
    wl.load(wq, 0, 8, 0, 256, wq_b, "wq_b")
    wl.load(wf, 0, 8, 0, 256, wf_b, "wf_b")
    wl.load(wi, 0, 8, 0, 256, wi_b, "wi_b")

    lbt = P.sb("lbt", [128, 4], F32)
    lb = P.sb("lb", [128, 2], F32)
    oml = P.sb("oml", [128, 2], F32)
    P.dma("sp", lbt[:], lbl[:, :], writes=["lbt"])
    P.op("dve", lambda e: e.tensor_tensor(out=lb[:], in0=lbt[:, 0:2], in1=lbt[:, 2:4], op=ALU.subtract),
         reads=["lbt"], writes=["lb"])
    P.op("act", lambda e: e.activation(out=lb[:], in_=lb[:], func=AF.Sigmoid), reads=["lb"], writes=["lb"])
    P.op("dve", lambda e: e.tensor_scalar(out=oml[:], in0=lb[:], scalar1=-1.0, scalar2=1.0, op0=ALU.mult, op1=ALU.add),
         reads=["lb"], writes=["oml"])

    onesf = P.sb("onesf", [128, 2, 128], F32)
    mask = P.sb("mask", [128, 2, 128], F32)
    P.op("pool", lambda e: e.memset(onesf[:], 1.0), writes=["onesf"])
    P.op("pool", lambda e: e.affine_select(out=mask[:], in_=onesf[:], pattern=[[0, 2], [1, 128]],
                                            compare_op=ALU.is_ge, fill=0.0, base=0, channel_multiplier=-1),
         reads=["onesf"], writes=["mask"])

    state = P.sb("state", [128, 2, 128], F32)
    state_b = P.sb("state_b", [128, 2, 128], BF16)
    P.op("dve", lambda e: e.memset(state[:], 0.0), writes=["state"])
    P.op("dve", lambda e: e.memset(state_b[:], 0.0), writes=["state_b"])

    NB = 2
    xt = [P.sb(f"xt{i}", [128, D], F32) for i in range(NB)]
    yT = [P.sb(f"yT{i}", [128, 8, 128], BF16) for i in range(NB)]
    qs = [P.sb(f"qs{i}", [128, 2, 128], F32) for i in range(NB)]
    fg = [P.sb(f"fg{i}", [128, 2, 128], F32) for i in range(NB)]
    lf = [P.sb(f"lf{i}", [128, 2, 128], F32) for i in range(NB)]
    kk = [P.sb(f"kk{i}", [128, 2, 128], F32) for i in range(NB)]
    cum = [P.sb(f"cum{i}", [128, 2, 128], F32) for i in range(NB)]
    ex = [P.sb(f"ex{i}", [128, 4, 2, 128], F32) for i in range(NB)]
    sc4 = [P.sb(f"sc4{i}", [128, 4], F32) for i in range(NB)]
    qc = [P.sb(f"qc{i}", [128, 2, 128], BF16) for i in range(NB)]
    qd = [P.sb(f"qd{i}", [128, 2, 128], BF16) for i in range(NB)]
    kc = [P.sb(f"kc{i}", [128, 2, 128], BF16) for i in range(NB)]
    kh = [P.sb(f"kh{i}", [128, 2, 128], BF16) for i in range(NB)]
    khT = [P.sb(f"khT{i}", [128, 2, 128], BF16) for i in range(NB)]
    ib = [P.sb(f"ib{i}", [128, 256], BF16) for i in range(NB)]
    scm = [P.sb(f"scm{i}", [128, 2, 128], BF16) for i in range(NB)]
    osb = [P.sb(f"osb{i}", [128, 256], F32) for i in range(NB)]
    pQF = P.ps("pQF", [128, 4, 128], F32)
    pI = P.ps("pI", [128, 512], F32)[:, 0:256]
    pK = P.ps("pK", [128, 1024], BF16)[:, 0:256].rearrange("p (a b) -> p a b", a=2)
    pS = P.ps("pS", [128, 512], F32)[:, 0:256].rearrange("p (a b) -> p a b", a=2)
    pO = P.ps("pO", [128, 512], F32)[:, 0:256]
    pU = P.ps("pU", [128, 512], F32)[:, 0:256].rearrange("p (a b) -> p a b", a=2)

    for t in range(NTILE):
        b = t % NB
        R = lambda n: f"{n}{b}"
        P.dma("sp", xt[b][:], x[t * 128:(t + 1) * 128, :], writes=[R("xt")])
        nm.norm_T(xt[b][:], R("xt"), gB[:, 0, :], yT[b][:], R("yT"))
        for j, w in enumerate((wq_b, wf_b)):
            wres = "wq_b" if j == 0 else "wf_b"
            for hh in range(2):
                for k in range(8):
                    P.op("pe", lambda e, k=k, w=w, hh=hh, j=j: e.matmul(
                        pQF[:, j * 2 + hh, :], lhsT=w[:, k, hh * 128:(hh + 1) * 128], rhs=yT[b][:, k, :],
                        start=(k == 0), stop=(k == 7)), reads=[R("yT"), wres], writes=["pQF"])
        for k in range(8):
            P.op("pe", lambda e, k=k: e.matmul(pI, lhsT=yT[b][:, k, :], rhs=wi_b[:, k, :],
                                               start=(k == 0), stop=(k == 7)), reads=[R("yT"), "wi_b"], writes=["pI"])
        P.op("act", lambda e: e.activation(out=qs[b][:], in_=pQF[:, 0:2, :], func=AF.Silu), reads=["pQF"], writes=[R("qs")])
        P.op("act", lambda e: e.activation(out=fg[b][:], in_=pQF[:, 2:4, :], func=AF.Sigmoid), reads=["pQF"], writes=[R("fg")])
        P.op("act", lambda e: e.copy(out=ib[b][:], in_=pI), reads=["pI"], writes=[R("ib")])
        for hh in range(2):
            P.op("dve", lambda e, hh=hh: e.tensor_scalar(out=fg[b][:, hh, :], in0=fg[b][:, hh, :], scalar1=oml[:, hh:hh + 1],
                                                         scalar2=lb[:, hh:hh + 1], op0=ALU.mult, op1=ALU.add),
                 reads=[R("fg"), "oml", "lb"], writes=[R("fg")])
        P.op("act", lambda e: e.activation(out=lf[b][:], in_=fg[b][:], func=AF.Ln), reads=[R("fg")], writes=[R("lf")])
        P.op("pool", lambda e: e.tensor_scalar(out=kk[b][:], in0=fg[b][:], scalar1=-1.0, scalar2=1.0, op0=ALU.mult, op1=ALU.add),
             reads=[R("fg")], writes=[R("kk")])
        for hh in range(2):
            P.op("dve", lambda e, hh=hh: e.tensor_tensor_scan(out=cum[b][:, hh, :], data0=onesf[:, 0, :], data1=lf[b][:, hh, :],
                                                              initial=0.0, op0=ALU.mult, op1=ALU.add),
                 reads=[R("lf"), "onesf"], writes=[R("cum")])
        P.op("dve", lambda e: e.tensor_scalar(out=sc4[b][:, 0:2], in0=cum[b][:, :, 63], scalar1=-1.0, scalar2=None, op0=ALU.mult),
             reads=[R("cum")], writes=[R("sc4")])
        P.op("act", lambda e: e.activation(out=sc4[b][:, 2:4], in_=cum[b][:, :, 127], func=AF.Exp), reads=[R("cum")], writes=[R("sc4")])
        for hh in range(2):
            P.op("act", lambda e, hh=hh: e.activation(out=ex[b][:, 0, hh, :], in_=cum[b][:, hh, :], func=AF.Exp,
                                                      bias=sc4[b][:, hh:hh + 1], scale=1.0),
                 reads=[R("cum"), R("sc4")], writes=[R("ex")])
            P.op("act", lambda e, hh=hh: e.activation(out=ex[b][:, 1, hh, :], in_=cum[b][:, hh, :], func=AF.Exp,
                                                      bias=cum[b][:, hh, 63:64], scale=-1.0),
                 reads=[R("cum")], writes=[R("ex")])
            P.op("act", lambda e, hh=hh: e.activation(out=ex[b][:, 3, hh, :], in_=cum[b][:, hh, :], func=AF.Exp,
                                                      bias=cum[b][:, hh, 127:128], scale=-1.0),
                 reads=[R("cum")], writes=[R("ex")])
        P.op("act", lambda e: e.activation(out=ex[b][:, 2, :, :], in_=cum[b][:], func=AF.Exp), reads=[R("cum")], writes=[R("ex")])
        P.op("dve", lambda e: e.tensor_tensor(out=qc[b][:], in0=qs[b][:], in1=ex[b][:, 0, :, :], op=ALU.mult),
             reads=[R("qs"), R("ex")], writes=[R("qc")])
        P.op("pool", lambda e: e.tensor_tensor(out=qd[b][:], in0=qs[b][:], in1=ex[b][:, 2, :, :], op=ALU.mult),
             reads=[R("qs"), R("ex")], writes=[R("qd")])
        P.op("dve", lambda e: e.tensor_tensor(out=kc[b][:], in0=kk[b][:], in1=ex[b][:, 1, :, :], op=ALU.mult),
             reads=[R("kk"), R("ex")], writes=[R("kc")])
        P.op("pool", lambda e: e.tensor_tensor(out=kh[b][:], in0=kk[b][:], in1=ex[b][:, 3, :, :], op=ALU.mult),
             reads=[R("kk"), R("ex")], writes=[R("kh")])
        for hh in range(2):
            P.op("pe", lambda e, hh=hh: e.transpose(out=pK[:, hh, :], in_=kh[b][:, hh, :], identity=ident[:]),
                 reads=[R("kh"), "ident"], writes=["pK"])
        P.op("act", lambda e: e.copy(out=khT[b][:], in_=pK), reads=["pK"], writes=[R("khT")])
        for hh in range(2):
            P.op("pe", lambda e, hh=hh: e.matmul(pS[:, hh, :], lhsT=kc[b][:, hh, :], rhs=qc[b][:, hh, :], start=True, stop=True),
                 reads=[R("kc"), R("qc")], writes=["pS"])
        P.op("dve", lambda e: e.tensor_tensor(out=scm[b][:], in0=pS, in1=mask[:], op=ALU.mult),
             reads=["pS", "mask"], writes=[R("scm")])
        for hh in range(2):
            P.op("pe", lambda e, hh=hh: e.matmul(pO[:, hh * 128:(hh + 1) * 128], lhsT=scm[b][:, hh, :],
                                                 rhs=ib[b][:, hh * 128:(hh + 1) * 128], start=True, stop=False),
                 reads=[R("scm"), R("ib")], writes=["pO"])
            P.op("pe", lambda e, hh=hh: e.matmul(pO[:, hh * 128:(hh + 1) * 128], lhsT=qd[b][:, hh, :],
                                                 rhs=state_b[:, hh, :], start=False, stop=True),
                 reads=[R("qd"), "state_b"], writes=["pO"])
        P.op("act", lambda e: e.copy(out=osb[b][:], in_=pO), reads=["pO"], writes=[R("osb")])
        P.dma("sp", oout[t * 128:(t + 1) * 128, :], osb[b][:], reads=[R("osb")], writes=[("o", t)])
        for hh in range(2):
            P.op("pe", lambda e, hh=hh: e.matmul(pU[:, hh, :], lhsT=khT[b][:, hh, :], rhs=ib[b][:, hh * 128:(hh + 1) * 128],
                                                 start=True, stop=True),
                 reads=[R("khT"), R("ib")], writes=["pU"])
        for hh in range(2):
            P.op("dve", lambda e, hh=hh: e.scalar_tensor_tensor(out=state[:, hh, :], in0=state[:, hh, :],
                                                                scalar=sc4[b][:, 2 + hh:3 + hh], in1=pU[:, hh, :],
                                                                op0=ALU.mult, op1=ALU.add),
                 reads=["state", R("sc4"), "pU"], writes=["state"])
        P.op("pool", lambda e: e.tensor_copy(out=state_b[:], in_=state[:]), reads=["state"], writes=["state_b"])
    return P.finish()


def run_hgrn(x, g_mix, w_in, lb_logits):
    nc = _get("hgrn", build_hgrn)
    maps = []
    for c in range(8):
        b, hp = c // 4, c % 4
        cs = slice(hp * 256, (hp + 1) * 256)
        lbl = lb_logits[:, cs].reshape(2, 2, 128).transpose(2, 0, 1).reshape(128, 4)
        maps.append({"x": np.ascontiguousarray(x[b]), "gains": np.ascontiguousarray(g_mix[None, :]),
                     "wq": np.ascontiguousarray(w_in[:, cs]),
                     "wf": np.ascontiguousarray(w_in[:, 1024 + hp * 256:1024 + (hp + 1) * 256]),
                     "wi": np.ascontiguousarray(w_in[:, 2048 + hp * 256:2048 + (hp + 1) * 256]),
                     "lbl": np.ascontiguousarray(lbl)})
    res = run_bass_kernel_spmd(nc, maps, core_ids=list(range(8)))
    o = np.empty((2, SEQ, D), np.float32)
    for c in range(8):
        b, hp = c // 4, c % 4
        o[b, :, hp * 256:(hp + 1) * 256] = res.results[c]["o"]
    return o.reshape(2 * SEQ, D)


GL = 1536
MNEG = -30000.0


def t5_bucket_np(dist):
    n = np.maximum(dist, 0)
    nf = np.maximum(n, 16).astype(np.float32)
    large = 16 + (np.log(nf / np.float32(16)) / np.float32(np.log(64.0)) * np.float32(16)).astype(np.int32)
    large = np.minimum(large, 31)
    return np.where(n < 16, n, large)


def moba_consts():
    i = np.arange(GL)
    bk = t5_bucket_np(i - 255)
    oh = np.zeros((32, GL), np.float32)
    valid = i >= 255
    oh[bk[valid], i[valid]] = 1.0
    neg = np.zeros((128, 256), np.float32)
    neg[:, :255] = MNEG
    return oh, neg


def build_moba(S=SEQ, stop=99, skip=()):
    P = Prog()
    nc = P.nc
    NTILE = S // 128
    NBLK = S // 256
    SCALE = 128 ** -0.5
    x = P.dram("x", [S, D], F32, "ExternalInput")
    gains = P.dram("gains", [1, D], F32, "ExternalInput")
    wq = P.dram("wq", [D, 256], F32, "ExternalInput")
    wk = P.dram("wk", [D, 256], F32, "ExternalInput")
    wv = P.dram("wv", [D, 256], F32, "ExternalInput")
    tab = P.dram("tab", [32, 2], F32, "ExternalInput")
    oh = P.dram("oh", [32, GL], F32, "ExternalInput")
    negrow = P.dram("negrow", [128, 256], F32, "ExternalInput")
    oout = P.dram("o", [S, 256], F32, "ExternalOutput")
    scr_t = [nc.dram_tensor(f"scr{h}", [128, GL], F32, kind="Internal") for h in range(2)]

    ident = make_ident(P)
    gB = load_gains(P, gains, 1)
    nm = Normer(P, ident)
    wl = WLoader(P, stage_elems=1024)
    wq_b = P.sb("wq_b", [128, 8, 256], BF16)
    wk_b = P.sb("wk_b", [128, 8, 256], BF16)
    wv_b = P.sb("wv_b", [128, 8, 256], BF16)
    wl.load(wq, 0, 8, 0, 256, wq_b, "wq_b")
    wl.load(wk, 0, 8, 0, 256, wk_b, "wk_b")
    wl.load(wv, 0, 8, 0, 256, wv_b, "wv_b")

    QT = P.sb("QT", [128, 2, S], BF16)
    KT = P.sb("KT", [128, 2, S], BF16)
    VA = P.sb("VA", [128, NTILE, 2, 132], BF16)
    sel = P.sb("sel", [128, NTILE, 2, 32], F32)
    BT = P.sb("BT", [128, 2, 5, 2, 256], F32)
    c31 = P.sb("c31", [128, 2], F32)
    ksum = P.sb("ksum", [128, 2, NTILE], F32)
    kmT = P.sb("kmT", [128, 2, 32], F32)
    kmT_b = P.sb("kmT_b", [128, 2, 32], BF16)
    maskadd = P.sb("maskadd", [128, 32, 32], F32)
    gm = P.sb("gm", [128, 2, 32], F32)
    top8 = P.sb("top8", [128, 2, 8], F32)
    xt = [P.sb(f"xt{i}", [128, D], F32) for i in range(2)]
    yT = [P.sb(f"yT{i}", [128, 8, 128], BF16) for i in range(2)]
    tmpS = [P.sb(f"tmpS{i}", [128, 2, 256], F32) for i in range(2)]
    PT = [P.sb(f"PT{i}", [128, 2, 256], BF16) for i in range(2)]
    acc = [P.sb(f"acc{i}", [128, 2, 132], F32) for i in range(2)]
    osb = [P.sb(f"osb{i}", [128, 2, 128], F32) for i in range(2)]
    rec = [P.sb(f"rec{i}", [128, 2], F32) for i in range(2)]
    bk = [P.ps(f"bk{i}", [128, 512], F32) for i in range(6)]

    ohs = P.sb("ohs", [32, GL], F32)
    tabs = P.sb("tabs", [32, 2], F32)
    tabc = P.sb("tabc", [32, 2, 128], F32)
    negs = tmpS
    gp = P.sb("gp", [128, GL], F32)
    ngs = P.sb("ngs", [128, 256], F32)
    P.dma("sp", ohs[:], oh[:, :], writes=["ohs"])
    P.dma("sp", tabs[:], tab[:, :], writes=["tabs"])
    P.dma("sp", ngs[:], negrow[:, :], writes=["ngs"])
    for h in range(2):
        if "bc" in skip:
            continue
        P.op("dve", lambda e, h=h: e.tensor_copy(out=tabc[:, h, :], in_=tabs[:, h:h + 1].to_broadcast([32, 128])),
             reads=["tabs"], writes=["tabc"])
    for h in range(2):
        if "bias" in skip:
            continue
        for c in range(GL // 512):
            P.op("pe", lambda e, h=h, c=c: e.matmul(bk[c][:], lhsT=tabc[:, h, :], rhs=ohs[:, c * 512:(c + 1) * 512],
                                                    start=True, stop=True), reads=["tabc", "ohs"], writes=[f"bk{c}"])
            P.op("dve", lambda e, c=c: e.tensor_copy(out=gp[:, c * 512:(c + 1) * 512], in_=bk[c][:]),
                 reads=[f"bk{c}"], writes=["gp"])
        P.op("dve", lambda e: e.tensor_tensor(out=gp[:, 0:256], in0=gp[:, 0:256], in1=ngs[:], op=ALU.add),
             reads=["gp", "ngs"], writes=["gp"])
        P.op("act", lambda e, h=h: e.copy(out=c31[:, h:h + 1], in_=gp[:, GL - 1:GL]), reads=["gp"], writes=["c31"])
        P.dma("sp", scr_t[h].ap(), gp[:], reads=["gp"], writes=[f"scr{h}"])
        for dJ in range(5):
            if "skew" in skip:
                continue
            for half in range(2):
                off = dJ * 256 - half * 128 + 255
                src = bass.AP(tensor=scr_t[h], offset=off, ap=[[GL - 1, 128], [1, 256]])
                P.dma("sp", BT[:, h, dJ, half, :], src, reads=[f"scr{h}"], writes=["BT"])
    P.op("pool", lambda e: e.memset(maskadd[:], 0.0), writes=["maskadd"])
    if "maskadd" not in skip:
      P.op("pool", lambda e: e.affine_select(out=maskadd[:], in_=maskadd[:], pattern=[[1, 32], [-1, 32]],
                                            compare_op=ALU.is_ge, fill=NEG, base=-1, channel_multiplier=0),
         reads=["maskadd"], writes=["maskadd"])
    if "vaones" not in skip:
        P.op("pool", lambda e: e.memset(VA[:, :, :, 128:129], 1.0), writes=["VAones"])

    if stop <= 0:
        return P.finish()
    pQK = bk[3][:].rearrange("p (a b) -> p a b", a=4)
    pV = bk[4][:, 0:256]
    for t in range(NTILE):
        b = t % 2
        P.dma("sp", xt[b][:], x[t * 128:(t + 1) * 128, :], writes=[f"xt{b}"])
        nm.norm_T(xt[b][:], f"xt{b}", gB[:, 0, :], yT[b][:], f"yT{b}")
        for j, w in enumerate((wq_b, wk_b)):
            wres = "wq_b" if j == 0 else "wk_b"
            for hh in range(2):
                for k in range(8):
                    P.op("pe", lambda e, k=k, w=w, hh=hh, j=j: e.matmul(
                        pQK[:, j * 2 + hh, :], lhsT=w[:, k, hh * 128:(hh + 1) * 128], rhs=yT[b][:, k, :],
                        start=(k == 0), stop=(k == 7)), reads=[f"yT{b}", wres], writes=["bk3"])
        for k in range(8):
            P.op("pe", lambda e, k=k: e.matmul(pV, lhsT=yT[b][:, k, :], rhs=wv_b[:, k, :],
                                               start=(k == 0), stop=(k == 7)), reads=[f"yT{b}", "wv_b"], writes=["bk4"])
        if "qtc" not in skip:
            P.op("act", lambda e: e.copy(out=QT[:, :, t * 128:(t + 1) * 128], in_=pQK[:, 0:2, :]), reads=["bk3"], writes=[("QT", t)])
        if "ktc" not in skip:
            P.op("act", lambda e: e.copy(out=KT[:, :, t * 128:(t + 1) * 128], in_=pQK[:, 2:4, :]), reads=["bk3"], writes=[("KT", t)])
        if "ksum" not in skip:
            P.op("dve", lambda e: e.tensor_reduce(out=ksum[:, :, t], in_=KT[:, :, t * 128:(t + 1) * 128], axis=AX.X, op=ALU.add),
                 reads=[("KT", t)], writes=["ksum"])
        if "vac" not in skip:
            P.op("act", lambda e: e.copy(out=VA[:, t, :, 0:128], in_=pV.rearrange("p (a b) -> p a b", a=2)),
                 reads=["bk4"], writes=[("VA", t)])
    if stop <= 1:
        return P.finish()
    ksv = ksum[:].rearrange("p h (n two) -> p h n two", two=2)
    P.op("dve", lambda e: e.tensor_tensor(out=kmT[:, :, 0:NBLK], in0=ksv[:, :, :, 0], in1=ksv[:, :, :, 1], op=ALU.add),
         reads=["ksum"], writes=["kmT"])
    if NBLK < 32:
        P.op("dve", lambda e: e.memset(kmT[:, :, NBLK:32], 0.0), writes=["kmT"])
    P.op("dve", lambda e: e.tensor_scalar(out=kmT_b[:], in0=kmT[:], scalar1=1.0 / 256, scalar2=None, op0=ALU.mult),
         reads=["kmT"], writes=["kmT_b"])

    pG = bk[5][:, 0:64].rearrange("p (a b) -> p a b", a=2)
    for t in range(NTILE):
        j = t // 2
        for hh in range(2):
            P.op("pe", lambda e, hh=hh: e.matmul(pG[:, hh, :], lhsT=QT[:, hh, t * 128:(t + 1) * 128], rhs=kmT_b[:, hh, :],
                                                 start=True, stop=True), reads=[("QT", t), "kmT_b"], writes=["bk5"])
        for hh in range(2):
            P.op("dve", lambda e, hh=hh: e.tensor_tensor(out=gm[:, hh, :], in0=pG[:, hh, :], in1=maskadd[:, j, :], op=ALU.add),
                 reads=["bk5", "maskadd"], writes=["gm"])
        for hh in range(2):
            P.op("dve", lambda e, hh=hh: e.max(out=top8[:, hh, :], in_=gm[:, hh, :]), reads=["gm"], writes=["top8"])
        for hh in range(2):
            P.op("dve", lambda e, hh=hh: e.tensor_scalar(out=sel[:, t, hh, :], in0=gm[:, hh, :], scalar1=top8[:, hh, 2:3],
                                                         scalar2=None, op0=ALU.is_ge),
                 reads=["gm", "top8"], writes=[("sel", t)])

    if stop <= 2:
        return P.finish()
    it = 0
    for hh in range(2):
        for J in range(NBLK):
            ab = J % 2
            ares = f"acc{ab}"
            qres = [("QT", 2 * J), ("QT", 2 * J + 1)]
            for n in range(J, -1, -1):
                dJ = J - n
                sb_ = it % 2
                it += 1
                pS = bk[sb_][:].rearrange("p (a b) -> p a b", a=2)
                pO = bk[2 + sb_][:, 0:264].rearrange("p (a b) -> p a b", a=2)
                for half in range(2):
                    P.op("pe", lambda e, half=half: e.matmul(
                        pS[:, half, :], lhsT=KT[:, hh, n * 256 + half * 128:n * 256 + (half + 1) * 128],
                        rhs=QT[:, hh, J * 256:(J + 1) * 256], start=True, stop=True),
                        reads=qres + [("KT", 2 * n + half)], writes=[f"bk{sb_}"])
                if dJ <= 4:
                    P.op("dve", lambda e: e.scalar_tensor_tensor(out=tmpS[sb_][:], in0=pS, scalar=SCALE, in1=BT[:, hh, dJ, :, :],
                                                                 op0=ALU.mult, op1=ALU.add),
                         reads=[f"bk{sb_}", "BT"], writes=[f"tmpS{sb_}"])
                    P.op("act", lambda e: e.activation(out=PT[sb_][:], in_=tmpS[sb_][:], func=AF.Exp),
                         reads=[f"tmpS{sb_}"], writes=[f"PT{sb_}"])
                else:
                    P.op("act", lambda e: e.activation(out=PT[sb_][:], in_=pS, func=AF.Exp, bias=c31[:, hh:hh + 1], scale=SCALE),
                         reads=[f"bk{sb_}", "c31"], writes=[f"PT{sb_}"])
                for qt in range(2):
                    for half in range(2):
                        P.op("pe", lambda e, qt=qt, half=half: e.matmul(
                            pO[:, qt, 0:129], lhsT=PT[sb_][:, half, qt * 128:(qt + 1) * 128],
                            rhs=VA[:, 2 * n + half, hh, 0:129], start=(half == 0), stop=(half == 1)),
                            reads=[f"PT{sb_}", ("VA", 2 * n + half), "VAones"], writes=[f"bk{2 + sb_}"])
                for qt in range(2):
                    if dJ == 0:
                        P.op("dve", lambda e, qt=qt: e.tensor_copy(out=acc[ab][:, qt, 0:129], in_=pO[:, qt, 0:129]),
                             reads=[f"bk{2 + sb_}"], writes=[ares])
                    else:
                        P.op("dve", lambda e, qt=qt: e.scalar_tensor_tensor(
                            out=acc[ab][:, qt, 0:129], in0=pO[:, qt, 0:129], scalar=sel[:, 2 * J + qt, hh, n:n + 1],
                            in1=acc[ab][:, qt, 0:129], op0=ALU.mult, op1=ALU.add),
                            reads=[f"bk{2 + sb_}", ("sel", 2 * J + qt), ares], writes=[ares])
            P.op("dve", lambda e: e.reciprocal(out=rec[ab][:], in_=acc[ab][:, :, 128]), reads=[ares], writes=[f"rec{ab}"])
            for qt in range(2):
                P.op("dve", lambda e, qt=qt: e.tensor_scalar(out=osb[ab][:, qt, :], in0=acc[ab][:, qt, 0:128],
                                                             scalar1=rec[ab][:, qt:qt + 1], scalar2=None, op0=ALU.mult),
                     reads=[ares, f"rec{ab}"], writes=[f"osb{ab}"])
            P.dma("sp", oout[J * 256:(J + 1) * 256, hh * 128:(hh + 1) * 128].rearrange("(q p) d -> p q d", p=128),
                  osb[ab][:], reads=[f"osb{ab}"], writes=[("o", hh, J)])
    return P.finish()


def run_moba(h1, g_mix, w_in, rel_table):
    nc = _get("moba", build_moba)
    oh, neg = moba_consts()
    maps = []
    for c in range(8):
        b, hp = c // 4, c % 4
        cs = slice(hp * 256, (hp + 1) * 256)
        maps.append({"x": np.ascontiguousarray(h1[b * SEQ:(b + 1) * SEQ]), "gains": np.ascontiguousarray(g_mix[None, :]),
                     "wq": np.ascontiguousarray(w_in[:, cs]),
                     "wk": np.ascontiguousarray(w_in[:, 1024 + hp * 256:1024 + (hp + 1) * 256]),
                     "wv": np.ascontiguousarray(w_in[:, 2048 + hp * 256:2048 + (hp + 1) * 256]),
                     "tab": np.ascontiguousarray(rel_table[:, hp * 2:hp * 2 + 2]), "oh": oh, "negrow": neg})
    res = run_bass_kernel_spmd(nc, maps, core_ids=list(range(8)))
    o = np.empty((2, SEQ, D), np.float32)
    for c in range(8):
        b, hp = c // 4, c % 4
        o[b, :, hp * 256:(hp + 1) * 256] = res.results[c]["o"]
    return o.reshape(2 * SEQ, D)


def kernel(x, norm_mix, norm_ffn, hgrn_w_in, hgrn_lb_logits, hgrn_out_norm, hgrn_w_out,
           moba_w_in, moba_w_out, rel_bias_table, ffn_w13, ffn_w2, final_norm):
    f = lambda a: np.ascontiguousarray(np.asarray(a, dtype=np.float32))
    x = f(x)
    norm_mix, norm_ffn, final_norm = f(norm_mix), f(norm_ffn), f(final_norm)
    hgrn_w_in, hgrn_lb_logits, hgrn_out_norm, hgrn_w_out = f(hgrn_w_in), f(hgrn_lb_logits), f(hgrn_out_norm), f(hgrn_w_out)
    moba_w_in, moba_w_out, rel_bias_table = f(moba_w_in), f(moba_w_out), f(rel_bias_table)
    ffn_w13, ffn_w2 = f(ffn_w13), f(ffn_w2)
    xf = x.reshape(2 * SEQ, D)
    o0 = run_hgrn(x, norm_mix[0], hgrn_w_in[0], hgrn_lb_logits)
    gains0 = f(np.stack([norm_mix[0], hgrn_out_norm[0], norm_ffn[0], final_norm]))
    h1 = run_tl("hgrn", False, xf, o0, gains0, hgrn_w_out[0], ffn_w13[0], ffn_w2[0],
                w_og=f(hgrn_w_in[0][:, 3072:4096]))
    o1 = run_moba(h1, norm_mix[1], moba_w_in[0], rel_bias_table)
    gains1 = f(np.stack([norm_mix[1], hgrn_out_norm[0], norm_ffn[1], final_norm]))
    out = run_tl("moba", True, h1, o1, gains1, moba_w_out[0], ffn_w13[1], ffn_w2[1])
    return out.reshape(2, SEQ, D)
```

```python
from contextlib import ExitStack

import numpy as np
import concourse.bass as bass
import concourse.mybir as mybir
from concourse.bass_utils import run_bass_kernel_spmd

F32 = mybir.dt.float32
BF16 = mybir.dt.bfloat16
ALU = mybir.AluOpType
AF = mybir.ActivationFunctionType
AX = mybir.AxisListType

D = 1024
DFF = 2816
SEQ = 8192
EPS = 1e-6
NEG = -1.0e30


class Prog:
    def __init__(self):
        self.nc = bass.Bass("TRN2", target_bir_lowering=False)
        nc = self.nc
        self.es = ExitStack()
        self.E = {"pe": nc.tensor, "act": nc.scalar, "dve": nc.vector, "pool": nc.gpsimd, "sp": nc.sync}
        self.semh = {}
        self.cnt = {}
        self.NCH = 8
        self.dn = {"sp": 0, "pool": 0}
        for k in ["pe", "act", "dve", "pool"] + [f"dq_{q}_{i}" for q in ("sp", "pool") for i in range(self.NCH)]:
            self.semh[k] = self.es.enter_context(nc.semaphore(k))
            self.cnt[k] = 0
        self.lastw = {}
        self.readers = {}
        self.waited = {}
        self._uid = 0

    def sb(self, name, shape, dt):
        return self.es.enter_context(self.nc.sbuf_tensor(name, list(shape), dt))

    def ps(self, name, shape, dt):
        return self.es.enter_context(self.nc.psum_tensor(name, list(shape), dt))

    def dram(self, name, shape, dt, kind):
        return self.nc.dram_tensor(name, list(shape), dt, kind=kind).ap()

    def _deps(self, reads, writes):
        deps = {}

        def add(k, v):
            if deps.get(k, 0) < v:
                deps[k] = v

        for r in reads:
            if r in self.lastw:
                add(*self.lastw[r])
        for w in writes:
            if w in self.lastw:
                add(*self.lastw[w])
            for k, v in self.readers.get(w, {}).items():
                add(k, v)
        return deps

    def _wait(self, e, deps, skip=None):
        eng = self.E[e]
        for k, v in deps.items():
            if k == skip:
                continue
            if self.waited.get((e, k), 0) >= v:
                continue
            eng.wait_ge(self.semh[k], v)
            self.waited[(e, k)] = v

    def _commit(self, key, val, reads, writes):
        for r in reads:
            d = self.readers.setdefault(r, {})
            if d.get(key, 0) < val:
                d[key] = val
        for w in writes:
            self.lastw[w] = (key, val)
            self.readers[w] = {}

    def op(self, e, fn, reads=(), writes=()):
        deps = self._deps(reads, writes)
        self._wait(e, deps, skip=("pe" if e == "pe" else None))
        ins = fn(self.E[e])
        self.cnt[e] += 1
        ins.then_inc(self.semh[e], 1)
        self._commit(e, self.cnt[e], reads, writes)

    def dma(self, q, out, in_, reads=(), writes=()):
        key = f"dq_{q}_{self.dn[q] % self.NCH}"
        self.dn[q] += 1
        deps = self._deps(reads, writes)
        if self.cnt[key] > deps.get(key, 0):
            deps[key] = self.cnt[key]
        self._wait(q, deps)
        ins = self.E[q].dma_start(out=out, in_=in_)
        self.cnt[key] += 16
        ins.then_inc(self.semh[key], 16)
        self._commit(key, self.cnt[key], reads, writes)

    def finish(self):
        for k, v in self.cnt.items():
            if v > 0:
                self.E["sp"].wait_ge(self.semh[k], v)
        self.es.close()
        return self.nc


def make_ident(P, name="ident"):
    ones = P.sb(name + "_ones", [128, 128], BF16)
    ident = P.sb(name, [128, 128], BF16)
    P.op("pool", lambda e: e.memset(ones[:], 1.0), writes=[name + "_ones"])
    P.op("pool", lambda e: e.affine_select(out=ident[:], in_=ones[:], pattern=[[-1, 128]],
                                            compare_op=ALU.is_equal, fill=0.0, base=0,
                                            channel_multiplier=1),
         reads=[name + "_ones"], writes=[name])
    return ident


class Normer:
    def __init__(self, P, ident, nbuf=2, npt=2, lnexp=False):
        self.P = P
        self.ident = ident
        self.lnexp = lnexp
        self.junk = P.sb("nm_junk", [128, D], BF16)
        self.ss = [P.sb(f"nm_ss{i}", [128, 1], F32) for i in range(nbuf)]
        self.rs = [P.sb(f"nm_rs{i}", [128, 1], F32) for i in range(nbuf)]
        self.yb = [P.sb(f"nm_yb{i}", [128, D], BF16) for i in range(nbuf)]
        self.pT = [P.ps(f"nm_pT{i}", [128, 8, 128], BF16) for i in range(npt)]
        self.n = 0
        self.nbuf = nbuf
        self.npt = npt

    def stats(self, x_ap, xres, i):
        P = self.P
        ss = self.ss[i]
        P.op("dve", lambda e: e.memset(ss[:], 0.0), writes=[f"nm_ss{i}"])
        P.op("act", lambda e: e.activation(out=self.junk[:], in_=x_ap, func=AF.Square, accum_out=ss[:]),
             reads=[xres, f"nm_ss{i}"], writes=["nm_junk", f"nm_ss{i}"])

    def finish(self, i):
        P = self.P
        ss, rs = self.ss[i], self.rs[i]
        P.op("dve", lambda e: e.tensor_scalar(out=ss[:], in0=ss[:], scalar1=1.0 / D, scalar2=EPS,
                                              op0=ALU.mult, op1=ALU.add),
             reads=[f"nm_ss{i}"], writes=[f"nm_ss{i}"])
        if self.lnexp:
            P.op("act", lambda e: e.activation(out=ss[:], in_=ss[:], func=AF.Ln),
                 reads=[f"nm_ss{i}"], writes=[f"nm_ss{i}"])
            P.op("act", lambda e: e.activation(out=rs[:], in_=ss[:], func=AF.Exp, scale=-0.5),
                 reads=[f"nm_ss{i}"], writes=[f"nm_rs{i}"])
        else:
            P.op("act", lambda e: e.activation(out=ss[:], in_=ss[:], func=AF.Sqrt),
                 reads=[f"nm_ss{i}"], writes=[f"nm_ss{i}"])
            P.op("dve", lambda e: e.reciprocal(out=rs[:], in_=ss[:]),
                 reads=[f"nm_ss{i}"], writes=[f"nm_rs{i}"])
        return rs

    def rstd(self, x_ap, xres, i):
        self.stats(x_ap, xres, i)
        return self.finish(i)

    def scale(self, x_ap, xres, gB_ap, i):
        P = self.P
        yb, rs = self.yb[i], self.rs[i]
        P.op("dve", lambda e: e.scalar_tensor_tensor(out=yb[:], in0=x_ap, scalar=rs[:, 0:1], in1=gB_ap,
                                                     op0=ALU.mult, op1=ALU.mult),
             reads=[xres, f"nm_rs{i}", "gB"], writes=[f"nm_yb{i}"])
        return yb

    def transpose(self, src_bf, srcres, dst_ap, dstres, i, evac="act"):
        P = self.P
        j = i % self.npt
        pT = self.pT[j]
        for k in range(8):
            P.op("pe", lambda e, k=k: e.transpose(out=pT[:, k, :], in_=src_bf[:, k * 128:(k + 1) * 128],
                                                   identity=self.ident[:]),
                 reads=[srcres, "ident"], writes=[f"nm_pT{j}"])
        if evac == "act":
            P.op("act", lambda e: e.copy(out=dst_ap, in_=pT[:]), reads=[f"nm_pT{j}"], writes=[dstres])
        else:
            P.op(evac, lambda e: e.tensor_copy(out=dst_ap, in_=pT[:]), reads=[f"nm_pT{j}"], writes=[dstres])

    def norm_T(self, x_ap, xres, gB_ap, dst_ap, dstres, evac="act"):
        i = self.n % self.nbuf
        self.n += 1
        self.rstd(x_ap, xres, i)
        yb = self.scale(x_ap, xres, gB_ap, i)
        self.transpose(yb, f"nm_yb{i}", dst_ap, dstres, i, evac)


class WLoader:
    def __init__(self, P, nstage=2, stage_elems=2048, cast_engines=("pool", "dve"), direct=False):
        self.P = P
        self.direct = direct
        self.stage = [] if direct else [P.sb(f"wst{i}", [128, stage_elems], F32) for i in range(nstage)]
        self.n = 0
        self.stage_elems = stage_elems
        self.cast_engines = cast_engines
        self.queues = ("sp", "pool")

    def load(self, w_ap, r0, nk, c0, ncols, dst, dstres, dk0=0, dc0=0):
        P = self.P
        if self.direct:
            src = w_ap[r0:r0 + nk * 128, c0:c0 + ncols].rearrange("(k p) c -> p k c", p=128)
            P.dma("pool", dst[:, dk0:dk0 + nk, dc0:dc0 + ncols], src, writes=[dstres])
            return
        kmax = max(1, self.stage_elems // ncols)
        k = 0
        while k < nk:
            kk = min(kmax, nk - k)
            i = self.n % len(self.stage)
            q = self.queues[self.n % len(self.queues)]
            ce = self.cast_engines[self.n % len(self.cast_engines)]
            self.n += 1
            st = self.stage[i]
            stv = st[:, 0:kk * ncols].rearrange("p (k c) -> p k c", k=kk)
            src = w_ap[r0 + k * 128:r0 + (k + kk) * 128, c0:c0 + ncols].rearrange("(k p) c -> p k c", p=128)
            P.dma(q, stv, src, writes=[f"wst{i}"])
            dv = dst[:, dk0 + k:dk0 + k + kk, dc0:dc0 + ncols]
            P.op(ce, lambda e, dv=dv, stv=stv: e.tensor_copy(out=dv, in_=stv), reads=[f"wst{i}"], writes=[dstres])
            k += kk


def load_gains(P, gains_ap, n):
    gB = P.sb("gB", [128, n, D], F32)
    for i in range(n):
        P.dma("sp", gB[:, i:i + 1, :], gains_ap[i:i + 1, :].partition_broadcast(128), writes=["gB"])
    return gB


def build_tl(mode, final):
    P = Prog()
    T = 2048
    NP = 2
    TP = T // NP
    NT = TP // 128
    hin = P.dram("hin", [T, D], F32, "ExternalInput")
    oin = P.dram("oin", [T, D], F32, "ExternalInput")
    gains = P.dram("gains", [4, D], F32, "ExternalInput")
    w_out = P.dram("w_out", [D, D], F32, "ExternalInput")
    w13 = P.dram("w13", [D, 2 * DFF], F32, "ExternalInput")
    w2 = P.dram("w2", [DFF, D], F32, "ExternalInput")
    if mode == "hgrn":
        w_og = P.dram("w_og", [D, D], F32, "ExternalInput")
    hout = P.dram("hout", [T, D], F32, "ExternalOutput")

    ident = make_ident(P)
    gB = load_gains(P, gains, 4)
    nm = Normer(P, ident)
    wl = WLoader(P)

    h = P.sb("h", [128, NT, D], F32)
    xT = P.sb("xT", [128, 8, TP], BF16)
    oT = P.sb("oT", [128, 8, TP], BF16)
    aT = P.sb("aT", [128, 11, TP], BF16)
    wsl = [P.sb(f"wsl{i}", [128, 8, 512], BF16) for i in range(2)]
    w2s = [P.sb(f"w2s{i}", [128, 11, 512], BF16) for i in range(2)]
    ot = [P.sb(f"ot{i}", [128, D], F32) for i in range(2)]
    sg = P.sb("sg", [128, D], F32)
    tmpf = P.sb("tmpf", [128, D], F32)
    onb = [P.sb(f"onb{i}", [128, D], BF16) for i in range(2)]
    sgl = [P.sb(f"sgl{i}", [128, 512], F32) for i in range(2)]
    pacc = [P.ps(f"pacc{i}", [128, 512], F32) for i in range(6)]
    nacc = [0]

    def acc():
        i = nacc[0] % 6
        nacc[0] += 1
        return pacc[i], f"pacc{i}"

    nsl = [0]

    def next_wsl():
        i = nsl[0] % 2
        nsl[0] += 1
        return wsl[i], f"wsl{i}"

    for ps_ in range(NP):
        t0 = ps_ * TP
        for t in range(NT):
            P.dma("sp", h[:, t, :], hin[t0 + t * 128:t0 + (t + 1) * 128, :], writes=[("h", t)])
        if mode == "hgrn":
            for t in range(NT):
                nm.norm_T(h[:, t, :], ("h", t), gB[:, 0, :], xT[:, :, t * 128:(t + 1) * 128], ("xT", t))
        wog_sl = [None, None]
        if mode == "hgrn":
            for cg in range(2):
                s, sres = next_wsl()
                wl.load(w_og, 0, 8, cg * 512, 512, s, sres)
                wog_sl[cg] = (s, sres)
        for t in range(NT):
            b = t % 2
            P.dma("pool", ot[b][:], oin[t0 + t * 128:t0 + (t + 1) * 128, :], writes=[f"ot{b}"])
            if mode == "hgrn":
                for cg in range(2):
                    s, sres = wog_sl[cg]
                    pa, pres = acc()
                    for k in range(8):
                        P.op("pe", lambda e, k=k, pa=pa, s=s: e.matmul(pa[:], lhsT=xT[:, k, t * 128:(t + 1) * 128],
                                                                      rhs=s[:, k, :], start=(k == 0), stop=(k == 7)),
                             reads=[("xT", t), sres], writes=[pres])
                    P.op("act", lambda e, pa=pa, cg=cg: e.activation(out=sg[:, cg * 512:(cg + 1) * 512], in_=pa[:],
                                                                     func=AF.Sigmoid),
                         reads=[pres], writes=["sg"])
                i = nm.n % nm.nbuf
                nm.n += 1
                rs = nm.rstd(ot[b][:], f"ot{b}", i)
                P.op("dve", lambda e, rs=rs: e.scalar_tensor_tensor(out=tmpf[:], in0=ot[b][:], scalar=rs[:, 0:1],
                                                                    in1=gB[:, 1, :], op0=ALU.mult, op1=ALU.mult),
                     reads=[f"ot{b}", f"nm_rs{i}", "gB"], writes=["tmpf"])
                P.op("dve", lambda e: e.tensor_tensor(out=onb[b][:], in0=tmpf[:], in1=sg[:], op=ALU.mult),
                     reads=["tmpf", "sg"], writes=[f"onb{b}"])
            else:
                i = nm.n % nm.nbuf
                nm.n += 1
                P.op("act", lambda e: e.copy(out=onb[b][:], in_=ot[b][:]), reads=[f"ot{b}"], writes=[f"onb{b}"])
            nm.transpose(onb[b], f"onb{b}", oT[:, :, t * 128:(t + 1) * 128], ("oT", t), i, evac="act")
        for cg in range(2):
            s, sres = next_wsl()
            wl.load(w_out, 0, 8, cg * 512, 512, s, sres)
            for t in range(NT):
                pa, pres = acc()
                for k in range(8):
                    P.op("pe", lambda e, k=k, pa=pa, s=s: e.matmul(pa[:], lhsT=oT[:, k, t * 128:(t + 1) * 128],
                                                                  rhs=s[:, k, :], start=(k == 0), stop=(k == 7)),
                         reads=[("oT", t), sres], writes=[pres])
                hv = h[:, t, cg * 512:(cg + 1) * 512]
                P.op("dve", lambda e, pa=pa, hv=hv: e.tensor_tensor(out=hv, in0=hv, in1=pa[:], op=ALU.add),
                     reads=[pres, ("h", t)], writes=[("h", t)])
        for t in range(NT):
            nm.norm_T(h[:, t, :], ("h", t), gB[:, 2, :], xT[:, :, t * 128:(t + 1) * 128], ("xT", t))
        xT_all = [("xT", t) for t in range(NT)]
        for half in range(2):
            c0 = half * 11
            ci = 0
            while ci < 11:
                ncg = min(2, 11 - ci)
                s, sres = next_wsl()
                wl.load(w13, 0, 8, (c0 + ci) * 128, ncg * 128, s, sres, dc0=0)
                wl.load(w13, 0, 8, DFF + (c0 + ci) * 128, ncg * 128, s, sres, dc0=256)
                for fc in range(ncg):
                    for tg in range(TP // 512):
                        pg, pgres = acc()
                        pu, pures = acc()
                        for k in range(8):
                            P.op("pe", lambda e, k=k, pg=pg, s=s, fc=fc, tg=tg: e.matmul(
                                pg[:], lhsT=s[:, k, fc * 128:(fc + 1) * 128], rhs=xT[:, k, tg * 512:(tg + 1) * 512],
                                start=(k == 0), stop=(k == 7)), reads=xT_all[tg * 4:tg * 4 + 4] + [sres], writes=[pgres])
                        for k in range(8):
                            P.op("pe", lambda e, k=k, pu=pu, s=s, fc=fc, tg=tg: e.matmul(
                                pu[:], lhsT=s[:, k, 256 + fc * 128:256 + (fc + 1) * 128],
                                rhs=xT[:, k, tg * 512:(tg + 1) * 512],
                                start=(k == 0), stop=(k == 7)), reads=xT_all[tg * 4:tg * 4 + 4] + [sres], writes=[pures])
                        sb_ = nacc[0] % 2
                        P.op("act", lambda e, pg=pg, sb_=sb_: e.activation(out=sgl[sb_][:], in_=pg[:], func=AF.Silu),
                             reads=[pgres], writes=[f"sgl{sb_}"])
                        av = aT[:, ci + fc, tg * 512:(tg + 1) * 512]
                        P.op("dve", lambda e, pu=pu, sb_=sb_, av=av: e.tensor_tensor(out=av, in0=sgl[sb_][:], in1=pu[:],
                                                                                     op=ALU.mult),
                             reads=[pures, f"sgl{sb_}"], writes=[("aT", ci + fc, tg)])
                ci += ncg
            aT_all = [("aT", c, tg) for c in range(11) for tg in range(TP // 512)]
            for cg in range(2):
                wi = (half * 2 + cg) % 2
                wl.load(w2, c0 * 128, 11, cg * 512, 512, w2s[wi], f"w2s{wi}")
                for t in range(NT):
                    pa, pres = acc()
                    for c in range(11):
                        P.op("pe", lambda e, c=c, pa=pa, wi=wi: e.matmul(pa[:], lhsT=aT[:, c, t * 128:(t + 1) * 128],
                                                                        rhs=w2s[wi][:, c, :], start=(c == 0), stop=(c == 10)),
                             reads=[("aT", c, t // 4) for c in range(11)] + [f"w2s{wi}"], writes=[pres])
                    hv = h[:, t, cg * 512:(cg + 1) * 512]
                    P.op("dve", lambda e, pa=pa, hv=hv: e.tensor_tensor(out=hv, in0=hv, in1=pa[:], op=ALU.add),
                         reads=[pres, ("h", t)], writes=[("h", t)])
        for t in range(NT):
            if final:
                i = nm.n % nm.nbuf
                nm.n += 1
                rs = nm.rstd(h[:, t, :], ("h", t), i)
                b = t % 2
                P.op("dve", lambda e, rs=rs, b=b: e.scalar_tensor_tensor(out=ot[b][:], in0=h[:, t, :], scalar=rs[:, 0:1],
                                                                         in1=gB[:, 3, :], op0=ALU.mult, op1=ALU.mult),
                     reads=[("h", t), f"nm_rs{i}", "gB"], writes=[f"ot{b}"])
                P.dma("sp", hout[t0 + t * 128:t0 + (t + 1) * 128, :], ot[b][:], reads=[f"ot{b}"], writes=[("hout", ps_, t)])
            else:
                P.dma("sp", hout[t0 + t * 128:t0 + (t + 1) * 128, :], h[:, t, :], reads=[("h", t)],
                      writes=[("hout", ps_, t)])
    return P.finish()


_CACHE = {}


def _get(name, builder):
    if name not in _CACHE:
        _CACHE[name] = builder()
    return _CACHE[name]


def run_tl(mode, final, hin, oin, gains, w_out, w13, w2, w_og=None):
    nc = _get(("tl", mode, final), lambda: build_tl(mode, final))
    maps = []
    for c in range(8):
        m = {"hin": np.ascontiguousarray(hin[c * 2048:(c + 1) * 2048]),
             "oin": np.ascontiguousarray(oin[c * 2048:(c + 1) * 2048]),
             "gains": gains, "w_out": w_out, "w13": w13, "w2": w2}
        if mode == "hgrn":
            m["w_og"] = w_og
        maps.append(m)
    res = run_bass_kernel_spmd(nc, maps, core_ids=list(range(8)))
    return np.concatenate([res.results[c]["hout"] for c in range(8)], axis=0)


def build_hgrn(NTILE=SEQ // 128):
    P = Prog()
    x = P.dram("x", [SEQ, D], F32, "ExternalInput")
    gains = P.dram("gains", [1, D], F32, "ExternalInput")
    wq = P.dram("wq", [D, 256], F32, "ExternalInput")
    wf = P.dram("wf", [D, 256], F32, "ExternalInput")
    wi = P.dram("wi", [D, 256], F32, "ExternalInput")
    lbl = P.dram("lbl", [128, 4], F32, "ExternalInput")
    oout = P.dram("o", [SEQ, 256], F32, "ExternalOutput")

    ident = make_ident(P)
    gB = load_gains(P, gains, 1)
    nm = Normer(P, ident)
    wl = WLoader(P)
    wq_b = P.sb("wq_b", [128, 8, 256], BF16)
    wf_b = P.sb("wf_b", [128, 8, 256], BF16)
    wi_b = P.sb("wi_b", [128, 8, 256], BF16)
    wl.load(wq, 0, 8, 0, 256, wq_b, "wq_b")
    wl.load(wf, 0, 8, 0, 256, wf_b, "wf_b")
    wl.load(wi, 0, 8, 0, 256, wi_b, "wi_b")

    lbt = P.sb("lbt", [128, 4], F32)
    lb = P.sb("lb", [128, 2], F32)
    oml = P.sb("oml", [128, 2], F32)
    P.dma("sp", lbt[:], lbl[:, :], writes=["lbt"])
    P.op("dve", lambda e: e.tensor_tensor(out=lb[:], in0=lbt[:, 0:2], in1=lbt[:, 2:4], op=ALU.subtract),
         reads=["lbt"], writes=["lb"])
    P.op("act", lambda e: e.activation(out=lb[:], in_=lb[:], func=AF.Sigmoid), reads=["lb"], writes=["lb"])
    P.op("dve", lambda e: e.tensor_scalar(out=oml[:], in0=lb[:], scalar1=-1.0, scalar2=1.0, op0=ALU.mult, op1=ALU.add),
         reads=["lb"], writes=["oml"])

    onesf = P.sb("onesf", [128, 2, 128], F32)
    mask = P.sb("mask", [128, 2, 128], F32)
    P.op("pool", lambda e: e.memset(onesf[:], 1.0), writes=["onesf"])
    P.op("pool", lambda e: e.affine_select(out=mask[:], in_=onesf[:], pattern=[[0, 2], [1, 128]],
                                            compare_op=ALU.is_ge, fill=0.0, base=0, channel_multiplier=-1),
         reads=["onesf"], writes=["mask"])

    state = P.sb("state", [128, 2, 128], F32)
    state_b = P.sb("state_b", [128, 2, 128], BF16)
    P.op("dve", lambda e: e.memset(state[:], 0.0), writes=["state"])
    P.op("dve", lambda e: e.memset(state_b[:], 0.0), writes=["state_b"])

    NB = 2
    xt = [P.sb(f"xt{i}", [128, D], F32) for i in range(NB)]
    yT = [P.sb(f"yT{i}", [128, 8, 128], BF16) for i in range(NB)]
    qs = [P.sb(f"qs{i}", [128, 2, 128], F32) for i in range(NB)]
    fg = [P.sb(f"fg{i}", [128, 2, 128], F32) for i in range(NB)]
    lf = [P.sb(f"lf{i}", [128, 2, 128], F32) for i in range(NB)]
    kk = [P.sb(f"kk{i}", [128, 2, 128], F32) for i in range(NB)]
    cum = [P.sb(f"cum{i}", [128, 2, 128], F32) for i in range(NB)]
    ex = [P.sb(f"ex{i}", [128, 4, 2, 128], F32) for i in range(NB)]
    sc4 = [P.sb(f"sc4{i}", [128, 4], F32) for i in range(NB)]
    qc = [P.sb(f"qc{i}", [128, 2, 128], BF16) for i in range(NB)]
    qd = [P.sb(f"qd{i}", [128, 2, 128], BF16) for i in range(NB)]
    kc = [P.sb(f"kc{i}", [128, 2, 128], BF16) for i in range(NB)]
    kh = [P.sb(f"kh{i}", [128, 2, 128], BF16) for i in range(NB)]
    khT = [P.sb(f"khT{i}", [128, 2, 128], BF16) for i in range(NB)]
    ib = [P.sb(f"ib{i}", [128, 256], BF16) for i in range(NB)]
    scm = [P.sb(f"scm{i}", [128, 2, 128], BF16) for i in range(NB)]
    osb = [P.sb(f"osb{i}", [128, 256], F32) for i in range(NB)]
    pQF = P.ps("pQF", [128, 4, 128], F32)
    pI = P.ps("pI", [128, 512], F32)[:, 0:256]
    pK = P.ps("pK", [128, 1024], BF16)[:, 0:256].rearrange("p (a b) -> p a b", a=2)
    pS = P.ps("pS", [128, 512], F32)[:, 0:256].rearrange("p (a b) -> p a b", a=2)
    pO = P.ps("pO", [128, 512], F32)[:, 0:256]
    pU = P.ps("pU", [128, 512], F32)[:, 0:256].rearrange("p (a b) -> p a b", a=2)

    for t in range(NTILE):
        b = t % NB
        R = lambda n: f"{n}{b}"
        P.dma("sp", xt[b][:], x[t * 128:(t + 1) * 128, :], writes=[R("xt")])
        nm.norm_T(xt[b][:], R("xt"), gB[:, 0, :], yT[b][:], R("yT"))
        for j, w in enumerate((wq_b, wf_b)):
            wres = "wq_b" if j == 0 else "wf_b"
            for hh in range(2):
                for k in range(8):
                    P.op("pe", lambda e, k=k, w=w, hh=hh, j=j: e.matmul(
                        pQF[:, j * 2 + hh, :], lhsT=w[:, k, hh * 128:(hh + 1) * 128], rhs=yT[b][:, k, :],
                        start=(k == 0), stop=(k == 7)), reads=[R("yT"), wres], writes=["pQF"])
        for k in range(8):
            P.op("pe", lambda e, k=k: e.matmul(pI, lhsT=yT[b][:, k, :], rhs=wi_b[:, k, :],
                                               start=(k == 0), stop=(k == 7)), reads=[R("yT"), "wi_b"], writes=["pI"])
        P.op("act", lambda e: e.activation(out=qs[b][:], in_=pQF[:, 0:2, :], func=AF.Silu), reads=["pQF"], writes=[R("qs")])
        P.op("act", lambda e: e.activation(out=fg[b][:], in_=pQF[:, 2:4, :], func=AF.Sigmoid), reads=["pQF"], writes=[R("fg")])
        P.op("act", lambda e: e.copy(out=ib[b][:], in_=pI), reads=["pI"], writes=[R("ib")])
        for hh in range(2):
            P.op("dve", lambda e, hh=hh: e.tensor_scalar(out=fg[b][:, hh, :], in0=fg[b][:, hh, :], scalar1=oml[:, hh:hh + 1],
                                                         scalar2=lb[:, hh:hh + 1], op0=ALU.mult, op1=ALU.add),
                 reads=[R("fg"), "oml", "lb"], writes=[R("fg")])
        P.op("act", lambda e: e.activation(out=lf[b][:], in_=fg[b][:], func=AF.Ln), reads=[R("fg")], writes=[R("lf")])
        P.op("pool", lambda e: e.tensor_scalar(out=kk[b][:], in0=fg[b][:], scalar1=-1.0, scalar2=1.0, op0=ALU.mult, op1=ALU.add),
             reads=[R("fg")], writes=[R("kk")])
        for hh in range(2):
            P.op("dve", lambda e, hh=hh: e.tensor_tensor_scan(out=cum[b][:, hh, :], data0=onesf[:, 0, :], data1=lf[b][:, hh, :],
                                                              initial=0.0, op0=ALU.mult, op1=ALU.add),
                 reads=[R("lf"), "onesf"], writes=[R("cum")])
        P.op("dve", lambda e: e.tensor_scalar(out=sc4[b][:, 0:2], in0=cum[b][:, :, 63], scalar1=-1.0, scalar2=None, op0=ALU.mult),
             reads=[R("cum")], writes=[R("sc4")])
        P.op("act", lambda e: e.activation(out=sc4[b][:, 2:4], in_=cum[b][:, :, 127], func=AF.Exp), reads=[R("cum")], writes=[R("sc4")])
        for hh in range(2):
            P.op("act", lambda e, hh=hh: e.activation(out=ex[b][:, 0, hh, :], in_=cum[b][:, hh, :], func=AF.Exp,
                                                      bias=sc4[b][:, hh:hh + 1], scale=1.0),
                 reads=[R("cum"), R("sc4")], writes=[R("ex")])
            P.op("act", lambda e, hh=hh: e.activation(out=ex[b][:, 1, hh, :], in_=cum[b][:, hh, :], func=AF.Exp,
                                                      bias=cum[b][:, hh, 63:64], scale=-1.0),
                 reads=[R("cum")], writes=[R("ex")])
            P.op("act", lambda e, hh=hh: e.activation(out=ex[b][:, 3, hh, :], in_=cum[b][:, hh, :], func=AF.Exp,
                                                      bias=cum[b][:, hh, 127:128], scale=-1.0),
                 reads=[R("cum")], writes=[R("ex")])
        P.op("act", lambda e: e.activation(out=ex[b][:, 2, :, :], in_=cum[b][:], func=AF.Exp), reads=[R("cum")], writes=[R("ex")])
        P.op("dve", lambda e: e.tensor_tensor(out=qc[b][:], in0=qs[b][:], in1=ex[b][:, 0, :, :], op=ALU.mult),
             reads=[R("qs"), R("ex")], writes=[R("qc")])
        P.op("pool", lambda e: e.tensor_tensor(out=qd[b][:], in0=qs[b][:], in1=ex[b][:, 2, :, :], op=ALU.mult),
             reads=[R("qs"), R("ex")], writes=[R("qd")])
        P.op("dve", lambda e: e.tensor_tensor(out=kc[b][:], in0=kk[b][:], in1=ex[b][:, 1, :, :], op=ALU.mult),
             reads=[R("kk"), R("ex")], writes=[R("kc")])
        P.op("pool", lambda e: e.tensor_tensor(out=kh[b][:], in0=kk[b][:], in1=ex[b][:, 3, :, :], op=ALU.mult),
             reads=[R("kk"), R("ex")], writes=[R("kh")])
        for hh in range(2):
            P.op("pe", lambda e, hh=hh: e.transpose(out=pK[:, hh, :], in_=kh[b][:, hh, :], identity=ident[:]),
                 reads=[R("kh"), "ident"], writes=["pK"])
        P.op("act", lambda e: e.copy(out=khT[b][:], in_=pK), reads=["pK"], writes=[R("khT")])
        for hh in range(2):
            P.op("pe", lambda e, hh=hh: e.matmul(pS[:, hh, :], lhsT=kc[b][:, hh, :], rhs=qc[b][:, hh, :], start=True, stop=True),
                 reads=[R("kc"), R("qc")], writes=["pS"])
        P.op("dve", lambda e: e.tensor_tensor(out=scm[b][:], in0=pS, in1=mask[:], op=ALU.mult),
             reads=["pS", "mask"], writes=[R("scm")])
        for hh in range(2):
            P.op("pe", lambda e, hh=hh: e.matmul(pO[:, hh * 128:(hh + 1) * 128], lhsT=scm[b][:, hh, :],
                                                 rhs=ib[b][:, hh * 128:(hh + 1) * 128], start=True, stop=False),
                 reads=[R("scm"), R("ib")], writes=["pO"])
            P.op("pe", lambda e, hh=hh: e.matmul(pO[:, hh * 128:(hh + 1) * 128], lhsT=qd[b][:, hh, :],
                                                 rhs=state_b[:, hh, :], start=False, stop=True),
                 reads=[R("qd"), "state_b"], writes=["pO"])
        P.op("act", lambda e: e.copy(out=osb[b][:], in_=pO), reads=["pO"], writes=[R("osb")])
        P.dma("sp", oout[t * 128:(t + 1) * 128, :], osb[b][:], reads=[R("osb")], writes=[("o", t)])
        for hh in range(2):
            P.op("pe", lambda e, hh=hh: e.matmul(pU[:, hh, :], lhsT=khT[b][:, hh, :], rhs=ib[b][:, hh * 128:(hh + 1) * 128],
                                                 start=True, stop=True),
                 reads=[R("khT"), R("ib")], writes=["pU"])
        for hh in range(2):
            P.op("dve", lambda e, hh=hh: e.scalar_tensor_tensor(out=state[:, hh, :], in0=state[:, hh, :],
                                                                scalar=sc4[b][:, 2 + hh:3 + hh], in1=pU[:, hh, :],
                                                                op0=ALU.mult, op1=ALU.add),
                 reads=["state", R("sc4"), "pU"], writes=["state"])
        P.op("pool", lambda e: e.tensor_copy(out=state_b[:], in_=state[:]), reads=["state"], writes=["state_b"])
    return P.finish()


def run_hgrn(x, g_mix, w_in, lb_logits):
    nc = _get("hgrn", build_hgrn)
    maps = []
    for c in range(8):
        b, hp = c // 4, c % 4
        cs = slice(hp * 256, (hp + 1) * 256)
        lbl = lb_logits[:, cs].reshape(2, 2, 128).transpose(2, 0, 1).reshape(128, 4)
        maps.append({"x": np.ascontiguousarray(x[b]), "gains": np.ascontiguousarray(g_mix[None, :]),
                     "wq": np.ascontiguousarray(w_in[:, cs]),
                     "wf": np.ascontiguousarray(w_in[:, 1024 + hp * 256:1024 + (hp + 1) * 256]),
                     "wi": np.ascontiguousarray(w_in[:, 2048 + hp * 256:2048 + (hp + 1) * 256]),
                     "lbl": np.ascontiguousarray(lbl)})
    res = run_bass_kernel_spmd(nc, maps, core_ids=list(range(8)))
    o = np.empty((2, SEQ, D), np.float32)
    for c in range(8):
        b, hp = c // 4, c % 4
        o[b, :, hp * 256:(hp + 1) * 256] = res.results[c]["o"]
    return o.reshape(2 * SEQ, D)


GL = 1536
MNEG = -30000.0


def t5_bucket_np(dist):
    n = np.maximum(dist, 0)
    nf = np.maximum(n, 16).astype(np.float32)
    large = 16 + (np.log(nf / np.float32(16)) / np.float32(np.log(64.0)) * np.float32(16)).astype(np.int32)
    large = np.minimum(large, 31)
    return np.where(n < 16, n, large)


def moba_consts():
    i = np.arange(GL)
    bk = t5_bucket_np(i - 255)
    oh = np.zeros((32, GL), np.float32)
    valid = i >= 255
    oh[bk[valid], i[valid]] = 1.0
    neg = np.zeros((128, 256), np.float32)
    neg[:, :255] = MNEG
    return oh, neg


def build_moba(S=SEQ, stop=99, skip=()):
    P = Prog()
    nc = P.nc
    NTILE = S // 128
    NBLK = S // 256
    SCALE = 128 ** -0.5
    x = P.dram("x", [S, D], F32, "ExternalInput")
    gains = P.dram("gains", [1, D], F32, "ExternalInput")
    wq = P.dram("wq", [D, 256], F32, "ExternalInput")
    wk = P.dram("wk", [D, 256], F32, "ExternalInput")
    wv = P.dram("wv", [D, 256], F32, "ExternalInput")
    tab = P.dram("tab", [32, 2], F32, "ExternalInput")
    oh = P.dram("oh", [32, GL], F32, "ExternalInput")
    negrow = P.dram("negrow", [128, 256], F32, "ExternalInput")
    oout = P.dram("o", [S, 256], F32, "ExternalOutput")
    scr_t = [nc.dram_tensor(f"scr{h}", [128, GL], F32, kind="Internal") for h in range(2)]

    ident = make_ident(P)
    gB = load_gains(P, gains, 1)
    nm = Normer(P, ident, nbuf=4, npt=2)
    wl = WLoader(P, direct=True)
    wq_b = P.sb("wq_b", [128, 8, 256], BF16)
    wk_b = P.sb("wk_b", [128, 8, 256], BF16)
    wv_b = P.sb("wv_b", [128, 8, 256], BF16)
    wl.load(wq, 0, 8, 0, 256, wq_b, "wq_b")
    wl.load(wk, 0, 8, 0, 256, wk_b, "wk_b")
    wl.load(wv, 0, 8, 0, 256, wv_b, "wv_b")

    QT = P.sb("QT", [128, 2, S], BF16)
    KT = P.sb("KT", [128, 2, S], BF16)
    VA = P.sb("VA", [128, NTILE, 2, 132], BF16)
    sel = P.sb("sel", [128, NTILE, 2, 32], F32)
    BT = P.sb("BT", [128, 2, 5, 2, 256], F32)
    c31 = P.sb("c31", [128, 2], F32)
    ksum = P.sb("ksum", [128, 2, NTILE], F32)
    kmT = P.sb("kmT", [128, 2, 32], F32)
    kmT_b = P.sb("kmT_b", [128, 2, 32], BF16)
    maskadd = P.sb("maskadd", [128, 32, 32], F32)
    gm = P.sb("gm", [128, 2, 32], F32)
    top8 = P.sb("top8", [128, 2, 8], F32)
    xt = [P.sb(f"xt{i}", [128, D], F32) for i in range(4)]
    yT = [P.sb(f"yT{i}", [128, 8, 128], BF16) for i in range(3)]
    acc = [P.sb(f"acc{i}", [128, 2, 132], F32) for i in range(2)]
    osb = [P.sb(f"osb{i}", [128, 2, 128], F32) for i in range(2)]
    rec = [P.sb(f"rec{i}", [128, 2], F32) for i in range(2)]
    bk = [P.ps(f"bk{i}", [128, 512], F32) for i in range(6)]

    ohs = P.sb("ohs", [32, GL], F32)
    tabs = P.sb("tabs", [32, 2], F32)
    tabc = P.sb("tabc", [32, 2, 128], F32)
    gp = P.sb("gp", [128, GL], F32)
    ngs = P.sb("ngs", [128, 256], F32)
    P.dma("sp", ohs[:], oh[:, :], writes=["ohs"])
    P.dma("sp", tabs[:], tab[:, :], writes=["tabs"])
    P.dma("sp", ngs[:], negrow[:, :], writes=["ngs"])
    for h in range(2):
        if "bc" in skip:
            continue
        P.op("dve", lambda e, h=h: e.tensor_copy(out=tabc[:, h, :], in_=tabs[:, h:h + 1].to_broadcast([32, 128])),
             reads=["tabs"], writes=["tabc"])
    for h in range(2):
        if "bias" in skip:
            continue
        for c in range(GL // 512):
            P.op("pe", lambda e, h=h, c=c: e.matmul(bk[c][:], lhsT=tabc[:, h, :], rhs=ohs[:, c * 512:(c + 1) * 512],
                                                    start=True, stop=True), reads=["tabc", "ohs"], writes=[f"bk{c}"])
            P.op("dve", lambda e, c=c: e.tensor_copy(out=gp[:, c * 512:(c + 1) * 512], in_=bk[c][:]),
                 reads=[f"bk{c}"], writes=["gp"])
        P.op("dve", lambda e: e.tensor_tensor(out=gp[:, 0:256], in0=gp[:, 0:256], in1=ngs[:], op=ALU.add),
             reads=["gp", "ngs"], writes=["gp"])
        P.op("act", lambda e, h=h: e.copy(out=c31[:, h:h + 1], in_=gp[:, GL - 1:GL]), reads=["gp"], writes=["c31"])
        P.dma("sp", scr_t[h].ap(), gp[:], reads=["gp"], writes=[f"scr{h}"])
        for dJ in range(5):
            if "skew" in skip:
                continue
            for half in range(2):
                off = dJ * 256 - half * 128 + 255
                src = bass.AP(tensor=scr_t[h], offset=off, ap=[[GL - 1, 128], [1, 256]])
                P.dma("sp", BT[:, h, dJ, half, :], src, reads=[f"scr{h}"], writes=["BT"])
    P.op("pool", lambda e: e.memset(maskadd[:], 0.0), writes=["maskadd"])
    if "maskadd" not in skip:
      P.op("pool", lambda e: e.affine_select(out=maskadd[:], in_=maskadd[:], pattern=[[1, 32], [-1, 32]],
                                            compare_op=ALU.is_ge, fill=NEG, base=-1, channel_multiplier=0),
         reads=["maskadd"], writes=["maskadd"])
    if "vaones" not in skip:
        P.op("pool", lambda e: e.memset(VA[:, :, :, 128:129], 1.0), writes=["VAones"])

    if stop <= 0:
        return P.finish()
    pQK = bk[3][:].rearrange("p (a b) -> p a b", a=4)
    pV = bk[4][:, 0:256]
    XR, YR = 4, 3

    def p1a(t):
        P.dma("sp", xt[t % XR][:], x[t * 128:(t + 1) * 128, :], writes=[f"xt{t % XR}"])
        nm.stats(xt[t % XR][:], f"xt{t % XR}", t % XR)

    def p1b(t):
        nm.finish(t % XR)
        nm.scale(xt[t % XR][:], f"xt{t % XR}", gB[:, 0, :], t % XR)

    def p1c(t):
        nm.transpose(nm.yb[t % XR], f"nm_yb{t % XR}", yT[t % YR][:], f"yT{t % YR}", t)

    def p1d(t):
        b = t % YR
        for j, w in enumerate((wq_b, wk_b)):
            wres = "wq_b" if j == 0 else "wk_b"
            for hh in range(2):
                for k in range(8):
                    P.op("pe", lambda e, k=k, w=w, hh=hh, j=j: e.matmul(
                        pQK[:, j * 2 + hh, :], lhsT=w[:, k, hh * 128:(hh + 1) * 128], rhs=yT[b][:, k, :],
                        start=(k == 0), stop=(k == 7)), reads=[f"yT{b}", wres], writes=["bk3"])
        for k in range(8):
            P.op("pe", lambda e, k=k: e.matmul(pV, lhsT=yT[b][:, k, :], rhs=wv_b[:, k, :],
                                               start=(k == 0), stop=(k == 7)), reads=[f"yT{b}", "wv_b"], writes=["bk4"])
        P.op("act", lambda e: e.copy(out=QT[:, :, t * 128:(t + 1) * 128], in_=pQK[:, 0:2, :]), reads=["bk3"], writes=[("QT", t)])
        P.op("act", lambda e: e.copy(out=KT[:, :, t * 128:(t + 1) * 128], in_=pQK[:, 2:4, :]), reads=["bk3"], writes=[("KT", t)])
        P.op("dve", lambda e: e.tensor_reduce(out=ksum[:, :, t], in_=KT[:, :, t * 128:(t + 1) * 128], axis=AX.X, op=ALU.add),
             reads=[("KT", t)], writes=["ksum"])
        P.op("act", lambda e: e.copy(out=VA[:, t, :, 0:128], in_=pV.rearrange("p (a b) -> p a b", a=2)),
             reads=["bk4"], writes=[("VA", t)])

    for step in range(NTILE + 3):
        for s_, f_ in enumerate((p1a, p1b, p1c, p1d)):
            t = step - s_
            if 0 <= t < NTILE:
                f_(t)
    if stop <= 1:
        return P.finish()
    ksv = ksum[:].rearrange("p h (n two) -> p h n two", two=2)
    P.op("dve", lambda e: e.tensor_tensor(out=kmT[:, :, 0:NBLK], in0=ksv[:, :, :, 0], in1=ksv[:, :, :, 1], op=ALU.add),
         reads=["ksum"], writes=["kmT"])
    if NBLK < 32:
        P.op("dve", lambda e: e.memset(kmT[:, :, NBLK:32], 0.0), writes=["kmT"])
    P.op("dve", lambda e: e.tensor_scalar(out=kmT_b[:], in0=kmT[:], scalar1=1.0 / 256, scalar2=None, op0=ALU.mult),
         reads=["kmT"], writes=["kmT_b"])

    pG = bk[5][:, 0:64].rearrange("p (a b) -> p a b", a=2)
    for t in range(NTILE):
        j = t // 2
        for hh in range(2):
            P.op("pe", lambda e, hh=hh: e.matmul(pG[:, hh, :], lhsT=QT[:, hh, t * 128:(t + 1) * 128], rhs=kmT_b[:, hh, :],
                                                 start=True, stop=True), reads=[("QT", t), "kmT_b"], writes=["bk5"])
        for hh in range(2):
            P.op("dve", lambda e, hh=hh: e.tensor_tensor(out=gm[:, hh, :], in0=pG[:, hh, :], in1=maskadd[:, j, :], op=ALU.add),
                 reads=["bk5", "maskadd"], writes=["gm"])
        for hh in range(2):
            P.op("dve", lambda e, hh=hh: e.max(out=top8[:, hh, :], in_=gm[:, hh, :]), reads=["gm"], writes=["top8"])
        for hh in range(2):
            P.op("dve", lambda e, hh=hh: e.tensor_scalar(out=sel[:, t, hh, :], in0=gm[:, hh, :], scalar1=top8[:, hh, 2:3],
                                                         scalar2=None, op0=ALU.is_ge),
                 reads=["gm", "top8"], writes=[("sel", t)])

    if stop <= 2:
        return P.finish()
    tmpS = [gp[:, i * 512:(i + 1) * 512].rearrange("p (a b) -> p a b", a=2) for i in range(3)]
    PT = [nm.yb[i][:, 0:512].rearrange("p (a b) -> p a b", a=2) for i in range(3)]
    iters = [(hh, J, n) for hh in range(2) for J in range(NBLK) for n in range(J, -1, -1)]
    NI = len(iters)

    def views(k):
        r = k % 3
        pS = bk[r][:].rearrange("p (a b) -> p a b", a=2)
        pO = bk[3 + r][:, 0:264].rearrange("p (a b) -> p a b", a=2)
        return r, pS, pO

    def stA(k):
        hh, J, n = iters[k]
        r, pS, pO = views(k)
        for half in range(2):
            P.op("pe", lambda e, half=half: e.matmul(
                pS[:, half, :], lhsT=KT[:, hh, n * 256 + half * 128:n * 256 + (half + 1) * 128],
                rhs=QT[:, hh, J * 256:(J + 1) * 256], start=True, stop=True),
                reads=[("QT", 2 * J), ("QT", 2 * J + 1), ("KT", 2 * n + half)],
                writes=[f"bk{r}"])

    def stB(k):
        hh, J, n = iters[k]
        r, pS, pO = views(k)
        dJ = J - n
        if dJ <= 4:
            P.op("dve", lambda e: e.scalar_tensor_tensor(out=tmpS[r], in0=pS, scalar=SCALE, in1=BT[:, hh, dJ, :, :],
                                                         op0=ALU.mult, op1=ALU.add),
                 reads=[f"bk{r}", "BT"], writes=[f"tmpS{r}"])
            P.op("act", lambda e: e.activation(out=PT[r], in_=tmpS[r], func=AF.Exp),
                 reads=[f"tmpS{r}"], writes=[f"PT{r}"])
        else:
            P.op("act", lambda e: e.activation(out=PT[r], in_=pS, func=AF.Exp, bias=c31[:, hh:hh + 1], scale=SCALE),
                 reads=[f"bk{r}", "c31"], writes=[f"PT{r}"])

    def stC(k):
        hh, J, n = iters[k]
        r, pS, pO = views(k)
        for qt in range(2):
            for half in range(2):
                P.op("pe", lambda e, qt=qt, half=half: e.matmul(
                    pO[:, qt, 0:129], lhsT=PT[r][:, half, qt * 128:(qt + 1) * 128],
                    rhs=VA[:, 2 * n + half, hh, 0:129], start=(half == 0), stop=(half == 1)),
                    reads=[f"PT{r}", ("VA", 2 * n + half), "VAones"],
                    writes=[f"bk{3 + r}"])

    def stD(k):
        hh, J, n = iters[k]
        r, pS, pO = views(k)
        dJ = J - n
        ab = J % 2
        for qt in range(2):
            ares = (f"acc{ab}", qt)
            if dJ == 0:
                P.op("dve", lambda e, qt=qt: e.tensor_copy(out=acc[ab][:, qt, 0:129], in_=pO[:, qt, 0:129]),
                     reads=[f"bk{3 + r}"], writes=[ares])
            else:
                P.op("dve", lambda e, qt=qt: e.scalar_tensor_tensor(
                    out=acc[ab][:, qt, 0:129], in0=pO[:, qt, 0:129], scalar=sel[:, 2 * J + qt, hh, n:n + 1],
                    in1=acc[ab][:, qt, 0:129], op0=ALU.mult, op1=ALU.add),
                    reads=[f"bk{3 + r}", ("sel", 2 * J + qt), ares], writes=[ares])
        if n == 0:
            both = [(f"acc{ab}", 0), (f"acc{ab}", 1)]
            P.op("dve", lambda e: e.reciprocal(out=rec[ab][:], in_=acc[ab][:, :, 128]), reads=both, writes=[f"rec{ab}"])
            for qt in range(2):
                P.op("dve", lambda e, qt=qt: e.tensor_scalar(out=osb[ab][:, qt, :], in0=acc[ab][:, qt, 0:128],
                                                             scalar1=rec[ab][:, qt:qt + 1], scalar2=None, op0=ALU.mult),
                     reads=[(f"acc{ab}", qt), f"rec{ab}"], writes=[(f"osb{ab}", qt)])
            P.dma("sp", oout[J * 256:(J + 1) * 256, hh * 128:(hh + 1) * 128].rearrange("(q p) d -> p q d", p=128),
                  osb[ab][:], reads=[(f"osb{ab}", 0), (f"osb{ab}", 1)], writes=[("o", hh, J)])

    for step in range(NI + 3):
        for s_, f_ in enumerate((stA, stB, stC, stD)):
            k = step - s_
            if 0 <= k < NI:
                f_(k)
    return P.finish()


def run_moba(h1, g_mix, w_in, rel_table):
    nc = _get("moba", build_moba)
    oh, neg = moba_consts()
    maps = []
    for c in range(8):
        b, hp = c // 4, c % 4
        cs = slice(hp * 256, (hp + 1) * 256)
        maps.append({"x": np.ascontiguousarray(h1[b * SEQ:(b + 1) * SEQ]), "gains": np.ascontiguousarray(g_mix[None, :]),
                     "wq": np.ascontiguousarray(w_in[:, cs]),
                     "wk": np.ascontiguousarray(w_in[:, 1024 + hp * 256:1024 + (hp + 1) * 256]),
                     "wv": np.ascontiguousarray(w_in[:, 2048 + hp * 256:2048 + (hp + 1) * 256]),
                     "tab": np.ascontiguousarray(rel_table[:, hp * 2:hp * 2 + 2]), "oh": oh, "negrow": neg})
    res = run_bass_kernel_spmd(nc, maps, core_ids=list(range(8)))
    o = np.empty((2, SEQ, D), np.float32)
    for c in range(8):
        b, hp = c // 4, c % 4
        o[b, :, hp * 256:(hp + 1) * 256] = res.results[c]["o"]
    return o.reshape(2 * SEQ, D)


def kernel(x, norm_mix, norm_ffn, hgrn_w_in, hgrn_lb_logits, hgrn_out_norm, hgrn_w_out,
           moba_w_in, moba_w_out, rel_bias_table, ffn_w13, ffn_w2, final_norm):
    f = lambda a: np.ascontiguousarray(np.asarray(a, dtype=np.float32))
    x = f(x)
    norm_mix, norm_ffn, final_norm = f(norm_mix), f(norm_ffn), f(final_norm)
    hgrn_w_in, hgrn_lb_logits, hgrn_out_norm, hgrn_w_out = f(hgrn_w_in), f(hgrn_lb_logits), f(hgrn_out_norm), f(hgrn_w_out)
    moba_w_in, moba_w_out, rel_bias_table = f(moba_w_in), f(moba_w_out), f(rel_bias_table)
    ffn_w13, ffn_w2 = f(ffn_w13), f(ffn_w2)
    xf = x.reshape(2 * SEQ, D)
    o0 = run_hgrn(x, norm_mix[0], hgrn_w_in[0], hgrn_lb_logits)
    gains0 = f(np.stack([norm_mix[0], hgrn_out_norm[0], norm_ffn[0], final_norm]))
    h1 = run_tl("hgrn", False, xf, o0, gains0, hgrn_w_out[0], ffn_w13[0], ffn_w2[0],
                w_og=f(hgrn_w_in[0][:, 3072:4096]))
    o1 = run_moba(h1, norm_mix[1], moba_w_in[0], rel_bias_table)
    gains1 = f(np.stack([norm_mix[1], hgrn_out_norm[0], norm_ffn[1], final_norm]))
    out = run_tl("moba", True, h1, o1, gains1, moba_w_out[0], ffn_w13[1], ffn_w2[1])
    return out.reshape(2, SEQ, D)
```

```python
from contextlib import ExitStack

import numpy as np
import concourse.bass as bass
import concourse.mybir as mybir
from concourse.bass_utils import run_bass_kernel_spmd

F32 = mybir.dt.float32
BF16 = mybir.dt.bfloat16
ALU = mybir.AluOpType
AF = mybir.ActivationFunctionType
AX = mybir.AxisListType

D = 1024
DFF = 2816
SEQ = 8192
EPS = 1e-6
NEG = -1.0e30


class Prog:
    def __init__(self):
        self.nc = bass.Bass("TRN2", target_bir_lowering=False)
        nc = self.nc
        self.es = ExitStack()
        self.E = {"pe": nc.tensor, "act": nc.scalar, "dve": nc.vector, "pool": nc.gpsimd, "sp": nc.sync}
        self.semh = {}
        self.cnt = {}
        self.NCH = 8
        self.dn = {"sp": 0, "pool": 0}
        for k in ["pe", "act", "dve", "pool"] + [f"dq_{q}_{i}" for q in ("sp", "pool") for i in range(self.NCH)]:
            self.semh[k] = self.es.enter_context(nc.semaphore(k))
            self.cnt[k] = 0
        self.lastw = {}
        self.readers = {}
        self.waited = {}
        self._uid = 0

    def sb(self, name, shape, dt):
        return self.es.enter_context(self.nc.sbuf_tensor(name, list(shape), dt))

    def ps(self, name, shape, dt):
        return self.es.enter_context(self.nc.psum_tensor(name, list(shape), dt))

    def dram(self, name, shape, dt, kind):
        return self.nc.dram_tensor(name, list(shape), dt, kind=kind).ap()

    def _deps(self, reads, writes):
        deps = {}

        def add(k, v):
            if deps.get(k, 0) < v:
                deps[k] = v

        for r in reads:
            if r in self.lastw:
                add(*self.lastw[r])
        for w in writes:
            if w in self.lastw:
                add(*self.lastw[w])
            for k, v in self.readers.get(w, {}).items():
                add(k, v)
        return deps

    def _wait(self, e, deps, skip=None):
        eng = self.E[e]
        for k, v in deps.items():
            if k == skip:
                continue
            if self.waited.get((e, k), 0) >= v:
                continue
            eng.wait_ge(self.semh[k], v)
            self.waited[(e, k)] = v

    def _commit(self, key, val, reads, writes):
        for r in reads:
            d = self.readers.setdefault(r, {})
            if d.get(key, 0) < val:
                d[key] = val
        for w in writes:
            self.lastw[w] = (key, val)
            self.readers[w] = {}

    def op(self, e, fn, reads=(), writes=()):
        deps = self._deps(reads, writes)
        self._wait(e, deps, skip=("pe" if e == "pe" else None))
        ins = fn(self.E[e])
        self.cnt[e] += 1
        ins.then_inc(self.semh[e], 1)
        self._commit(e, self.cnt[e], reads, writes)

    def dma(self, q, out, in_, reads=(), writes=()):
        key = f"dq_{q}_{self.dn[q] % self.NCH}"
        self.dn[q] += 1
        deps = self._deps(reads, writes)
        if self.cnt[key] > deps.get(key, 0):
            deps[key] = self.cnt[key]
        self._wait(q, deps)
        ins = self.E[q].dma_start(out=out, in_=in_)
        self.cnt[key] += 16
        ins.then_inc(self.semh[key], 16)
        self._commit(key, self.cnt[key], reads, writes)

    def finish(self):
        for k, v in self.cnt.items():
            if v > 0:
                self.E["sp"].wait_ge(self.semh[k], v)
        self.es.close()
        return self.nc


def make_ident(P, name="ident"):
    ones = P.sb(name + "_ones", [128, 128], BF16)
    ident = P.sb(name, [128, 128], BF16)
    P.op("pool", lambda e: e.memset(ones[:], 1.0), writes=[name + "_ones"])
    P.op("pool", lambda e: e.affine_select(out=ident[:], in_=ones[:], pattern=[[-1, 128]],
                                            compare_op=ALU.is_equal, fill=0.0, base=0,
                                            channel_multiplier=1),
         reads=[name + "_ones"], writes=[name])
    return ident


class Normer:
    def __init__(self, P, ident, nbuf=2, npt=2, lnexp=False):
        self.P = P
        self.ident = ident
        self.lnexp = lnexp
        self.junk = P.sb("nm_junk", [128, D], BF16)
        self.ss = [P.sb(f"nm_ss{i}", [128, 1], F32) for i in range(nbuf)]
        self.rs = [P.sb(f"nm_rs{i}", [128, 1], F32) for i in range(nbuf)]
        self.yb = [P.sb(f"nm_yb{i}", [128, D], BF16) for i in range(nbuf)]
        self.pT = [P.ps(f"nm_pT{i}", [128, 8, 128], BF16) for i in range(npt)]
        self.n = 0
        self.nbuf = nbuf
        self.npt = npt

    def stats(self, x_ap, xres, i):
        P = self.P
        ss = self.ss[i]
        P.op("dve", lambda e: e.memset(ss[:], 0.0), writes=[f"nm_ss{i}"])
        P.op("act", lambda e: e.activation(out=self.junk[:], in_=x_ap, func=AF.Square, accum_out=ss[:]),
             reads=[xres, f"nm_ss{i}"], writes=["nm_junk", f"nm_ss{i}"])

    def finish(self, i):
        P = self.P
        ss, rs = self.ss[i], self.rs[i]
        P.op("dve", lambda e: e.tensor_scalar(out=ss[:], in0=ss[:], scalar1=1.0 / D, scalar2=EPS,
                                              op0=ALU.mult, op1=ALU.add),
             reads=[f"nm_ss{i}"], writes=[f"nm_ss{i}"])
        if self.lnexp:
            P.op("act", lambda e: e.activation(out=ss[:], in_=ss[:], func=AF.Ln),
                 reads=[f"nm_ss{i}"], writes=[f"nm_ss{i}"])
            P.op("act", lambda e: e.activation(out=rs[:], in_=ss[:], func=AF.Exp, scale=-0.5),
                 reads=[f"nm_ss{i}"], writes=[f"nm_rs{i}"])
        else:
            P.op("act", lambda e: e.activation(out=ss[:], in_=ss[:], func=AF.Sqrt),
                 reads=[f"nm_ss{i}"], writes=[f"nm_ss{i}"])
            P.op("dve", lambda e: e.reciprocal(out=rs[:], in_=ss[:]),
                 reads=[f"nm_ss{i}"], writes=[f"nm_rs{i}"])
        return rs

    def rstd(self, x_ap, xres, i):
        self.stats(x_ap, xres, i)
        return self.finish(i)

    def scale(self, x_ap, xres, gB_ap, i):
        P = self.P
        yb, rs = self.yb[i], self.rs[i]
        P.op("dve", lambda e: e.scalar_tensor_tensor(out=yb[:], in0=x_ap, scalar=rs[:, 0:1], in1=gB_ap,
                                                     op0=ALU.mult, op1=ALU.mult),
             reads=[xres, f"nm_rs{i}", "gB"], writes=[f"nm_yb{i}"])
        return yb

    def transpose(self, src_bf, srcres, dst_ap, dstres, i, evac="act"):
        P = self.P
        j = i % self.npt
        pT = self.pT[j]
        for k in range(8):
            P.op("pe", lambda e, k=k: e.transpose(out=pT[:, k, :], in_=src_bf[:, k * 128:(k + 1) * 128],
                                                   identity=self.ident[:]),
                 reads=[srcres, "ident"], writes=[f"nm_pT{j}"])
        if evac == "act":
            P.op("act", lambda e: e.copy(out=dst_ap, in_=pT[:]), reads=[f"nm_pT{j}"], writes=[dstres])
        else:
            P.op(evac, lambda e: e.tensor_copy(out=dst_ap, in_=pT[:]), reads=[f"nm_pT{j}"], writes=[dstres])

    def norm_T(self, x_ap, xres, gB_ap, dst_ap, dstres, evac="act"):
        i = self.n % self.nbuf
        self.n += 1
        self.rstd(x_ap, xres, i)
        yb = self.scale(x_ap, xres, gB_ap, i)
        self.transpose(yb, f"nm_yb{i}", dst_ap, dstres, i, evac)


class WLoader:
    def __init__(self, P, nstage=2, stage_elems=2048, cast_engines=("pool", "dve"), direct=False):
        self.P = P
        self.direct = direct
        self.stage = [] if direct else [P.sb(f"wst{i}", [128, stage_elems], F32) for i in range(nstage)]
        self.n = 0
        self.stage_elems = stage_elems
        self.cast_engines = cast_engines
        self.queues = ("sp", "pool")

    def load(self, w_ap, r0, nk, c0, ncols, dst, dstres, dk0=0, dc0=0):
        P = self.P
        if self.direct:
            src = w_ap[r0:r0 + nk * 128, c0:c0 + ncols].rearrange("(k p) c -> p k c", p=128)
            P.dma("pool", dst[:, dk0:dk0 + nk, dc0:dc0 + ncols], src, writes=[dstres])
            return
        kmax = max(1, self.stage_elems // ncols)
        k = 0
        while k < nk:
            kk = min(kmax, nk - k)
            i = self.n % len(self.stage)
            q = self.queues[self.n % len(self.queues)]
            ce = self.cast_engines[self.n % len(self.cast_engines)]
            self.n += 1
            st = self.stage[i]
            stv = st[:, 0:kk * ncols].rearrange("p (k c) -> p k c", k=kk)
            src = w_ap[r0 + k * 128:r0 + (k + kk) * 128, c0:c0 + ncols].rearrange("(k p) c -> p k c", p=128)
            P.dma(q, stv, src, writes=[f"wst{i}"])
            dv = dst[:, dk0 + k:dk0 + k + kk, dc0:dc0 + ncols]
            P.op(ce, lambda e, dv=dv, stv=stv: e.tensor_copy(out=dv, in_=stv), reads=[f"wst{i}"], writes=[dstres])
            k += kk


def load_gains(P, gains_ap, n):
    gB = P.sb("gB", [128, n, D], F32)
    for i in range(n):
        P.dma("sp", gB[:, i:i + 1, :], gains_ap[i:i + 1, :].partition_broadcast(128), writes=["gB"])
    return gB


def build_tl(mode, final):
    P = Prog()
    T = 2048
    NP = 2
    TP = T // NP
    NT = TP // 128
    hin = P.dram("hin", [T, D], F32, "ExternalInput")
    oin = P.dram("oin", [T, D], F32, "ExternalInput")
    gains = P.dram("gains", [4, D], F32, "ExternalInput")
    w_out = P.dram("w_out", [D, D], F32, "ExternalInput")
    w13 = P.dram("w13", [D, 2 * DFF], F32, "ExternalInput")
    w2 = P.dram("w2", [DFF, D], F32, "ExternalInput")
    if mode == "hgrn":
        w_og = P.dram("w_og", [D, D], F32, "ExternalInput")
    hout = P.dram("hout", [T, D], F32, "ExternalOutput")

    ident = make_ident(P)
    gB = load_gains(P, gains, 4)
    nm = Normer(P, ident)
    wl = WLoader(P)

    h = P.sb("h", [128, NT, D], F32)
    xT = P.sb("xT", [128, 8, TP], BF16)
    oT = P.sb("oT", [128, 8, TP], BF16)
    aT = P.sb("aT", [128, 11, TP], BF16)
    wsl = [P.sb(f"wsl{i}", [128, 8, 512], BF16) for i in range(2)]
    w2s = [P.sb(f"w2s{i}", [128, 11, 512], BF16) for i in range(2)]
    ot = [P.sb(f"ot{i}", [128, D], F32) for i in range(2)]
    sg = P.sb("sg", [128, D], F32)
    tmpf = P.sb("tmpf", [128, D], F32)
    onb = [P.sb(f"onb{i}", [128, D], BF16) for i in range(2)]
    sgl = [P.sb(f"sgl{i}", [128, 512], F32) for i in range(2)]
    pacc = [P.ps(f"pacc{i}", [128, 512], F32) for i in range(6)]
    nacc = [0]

    def acc():
        i = nacc[0] % 6
        nacc[0] += 1
        return pacc[i], f"pacc{i}"

    nsl = [0]

    def next_wsl():
        i = nsl[0] % 2
        nsl[0] += 1
        return wsl[i], f"wsl{i}"

    for ps_ in range(NP):
        t0 = ps_ * TP
        for t in range(NT):
            P.dma("sp", h[:, t, :], hin[t0 + t * 128:t0 + (t + 1) * 128, :], writes=[("h", t)])
        if mode == "hgrn":
            for t in range(NT):
                nm.norm_T(h[:, t, :], ("h", t), gB[:, 0, :], xT[:, :, t * 128:(t + 1) * 128], ("xT", t))
        wog_sl = [None, None]
        if mode == "hgrn":
            for cg in range(2):
                s, sres = next_wsl()
                wl.load(w_og, 0, 8, cg * 512, 512, s, sres)
                wog_sl[cg] = (s, sres)
        for t in range(NT):
            b = t % 2
            P.dma("pool", ot[b][:], oin[t0 + t * 128:t0 + (t + 1) * 128, :], writes=[f"ot{b}"])
            if mode == "hgrn":
                for cg in range(2):
                    s, sres = wog_sl[cg]
                    pa, pres = acc()
                    for k in range(8):
                        P.op("pe", lambda e, k=k, pa=pa, s=s: e.matmul(pa[:], lhsT=xT[:, k, t * 128:(t + 1) * 128],
                                                                      rhs=s[:, k, :], start=(k == 0), stop=(k == 7)),
                             reads=[("xT", t), sres], writes=[pres])
                    P.op("act", lambda e, pa=pa, cg=cg: e.activation(out=sg[:, cg * 512:(cg + 1) * 512], in_=pa[:],
                                                                     func=AF.Sigmoid),
                         reads=[pres], writes=["sg"])
                i = nm.n % nm.nbuf
                nm.n += 1
                rs = nm.rstd(ot[b][:], f"ot{b}", i)
                P.op("dve", lambda e, rs=rs: e.scalar_tensor_tensor(out=tmpf[:], in0=ot[b][:], scalar=rs[:, 0:1],
                                                                    in1=gB[:, 1, :], op0=ALU.mult, op1=ALU.mult),
                     reads=[f"ot{b}", f"nm_rs{i}", "gB"], writes=["tmpf"])
                P.op("dve", lambda e: e.tensor_tensor(out=onb[b][:], in0=tmpf[:], in1=sg[:], op=ALU.mult),
                     reads=["tmpf", "sg"], writes=[f"onb{b}"])
            else:
                i = nm.n % nm.nbuf
                nm.n += 1
                P.op("act", lambda e: e.copy(out=onb[b][:], in_=ot[b][:]), reads=[f"ot{b}"], writes=[f"onb{b}"])
            nm.transpose(onb[b], f"onb{b}", oT[:, :, t * 128:(t + 1) * 128], ("oT", t), i, evac="act")
        for cg in range(2):
            s, sres = next_wsl()
            wl.load(w_out, 0, 8, cg * 512, 512, s, sres)
            for t in range(NT):
                pa, pres = acc()
                for k in range(8):
                    P.op("pe", lambda e, k=k, pa=pa, s=s: e.matmul(pa[:], lhsT=oT[:, k, t * 128:(t + 1) * 128],
                                                                  rhs=s[:, k, :], start=(k == 0), stop=(k == 7)),
                         reads=[("oT", t), sres], writes=[pres])
                hv = h[:, t, cg * 512:(cg + 1) * 512]
                P.op("dve", lambda e, pa=pa, hv=hv: e.tensor_tensor(out=hv, in0=hv, in1=pa[:], op=ALU.add),
                     reads=[pres, ("h", t)], writes=[("h", t)])
        for t in range(NT):
            nm.norm_T(h[:, t, :], ("h", t), gB[:, 2, :], xT[:, :, t * 128:(t + 1) * 128], ("xT", t))
        xT_all = [("xT", t) for t in range(NT)]
        for half in range(2):
            c0 = half * 11
            ci = 0
            while ci < 11:
                ncg = min(2, 11 - ci)
                s, sres = next_wsl()
                wl.load(w13, 0, 8, (c0 + ci) * 128, ncg * 128, s, sres, dc0=0)
                wl.load(w13, 0, 8, DFF + (c0 + ci) * 128, ncg * 128, s, sres, dc0=256)
                for fc in range(ncg):
                    for tg in range(TP // 512):
                        pg, pgres = acc()
                        pu, pures = acc()
                        for k in range(8):
                            P.op("pe", lambda e, k=k, pg=pg, s=s, fc=fc, tg=tg: e.matmul(
                                pg[:], lhsT=s[:, k, fc * 128:(fc + 1) * 128], rhs=xT[:, k, tg * 512:(tg + 1) * 512],
                                start=(k == 0), stop=(k == 7)), reads=xT_all[tg * 4:tg * 4 + 4] + [sres], writes=[pgres])
                        for k in range(8):
                            P.op("pe", lambda e, k=k, pu=pu, s=s, fc=fc, tg=tg: e.matmul(
                                pu[:], lhsT=s[:, k, 256 + fc * 128:256 + (fc + 1) * 128],
                                rhs=xT[:, k, tg * 512:(tg + 1) * 512],
                                start=(k == 0), stop=(k == 7)), reads=xT_all[tg * 4:tg * 4 + 4] + [sres], writes=[pures])
                        sb_ = nacc[0] % 2
                        P.op("act", lambda e, pg=pg, sb_=sb_: e.activation(out=sgl[sb_][:], in_=pg[:], func=AF.Silu),
                             reads=[pgres], writes=[f"sgl{sb_}"])
                        av = aT[:, ci + fc, tg * 512:(tg + 1) * 512]
                        P.op("dve", lambda e, pu=pu, sb_=sb_, av=av: e.tensor_tensor(out=av, in0=sgl[sb_][:], in1=pu[:],
                                                                                     op=ALU.mult),
                             reads=[pures, f"sgl{sb_}"], writes=[("aT", ci + fc, tg)])
                ci += ncg
            aT_all = [("aT", c, tg) for c in range(11) for tg in range(TP // 512)]
            for cg in range(2):
                wi = (half * 2 + cg) % 2
                wl.load(w2, c0 * 128, 11, cg * 512, 512, w2s[wi], f"w2s{wi}")
                for t in range(NT):
                    pa, pres = acc()
                    for c in range(11):
                        P.op("pe", lambda e, c=c, pa=pa, wi=wi: e.matmul(pa[:], lhsT=aT[:, c, t * 128:(t + 1) * 128],
                                                                        rhs=w2s[wi][:, c, :], start=(c == 0), stop=(c == 10)),
                             reads=[("aT", c, t // 4) for c in range(11)] + [f"w2s{wi}"], writes=[pres])
                    hv = h[:, t, cg * 512:(cg + 1) * 512]
                    P.op("dve", lambda e, pa=pa, hv=hv: e.tensor_tensor(out=hv, in0=hv, in1=pa[:], op=ALU.add),
                         reads=[pres, ("h", t)], writes=[("h", t)])
        for t in range(NT):
            if final:
                i = nm.n % nm.nbuf
                nm.n += 1
                rs = nm.rstd(h[:, t, :], ("h", t), i)
                b = t % 2
                P.op("dve", lambda e, rs=rs, b=b: e.scalar_tensor_tensor(out=ot[b][:], in0=h[:, t, :], scalar=rs[:, 0:1],
                                                                         in1=gB[:, 3, :], op0=ALU.mult, op1=ALU.mult),
                     reads=[("h", t), f"nm_rs{i}", "gB"], writes=[f"ot{b}"])
                P.dma("sp", hout[t0 + t * 128:t0 + (t + 1) * 128, :], ot[b][:], reads=[f"ot{b}"], writes=[("hout", ps_, t)])
            else:
                P.dma("sp", hout[t0 + t * 128:t0 + (t + 1) * 128, :], h[:, t, :], reads=[("h", t)],
                      writes=[("hout", ps_, t)])
    return P.finish()


_CACHE = {}


def _get(name, builder):
    if name not in _CACHE:
        _CACHE[name] = builder()
    return _CACHE[name]


def run_tl(mode, final, hin, oin, gains, w_out, w13, w2, w_og=None):
    nc = _get(("tl", mode, final), lambda: build_tl(mode, final))
    maps = []
    for c in range(8):
        m = {"hin": np.ascontiguousarray(hin[c * 2048:(c + 1) * 2048]),
             "oin": np.ascontiguousarray(oin[c * 2048:(c + 1) * 2048]),
             "gains": gains, "w_out": w_out, "w13": w13, "w2": w2}
        if mode == "hgrn":
            m["w_og"] = w_og
        maps.append(m)
    res = run_bass_kernel_spmd(nc, maps, core_ids=list(range(8)))
    return np.concatenate([res.results[c]["hout"] for c in range(8)], axis=0)


def build_hgrn(NTILE=SEQ // 128):
    P = Prog()
    x = P.dram("x", [SEQ, D], F32, "ExternalInput")
    gains = P.dram("gains", [1, D], F32, "ExternalInput")
    wq = P.dram("wq", [D, 256], F32, "ExternalInput")
    wf = P.dram("wf", [D, 256], F32, "ExternalInput")
    wi = P.dram("wi", [D, 256], F32, "ExternalInput")
    lbl = P.dram("lbl", [128, 4], F32, "ExternalInput")
    oout = P.dram("o", [SEQ, 256], F32, "ExternalOutput")

    ident = make_ident(P)
    gB = load_gains(P, gains, 1)
    nm = Normer(P, ident, nbuf=5, npt=2, lnexp=True)
    wl = WLoader(P, direct=True)
    wqf_b = P.sb("wqf_b", [128, 8, 512], BF16)
    wi_b = P.sb("wi_b", [128, 8, 256], BF16)
    wl.load(wq, 0, 8, 0, 256, wqf_b, "wqf_b", dc0=0)
    wl.load(wf, 0, 8, 0, 256, wqf_b, "wqf_b", dc0=256)
    wl.load(wi, 0, 8, 0, 256, wi_b, "wi_b")

    lbt = P.sb("lbt", [128, 4], F32)
    lb = P.sb("lb", [128, 2], F32)
    oml = P.sb("oml", [128, 2], F32)
    P.dma("sp", lbt[:], lbl[:, :], writes=["lbt"])
    P.op("dve", lambda e: e.tensor_tensor(out=lb[:], in0=lbt[:, 2:4], in1=lbt[:, 0:2], op=ALU.subtract),
         reads=["lbt"], writes=["lb"])
    P.op("act", lambda e: e.activation(out=lb[:], in_=lb[:], func=AF.Exp), reads=["lb"], writes=["lb"])
    P.op("dve", lambda e: e.tensor_scalar(out=lb[:], in0=lb[:], scalar1=1.0, scalar2=None, op0=ALU.add),
         reads=["lb"], writes=["lb"])
    P.op("dve", lambda e: e.reciprocal(out=lb[:], in_=lb[:]), reads=["lb"], writes=["lb"])
    P.op("dve", lambda e: e.tensor_scalar(out=oml[:], in0=lb[:], scalar1=-1.0, scalar2=1.0, op0=ALU.mult, op1=ALU.add),
         reads=["lb"], writes=["oml"])

    onesf = P.sb("onesf", [128, 2, 128], F32)
    mask = P.sb("mask", [128, 2, 128], F32)
    P.op("pool", lambda e: e.memset(onesf[:], 1.0), writes=["onesf"])
    P.op("pool", lambda e: e.affine_select(out=mask[:], in_=onesf[:], pattern=[[0, 2], [1, 128]],
                                            compare_op=ALU.is_ge, fill=0.0, base=0, channel_multiplier=-1),
         reads=["onesf"], writes=["mask"])

    state = P.sb("state", [128, 2, 128], F32)
    state_b = P.sb("state_b", [128, 2, 128], BF16)
    P.op("dve", lambda e: e.memset(state[:], 0.0), writes=[("state", 0), ("state", 1)])
    P.op("dve", lambda e: e.memset(state_b[:], 0.0), writes=["state_b"])

    rings = {}

    def ring(name, n, shape, dt):
        rings[name] = [P.sb(f"{name}{i}", shape, dt) for i in range(n)]

    def X(name, t):
        n = len(rings[name])
        return rings[name][t % n], f"{name}{t % n}"

    ring("xt", 4, [128, D], F32)
    ring("yT", 3, [128, 8, 128], BF16)
    ring("EE", 4, [128, 4, 128], F32)
    ring("qr", 4, [128, 2, 128], F32)
    ring("ib", 12, [128, 256], BF16)
    ring("qs", 7, [128, 2, 128], F32)
    ring("fg", 3, [128, 2, 128], F32)
    ring("kk", 7, [128, 2, 128], F32)
    ring("lf", 3, [128, 2, 128], F32)
    ring("AA", 4, [128, 4, 2, 128], F32)
    ring("ex", 7, [128, 4, 2, 128], F32)
    ring("qc", 3, [128, 2, 128], BF16)
    ring("kc", 3, [128, 2, 128], BF16)
    ring("qd", 5, [128, 2, 128], BF16)
    ring("kh", 3, [128, 2, 128], BF16)
    ring("khT", 3, [128, 2, 128], BF16)
    ring("scm", 3, [128, 2, 128], BF16)
    ring("osb", 3, [128, 256], F32)
    pQF = [P.ps(f"pQF{i}", [128, 4, 128], F32) for i in range(2)]
    pI = P.ps("pI", [128, 512], F32)[:, 0:256]
    pKS = P.ps("pKS", [128, 512], F32)
    pK = pKS[:, 0:128].bitcast(BF16).rearrange("p (a b) -> p a b", a=2)
    pS = pKS[:, 256:512].rearrange("p (a b) -> p a b", a=2)
    pO = P.ps("pO", [128, 512], F32)[:, 0:256]
    pU = P.ps("pU", [128, 512], F32)[:, 0:256].rearrange("p (a b) -> p a b", a=2)
    NMB = nm.nbuf

    def s_load(t):
        xt, xr = X("xt", t)
        P.dma("sp", xt[:], x[t * 128:(t + 1) * 128, :], writes=[xr])
        nm.stats(xt[:], xr, t % NMB)

    def s_rs(t):
        nm.finish(t % NMB)

    def s_scale(t):
        xt, xr = X("xt", t)
        nm.scale(xt[:], xr, gB[:, 0, :], t % NMB)

    def s_tr(t):
        i = t % NMB
        j = t % 2
        for k in range(8):
            P.op("pe", lambda e, k=k: e.transpose(out=nm.pT[j][:, k, :], in_=nm.yb[i][:, k * 128:(k + 1) * 128], identity=ident[:]),
                 reads=[f"nm_yb{i}", "ident"], writes=[f"nm_pT{j}"])

    def s_trc(t):
        yT, yr = X("yT", t)
        j = t % 2
        P.op("act", lambda e: e.copy(out=yT[:], in_=nm.pT[j][:]), reads=[f"nm_pT{j}"], writes=[yr])

    def s_proj(t):
        yT, yr = X("yT", t)
        pq, pres = pQF[t % 2], f"pQF{t % 2}"
        for c in range(4):
            for k in range(8):
                P.op("pe", lambda e, k=k, c=c: e.matmul(pq[:, c, :], lhsT=wqf_b[:, k, c * 128:(c + 1) * 128], rhs=yT[:, k, :],
                                                        start=(k == 0), stop=(k == 7)), reads=[yr, "wqf_b"], writes=[pres])
        for k in range(8):
            P.op("pe", lambda e, k=k: e.matmul(pI, lhsT=yT[:, k, :], rhs=wi_b[:, k, :],
                                               start=(k == 0), stop=(k == 7)), reads=[yr, "wi_b"], writes=["pI"])

    def s_evac(t):
        pq, pres = pQF[t % 2], f"pQF{t % 2}"
        EE, er = X("EE", t)
        qr, qrr = X("qr", t)
        ib, ibr = X("ib", t)
        P.op("act", lambda e: e.activation(out=EE[:], in_=pq[:], func=AF.Exp, scale=-1.0), reads=[pres], writes=[er])
        P.op("act", lambda e: e.copy(out=qr[:], in_=pq[:, 0:2, :]), reads=[pres], writes=[qrr])
        P.op("dve", lambda e: e.tensor_copy(out=ib[:], in_=pI), reads=["pI"], writes=[ibr])

    def s_sig(t):
        EE, er = X("EE", t)
        P.op("act", lambda e: e.activation(out=EE[:], in_=EE[:], func=AF.Ln, bias=1.0, scale=1.0), reads=[er], writes=[er])
        P.op("act", lambda e: e.activation(out=EE[:], in_=EE[:], func=AF.Exp, scale=-1.0), reads=[er], writes=[er])

    def s_gate(t):
        EE, er = X("EE", t)
        qr, qrr = X("qr", t)
        qs, qsr = X("qs", t)
        fg, fr = X("fg", t)
        kk, kr = X("kk", t)
        P.op("pool", lambda e: e.tensor_tensor(out=qs[:], in0=qr[:], in1=EE[:, 0:2, :], op=ALU.mult), reads=[qrr, er], writes=[qsr])
        for hh in range(2):
            P.op("dve", lambda e, hh=hh: e.tensor_scalar(out=fg[:, hh, :], in0=EE[:, 2 + hh, :], scalar1=oml[:, hh:hh + 1],
                                                         scalar2=lb[:, hh:hh + 1], op0=ALU.mult, op1=ALU.add),
                 reads=[er, "oml", "lb"], writes=[(fr, hh)])
        P.op("pool", lambda e: e.tensor_scalar(out=kk[:], in0=fg[:], scalar1=-1.0, scalar2=1.0, op0=ALU.mult, op1=ALU.add),
             reads=[(fr, 0), (fr, 1)], writes=[kr])

    def s_lf(t):
        fg, fr = X("fg", t)
        lf, lr = X("lf", t)
        P.op("act", lambda e: e.activation(out=lf[:], in_=fg[:], func=AF.Ln), reads=[(fr, 0), (fr, 1)], writes=[lr])

    def s_scan(t):
        lf, lr = X("lf", t)
        AA, ar = X("AA", t)
        for hh in range(2):
            P.op("dve", lambda e, hh=hh: e.tensor_tensor_scan(out=AA[:, 2, hh, :], data0=onesf[:, 0, :], data1=lf[:, hh, :],
                                                              initial=0.0, op0=ALU.mult, op1=ALU.add),
                 reads=[lr, "onesf"], writes=[(ar, 2, hh)])

    def s_aa(t):
        AA, ar = X("AA", t)
        for hh in range(2):
            eng = "dve" if hh == 0 else "pool"
            cum_ = AA[:, 2, hh, :]
            cenB = AA[:, 2, hh, 63:64].to_broadcast([128, 128])
            lastB = AA[:, 2, hh, 127:128].to_broadcast([128, 128])
            P.op(eng, lambda e, hh=hh: e.tensor_tensor(out=AA[:, 0, hh, :], in0=cum_, in1=cenB, op=ALU.subtract),
                 reads=[(ar, 2, hh)], writes=[(ar, 0, hh)])
            P.op(eng, lambda e, hh=hh: e.tensor_tensor(out=AA[:, 1, hh, :], in0=cenB, in1=cum_, op=ALU.subtract),
                 reads=[(ar, 2, hh)], writes=[(ar, 1, hh)])
            P.op(eng, lambda e, hh=hh: e.tensor_tensor(out=AA[:, 3, hh, :], in0=lastB, in1=cum_, op=ALU.subtract),
                 reads=[(ar, 2, hh)], writes=[(ar, 3, hh)])

    def s_exp(t):
        AA, ar = X("AA", t)
        ex, exr = X("ex", t)
        P.op("act", lambda e: e.activation(out=ex[:], in_=AA[:], func=AF.Exp),
             reads=[(ar, i, hh) for i in range(4) for hh in range(2)], writes=[exr])

    def s_prod(t):
        ex, exr = X("ex", t)
        qs, qsr = X("qs", t)
        kk, kr = X("kk", t)
        qc, qcr = X("qc", t)
        kc, kcr = X("kc", t)
        qd, qdr = X("qd", t)
        kh, khr = X("kh", t)
        P.op("dve", lambda e: e.tensor_tensor(out=qc[:], in0=qs[:], in1=ex[:, 0, :, :], op=ALU.mult), reads=[qsr, exr], writes=[qcr])
        P.op("pool", lambda e: e.tensor_tensor(out=qd[:], in0=qs[:], in1=ex[:, 2, :, :], op=ALU.mult), reads=[qsr, exr], writes=[qdr])
        P.op("dve", lambda e: e.tensor_tensor(out=kc[:], in0=kk[:], in1=ex[:, 1, :, :], op=ALU.mult), reads=[kr, exr], writes=[kcr])
        P.op("pool", lambda e: e.tensor_tensor(out=kh[:], in0=kk[:], in1=ex[:, 3, :, :], op=ALU.mult), reads=[kr, exr], writes=[khr])

    def s_pe5(t):
        qc, qcr = X("qc", t)
        kc, kcr = X("kc", t)
        kh, khr = X("kh", t)
        for hh in range(2):
            P.op("pe", lambda e, hh=hh: e.transpose(out=pK[:, hh, :], in_=kh[:, hh, :], identity=ident[:]),
                 reads=[khr, "ident"], writes=["pKS"])
        for hh in range(2):
            P.op("pe", lambda e, hh=hh: e.matmul(pS[:, hh, :], lhsT=kc[:, hh, :], rhs=qc[:, hh, :], start=True, stop=True),
                 reads=[kcr, qcr], writes=["pKS"])

    def s_ev5(t):
        khT, ktr = X("khT", t)
        scm, scr_ = X("scm", t)
        P.op("dve", lambda e: e.tensor_copy(out=khT[:], in_=pK), reads=["pKS"], writes=[ktr])
        P.op("dve", lambda e: e.tensor_tensor(out=scm[:], in0=pS, in1=mask[:], op=ALU.mult), reads=["pKS", "mask"], writes=[scr_])

    def s_pe6(t):
        khT, ktr = X("khT", t)
        scm, scr_ = X("scm", t)
        ib, ibr = X("ib", t)
        qd, qdr = X("qd", t)
        for hh in range(2):
            P.op("pe", lambda e, hh=hh: e.matmul(pO[:, hh * 128:(hh + 1) * 128], lhsT=scm[:, hh, :],
                                                 rhs=ib[:, hh * 128:(hh + 1) * 128], start=True, stop=False),
                 reads=[scr_, ibr], writes=["pO"])
            P.op("pe", lambda e, hh=hh: e.matmul(pO[:, hh * 128:(hh + 1) * 128], lhsT=qd[:, hh, :],
                                                 rhs=state_b[:, hh, :], start=False, stop=True),
                 reads=[qdr, "state_b"], writes=["pO"])
        for hh in range(2):
            P.op("pe", lambda e, hh=hh: e.matmul(pU[:, hh, :], lhsT=khT[:, hh, :], rhs=ib[:, hh * 128:(hh + 1) * 128],
                                                 start=True, stop=True),
                 reads=[ktr, ibr], writes=["pU"])

    def s_fin(t):
        ex, exr = X("ex", t)
        osb, osr = X("osb", t)
        for hh in range(2):
            P.op("dve", lambda e, hh=hh: e.scalar_tensor_tensor(out=state[:, hh, :], in0=state[:, hh, :],
                                                                scalar=ex[:, 2, hh, 127:128], in1=pU[:, hh, :],
                                                                op0=ALU.mult, op1=ALU.add),
                 reads=[("state", hh), exr, "pU"], writes=[("state", hh)])
        P.op("pool", lambda e: e.tensor_copy(out=state_b[:], in_=state[:]), reads=[("state", 0), ("state", 1)], writes=["state_b"])
        P.op("act", lambda e: e.copy(out=osb[:], in_=pO), reads=["pO"], writes=[osr])
        P.dma("sp", oout[t * 128:(t + 1) * 128, :], osb[:], reads=[osr], writes=[("o", t)])

    stages = [s_load, s_rs, s_scale, s_tr, s_trc, s_proj, s_evac, s_sig, s_gate, s_lf, s_scan, s_aa, s_exp, s_prod,
              s_pe5, s_ev5, s_pe6, s_fin]
    NS = len(stages)
    order = [NS - 1] + [i for i in range(NS - 3, -1, -1)] + [NS - 2]
    for step in range(NTILE + NS - 1):
        for si in order:
            t = step - si
            if 0 <= t < NTILE:
                stages[si](t)
    return P.finish()


def run_hgrn(x, g_mix, w_in, lb_logits):
    nc = _get("hgrn", build_hgrn)
    maps = []
    for c in range(8):
        b, hp = c // 4, c % 4
        cs = slice(hp * 256, (hp + 1) * 256)
        lbl = lb_logits[:, cs].reshape(2, 2, 128).transpose(2, 0, 1).reshape(128, 4)
        maps.append({"x": np.ascontiguousarray(x[b]), "gains": np.ascontiguousarray(g_mix[None, :]),
                     "wq": np.ascontiguousarray(w_in[:, cs]),
                     "wf": np.ascontiguousarray(w_in[:, 1024 + hp * 256:1024 + (hp + 1) * 256]),
                     "wi": np.ascontiguousarray(w_in[:, 2048 + hp * 256:2048 + (hp + 1) * 256]),
                     "lbl": np.ascontiguousarray(lbl)})
    res = run_bass_kernel_spmd(nc, maps, core_ids=list(range(8)))
    o = np.empty((2, SEQ, D), np.float32)
    for c in range(8):
        b, hp = c // 4, c % 4
        o[b, :, hp * 256:(hp + 1) * 256] = res.results[c]["o"]
    return o.reshape(2 * SEQ, D)


GL = 1536
MNEG = -30000.0


def t5_bucket_np(dist):
    n = np.maximum(dist, 0)
    nf = np.maximum(n, 16).astype(np.float32)
    large = 16 + (np.log(nf / np.float32(16)) / np.float32(np.log(64.0)) * np.float32(16)).astype(np.int32)
    large = np.minimum(large, 31)
    return np.where(n < 16, n, large)


def moba_consts():
    i = np.arange(GL)
    bk = t5_bucket_np(i - 255)
    oh = np.zeros((32, GL), np.float32)
    valid = i >= 255
    oh[bk[valid], i[valid]] = 1.0
    neg = np.zeros((128, 256), np.float32)
    neg[:, :255] = MNEG
    return oh, neg


def build_moba(S=SEQ, stop=99, skip=()):
    P = Prog()
    nc = P.nc
    NTILE = S // 128
    NBLK = S // 256
    SCALE = 128 ** -0.5
    x = P.dram("x", [S, D], F32, "ExternalInput")
    gains = P.dram("gains", [1, D], F32, "ExternalInput")
    wq = P.dram("wq", [D, 256], F32, "ExternalInput")
    wk = P.dram("wk", [D, 256], F32, "ExternalInput")
    wv = P.dram("wv", [D, 256], F32, "ExternalInput")
    tab = P.dram("tab", [32, 2], F32, "ExternalInput")
    oh = P.dram("oh", [32, GL], F32, "ExternalInput")
    negrow = P.dram("negrow", [128, 256], F32, "ExternalInput")
    oout = P.dram("o", [S, 256], F32, "ExternalOutput")
    scr_t = [nc.dram_tensor(f"scr{h}", [128, GL], F32, kind="Internal") for h in range(2)]

    ident = make_ident(P)
    gB = load_gains(P, gains, 1)
    nm = Normer(P, ident, nbuf=4, npt=2)
    wl = WLoader(P, direct=True)
    wq_b = P.sb("wq_b", [128, 8, 256], BF16)
    wk_b = P.sb("wk_b", [128, 8, 256], BF16)
    wv_b = P.sb("wv_b", [128, 8, 256], BF16)
    wl.load(wq, 0, 8, 0, 256, wq_b, "wq_b")
    wl.load(wk, 0, 8, 0, 256, wk_b, "wk_b")
    wl.load(wv, 0, 8, 0, 256, wv_b, "wv_b")

    QT = P.sb("QT", [128, 2, S], BF16)
    KT = P.sb("KT", [128, 2, S], BF16)
    VA = P.sb("VA", [128, NTILE, 2, 132], BF16)
    sel = P.sb("sel", [128, NTILE, 2, 32], F32)
    BT = P.sb("BT", [128, 2, 5, 2, 256], F32)
    c31 = P.sb("c31", [128, 2], F32)
    ksum = P.sb("ksum", [128, 2, NTILE], F32)
    kmT = P.sb("kmT", [128, 2, 32], F32)
    kmT_b = P.sb("kmT_b", [128, 2, 32], BF16)
    maskadd = P.sb("maskadd", [128, 32, 32], F32)
    gm = P.sb("gm", [128, 2, 32], F32)
    top8 = P.sb("top8", [128, 2, 8], F32)
    xt = [P.sb(f"xt{i}", [128, D], F32) for i in range(4)]
    yT = [P.sb(f"yT{i}", [128, 8, 128], BF16) for i in range(3)]
    acc = [P.sb(f"acc{i}", [128, 2, 132], F32) for i in range(2)]
    osb = [P.sb(f"osb{i}", [128, 2, 128], F32) for i in range(2)]
    rec = [P.sb(f"rec{i}", [128, 2], F32) for i in range(2)]
    bk = [P.ps(f"bk{i}", [128, 512], F32) for i in range(6)]

    ohs = P.sb("ohs", [32, GL], F32)
    tabs = P.sb("tabs", [32, 2], F32)
    tabc = P.sb("tabc", [32, 2, 128], F32)
    gp = P.sb("gp", [128, GL], F32)
    ngs = P.sb("ngs", [128, 256], F32)
    P.dma("sp", ohs[:], oh[:, :], writes=["ohs"])
    P.dma("sp", tabs[:], tab[:, :], writes=["tabs"])
    P.dma("sp", ngs[:], negrow[:, :], writes=["ngs"])
    for h in range(2):
        if "bc" in skip:
            continue
        P.op("dve", lambda e, h=h: e.tensor_copy(out=tabc[:, h, :], in_=tabs[:, h:h + 1].to_broadcast([32, 128])),
             reads=["tabs"], writes=["tabc"])
    for h in range(2):
        if "bias" in skip:
            continue
        for c in range(GL // 512):
            P.op("pe", lambda e, h=h, c=c: e.matmul(bk[c][:], lhsT=tabc[:, h, :], rhs=ohs[:, c * 512:(c + 1) * 512],
                                                    start=True, stop=True), reads=["tabc", "ohs"], writes=[f"bk{c}"])
            P.op("dve", lambda e, c=c: e.tensor_copy(out=gp[:, c * 512:(c + 1) * 512], in_=bk[c][:]),
                 reads=[f"bk{c}"], writes=["gp"])
        P.op("dve", lambda e: e.tensor_tensor(out=gp[:, 0:256], in0=gp[:, 0:256], in1=ngs[:], op=ALU.add),
             reads=["gp", "ngs"], writes=["gp"])
        P.op("act", lambda e, h=h: e.copy(out=c31[:, h:h + 1], in_=gp[:, GL - 1:GL]), reads=["gp"], writes=["c31"])
        P.dma("sp", scr_t[h].ap(), gp[:], reads=["gp"], writes=[f"scr{h}"])
        for dJ in range(5):
            if "skew" in skip:
                continue
            for half in range(2):
                off = dJ * 256 - half * 128 + 255
                src = bass.AP(tensor=scr_t[h], offset=off, ap=[[GL - 1, 128], [1, 256]])
                P.dma("sp", BT[:, h, dJ, half, :], src, reads=[f"scr{h}"], writes=["BT"])
    P.op("pool", lambda e: e.memset(maskadd[:], 0.0), writes=["maskadd"])
    if "maskadd" not in skip:
      P.op("pool", lambda e: e.affine_select(out=maskadd[:], in_=maskadd[:], pattern=[[1, 32], [-1, 32]],
                                            compare_op=ALU.is_ge, fill=NEG, base=-1, channel_multiplier=0),
         reads=["maskadd"], writes=["maskadd"])
    if "vaones" not in skip:
        P.op("pool", lambda e: e.memset(VA[:, :, :, 128:129], 1.0), writes=["VAones"])

    if stop <= 0:
        return P.finish()
    pQK = bk[3][:].rearrange("p (a b) -> p a b", a=4)
    pV = bk[4][:, 0:256]
    XR, YR = 4, 3

    def p1a(t):
        P.dma("sp", xt[t % XR][:], x[t * 128:(t + 1) * 128, :], writes=[f"xt{t % XR}"])
        nm.stats(xt[t % XR][:], f"xt{t % XR}", t % XR)

    def p1b(t):
        nm.finish(t % XR)
        nm.scale(xt[t % XR][:], f"xt{t % XR}", gB[:, 0, :], t % XR)

    def p1c(t):
        nm.transpose(nm.yb[t % XR], f"nm_yb{t % XR}", yT[t % YR][:], f"yT{t % YR}", t)

    def p1d(t):
        b = t % YR
        for j, w in enumerate((wq_b, wk_b)):
            wres = "wq_b" if j == 0 else "wk_b"
            for hh in range(2):
                for k in range(8):
                    P.op("pe", lambda e, k=k, w=w, hh=hh, j=j: e.matmul(
                        pQK[:, j * 2 + hh, :], lhsT=w[:, k, hh * 128:(hh + 1) * 128], rhs=yT[b][:, k, :],
                        start=(k == 0), stop=(k == 7)), reads=[f"yT{b}", wres], writes=["bk3"])
        for k in range(8):
            P.op("pe", lambda e, k=k: e.matmul(pV, lhsT=yT[b][:, k, :], rhs=wv_b[:, k, :],
                                               start=(k == 0), stop=(k == 7)), reads=[f"yT{b}", "wv_b"], writes=["bk4"])
        P.op("act", lambda e: e.copy(out=QT[:, :, t * 128:(t + 1) * 128], in_=pQK[:, 0:2, :]), reads=["bk3"], writes=[("QT", t)])
        P.op("act", lambda e: e.copy(out=KT[:, :, t * 128:(t + 1) * 128], in_=pQK[:, 2:4, :]), reads=["bk3"], writes=[("KT", t)])
        P.op("dve", lambda e: e.tensor_reduce(out=ksum[:, :, t], in_=KT[:, :, t * 128:(t + 1) * 128], axis=AX.X, op=ALU.add),
             reads=[("KT", t)], writes=["ksum"])
        P.op("act", lambda e: e.copy(out=VA[:, t, :, 0:128], in_=pV.rearrange("p (a b) -> p a b", a=2)),
             reads=["bk4"], writes=[("VA", t)])

    for step in range(NTILE + 3):
        for s_, f_ in enumerate((p1a, p1b, p1c, p1d)):
            t = step - s_
            if 0 <= t < NTILE:
                f_(t)
    if stop <= 1:
        return P.finish()
    ksv = ksum[:].rearrange("p h (n two) -> p h n two", two=2)
    P.op("dve", lambda e: e.tensor_tensor(out=kmT[:, :, 0:NBLK], in0=ksv[:, :, :, 0], in1=ksv[:, :, :, 1], op=ALU.add),
         reads=["ksum"], writes=["kmT"])
    if NBLK < 32:
        P.op("dve", lambda e: e.memset(kmT[:, :, NBLK:32], 0.0), writes=["kmT"])
    P.op("dve", lambda e: e.tensor_scalar(out=kmT_b[:], in0=kmT[:], scalar1=1.0 / 256, scalar2=None, op0=ALU.mult),
         reads=["kmT"], writes=["kmT_b"])

    pG = bk[5][:, 0:64].rearrange("p (a b) -> p a b", a=2)
    for t in range(NTILE):
        j = t // 2
        for hh in range(2):
            P.op("pe", lambda e, hh=hh: e.matmul(pG[:, hh, :], lhsT=QT[:, hh, t * 128:(t + 1) * 128], rhs=kmT_b[:, hh, :],
                                                 start=True, stop=True), reads=[("QT", t), "kmT_b"], writes=["bk5"])
        for hh in range(2):
            P.op("dve", lambda e, hh=hh: e.tensor_tensor(out=gm[:, hh, :], in0=pG[:, hh, :], in1=maskadd[:, j, :], op=ALU.add),
                 reads=["bk5", "maskadd"], writes=["gm"])
        for hh in range(2):
            P.op("dve", lambda e, hh=hh: e.max(out=top8[:, hh, :], in_=gm[:, hh, :]), reads=["gm"], writes=["top8"])
        for hh in range(2):
            P.op("dve", lambda e, hh=hh: e.tensor_scalar(out=sel[:, t, hh, :], in0=gm[:, hh, :], scalar1=top8[:, hh, 2:3],
                                                         scalar2=None, op0=ALU.is_ge),
                 reads=["gm", "top8"], writes=[("sel", t)])

    if stop <= 2:
        return P.finish()
    tmpS = [gp[:, i * 512:(i + 1) * 512].rearrange("p (a b) -> p a b", a=2) for i in range(3)]
    PT = [nm.yb[i][:, 0:512].rearrange("p (a b) -> p a b", a=2) for i in range(3)]
    iters = [(hh, J, n) for hh in range(2) for J in range(NBLK) for n in range(J, -1, -1)]
    NI = len(iters)

    def views(k):
        r = k % 3
        pS = bk[r][:].rearrange("p (a b) -> p a b", a=2)
        pO = bk[3 + r][:, 0:264].rearrange("p (a b) -> p a b", a=2)
        return r, pS, pO

    def stA(k):
        hh, J, n = iters[k]
        r, pS, pO = views(k)
        for half in range(2):
            P.op("pe", lambda e, half=half: e.matmul(
                pS[:, half, :], lhsT=KT[:, hh, n * 256 + half * 128:n * 256 + (half + 1) * 128],
                rhs=QT[:, hh, J * 256:(J + 1) * 256], start=True, stop=True),
                reads=[("QT", 2 * J), ("QT", 2 * J + 1), ("KT", 2 * n + half)],
                writes=[f"bk{r}"])

    def stB(k):
        hh, J, n = iters[k]
        r, pS, pO = views(k)
        dJ = J - n
        if dJ <= 4:
            P.op("dve", lambda e: e.scalar_tensor_tensor(out=tmpS[r], in0=pS, scalar=SCALE, in1=BT[:, hh, dJ, :, :],
                                                         op0=ALU.mult, op1=ALU.add),
                 reads=[f"bk{r}", "BT"], writes=[f"tmpS{r}"])
            P.op("act", lambda e: e.activation(out=PT[r], in_=tmpS[r], func=AF.Exp),
                 reads=[f"tmpS{r}"], writes=[f"PT{r}"])
        else:
            P.op("act", lambda e: e.activation(out=PT[r], in_=pS, func=AF.Exp, bias=c31[:, hh:hh + 1], scale=SCALE),
                 reads=[f"bk{r}", "c31"], writes=[f"PT{r}"])

    def stC(k):
        hh, J, n = iters[k]
        r, pS, pO = views(k)
        for qt in range(2):
            for half in range(2):
                P.op("pe", lambda e, qt=qt, half=half: e.matmul(
                    pO[:, qt, 0:129], lhsT=PT[r][:, half, qt * 128:(qt + 1) * 128],
                    rhs=VA[:, 2 * n + half, hh, 0:129], start=(half == 0), stop=(half == 1)),
                    reads=[f"PT{r}", ("VA", 2 * n + half), "VAones"],
                    writes=[f"bk{3 + r}"])

    def stD(k):
        hh, J, n = iters[k]
        r, pS, pO = views(k)
        dJ = J - n
        ab = J % 2
        for qt in range(2):
            ares = (f"acc{ab}", qt)
            if dJ == 0:
                P.op("dve", lambda e, qt=qt: e.tensor_copy(out=acc[ab][:, qt, 0:129], in_=pO[:, qt, 0:129]),
                     reads=[f"bk{3 + r}"], writes=[ares])
            else:
                P.op("dve", lambda e, qt=qt: e.scalar_tensor_tensor(
                    out=acc[ab][:, qt, 0:129], in0=pO[:, qt, 0:129], scalar=sel[:, 2 * J + qt, hh, n:n + 1],
                    in1=acc[ab][:, qt, 0:129], op0=ALU.mult, op1=ALU.add),
                    reads=[f"bk{3 + r}", ("sel", 2 * J + qt), ares], writes=[ares])
        if n == 0:
            both = [(f"acc{ab}", 0), (f"acc{ab}", 1)]
            P.op("dve", lambda e: e.reciprocal(out=rec[ab][:], in_=acc[ab][:, :, 128]), reads=both, writes=[f"rec{ab}"])
            for qt in range(2):
                P.op("dve", lambda e, qt=qt: e.tensor_scalar(out=osb[ab][:, qt, :], in0=acc[ab][:, qt, 0:128],
                                                             scalar1=rec[ab][:, qt:qt + 1], scalar2=None, op0=ALU.mult),
                     reads=[(f"acc{ab}", qt), f"rec{ab}"], writes=[(f"osb{ab}", qt)])
            P.dma("sp", oout[J * 256:(J + 1) * 256, hh * 128:(hh + 1) * 128].rearrange("(q p) d -> p q d", p=128),
                  osb[ab][:], reads=[(f"osb{ab}", 0), (f"osb{ab}", 1)], writes=[("o", hh, J)])

    for step in range(NI + 3):
        for s_, f_ in enumerate((stA, stB, stC, stD)):
            k = step - s_
            if 0 <= k < NI:
                f_(k)
    return P.finish()


def run_moba(h1, g_mix, w_in, rel_table):
    nc = _get("moba", build_moba)
    oh, neg = moba_consts()
    maps = []
    for c in range(8):
        b, hp = c // 4, c % 4
        cs = slice(hp * 256, (hp + 1) * 256)
        maps.append({"x": np.ascontiguousarray(h1[b * SEQ:(b + 1) * SEQ]), "gains": np.ascontiguousarray(g_mix[None, :]),
                     "wq": np.ascontiguousarray(w_in[:, cs]),
                     "wk": np.ascontiguousarray(w_in[:, 1024 + hp * 256:1024 + (hp + 1) * 256]),
                     "wv": np.ascontiguousarray(w_in[:, 2048 + hp * 256:2048 + (hp + 1) * 256]),
                     "tab": np.ascontiguousarray(rel_table[:, hp * 2:hp * 2 + 2]), "oh": oh, "negrow": neg})
    res = run_bass_kernel_spmd(nc, maps, core_ids=list(range(8)))
    o = np.empty((2, SEQ, D), np.float32)
    for c in range(8):
        b, hp = c // 4, c % 4
        o[b, :, hp * 256:(hp + 1) * 256] = res.results[c]["o"]
    return o.reshape(2 * SEQ, D)


def kernel(x, norm_mix, norm_ffn, hgrn_w_in, hgrn_lb_logits, hgrn_out_norm, hgrn_w_out,
           moba_w_in, moba_w_out, rel_bias_table, ffn_w13, ffn_w2, final_norm):
    f = lambda a: np.ascontiguousarray(np.asarray(a, dtype=np.float32))
    x = f(x)
    norm_mix, norm_ffn, final_norm = f(norm_mix), f(norm_ffn), f(final_norm)
    hgrn_w_in, hgrn_lb_logits, hgrn_out_norm, hgrn_w_out = f(hgrn_w_in), f(hgrn_lb_logits), f(hgrn_out_norm), f(hgrn_w_out)
    moba_w_in, moba_w_out, rel_bias_table = f(moba_w_in), f(moba_w_out), f(rel_bias_table)
    ffn_w13, ffn_w2 = f(ffn_w13), f(ffn_w2)
    xf = x.reshape(2 * SEQ, D)
    o0 = run_hgrn(x, norm_mix[0], hgrn_w_in[0], hgrn_lb_logits)
    gains0 = f(np.stack([norm_mix[0], hgrn_out_norm[0], norm_ffn[0], final_norm]))
    h1 = run_tl("hgrn", False, xf, o0, gains0, hgrn_w_out[0], ffn_w13[0], ffn_w2[0],
                w_og=f(hgrn_w_in[0][:, 3072:4096]))
    o1 = run_moba(h1, norm_mix[1], moba_w_in[0], rel_bias_table)
    gains1 = f(np.stack([norm_mix[1], hgrn_out_norm[0], norm_ffn[1], final_norm]))
    out = run_tl("moba", True, h1, o1, gains1, moba_w_out[0], ffn_w13[1], ffn_w2[1])
    return out.reshape(2, SEQ, D)
```

```python
from contextlib import ExitStack

import numpy as np
import concourse.bass as bass
import concourse.mybir as mybir
from concourse.bass_utils import run_bass_kernel_spmd

F32 = mybir.dt.float32
BF16 = mybir.dt.bfloat16
ALU = mybir.AluOpType
AF = mybir.ActivationFunctionType
AX = mybir.AxisListType

D = 1024
DFF = 2816
SEQ = 8192
EPS = 1e-6
NEG = -1.0e30


class Prog:
    def __init__(self):
        self.nc = bass.Bass("TRN2", target_bir_lowering=False)
        nc = self.nc
        self.es = ExitStack()
        self.E = {"pe": nc.tensor, "act": nc.scalar, "dve": nc.vector, "pool": nc.gpsimd, "sp": nc.sync}
        self.semh = {}
        self.cnt = {}
        self.NCH = 8
        self.dn = {"sp": 0, "pool": 0}
        for k in ["pe", "act", "dve", "pool"] + [f"dq_{q}_{i}" for q in ("sp", "pool") for i in range(self.NCH)]:
            self.semh[k] = self.es.enter_context(nc.semaphore(k))
            self.cnt[k] = 0
        self.lastw = {}
        self.readers = {}
        self.waited = {}
        self._uid = 0

    def sb(self, name, shape, dt):
        return self.es.enter_context(self.nc.sbuf_tensor(name, list(shape), dt))

    def ps(self, name, shape, dt):
        return self.es.enter_context(self.nc.psum_tensor(name, list(shape), dt))

    def dram(self, name, shape, dt, kind):
        return self.nc.dram_tensor(name, list(shape), dt, kind=kind).ap()

    def _deps(self, reads, writes):
        deps = {}

        def add(k, v):
            if deps.get(k, 0) < v:
                deps[k] = v

        for r in reads:
            if r in self.lastw:
                add(*self.lastw[r])
        for w in writes:
            if w in self.lastw:
                add(*self.lastw[w])
            for k, v in self.readers.get(w, {}).items():
                add(k, v)
        return deps

    def _wait(self, e, deps, skip=None):
        eng = self.E[e]
        for k, v in deps.items():
            if k == skip:
                continue
            if self.waited.get((e, k), 0) >= v:
                continue
            eng.wait_ge(self.semh[k], v)
            self.waited[(e, k)] = v

    def _commit(self, key, val, reads, writes):
        for r in reads:
            d = self.readers.setdefault(r, {})
            if d.get(key, 0) < val:
                d[key] = val
        for w in writes:
            self.lastw[w] = (key, val)
            self.readers[w] = {}

    def op(self, e, fn, reads=(), writes=()):
        deps = self._deps(reads, writes)
        self._wait(e, deps, skip=("pe" if e == "pe" else None))
        ins = fn(self.E[e])
        self.cnt[e] += 1
        ins.then_inc(self.semh[e], 1)
        self._commit(e, self.cnt[e], reads, writes)

    def dma(self, q, out, in_, reads=(), writes=()):
        key = f"dq_{q}_{self.dn[q] % self.NCH}"
        self.dn[q] += 1
        deps = self._deps(reads, writes)
        if self.cnt[key] > deps.get(key, 0):
            deps[key] = self.cnt[key]
        self._wait(q, deps)
        ins = self.E[q].dma_start(out=out, in_=in_)
        self.cnt[key] += 16
        ins.then_inc(self.semh[key], 16)
        self._commit(key, self.cnt[key], reads, writes)

    def finish(self):
        for k, v in self.cnt.items():
            if v > 0:
                self.E["sp"].wait_ge(self.semh[k], v)
        self.es.close()
        return self.nc


def make_ident(P, name="ident"):
    ones = P.sb(name + "_ones", [128, 128], BF16)
    ident = P.sb(name, [128, 128], BF16)
    P.op("pool", lambda e: e.memset(ones[:], 1.0), writes=[name + "_ones"])
    P.op("pool", lambda e: e.affine_select(out=ident[:], in_=ones[:], pattern=[[-1, 128]],
                                            compare_op=ALU.is_equal, fill=0.0, base=0,
                                            channel_multiplier=1),
         reads=[name + "_ones"], writes=[name])
    return ident


class Normer:
    def __init__(self, P, ident, nbuf=2, npt=2, lnexp=False):
        self.P = P
        self.ident = ident
        self.lnexp = lnexp
        self.junk = P.sb("nm_junk", [128, D], BF16)
        self.ss = [P.sb(f"nm_ss{i}", [128, 1], F32) for i in range(nbuf)]
        self.rs = [P.sb(f"nm_rs{i}", [128, 1], F32) for i in range(nbuf)]
        self.yb = [P.sb(f"nm_yb{i}", [128, D], BF16) for i in range(nbuf)]
        self.pT = [P.ps(f"nm_pT{i}", [128, 8, 128], BF16) for i in range(npt)]
        self.n = 0
        self.nbuf = nbuf
        self.npt = npt

    def stats(self, x_ap, xres, i):
        P = self.P
        ss = self.ss[i]
        P.op("dve", lambda e: e.memset(ss[:], 0.0), writes=[f"nm_ss{i}"])
        P.op("act", lambda e: e.activation(out=self.junk[:], in_=x_ap, func=AF.Square, accum_out=ss[:]),
             reads=[xres, f"nm_ss{i}"], writes=["nm_junk", f"nm_ss{i}"])

    def finish(self, i):
        P = self.P
        ss, rs = self.ss[i], self.rs[i]
        P.op("dve", lambda e: e.tensor_scalar(out=ss[:], in0=ss[:], scalar1=1.0 / D, scalar2=EPS,
                                              op0=ALU.mult, op1=ALU.add),
             reads=[f"nm_ss{i}"], writes=[f"nm_ss{i}"])
        if self.lnexp:
            P.op("act", lambda e: e.activation(out=ss[:], in_=ss[:], func=AF.Ln),
                 reads=[f"nm_ss{i}"], writes=[f"nm_ss{i}"])
            P.op("act", lambda e: e.activation(out=rs[:], in_=ss[:], func=AF.Exp, scale=-0.5),
                 reads=[f"nm_ss{i}"], writes=[f"nm_rs{i}"])
        else:
            P.op("act", lambda e: e.activation(out=ss[:], in_=ss[:], func=AF.Sqrt),
                 reads=[f"nm_ss{i}"], writes=[f"nm_ss{i}"])
            P.op("dve", lambda e: e.reciprocal(out=rs[:], in_=ss[:]),
                 reads=[f"nm_ss{i}"], writes=[f"nm_rs{i}"])
        return rs

    def rstd(self, x_ap, xres, i):
        self.stats(x_ap, xres, i)
        return self.finish(i)

    def scale(self, x_ap, xres, gB_ap, i):
        P = self.P
        yb, rs = self.yb[i], self.rs[i]
        P.op("dve", lambda e: e.scalar_tensor_tensor(out=yb[:], in0=x_ap, scalar=rs[:, 0:1], in1=gB_ap,
                                                     op0=ALU.mult, op1=ALU.mult),
             reads=[xres, f"nm_rs{i}", "gB"], writes=[f"nm_yb{i}"])
        return yb

    def transpose(self, src_bf, srcres, dst_ap, dstres, i, evac="act"):
        P = self.P
        j = i % self.npt
        pT = self.pT[j]
        for k in range(8):
            P.op("pe", lambda e, k=k: e.transpose(out=pT[:, k, :], in_=src_bf[:, k * 128:(k + 1) * 128],
                                                   identity=self.ident[:]),
                 reads=[srcres, "ident"], writes=[f"nm_pT{j}"])
        if evac == "act":
            P.op("act", lambda e: e.copy(out=dst_ap, in_=pT[:]), reads=[f"nm_pT{j}"], writes=[dstres])
        else:
            P.op(evac, lambda e: e.tensor_copy(out=dst_ap, in_=pT[:]), reads=[f"nm_pT{j}"], writes=[dstres])

    def norm_T(self, x_ap, xres, gB_ap, dst_ap, dstres, evac="act"):
        i = self.n % self.nbuf
        self.n += 1
        self.rstd(x_ap, xres, i)
        yb = self.scale(x_ap, xres, gB_ap, i)
        self.transpose(yb, f"nm_yb{i}", dst_ap, dstres, i, evac)


class WLoader:
    def __init__(self, P, nstage=2, stage_elems=2048, cast_engines=("pool", "dve"), direct=False):
        self.P = P
        self.direct = direct
        self.stage = [] if direct else [P.sb(f"wst{i}", [128, stage_elems], F32) for i in range(nstage)]
        self.n = 0
        self.stage_elems = stage_elems
        self.cast_engines = cast_engines
        self.queues = ("sp", "pool")

    def load(self, w_ap, r0, nk, c0, ncols, dst, dstres, dk0=0, dc0=0):
        P = self.P
        if self.direct:
            src = w_ap[r0:r0 + nk * 128, c0:c0 + ncols].rearrange("(k p) c -> p k c", p=128)
            P.dma("pool", dst[:, dk0:dk0 + nk, dc0:dc0 + ncols], src, writes=[dstres])
            return
        kmax = max(1, self.stage_elems // ncols)
        k = 0
        while k < nk:
            kk = min(kmax, nk - k)
            i = self.n % len(self.stage)
            q = self.queues[self.n % len(self.queues)]
            ce = self.cast_engines[self.n % len(self.cast_engines)]
            self.n += 1
            st = self.stage[i]
            stv = st[:, 0:kk * ncols].rearrange("p (k c) -> p k c", k=kk)
            src = w_ap[r0 + k * 128:r0 + (k + kk) * 128, c0:c0 + ncols].rearrange("(k p) c -> p k c", p=128)
            P.dma(q, stv, src, writes=[f"wst{i}"])
            dv = dst[:, dk0 + k:dk0 + k + kk, dc0:dc0 + ncols]
            P.op(ce, lambda e, dv=dv, stv=stv: e.tensor_copy(out=dv, in_=stv), reads=[f"wst{i}"], writes=[dstres])
            k += kk


def load_gains(P, gains_ap, n):
    gB = P.sb("gB", [128, n, D], F32)
    for i in range(n):
        P.dma("sp", gB[:, i:i + 1, :], gains_ap[i:i + 1, :].partition_broadcast(128), writes=["gB"])
    return gB


def build_tl(mode, final):
    P = Prog()
    hg = mode == "hgrn"
    T = 2048
    NP = 2
    TP = T // NP
    NT = TP // 128
    hin = P.dram("hin", [T, D], F32, "ExternalInput")
    oin = P.dram("oin", [T, D], F32, "ExternalInput")
    gains = P.dram("gains", [4, D], F32, "ExternalInput")
    w_out = P.dram("w_out", [D, D], F32, "ExternalInput")
    w13 = P.dram("w13", [D, 2 * DFF], F32, "ExternalInput")
    w2 = P.dram("w2", [DFF, D], F32, "ExternalInput")
    if hg:
        w_og = P.dram("w_og", [D, D], F32, "ExternalInput")
    hout = P.dram("hout", [T, D], F32, "ExternalOutput")

    ident = make_ident(P)
    need = ([0, 1] if hg else []) + [2] + ([3] if final else [])
    gBt = P.sb("gB", [128, len(need), D], F32)
    for i_, r_ in enumerate(need):
        P.dma("sp", gBt[:, i_:i_ + 1, :], gains[r_:r_ + 1, :].partition_broadcast(128), writes=["gB"])

    class _G:
        def __getitem__(self, key):
            return gBt[key[0], need.index(key[1]), key[2]]
    gB = _G()
    nm = Normer(P, ident, nbuf=3, npt=2)
    wl = WLoader(P, direct=True)

    h = P.sb("h", [128, NT, D], F32)
    zT = P.sb("zT", [128, 8, TP], BF16)
    aT = P.sb("aT", [128, 11, TP], BF16)
    NSL = 2 if hg else 3
    wsl = [P.sb(f"wsl{i}", [128, 8, 512], BF16) for i in range(NSL)]
    w2s = [P.sb(f"w2s{i}", [128, 11, 512], BF16) for i in range(2)]
    wout_b = P.sb("wout_b", [128, 8, D], BF16)
    sgl = [P.sb(f"sgl{i}", [128, 512], F32) for i in range(2)]
    pacc = [P.ps(f"pacc{i}", [128, 512], F32) for i in range(6)]
    nacc = [0]

    def acc():
        i = nacc[0] % 6
        nacc[0] += 1
        return pacc[i], f"pacc{i}"

    nsl = [0]

    def next_wsl():
        i = nsl[0] % NSL
        nsl[0] += 1
        return wsl[i], f"wsl{i}"

    rings = {}

    def ring(name, n, shape, dt):
        rings[name] = [P.sb(f"{name}{i}", shape, dt) for i in range(n)]

    def X(name, t):
        n = len(rings[name])
        return rings[name][t % n], f"{name}{t % n}"

    ring("ot", 5 if hg else 3, [128, D], F32)
    ring("onb", 3, [128, D], BF16)
    ring("oT", 3, [128, 8, 128], BF16)
    ring("zb", 3, [128, D], BF16)
    for fam in ("o", "z"):
        ring("ss" + fam, 4, [128, 1], F32)
        ring("rs" + fam, 4, [128, 1], F32)
    if hg:
        ring("yb", 3, [128, D], BF16)
        ring("yT", 3, [128, 8, 128], BF16)
        ring("sg", 2, [128, D], F32)
        ring("ssy", 4, [128, 1], F32)
        ring("rsy", 4, [128, 1], F32)
        wog_b = aT[:, 0:8, :]
    aT_keys = [("aT", c, tg) for c in range(11) for tg in range(TP // 512)]
    wl.load(w_out, 0, 8, 0, D, wout_b, "wout_b")

    def stats(x_ap, xres, fam, g):
        ss, sr = X("ss" + fam, g)
        P.op("dve", lambda e: e.memset(ss[:], 0.0), writes=[sr])
        P.op("act", lambda e: e.activation(out=nm.junk[:], in_=x_ap, func=AF.Square, accum_out=ss[:]),
             reads=[xres, sr], writes=["nm_junk", sr])

    def finish(fam, g):
        ss, sr = X("ss" + fam, g)
        rs, rr = X("rs" + fam, g)
        P.op("dve", lambda e: e.tensor_scalar(out=ss[:], in0=ss[:], scalar1=1.0 / D, scalar2=EPS, op0=ALU.mult, op1=ALU.add),
             reads=[sr], writes=[sr])
        P.op("act", lambda e: e.activation(out=ss[:], in_=ss[:], func=AF.Sqrt), reads=[sr], writes=[sr])
        P.op("dve", lambda e: e.reciprocal(out=rs[:], in_=ss[:]), reads=[sr], writes=[rr])

    def transposes(src, sres, g):
        j = g % 2
        for k in range(8):
            P.op("pe", lambda e, k=k: e.transpose(out=nm.pT[j][:, k, :], in_=src[:, k * 128:(k + 1) * 128], identity=ident[:]),
                 reads=[sres, "ident"], writes=[f"nm_pT{j}"])

    for ps_ in range(NP):
        t0 = ps_ * TP
        G = lambda t: ps_ * NT + t

        def f_load(t):
            g = G(t)
            P.dma("sp", h[:, t, :], hin[t0 + t * 128:t0 + (t + 1) * 128, :], writes=[("h", t)])
            if hg:
                stats(h[:, t, :], ("h", t), "y", g)
            else:
                ot, otr = X("ot", g)
                P.dma("sp", ot[:], oin[t0 + t * 128:t0 + (t + 1) * 128, :], writes=[otr])

        def f_rs(t):
            finish("y", G(t))

        def f_scale(t):
            g = G(t)
            if hg:
                yb, ybr = X("yb", g)
                rs, rr = X("rsy", g)
                P.op("dve", lambda e: e.scalar_tensor_tensor(out=yb[:], in0=h[:, t, :], scalar=rs[:, 0:1], in1=gB[:, 0, :],
                                                             op0=ALU.mult, op1=ALU.mult), reads=[("h", t), rr, "gB"], writes=[ybr])
            else:
                ot, otr = X("ot", g)
                onb, onr = X("onb", g)
                P.op("act", lambda e: e.copy(out=onb[:], in_=ot[:]), reads=[otr], writes=[onr])

        def f_tr1(t):
            if hg:
                yb, ybr = X("yb", G(t))
                transposes(yb, ybr, 2 * G(t))

        def f_ev1(t):
            g = G(t)
            yT, ytr = X("yT", g)
            j = (2 * g) % 2
            P.op("act", lambda e: e.copy(out=yT[:], in_=nm.pT[j][:]), reads=[f"nm_pT{j}"], writes=[ytr])
            ot, otr = X("ot", g)
            P.dma("sp", ot[:], oin[t0 + t * 128:t0 + (t + 1) * 128, :], writes=[otr])
            stats(ot[:], otr, "o", g)

        og_banks = {}

        def f_og(t):
            if hg:
                finish("o", G(t))
                yT, ytr = X("yT", G(t))
                og_banks[t] = []
                for cg in range(2):
                    pa, pres = acc()
                    og_banks[t].append((pa, pres))
                    for k in range(8):
                        P.op("pe", lambda e, k=k, pa=pa, cg=cg: e.matmul(pa[:], lhsT=yT[:, k, :], rhs=wog_b[:, k, cg * 512:(cg + 1) * 512],
                                                                         start=(k == 0), stop=(k == 7)),
                             reads=[ytr] + aT_keys, writes=[pres])

        def f_sg(t):
            if hg:
                ot, otr = X("ot", G(t))
                rso, rro = X("rso", G(t))
                P.op("dve", lambda e: e.scalar_tensor_tensor(out=ot[:], in0=ot[:], scalar=rso[:, 0:1], in1=gB[:, 1, :],
                                                             op0=ALU.mult, op1=ALU.mult), reads=[otr, rro, "gB"], writes=[otr])
                sg, sgr = X("sg", G(t))
                for cg in range(2):
                    pa, pres = og_banks[t][cg]
                    P.op("act", lambda e, pa=pa, cg=cg: e.activation(out=sg[:, cg * 512:(cg + 1) * 512], in_=pa[:], func=AF.Sigmoid),
                         reads=[pres], writes=[(sgr, cg)])

        def f_gate(t):
            if hg:
                g = G(t)
                ot, otr = X("ot", g)
                sg, sgr = X("sg", g)
                onb, onr = X("onb", g)
                P.op("dve", lambda e: e.tensor_tensor(out=onb[:], in0=ot[:], in1=sg[:], op=ALU.mult),
                     reads=[otr, (sgr, 0), (sgr, 1)], writes=[onr])

        def f_tr2(t):
            onb, onr = X("onb", G(t))
            transposes(onb, onr, 2 * G(t) + 1)

        def f_ev2(t):
            oT, otr_ = X("oT", G(t))
            j = (2 * G(t) + 1) % 2
            P.op("act", lambda e: e.copy(out=oT[:], in_=nm.pT[j][:]), reads=[f"nm_pT{j}"], writes=[otr_])

        wo_banks = {}

        def f_wo(t):
            oT, otr_ = X("oT", G(t))
            wo_banks[t] = []
            for cg in range(2):
                pa, pres = acc()
                wo_banks[t].append((pa, pres))
                for k in range(8):
                    P.op("pe", lambda e, k=k, pa=pa, cg=cg: e.matmul(pa[:], lhsT=oT[:, k, :], rhs=wout_b[:, k, cg * 512:(cg + 1) * 512],
                                                                     start=(k == 0), stop=(k == 7)),
                         reads=[otr_, "wout_b"], writes=[pres])

        def f_res(t):
            for cg in range(2):
                pa, pres = wo_banks[t][cg]
                hv = h[:, t, cg * 512:(cg + 1) * 512]
                P.op("dve", lambda e, pa=pa, hv=hv: e.tensor_tensor(out=hv, in0=hv, in1=pa[:], op=ALU.add),
                     reads=[pres, ("h", t)], writes=[("h", t)])
            stats(h[:, t, :], ("h", t), "z", G(t))

        def f_rs3(t):
            finish("z", G(t))

        def f_sc3(t):
            zb, zbr = X("zb", G(t))
            rs, rr = X("rsz", G(t))
            P.op("dve", lambda e: e.scalar_tensor_tensor(out=zb[:], in0=h[:, t, :], scalar=rs[:, 0:1], in1=gB[:, 2, :],
                                                         op0=ALU.mult, op1=ALU.mult), reads=[("h", t), rr, "gB"], writes=[zbr])

        def f_tr3(t):
            zb, zbr = X("zb", G(t))
            transposes(zb, zbr, 2 * G(t))

        def f_ev3(t):
            j = (2 * G(t)) % 2
            P.op("act", lambda e: e.copy(out=zT[:, :, t * 128:(t + 1) * 128], in_=nm.pT[j][:]), reads=[f"nm_pT{j}"], writes=[("zT", t)])

        if hg:
            P.dma("pool", wog_b, w_og[:, :].rearrange("(k p) c -> p k c", p=128), writes=aT_keys)
        if hg:
            stages = [f_load, f_rs, f_scale, f_tr1, f_ev1, f_og, f_sg, f_gate, f_tr2, f_ev2, f_wo, f_res, f_rs3, f_sc3, f_tr3, f_ev3]
        else:
            stages = [f_load, f_scale, f_tr2, f_ev2, f_wo, f_res, f_rs3, f_sc3, f_tr3, f_ev3]
        NS = len(stages)
        for step in range(NT + NS - 1):
            for si in range(NS - 1, -1, -1):
                t = step - si
                if 0 <= t < NT:
                    stages[si](t)

        zT_all = [("zT", t) for t in range(NT)]
        for half in range(2):
            c0 = half * 11
            ci = 0
            while ci < 11:
                ncg = min(2, 11 - ci)
                s, sres = next_wsl()
                wl.load(w13, 0, 8, (c0 + ci) * 128, ncg * 128, s, sres, dc0=0)
                wl.load(w13, 0, 8, DFF + (c0 + ci) * 128, ncg * 128, s, sres, dc0=256)
                for fc in range(ncg):
                    for tg in range(TP // 512):
                        pg, pgres = acc()
                        pu, pures = acc()
                        for k in range(8):
                            P.op("pe", lambda e, k=k, pg=pg, s=s, fc=fc, tg=tg: e.matmul(
                                pg[:], lhsT=s[:, k, fc * 128:(fc + 1) * 128], rhs=zT[:, k, tg * 512:(tg + 1) * 512],
                                start=(k == 0), stop=(k == 7)), reads=zT_all[tg * 4:tg * 4 + 4] + [sres], writes=[pgres])
                        for k in range(8):
                            P.op("pe", lambda e, k=k, pu=pu, s=s, fc=fc, tg=tg: e.matmul(
                                pu[:], lhsT=s[:, k, 256 + fc * 128:256 + (fc + 1) * 128],
                                rhs=zT[:, k, tg * 512:(tg + 1) * 512],
                                start=(k == 0), stop=(k == 7)), reads=zT_all[tg * 4:tg * 4 + 4] + [sres], writes=[pures])
                        sb_ = nacc[0] % 2
                        P.op("act", lambda e, pg=pg, sb_=sb_: e.activation(out=sgl[sb_][:], in_=pg[:], func=AF.Silu),
                             reads=[pgres], writes=[f"sgl{sb_}"])
                        av = aT[:, ci + fc, tg * 512:(tg + 1) * 512]
                        P.op("dve", lambda e, pu=pu, sb_=sb_, av=av: e.tensor_tensor(out=av, in0=sgl[sb_][:], in1=pu[:],
                                                                                     op=ALU.mult),
                             reads=[pures, f"sgl{sb_}"], writes=[("aT", ci + fc, tg)])
                ci += ncg
            for cg in range(2):
                wi = (half * 2 + cg) % 2
                wl.load(w2, c0 * 128, 11, cg * 512, 512, w2s[wi], f"w2s{wi}")
                for t in range(NT):
                    pa, pres = acc()
                    for c in range(11):
                        P.op("pe", lambda e, c=c, pa=pa, wi=wi: e.matmul(pa[:], lhsT=aT[:, c, t * 128:(t + 1) * 128],
                                                                        rhs=w2s[wi][:, c, :], start=(c == 0), stop=(c == 10)),
                             reads=[("aT", c, t // 4) for c in range(11)] + [f"w2s{wi}"], writes=[pres])
                    hv = h[:, t, cg * 512:(cg + 1) * 512]
                    P.op("dve", lambda e, pa=pa, hv=hv: e.tensor_tensor(out=hv, in0=hv, in1=pa[:], op=ALU.add),
                         reads=[pres, ("h", t)], writes=[("h", t)])
        for t in range(NT):
            if final:
                g = G(t)
                stats(h[:, t, :], ("h", t), "o", g + 1000 * 0)
                finish("o", g)
                rs, rr = X("rso", g)
                ot, otr = X("ot", g)
                P.op("dve", lambda e, rs=rs, ot=ot: e.scalar_tensor_tensor(out=ot[:], in0=h[:, t, :], scalar=rs[:, 0:1],
                                                                           in1=gB[:, 3, :], op0=ALU.mult, op1=ALU.mult),
                     reads=[("h", t), rr, "gB"], writes=[otr])
                P.dma("sp", hout[t0 + t * 128:t0 + (t + 1) * 128, :], ot[:], reads=[otr], writes=[("hout", ps_, t)])
            else:
                P.dma("sp", hout[t0 + t * 128:t0 + (t + 1) * 128, :], h[:, t, :], reads=[("h", t)],
                      writes=[("hout", ps_, t)])
    return P.finish()


_CACHE = {}


def _get(name, builder):
    if name not in _CACHE:
        _CACHE[name] = builder()
    return _CACHE[name]


def run_tl(mode, final, hin, oin, gains, w_out, w13, w2, w_og=None):
    nc = _get(("tl", mode, final), lambda: build_tl(mode, final))
    maps = []
    for c in range(8):
        m = {"hin": np.ascontiguousarray(hin[c * 2048:(c + 1) * 2048]),
             "oin": np.ascontiguousarray(oin[c * 2048:(c + 1) * 2048]),
             "gains": gains, "w_out": w_out, "w13": w13, "w2": w2}
        if mode == "hgrn":
            m["w_og"] = w_og
        maps.append(m)
    res = run_bass_kernel_spmd(nc, maps, core_ids=list(range(8)))
    return np.concatenate([res.results[c]["hout"] for c in range(8)], axis=0)


def build_hgrn(NTILE=SEQ // 128):
    P = Prog()
    x = P.dram("x", [SEQ, D], F32, "ExternalInput")
    gains = P.dram("gains", [1, D], F32, "ExternalInput")
    wq = P.dram("wq", [D, 256], F32, "ExternalInput")
    wf = P.dram("wf", [D, 256], F32, "ExternalInput")
    wi = P.dram("wi", [D, 256], F32, "ExternalInput")
    lbl = P.dram("lbl", [128, 4], F32, "ExternalInput")
    oout = P.dram("o", [SEQ, 256], F32, "ExternalOutput")

    ident = make_ident(P)
    gB = load_gains(P, gains, 1)
    nm = Normer(P, ident, nbuf=5, npt=2, lnexp=True)
    wl = WLoader(P, direct=True)
    wqf_b = P.sb("wqf_b", [128, 8, 512], BF16)
    wi_b = P.sb("wi_b", [128, 8, 256], BF16)
    wl.load(wq, 0, 8, 0, 256, wqf_b, "wqf_b", dc0=0)
    wl.load(wf, 0, 8, 0, 256, wqf_b, "wqf_b", dc0=256)
    wl.load(wi, 0, 8, 0, 256, wi_b, "wi_b")

    lbt = P.sb("lbt", [128, 4], F32)
    lb = P.sb("lb", [128, 2], F32)
    oml = P.sb("oml", [128, 2], F32)
    P.dma("sp", lbt[:], lbl[:, :], writes=["lbt"])
    P.op("dve", lambda e: e.tensor_tensor(out=lb[:], in0=lbt[:, 2:4], in1=lbt[:, 0:2], op=ALU.subtract),
         reads=["lbt"], writes=["lb"])
    P.op("act", lambda e: e.activation(out=lb[:], in_=lb[:], func=AF.Exp), reads=["lb"], writes=["lb"])
    P.op("dve", lambda e: e.tensor_scalar(out=lb[:], in0=lb[:], scalar1=1.0, scalar2=None, op0=ALU.add),
         reads=["lb"], writes=["lb"])
    P.op("dve", lambda e: e.reciprocal(out=lb[:], in_=lb[:]), reads=["lb"], writes=["lb"])
    P.op("dve", lambda e: e.tensor_scalar(out=oml[:], in0=lb[:], scalar1=-1.0, scalar2=1.0, op0=ALU.mult, op1=ALU.add),
         reads=["lb"], writes=["oml"])

    onesf = P.sb("onesf", [128, 2, 128], F32)
    mask = P.sb("mask", [128, 2, 128], F32)
    P.op("pool", lambda e: e.memset(onesf[:], 1.0), writes=["onesf"])
    P.op("pool", lambda e: e.affine_select(out=mask[:], in_=onesf[:], pattern=[[0, 2], [1, 128]],
                                            compare_op=ALU.is_ge, fill=0.0, base=0, channel_multiplier=-1),
         reads=["onesf"], writes=["mask"])

    state = P.sb("state", [128, 2, 128], F32)
    state_b = P.sb("state_b", [128, 2, 128], BF16)
    P.op("dve", lambda e: e.memset(state[:], 0.0), writes=[("state", 0), ("state", 1)])
    P.op("dve", lambda e: e.memset(state_b[:], 0.0), writes=["state_b"])

    rings = {}

    def ring(name, n, shape, dt):
        rings[name] = [P.sb(f"{name}{i}", shape, dt) for i in range(n)]

    def X(name, t):
        n = len(rings[name])
        return rings[name][t % n], f"{name}{t % n}"

    ring("xt", 4, [128, D], F32)
    ring("yT", 3, [128, 8, 128], BF16)
    ring("EE", 4, [128, 4, 128], F32)
    ring("qr", 4, [128, 2, 128], F32)
    ring("ib", 12, [128, 256], BF16)
    ring("qs", 7, [128, 2, 128], F32)
    ring("fg", 3, [128, 2, 128], F32)
    ring("kk", 7, [128, 2, 128], F32)
    ring("lf", 3, [128, 2, 128], F32)
    ring("AA", 4, [128, 4, 2, 128], F32)
    ring("ex", 7, [128, 4, 2, 128], F32)
    ring("qc", 3, [128, 2, 128], BF16)
    ring("kc", 3, [128, 2, 128], BF16)
    ring("qd", 5, [128, 2, 128], BF16)
    ring("kh", 3, [128, 2, 128], BF16)
    ring("khT", 3, [128, 2, 128], BF16)
    ring("scm", 3, [128, 2, 128], BF16)
    ring("osb", 3, [128, 256], F32)
    pQF = [P.ps(f"pQF{i}", [128, 4, 128], F32) for i in range(2)]
    pI = P.ps("pI", [128, 512], F32)[:, 0:256]
    pKS = P.ps("pKS", [128, 512], F32)
    pK = pKS[:, 0:128].bitcast(BF16).rearrange("p (a b) -> p a b", a=2)
    pS = pKS[:, 256:512].rearrange("p (a b) -> p a b", a=2)
    pO = P.ps("pO", [128, 512], F32)[:, 0:256]
    pU = P.ps("pU", [128, 512], F32)[:, 0:256].rearrange("p (a b) -> p a b", a=2)
    NMB = nm.nbuf

    def s_load(t):
        xt, xr = X("xt", t)
        P.dma("sp", xt[:], x[t * 128:(t + 1) * 128, :], writes=[xr])
        nm.stats(xt[:], xr, t % NMB)

    def s_rs(t):
        nm.finish(t % NMB)

    def s_scale(t):
        xt, xr = X("xt", t)
        nm.scale(xt[:], xr, gB[:, 0, :], t % NMB)

    def s_tr(t):
        i = t % NMB
        j = t % 2
        for k in range(8):
            P.op("pe", lambda e, k=k: e.transpose(out=nm.pT[j][:, k, :], in_=nm.yb[i][:, k * 128:(k + 1) * 128], identity=ident[:]),
                 reads=[f"nm_yb{i}", "ident"], writes=[f"nm_pT{j}"])

    def s_trc(t):
        yT, yr = X("yT", t)
        j = t % 2
        P.op("act", lambda e: e.copy(out=yT[:], in_=nm.pT[j][:]), reads=[f"nm_pT{j}"], writes=[yr])

    def s_proj(t):
        yT, yr = X("yT", t)
        pq, pres = pQF[t % 2], f"pQF{t % 2}"
        for c in range(4):
            for k in range(8):
                P.op("pe", lambda e, k=k, c=c: e.matmul(pq[:, c, :], lhsT=wqf_b[:, k, c * 128:(c + 1) * 128], rhs=yT[:, k, :],
                                                        start=(k == 0), stop=(k == 7)), reads=[yr, "wqf_b"], writes=[pres])
        for k in range(8):
            P.op("pe", lambda e, k=k: e.matmul(pI, lhsT=yT[:, k, :], rhs=wi_b[:, k, :],
                                               start=(k == 0), stop=(k == 7)), reads=[yr, "wi_b"], writes=["pI"])

    def s_evac(t):
        pq, pres = pQF[t % 2], f"pQF{t % 2}"
        EE, er = X("EE", t)
        qr, qrr = X("qr", t)
        ib, ibr = X("ib", t)
        P.op("act", lambda e: e.activation(out=EE[:], in_=pq[:], func=AF.Exp, scale=-1.0), reads=[pres], writes=[er])
        P.op("act", lambda e: e.copy(out=qr[:], in_=pq[:, 0:2, :]), reads=[pres], writes=[qrr])
        P.op("dve", lambda e: e.tensor_copy(out=ib[:], in_=pI), reads=["pI"], writes=[ibr])

    def s_sig(t):
        EE, er = X("EE", t)
        P.op("act", lambda e: e.activation(out=EE[:], in_=EE[:], func=AF.Ln, bias=1.0, scale=1.0), reads=[er], writes=[er])
        P.op("act", lambda e: e.activation(out=EE[:], in_=EE[:], func=AF.Exp, scale=-1.0), reads=[er], writes=[er])

    def s_gate(t):
        EE, er = X("EE", t)
        qr, qrr = X("qr", t)
        qs, qsr = X("qs", t)
        fg, fr = X("fg", t)
        kk, kr = X("kk", t)
        P.op("pool", lambda e: e.tensor_tensor(out=qs[:], in0=qr[:], in1=EE[:, 0:2, :], op=ALU.mult), reads=[qrr, er], writes=[qsr])
        for hh in range(2):
            P.op("dve", lambda e, hh=hh: e.tensor_scalar(out=fg[:, hh, :], in0=EE[:, 2 + hh, :], scalar1=oml[:, hh:hh + 1],
                                                         scalar2=lb[:, hh:hh + 1], op0=ALU.mult, op1=ALU.add),
                 reads=[er, "oml", "lb"], writes=[(fr, hh)])
        P.op("pool", lambda e: e.tensor_scalar(out=kk[:], in0=fg[:], scalar1=-1.0, scalar2=1.0, op0=ALU.mult, op1=ALU.add),
             reads=[(fr, 0), (fr, 1)], writes=[kr])

    def s_lf(t):
        fg, fr = X("fg", t)
        lf, lr = X("lf", t)
        P.op("act", lambda e: e.activation(out=lf[:], in_=fg[:], func=AF.Ln), reads=[(fr, 0), (fr, 1)], writes=[lr])

    def s_scan(t):
        lf, lr = X("lf", t)
        AA, ar = X("AA", t)
        for hh in range(2):
            P.op("dve", lambda e, hh=hh: e.tensor_tensor_scan(out=AA[:, 2, hh, :], data0=onesf[:, 0, :], data1=lf[:, hh, :],
                                                              initial=0.0, op0=ALU.mult, op1=ALU.add),
                 reads=[lr, "onesf"], writes=[(ar, 2, hh)])

    def s_aa(t):
        AA, ar = X("AA", t)
        for hh in range(2):
            eng = "dve" if hh == 0 else "pool"
            cum_ = AA[:, 2, hh, :]
            cenB = AA[:, 2, hh, 63:64].to_broadcast([128, 128])
            lastB = AA[:, 2, hh, 127:128].to_broadcast([128, 128])
            P.op(eng, lambda e, hh=hh: e.tensor_tensor(out=AA[:, 0, hh, :], in0=cum_, in1=cenB, op=ALU.subtract),
                 reads=[(ar, 2, hh)], writes=[(ar, 0, hh)])
            P.op(eng, lambda e, hh=hh: e.tensor_tensor(out=AA[:, 1, hh, :], in0=cenB, in1=cum_, op=ALU.subtract),
                 reads=[(ar, 2, hh)], writes=[(ar, 1, hh)])
            P.op(eng, lambda e, hh=hh: e.tensor_tensor(out=AA[:, 3, hh, :], in0=lastB, in1=cum_, op=ALU.subtract),
                 reads=[(ar, 2, hh)], writes=[(ar, 3, hh)])

    def s_exp(t):
        AA, ar = X("AA", t)
        ex, exr = X("ex", t)
        P.op("act", lambda e: e.activation(out=ex[:], in_=AA[:], func=AF.Exp),
             reads=[(ar, i, hh) for i in range(4) for hh in range(2)], writes=[exr])

    def s_prod(t):
        ex, exr = X("ex", t)
        qs, qsr = X("qs", t)
        kk, kr = X("kk", t)
        qc, qcr = X("qc", t)
        kc, kcr = X("kc", t)
        qd, qdr = X("qd", t)
        kh, khr = X("kh", t)
        P.op("dve", lambda e: e.tensor_tensor(out=qc[:], in0=qs[:], in1=ex[:, 0, :, :], op=ALU.mult), reads=[qsr, exr], writes=[qcr])
        P.op("pool", lambda e: e.tensor_tensor(out=qd[:], in0=qs[:], in1=ex[:, 2, :, :], op=ALU.mult), reads=[qsr, exr], writes=[qdr])
        P.op("dve", lambda e: e.tensor_tensor(out=kc[:], in0=kk[:], in1=ex[:, 1, :, :], op=ALU.mult), reads=[kr, exr], writes=[kcr])
        P.op("pool", lambda e: e.tensor_tensor(out=kh[:], in0=kk[:], in1=ex[:, 3, :, :], op=ALU.mult), reads=[kr, exr], writes=[khr])

    def s_pe5(t):
        qc, qcr = X("qc", t)
        kc, kcr = X("kc", t)
        kh, khr = X("kh", t)
        for hh in range(2):
            P.op("pe", lambda e, hh=hh: e.transpose(out=pK[:, hh, :], in_=kh[:, hh, :], identity=ident[:]),
                 reads=[khr, "ident"], writes=["pKS"])
        for hh in range(2):
            P.op("pe", lambda e, hh=hh: e.matmul(pS[:, hh, :], lhsT=kc[:, hh, :], rhs=qc[:, hh, :], start=True, stop=True),
                 reads=[kcr, qcr], writes=["pKS"])

    def s_ev5(t):
        khT, ktr = X("khT", t)
        scm, scr_ = X("scm", t)
        P.op("dve", lambda e: e.tensor_copy(out=khT[:], in_=pK), reads=["pKS"], writes=[ktr])
        P.op("dve", lambda e: e.tensor_tensor(out=scm[:], in0=pS, in1=mask[:], op=ALU.mult), reads=["pKS", "mask"], writes=[scr_])

    def s_pe6(t):
        khT, ktr = X("khT", t)
        scm, scr_ = X("scm", t)
        ib, ibr = X("ib", t)
        qd, qdr = X("qd", t)
        for hh in range(2):
            P.op("pe", lambda e, hh=hh: e.matmul(pO[:, hh * 128:(hh + 1) * 128], lhsT=scm[:, hh, :],
                                                 rhs=ib[:, hh * 128:(hh + 1) * 128], start=True, stop=False),
                 reads=[scr_, ibr], writes=["pO"])
            P.op("pe", lambda e, hh=hh: e.matmul(pO[:, hh * 128:(hh + 1) * 128], lhsT=qd[:, hh, :],
                                                 rhs=state_b[:, hh, :], start=False, stop=True),
                 reads=[qdr, "state_b"], writes=["pO"])
        for hh in range(2):
            P.op("pe", lambda e, hh=hh: e.matmul(pU[:, hh, :], lhsT=khT[:, hh, :], rhs=ib[:, hh * 128:(hh + 1) * 128],
                                                 start=True, stop=True),
                 reads=[ktr, ibr], writes=["pU"])

    def s_fin(t):
        ex, exr = X("ex", t)
        osb, osr = X("osb", t)
        for hh in range(2):
            P.op("dve", lambda e, hh=hh: e.scalar_tensor_tensor(out=state[:, hh, :], in0=state[:, hh, :],
                                                                scalar=ex[:, 2, hh, 127:128], in1=pU[:, hh, :],
                                                                op0=ALU.mult, op1=ALU.add),
                 reads=[("state", hh), exr, "pU"], writes=[("state", hh)])
        P.op("pool", lambda e: e.tensor_copy(out=state_b[:], in_=state[:]), reads=[("state", 0), ("state", 1)], writes=["state_b"])
        P.op("act", lambda e: e.copy(out=osb[:], in_=pO), reads=["pO"], writes=[osr])
        P.dma("sp", oout[t * 128:(t + 1) * 128, :], osb[:], reads=[osr], writes=[("o", t)])

    stages = [s_load, s_rs, s_scale, s_tr, s_trc, s_proj, s_evac, s_sig, s_gate, s_lf, s_scan, s_aa, s_exp, s_prod,
              s_pe5, s_ev5, s_pe6, s_fin]
    NS = len(stages)
    order = [NS - 1] + [i for i in range(NS - 3, -1, -1)] + [NS - 2]
    for step in range(NTILE + NS - 1):
        for si in order:
            t = step - si
            if 0 <= t < NTILE:
                stages[si](t)
    return P.finish()


def run_hgrn(x, g_mix, w_in, lb_logits):
    nc = _get("hgrn", build_hgrn)
    maps = []
    for c in range(8):
        b, hp = c // 4, c % 4
        cs = slice(hp * 256, (hp + 1) * 256)
        lbl = lb_logits[:, cs].reshape(2, 2, 128).transpose(2, 0, 1).reshape(128, 4)
        maps.append({"x": np.ascontiguousarray(x[b]), "gains": np.ascontiguousarray(g_mix[None, :]),
                     "wq": np.ascontiguousarray(w_in[:, cs]),
                     "wf": np.ascontiguousarray(w_in[:, 1024 + hp * 256:1024 + (hp + 1) * 256]),
                     "wi": np.ascontiguousarray(w_in[:, 2048 + hp * 256:2048 + (hp + 1) * 256]),
                     "lbl": np.ascontiguousarray(lbl)})
    res = run_bass_kernel_spmd(nc, maps, core_ids=list(range(8)))
    o = np.empty((2, SEQ, D), np.float32)
    for c in range(8):
        b, hp = c // 4, c % 4
        o[b, :, hp * 256:(hp + 1) * 256] = res.results[c]["o"]
    return o.reshape(2 * SEQ, D)


GL = 1536
MNEG = -30000.0


def t5_bucket_np(dist):
    n = np.maximum(dist, 0)
    nf = np.maximum(n, 16).astype(np.float32)
    large = 16 + (np.log(nf / np.float32(16)) / np.float32(np.log(64.0)) * np.float32(16)).astype(np.int32)
    large = np.minimum(large, 31)
    return np.where(n < 16, n, large)


def moba_consts():
    i = np.arange(GL)
    bk = t5_bucket_np(i - 255)
    oh = np.zeros((32, GL), np.float32)
    valid = i >= 255
    oh[bk[valid], i[valid]] = 1.0
    neg = np.zeros((128, 256), np.float32)
    neg[:, :255] = MNEG
    return oh, neg


def build_moba(S=SEQ, stop=99, skip=()):
    P = Prog()
    nc = P.nc
    NTILE = S // 128
    NBLK = S // 256
    SCALE = 128 ** -0.5
    x = P.dram("x", [S, D], F32, "ExternalInput")
    gains = P.dram("gains", [1, D], F32, "ExternalInput")
    wq = P.dram("wq", [D, 256], F32, "ExternalInput")
    wk = P.dram("wk", [D, 256], F32, "ExternalInput")
    wv = P.dram("wv", [D, 256], F32, "ExternalInput")
    tab = P.dram("tab", [32, 2], F32, "ExternalInput")
    oh = P.dram("oh", [32, GL], F32, "ExternalInput")
    negrow = P.dram("negrow", [128, 256], F32, "ExternalInput")
    oout = P.dram("o", [S, 256], F32, "ExternalOutput")
    scr_t = [nc.dram_tensor(f"scr{h}", [128, GL], F32, kind="Internal") for h in range(2)]

    ident = make_ident(P)
    gB = load_gains(P, gains, 1)
    nm = Normer(P, ident, nbuf=4, npt=2)
    wl = WLoader(P, direct=True)
    wq_b = P.sb("wq_b", [128, 8, 256], BF16)
    wk_b = P.sb("wk_b", [128, 8, 256], BF16)
    wv_b = P.sb("wv_b", [128, 8, 256], BF16)
    wl.load(wq, 0, 8, 0, 256, wq_b, "wq_b")
    wl.load(wk, 0, 8, 0, 256, wk_b, "wk_b")
    wl.load(wv, 0, 8, 0, 256, wv_b, "wv_b")

    QT = P.sb("QT", [128, 2, S], BF16)
    KT = P.sb("KT", [128, 2, S], BF16)
    VA = P.sb("VA", [128, NTILE, 2, 132], BF16)
    sel = P.sb("sel", [128, NTILE, 2, 32], F32)
    BT = P.sb("BT", [128, 2, 5, 2, 256], F32)
    c31 = P.sb("c31", [128, 2], F32)
    ksum = P.sb("ksum", [128, 2, NTILE], F32)
    kmT = P.sb("kmT", [128, 2, 32], F32)
    kmT_b = P.sb("kmT_b", [128, 2, 32], BF16)
    maskadd = P.sb("maskadd", [128, 32, 32], F32)
    gm = P.sb("gm", [128, 2, 32], F32)
    top8 = P.sb("top8", [128, 2, 8], F32)
    xt = [P.sb(f"xt{i}", [128, D], F32) for i in range(4)]
    yT = [P.sb(f"yT{i}", [128, 8, 128], BF16) for i in range(3)]
    acc = [P.sb(f"acc{i}", [128, 2, 132], F32) for i in range(2)]
    osb = [P.sb(f"osb{i}", [128, 2, 128], F32) for i in range(2)]
    rec = [P.sb(f"rec{i}", [128, 2], F32) for i in range(2)]
    bk = [P.ps(f"bk{i}", [128, 512], F32) for i in range(6)]

    ohs = P.sb("ohs", [32, GL], F32)
    tabs = P.sb("tabs", [32, 2], F32)
    tabc = P.sb("tabc", [32, 2, 128], F32)
    gp = P.sb("gp", [128, GL], F32)
    ngs = P.sb("ngs", [128, 256], F32)
    P.dma("sp", ohs[:], oh[:, :], writes=["ohs"])
    P.dma("sp", tabs[:], tab[:, :], writes=["tabs"])
    P.dma("sp", ngs[:], negrow[:, :], writes=["ngs"])
    for h in range(2):
        if "bc" in skip:
            continue
        P.op("dve", lambda e, h=h: e.tensor_copy(out=tabc[:, h, :], in_=tabs[:, h:h + 1].to_broadcast([32, 128])),
             reads=["tabs"], writes=["tabc"])
    for h in range(2):
        if "bias" in skip:
            continue
        for c in range(GL // 512):
            P.op("pe", lambda e, h=h, c=c: e.matmul(bk[c][:], lhsT=tabc[:, h, :], rhs=ohs[:, c * 512:(c + 1) * 512],
                                                    start=True, stop=True), reads=["tabc", "ohs"], writes=[f"bk{c}"])
            P.op("dve", lambda e, c=c: e.tensor_copy(out=gp[:, c * 512:(c + 1) * 512], in_=bk[c][:]),
                 reads=[f"bk{c}"], writes=["gp"])
        P.op("dve", lambda e: e.tensor_tensor(out=gp[:, 0:256], in0=gp[:, 0:256], in1=ngs[:], op=ALU.add),
             reads=["gp", "ngs"], writes=["gp"])
        P.op("act", lambda e, h=h: e.copy(out=c31[:, h:h + 1], in_=gp[:, GL - 1:GL]), reads=["gp"], writes=["c31"])
        P.dma("sp", scr_t[h].ap(), gp[:], reads=["gp"], writes=[f"scr{h}"])
        for dJ in range(5):
            if "skew" in skip:
                continue
            for half in range(2):
                off = dJ * 256 - half * 128 + 255
                src = bass.AP(tensor=scr_t[h], offset=off, ap=[[GL - 1, 128], [1, 256]])
                P.dma("sp", BT[:, h, dJ, half, :], src, reads=[f"scr{h}"], writes=["BT"])
    P.op("pool", lambda e: e.memset(maskadd[:], 0.0), writes=["maskadd"])
    if "maskadd" not in skip:
      P.op("pool", lambda e: e.affine_select(out=maskadd[:], in_=maskadd[:], pattern=[[1, 32], [-1, 32]],
                                            compare_op=ALU.is_ge, fill=NEG, base=-1, channel_multiplier=0),
         reads=["maskadd"], writes=["maskadd"])
    if "vaones" not in skip:
        P.op("pool", lambda e: e.memset(VA[:, :, :, 128:129], 1.0), writes=["VAones"])

    if stop <= 0:
        return P.finish()
    pQK = bk[3][:].rearrange("p (a b) -> p a b", a=4)
    pV = bk[4][:, 0:256]
    XR, YR = 4, 3

    def p1a(t):
        P.dma("sp", xt[t % XR][:], x[t * 128:(t + 1) * 128, :], writes=[f"xt{t % XR}"])
        nm.stats(xt[t % XR][:], f"xt{t % XR}", t % XR)

    def p1b(t):
        nm.finish(t % XR)
        nm.scale(xt[t % XR][:], f"xt{t % XR}", gB[:, 0, :], t % XR)

    def p1c(t):
        nm.transpose(nm.yb[t % XR], f"nm_yb{t % XR}", yT[t % YR][:], f"yT{t % YR}", t)

    def p1d(t):
        b = t % YR
        for j, w in enumerate((wq_b, wk_b)):
            wres = "wq_b" if j == 0 else "wk_b"
            for hh in range(2):
                for k in range(8):
                    P.op("pe", lambda e, k=k, w=w, hh=hh, j=j: e.matmul(
                        pQK[:, j * 2 + hh, :], lhsT=w[:, k, hh * 128:(hh + 1) * 128], rhs=yT[b][:, k, :],
                        start=(k == 0), stop=(k == 7)), reads=[f"yT{b}", wres], writes=["bk3"])
        for k in range(8):
            P.op("pe", lambda e, k=k: e.matmul(pV, lhsT=yT[b][:, k, :], rhs=wv_b[:, k, :],
                                               start=(k == 0), stop=(k == 7)), reads=[f"yT{b}", "wv_b"], writes=["bk4"])
        P.op("act", lambda e: e.copy(out=QT[:, :, t * 128:(t + 1) * 128], in_=pQK[:, 0:2, :]), reads=["bk3"], writes=[("QT", t)])
        P.op("act", lambda e: e.copy(out=KT[:, :, t * 128:(t + 1) * 128], in_=pQK[:, 2:4, :]), reads=["bk3"], writes=[("KT", t)])
        P.op("dve", lambda e: e.tensor_reduce(out=ksum[:, :, t], in_=KT[:, :, t * 128:(t + 1) * 128], axis=AX.X, op=ALU.add),
             reads=[("KT", t)], writes=["ksum"])
        P.op("act", lambda e: e.copy(out=VA[:, t, :, 0:128], in_=pV.rearrange("p (a b) -> p a b", a=2)),
             reads=["bk4"], writes=[("VA", t)])

    for step in range(NTILE + 3):
        for s_, f_ in enumerate((p1a, p1b, p1c, p1d)):
            t = step - s_
            if 0 <= t < NTILE:
                f_(t)
    if stop <= 1:
        return P.finish()
    ksv = ksum[:].rearrange("p h (n two) -> p h n two", two=2)
    P.op("dve", lambda e: e.tensor_tensor(out=kmT[:, :, 0:NBLK], in0=ksv[:, :, :, 0], in1=ksv[:, :, :, 1], op=ALU.add),
         reads=["ksum"], writes=["kmT"])
    if NBLK < 32:
        P.op("dve", lambda e: e.memset(kmT[:, :, NBLK:32], 0.0), writes=["kmT"])
    P.op("dve", lambda e: e.tensor_scalar(out=kmT_b[:], in0=kmT[:], scalar1=1.0 / 256, scalar2=None, op0=ALU.mult),
         reads=["kmT"], writes=["kmT_b"])

    pG = bk[5][:, 0:64].rearrange("p (a b) -> p a b", a=2)
    for t in range(NTILE):
        j = t // 2
        for hh in range(2):
            P.op("pe", lambda e, hh=hh: e.matmul(pG[:, hh, :], lhsT=QT[:, hh, t * 128:(t + 1) * 128], rhs=kmT_b[:, hh, :],
                                                 start=True, stop=True), reads=[("QT", t), "kmT_b"], writes=["bk5"])
        for hh in range(2):
            P.op("dve", lambda e, hh=hh: e.tensor_tensor(out=gm[:, hh, :], in0=pG[:, hh, :], in1=maskadd[:, j, :], op=ALU.add),
                 reads=["bk5", "maskadd"], writes=["gm"])
        for hh in range(2):
            P.op("dve", lambda e, hh=hh: e.max(out=top8[:, hh, :], in_=gm[:, hh, :]), reads=["gm"], writes=["top8"])
        for hh in range(2):
            P.op("dve", lambda e, hh=hh: e.tensor_scalar(out=sel[:, t, hh, :], in0=gm[:, hh, :], scalar1=top8[:, hh, 2:3],
                                                         scalar2=None, op0=ALU.is_ge),
                 reads=["gm", "top8"], writes=[("sel", t)])

    if stop <= 2:
        return P.finish()
    tmpS = [gp[:, i * 512:(i + 1) * 512].rearrange("p (a b) -> p a b", a=2) for i in range(3)]
    PT = [nm.yb[i][:, 0:512].rearrange("p (a b) -> p a b", a=2) for i in range(3)]
    iters = [(hh, J, n) for hh in range(2) for J in range(NBLK) for n in range(J, -1, -1)]
    NI = len(iters)

    def views(k):
        r = k % 3
        pS = bk[r][:].rearrange("p (a b) -> p a b", a=2)
        pO = bk[3 + r][:, 0:264].rearrange("p (a b) -> p a b", a=2)
        return r, pS, pO

    def stA(k):
        hh, J, n = iters[k]
        r, pS, pO = views(k)
        for half in range(2):
            P.op("pe", lambda e, half=half: e.matmul(
                pS[:, half, :], lhsT=KT[:, hh, n * 256 + half * 128:n * 256 + (half + 1) * 128],
                rhs=QT[:, hh, J * 256:(J + 1) * 256], start=True, stop=True),
                reads=[("QT", 2 * J), ("QT", 2 * J + 1), ("KT", 2 * n + half)],
                writes=[f"bk{r}"])

    def stB(k):
        hh, J, n = iters[k]
        r, pS, pO = views(k)
        dJ = J - n
        if dJ <= 4:
            P.op("dve", lambda e: e.scalar_tensor_tensor(out=tmpS[r], in0=pS, scalar=SCALE, in1=BT[:, hh, dJ, :, :],
                                                         op0=ALU.mult, op1=ALU.add),
                 reads=[f"bk{r}", "BT"], writes=[f"tmpS{r}"])
            P.op("act", lambda e: e.activation(out=PT[r], in_=tmpS[r], func=AF.Exp),
                 reads=[f"tmpS{r}"], writes=[f"PT{r}"])
        else:
            P.op("act", lambda e: e.activation(out=PT[r], in_=pS, func=AF.Exp, bias=c31[:, hh:hh + 1], scale=SCALE),
                 reads=[f"bk{r}", "c31"], writes=[f"PT{r}"])

    def stC(k):
        hh, J, n = iters[k]
        r, pS, pO = views(k)
        for qt in range(2):
            for half in range(2):
                P.op("pe", lambda e, qt=qt, half=half: e.matmul(
                    pO[:, qt, 0:129], lhsT=PT[r][:, half, qt * 128:(qt + 1) * 128],
                    rhs=VA[:, 2 * n + half, hh, 0:129], start=(half == 0), stop=(half == 1)),
                    reads=[f"PT{r}", ("VA", 2 * n + half), "VAones"],
                    writes=[f"bk{3 + r}"])

    def stD(k):
        hh, J, n = iters[k]
        r, pS, pO = views(k)
        dJ = J - n
        ab = J % 2
        for qt in range(2):
            ares = (f"acc{ab}", qt)
            if dJ == 0:
                P.op("dve", lambda e, qt=qt: e.tensor_copy(out=acc[ab][:, qt, 0:129], in_=pO[:, qt, 0:129]),
                     reads=[f"bk{3 + r}"], writes=[ares])
            else:
                P.op("dve", lambda e, qt=qt: e.scalar_tensor_tensor(
                    out=acc[ab][:, qt, 0:129], in0=pO[:, qt, 0:129], scalar=sel[:, 2 * J + qt, hh, n:n + 1],
                    in1=acc[ab][:, qt, 0:129], op0=ALU.mult, op1=ALU.add),
                    reads=[f"bk{3 + r}", ("sel", 2 * J + qt), ares], writes=[ares])
        if n == 0:
            both = [(f"acc{ab}", 0), (f"acc{ab}", 1)]
            P.op("dve", lambda e: e.reciprocal(out=rec[ab][:], in_=acc[ab][:, :, 128]), reads=both, writes=[f"rec{ab}"])
            for qt in range(2):
                P.op("dve", lambda e, qt=qt: e.tensor_scalar(out=osb[ab][:, qt, :], in0=acc[ab][:, qt, 0:128],
                                                             scalar1=rec[ab][:, qt:qt + 1], scalar2=None, op0=ALU.mult),
                     reads=[(f"acc{ab}", qt), f"rec{ab}"], writes=[(f"osb{ab}", qt)])
            P.dma("sp", oout[J * 256:(J + 1) * 256, hh * 128:(hh + 1) * 128].rearrange("(q p) d -> p q d", p=128),
                  osb[ab][:], reads=[(f"osb{ab}", 0), (f"osb{ab}", 1)], writes=[("o", hh, J)])

    for step in range(NI + 3):
        for s_, f_ in enumerate((stA, stB, stC, stD)):
            k = step - s_
            if 0 <= k < NI:
                f_(k)
    return P.finish()


def run_moba(h1, g_mix, w_in, rel_table):
    nc = _get("moba", build_moba)
    oh, neg = moba_consts()
    maps = []
    for c in range(8):
        b, hp = c // 4, c % 4
        cs = slice(hp * 256, (hp + 1) * 256)
        maps.append({"x": np.ascontiguousarray(h1[b * SEQ:(b + 1) * SEQ]), "gains": np.ascontiguousarray(g_mix[None, :]),
                     "wq": np.ascontiguousarray(w_in[:, cs]),
                     "wk": np.ascontiguousarray(w_in[:, 1024 + hp * 256:1024 + (hp + 1) * 256]),
                     "wv": np.ascontiguousarray(w_in[:, 2048 + hp * 256:2048 + (hp + 1) * 256]),
                     "tab": np.ascontiguousarray(rel_table[:, hp * 2:hp * 2 + 2]), "oh": oh, "negrow": neg})
    res = run_bass_kernel_spmd(nc, maps, core_ids=list(range(8)))
    o = np.empty((2, SEQ, D), np.float32)
    for c in range(8):
        b, hp = c // 4, c % 4
        o[b, :, hp * 256:(hp + 1) * 256] = res.results[c]["o"]
    return o.reshape(2 * SEQ, D)


def kernel(x, norm_mix, norm_ffn, hgrn_w_in, hgrn_lb_logits, hgrn_out_norm, hgrn_w_out,
           moba_w_in, moba_w_out, rel_bias_table, ffn_w13, ffn_w2, final_norm):
    f = lambda a: np.ascontiguousarray(np.asarray(a, dtype=np.float32))
    x = f(x)
    norm_mix, norm_ffn, final_norm = f(norm_mix), f(norm_ffn), f(final_norm)
    hgrn_w_in, hgrn_lb_logits, hgrn_out_norm, hgrn_w_out = f(hgrn_w_in), f(hgrn_lb_logits), f(hgrn_out_norm), f(hgrn_w_out)
    moba_w_in, moba_w_out, rel_bias_table = f(moba_w_in), f(moba_w_out), f(rel_bias_table)
    ffn_w13, ffn_w2 = f(ffn_w13), f(ffn_w2)
    xf = x.reshape(2 * SEQ, D)
    o0 = run_hgrn(x, norm_mix[0], hgrn_w_in[0], hgrn_lb_logits)
    gains0 = f(np.stack([norm_mix[0], hgrn_out_norm[0], norm_ffn[0], final_norm]))
    h1 = run_tl("hgrn", False, xf, o0, gains0, hgrn_w_out[0], ffn_w13[0], ffn_w2[0],
                w_og=f(hgrn_w_in[0][:, 3072:4096]))
    o1 = run_moba(h1, norm_mix[1], moba_w_in[0], rel_bias_table)
    gains1 = f(np.stack([norm_mix[1], hgrn_out_norm[0], norm_ffn[1], final_norm]))
    out = run_tl("moba", True, h1, o1, gains1, moba_w_out[0], ffn_w13[1], ffn_w2[1])
    return out.reshape(2, SEQ, D)
```

```python
from contextlib import ExitStack

import numpy as np
import concourse.bass as bass
import concourse.mybir as mybir
from concourse.bass_utils import run_bass_kernel_spmd

F32 = mybir.dt.float32
BF16 = mybir.dt.bfloat16
ALU = mybir.AluOpType
AF = mybir.ActivationFunctionType
AX = mybir.AxisListType

D = 1024
DFF = 2816
SEQ = 8192
EPS = 1e-6
NEG = -1.0e30


class Prog:
    def __init__(self):
        self.nc = bass.Bass("TRN2", target_bir_lowering=False)
        nc = self.nc
        self.es = ExitStack()
        self.E = {"pe": nc.tensor, "act": nc.scalar, "dve": nc.vector, "pool": nc.gpsimd, "sp": nc.sync}
        self.semh = {}
        self.cnt = {}
        self.NCH = 8
        self.dn = {"sp": 0, "pool": 0}
        for k in ["pe", "act", "dve", "pool"] + [f"dq_{q}_{i}" for q in ("sp", "pool") for i in range(self.NCH)]:
            self.semh[k] = self.es.enter_context(nc.semaphore(k))
            self.cnt[k] = 0
        self.lastw = {}
        self.readers = {}
        self.waited = {}
        self._uid = 0

    def sb(self, name, shape, dt):
        return self.es.enter_context(self.nc.sbuf_tensor(name, list(shape), dt))

    def ps(self, name, shape, dt):
        return self.es.enter_context(self.nc.psum_tensor(name, list(shape), dt))

    def dram(self, name, shape, dt, kind):
        return self.nc.dram_tensor(name, list(shape), dt, kind=kind).ap()

    def _deps(self, reads, writes):
        deps = {}

        def add(k, v):
            if deps.get(k, 0) < v:
                deps[k] = v

        for r in reads:
            if r in self.lastw:
                add(*self.lastw[r])
        for w in writes:
            if w in self.lastw:
                add(*self.lastw[w])
            for k, v in self.readers.get(w, {}).items():
                add(k, v)
        return deps

    def _wait(self, e, deps, skip=None):
        eng = self.E[e]
        for k, v in deps.items():
            if k == skip:
                continue
            if self.waited.get((e, k), 0) >= v:
                continue
            eng.wait_ge(self.semh[k], v)
            self.waited[(e, k)] = v

    def _commit(self, key, val, reads, writes):
        for r in reads:
            d = self.readers.setdefault(r, {})
            if d.get(key, 0) < val:
                d[key] = val
        for w in writes:
            self.lastw[w] = (key, val)
            self.readers[w] = {}

    def op(self, e, fn, reads=(), writes=()):
        deps = self._deps(reads, writes)
        self._wait(e, deps, skip=("pe" if e == "pe" else None))
        ins = fn(self.E[e])
        self.cnt[e] += 1
        ins.then_inc(self.semh[e], 1)
        self._commit(e, self.cnt[e], reads, writes)

    def dma(self, q, out, in_, reads=(), writes=()):
        key = f"dq_{q}_{self.dn[q] % self.NCH}"
        self.dn[q] += 1
        deps = self._deps(reads, writes)
        if self.cnt[key] > deps.get(key, 0):
            deps[key] = self.cnt[key]
        self._wait(q, deps)
        ins = self.E[q].dma_start(out=out, in_=in_)
        self.cnt[key] += 16
        ins.then_inc(self.semh[key], 16)
        self._commit(key, self.cnt[key], reads, writes)

    def finish(self):
        for k, v in self.cnt.items():
            if v > 0:
                self.E["sp"].wait_ge(self.semh[k], v)
        self.es.close()
        return self.nc


def make_ident(P, name="ident"):
    ones = P.sb(name + "_ones", [128, 128], BF16)
    ident = P.sb(name, [128, 128], BF16)
    P.op("pool", lambda e: e.memset(ones[:], 1.0), writes=[name + "_ones"])
    P.op("pool", lambda e: e.affine_select(out=ident[:], in_=ones[:], pattern=[[-1, 128]],
                                            compare_op=ALU.is_equal, fill=0.0, base=0,
                                            channel_multiplier=1),
         reads=[name + "_ones"], writes=[name])
    return ident


class Normer:
    def __init__(self, P, ident, nbuf=2, npt=2, lnexp=False):
        self.P = P
        self.ident = ident
        self.lnexp = lnexp
        self.junk = P.sb("nm_junk", [128, D], BF16)
        self.ss = [P.sb(f"nm_ss{i}", [128, 1], F32) for i in range(nbuf)]
        self.rs = [P.sb(f"nm_rs{i}", [128, 1], F32) for i in range(nbuf)]
        self.yb = [P.sb(f"nm_yb{i}", [128, D], BF16) for i in range(nbuf)]
        self.pT = [P.ps(f"nm_pT{i}", [128, 8, 128], BF16) for i in range(npt)]
        self.n = 0
        self.nbuf = nbuf
        self.npt = npt

    def stats(self, x_ap, xres, i):
        P = self.P
        ss = self.ss[i]
        P.op("dve", lambda e: e.memset(ss[:], 0.0), writes=[f"nm_ss{i}"])
        P.op("act", lambda e: e.activation(out=self.junk[:], in_=x_ap, func=AF.Square, accum_out=ss[:]),
             reads=[xres, f"nm_ss{i}"], writes=["nm_junk", f"nm_ss{i}"])

    def finish(self, i):
        P = self.P
        ss, rs = self.ss[i], self.rs[i]
        P.op("dve", lambda e: e.tensor_scalar(out=ss[:], in0=ss[:], scalar1=1.0 / D, scalar2=EPS,
                                              op0=ALU.mult, op1=ALU.add),
             reads=[f"nm_ss{i}"], writes=[f"nm_ss{i}"])
        if self.lnexp:
            P.op("act", lambda e: e.activation(out=ss[:], in_=ss[:], func=AF.Ln),
                 reads=[f"nm_ss{i}"], writes=[f"nm_ss{i}"])
            P.op("act", lambda e: e.activation(out=rs[:], in_=ss[:], func=AF.Exp, scale=-0.5),
                 reads=[f"nm_ss{i}"], writes=[f"nm_rs{i}"])
        else:
            P.op("act", lambda e: e.activation(out=ss[:], in_=ss[:], func=AF.Sqrt),
                 reads=[f"nm_ss{i}"], writes=[f"nm_ss{i}"])
            P.op("dve", lambda e: e.reciprocal(out=rs[:], in_=ss[:]),
                 reads=[f"nm_ss{i}"], writes=[f"nm_rs{i}"])
        return rs

    def rstd(self, x_ap, xres, i):
        self.stats(x_ap, xres, i)
        return self.finish(i)

    def scale(self, x_ap, xres, gB_ap, i):
        P = self.P
        yb, rs = self.yb[i], self.rs[i]
        P.op("dve", lambda e: e.scalar_tensor_tensor(out=yb[:], in0=x_ap, scalar=rs[:, 0:1], in1=gB_ap,
                                                     op0=ALU.mult, op1=ALU.mult),
             reads=[xres, f"nm_rs{i}", "gB"], writes=[f"nm_yb{i}"])
        return yb

    def transpose(self, src_bf, srcres, dst_ap, dstres, i, evac="act"):
        P = self.P
        j = i % self.npt
        pT = self.pT[j]
        for k in range(8):
            P.op("pe", lambda e, k=k: e.transpose(out=pT[:, k, :], in_=src_bf[:, k * 128:(k + 1) * 128],
                                                   identity=self.ident[:]),
                 reads=[srcres, "ident"], writes=[f"nm_pT{j}"])
        if evac == "act":
            P.op("act", lambda e: e.copy(out=dst_ap, in_=pT[:]), reads=[f"nm_pT{j}"], writes=[dstres])
        else:
            P.op(evac, lambda e: e.tensor_copy(out=dst_ap, in_=pT[:]), reads=[f"nm_pT{j}"], writes=[dstres])

    def norm_T(self, x_ap, xres, gB_ap, dst_ap, dstres, evac="act"):
        i = self.n % self.nbuf
        self.n += 1
        self.rstd(x_ap, xres, i)
        yb = self.scale(x_ap, xres, gB_ap, i)
        self.transpose(yb, f"nm_yb{i}", dst_ap, dstres, i, evac)


class WLoader:
    def __init__(self, P, nstage=2, stage_elems=2048, cast_engines=("pool", "dve"), direct=False):
        self.P = P
        self.direct = direct
        self.stage = [] if direct else [P.sb(f"wst{i}", [128, stage_elems], F32) for i in range(nstage)]
        self.n = 0
        self.stage_elems = stage_elems
        self.cast_engines = cast_engines
        self.queues = ("sp", "pool")

    def load(self, w_ap, r0, nk, c0, ncols, dst, dstres, dk0=0, dc0=0):
        P = self.P
        if self.direct:
            src = w_ap[r0:r0 + nk * 128, c0:c0 + ncols].rearrange("(k p) c -> p k c", p=128)
            P.dma("pool", dst[:, dk0:dk0 + nk, dc0:dc0 + ncols], src, writes=[dstres])
            return
        kmax = max(1, self.stage_elems // ncols)
        k = 0
        while k < nk:
            kk = min(kmax, nk - k)
            i = self.n % len(self.stage)
            q = self.queues[self.n % len(self.queues)]
            ce = self.cast_engines[self.n % len(self.cast_engines)]
            self.n += 1
            st = self.stage[i]
            stv = st[:, 0:kk * ncols].rearrange("p (k c) -> p k c", k=kk)
            src = w_ap[r0 + k * 128:r0 + (k + kk) * 128, c0:c0 + ncols].rearrange("(k p) c -> p k c", p=128)
            P.dma(q, stv, src, writes=[f"wst{i}"])
            dv = dst[:, dk0 + k:dk0 + k + kk, dc0:dc0 + ncols]
            P.op(ce, lambda e, dv=dv, stv=stv: e.tensor_copy(out=dv, in_=stv), reads=[f"wst{i}"], writes=[dstres])
            k += kk


def load_gains(P, gains_ap, n):
    gB = P.sb("gB", [128, n, D], F32)
    for i in range(n):
        P.dma("sp", gB[:, i:i + 1, :], gains_ap[i:i + 1, :].partition_broadcast(128), writes=["gB"])
    return gB


def build_tl(mode, final):
    P = Prog()
    hg = mode == "hgrn"
    T = 2048
    NP = 2
    TP = T // NP
    NT = TP // 128
    hin = P.dram("hin", [T, D], F32, "ExternalInput")
    oin = P.dram("oin", [T, D], F32, "ExternalInput")
    gains = P.dram("gains", [4, D], F32, "ExternalInput")
    w_out = P.dram("w_out", [D, D], F32, "ExternalInput")
    w13 = P.dram("w13", [D, 2 * DFF], F32, "ExternalInput")
    w2 = P.dram("w2", [DFF, D], F32, "ExternalInput")
    if hg:
        w_og = P.dram("w_og", [D, D], F32, "ExternalInput")
    hout = P.dram("hout", [T, D], F32, "ExternalOutput")

    ident = make_ident(P)
    need = ([0, 1] if hg else []) + [2] + ([3] if final else [])
    gBt = P.sb("gB", [128, len(need), D], F32)
    for i_, r_ in enumerate(need):
        P.dma("sp", gBt[:, i_:i_ + 1, :], gains[r_:r_ + 1, :].partition_broadcast(128), writes=["gB"])

    class _G:
        def __getitem__(self, key):
            return gBt[key[0], need.index(key[1]), key[2]]
    gB = _G()
    nm = Normer(P, ident, nbuf=3, npt=2)
    wl = WLoader(P, direct=True)

    h = P.sb("h", [128, NT, D], F32)
    zT = P.sb("zT", [128, 8, TP], BF16)
    aT = P.sb("aT", [128, 11, TP], BF16)
    NSL = 2 if hg else 3
    wsl = [P.sb(f"wsl{i}", [128, 8, 512], BF16) for i in range(NSL)]
    w2s = [P.sb(f"w2s{i}", [128, 11, 512], BF16) for i in range(2)]
    wout_b = P.sb("wout_b", [128, 8, D], BF16)
    sgl = [P.sb(f"sgl{i}", [128, 512], F32) for i in range(2)]
    pacc = [P.ps(f"pacc{i}", [128, 512], F32) for i in range(6)]
    nacc = [0]

    def acc():
        i = nacc[0] % 6
        nacc[0] += 1
        return pacc[i], f"pacc{i}"

    nsl = [0]

    def next_wsl():
        i = nsl[0] % NSL
        nsl[0] += 1
        return wsl[i], f"wsl{i}"

    rings = {}

    def ring(name, n, shape, dt):
        rings[name] = [P.sb(f"{name}{i}", shape, dt) for i in range(n)]

    def X(name, t):
        n = len(rings[name])
        return rings[name][t % n], f"{name}{t % n}"

    ring("ot", 5 if hg else 3, [128, D], F32)
    ring("onb", 3, [128, D], BF16)
    ring("oT", 3, [128, 8, 128], BF16)
    ring("zb", 3, [128, D], BF16)
    for fam in ("o", "z"):
        ring("ss" + fam, 4, [128, 1], F32)
        ring("rs" + fam, 4, [128, 1], F32)
    if hg:
        ring("yb", 3, [128, D], BF16)
        ring("yT", 3, [128, 8, 128], BF16)
        ring("sg", 2, [128, D], F32)
        ring("ssy", 4, [128, 1], F32)
        ring("rsy", 4, [128, 1], F32)
        wog_b = aT[:, 0:8, :]
    aT_keys = [("aT", c, tg) for c in range(11) for tg in range(TP // 512)]
    wl.load(w_out, 0, 8, 0, D, wout_b, "wout_b")

    def stats(x_ap, xres, fam, g):
        ss, sr = X("ss" + fam, g)
        P.op("dve", lambda e: e.memset(ss[:], 0.0), writes=[sr])
        P.op("act", lambda e: e.activation(out=nm.junk[:], in_=x_ap, func=AF.Square, accum_out=ss[:]),
             reads=[xres, sr], writes=["nm_junk", sr])

    def finish(fam, g):
        ss, sr = X("ss" + fam, g)
        rs, rr = X("rs" + fam, g)
        P.op("dve", lambda e: e.tensor_scalar(out=ss[:], in0=ss[:], scalar1=1.0 / D, scalar2=EPS, op0=ALU.mult, op1=ALU.add),
             reads=[sr], writes=[sr])
        P.op("act", lambda e: e.activation(out=ss[:], in_=ss[:], func=AF.Sqrt), reads=[sr], writes=[sr])
        P.op("dve", lambda e: e.reciprocal(out=rs[:], in_=ss[:]), reads=[sr], writes=[rr])

    def transposes(src, sres, g):
        j = g % 2
        for k in range(8):
            P.op("pe", lambda e, k=k: e.transpose(out=nm.pT[j][:, k, :], in_=src[:, k * 128:(k + 1) * 128], identity=ident[:]),
                 reads=[sres, "ident"], writes=[f"nm_pT{j}"])

    for ps_ in range(NP):
        t0 = ps_ * TP
        G = lambda t: ps_ * NT + t

        def f_load(t):
            g = G(t)
            P.dma("sp", h[:, t, :], hin[t0 + t * 128:t0 + (t + 1) * 128, :], writes=[("h", t)])
            if hg:
                stats(h[:, t, :], ("h", t), "y", g)
            else:
                ot, otr = X("ot", g)
                P.dma("sp", ot[:], oin[t0 + t * 128:t0 + (t + 1) * 128, :], writes=[otr])

        def f_rs(t):
            finish("y", G(t))

        def f_scale(t):
            g = G(t)
            if hg:
                yb, ybr = X("yb", g)
                rs, rr = X("rsy", g)
                P.op("dve", lambda e: e.scalar_tensor_tensor(out=yb[:], in0=h[:, t, :], scalar=rs[:, 0:1], in1=gB[:, 0, :],
                                                             op0=ALU.mult, op1=ALU.mult), reads=[("h", t), rr, "gB"], writes=[ybr])
            else:
                ot, otr = X("ot", g)
                onb, onr = X("onb", g)
                P.op("act", lambda e: e.copy(out=onb[:], in_=ot[:]), reads=[otr], writes=[onr])

        def f_tr1(t):
            if hg:
                yb, ybr = X("yb", G(t))
                transposes(yb, ybr, 2 * G(t))

        def f_ev1(t):
            g = G(t)
            yT, ytr = X("yT", g)
            j = (2 * g) % 2
            P.op("act", lambda e: e.copy(out=yT[:], in_=nm.pT[j][:]), reads=[f"nm_pT{j}"], writes=[ytr])
            ot, otr = X("ot", g)
            P.dma("sp", ot[:], oin[t0 + t * 128:t0 + (t + 1) * 128, :], writes=[otr])
            stats(ot[:], otr, "o", g)

        og_banks = {}

        def f_og(t):
            if hg:
                finish("o", G(t))
                yT, ytr = X("yT", G(t))
                og_banks[t] = []
                for cg in range(2):
                    pa, pres = acc()
                    og_banks[t].append((pa, pres))
                    for k in range(8):
                        P.op("pe", lambda e, k=k, pa=pa, cg=cg: e.matmul(pa[:], lhsT=yT[:, k, :], rhs=wog_b[:, k, cg * 512:(cg + 1) * 512],
                                                                         start=(k == 0), stop=(k == 7)),
                             reads=[ytr] + aT_keys, writes=[pres])

        def f_sg(t):
            if hg:
                ot, otr = X("ot", G(t))
                rso, rro = X("rso", G(t))
                P.op("dve", lambda e: e.scalar_tensor_tensor(out=ot[:], in0=ot[:], scalar=rso[:, 0:1], in1=gB[:, 1, :],
                                                             op0=ALU.mult, op1=ALU.mult), reads=[otr, rro, "gB"], writes=[otr])
                sg, sgr = X("sg", G(t))
                for cg in range(2):
                    pa, pres = og_banks[t][cg]
                    P.op("act", lambda e, pa=pa, cg=cg: e.activation(out=sg[:, cg * 512:(cg + 1) * 512], in_=pa[:], func=AF.Sigmoid),
                         reads=[pres], writes=[(sgr, cg)])

        def f_gate(t):
            if hg:
                g = G(t)
                ot, otr = X("ot", g)
                sg, sgr = X("sg", g)
                onb, onr = X("onb", g)
                P.op("dve", lambda e: e.tensor_tensor(out=onb[:], in0=ot[:], in1=sg[:], op=ALU.mult),
                     reads=[otr, (sgr, 0), (sgr, 1)], writes=[onr])

        def f_tr2(t):
            onb, onr = X("onb", G(t))
            transposes(onb, onr, 2 * G(t) + 1)

        def f_ev2(t):
            oT, otr_ = X("oT", G(t))
            j = (2 * G(t) + 1) % 2
            P.op("act", lambda e: e.copy(out=oT[:], in_=nm.pT[j][:]), reads=[f"nm_pT{j}"], writes=[otr_])

        wo_banks = {}

        def f_wo(t):
            oT, otr_ = X("oT", G(t))
            wo_banks[t] = []
            for cg in range(2):
                pa, pres = acc()
                wo_banks[t].append((pa, pres))
                for k in range(8):
                    P.op("pe", lambda e, k=k, pa=pa, cg=cg: e.matmul(pa[:], lhsT=oT[:, k, :], rhs=wout_b[:, k, cg * 512:(cg + 1) * 512],
                                                                     start=(k == 0), stop=(k == 7)),
                         reads=[otr_, "wout_b"], writes=[pres])

        def f_res(t):
            for cg in range(2):
                pa, pres = wo_banks[t][cg]
                hv = h[:, t, cg * 512:(cg + 1) * 512]
                P.op("dve", lambda e, pa=pa, hv=hv: e.tensor_tensor(out=hv, in0=hv, in1=pa[:], op=ALU.add),
                     reads=[pres, ("h", t)], writes=[("h", t)])
            stats(h[:, t, :], ("h", t), "z", G(t))

        def f_rs3(t):
            finish("z", G(t))

        def f_sc3(t):
            zb, zbr = X("zb", G(t))
            rs, rr = X("rsz", G(t))
            P.op("dve", lambda e: e.scalar_tensor_tensor(out=zb[:], in0=h[:, t, :], scalar=rs[:, 0:1], in1=gB[:, 2, :],
                                                         op0=ALU.mult, op1=ALU.mult), reads=[("h", t), rr, "gB"], writes=[zbr])

        def f_tr3(t):
            zb, zbr = X("zb", G(t))
            transposes(zb, zbr, 2 * G(t))

        def f_ev3(t):
            j = (2 * G(t)) % 2
            P.op("act", lambda e: e.copy(out=zT[:, :, t * 128:(t + 1) * 128], in_=nm.pT[j][:]), reads=[f"nm_pT{j}"], writes=[("zT", t)])

        if hg:
            P.dma("pool", wog_b, w_og[:, :].rearrange("(k p) c -> p k c", p=128), writes=aT_keys)
        if hg:
            stages = [f_load, f_rs, f_scale, f_tr1, f_ev1, f_og, f_sg, f_gate, f_tr2, f_ev2, f_wo, f_res, f_rs3, f_sc3, f_tr3, f_ev3]
        else:
            stages = [f_load, f_scale, f_tr2, f_ev2, f_wo, f_res, f_rs3, f_sc3, f_tr3, f_ev3]
        NS = len(stages)
        for step in range(NT + NS - 1):
            for si in range(NS - 1, -1, -1):
                t = step - si
                if 0 <= t < NT:
                    stages[si](t)

        zT_all = [("zT", t) for t in range(NT)]
        for half in range(2):
            c0 = half * 11
            ci = 0
            while ci < 11:
                ncg = min(2, 11 - ci)
                s, sres = next_wsl()
                wl.load(w13, 0, 8, (c0 + ci) * 128, ncg * 128, s, sres, dc0=0)
                wl.load(w13, 0, 8, DFF + (c0 + ci) * 128, ncg * 128, s, sres, dc0=256)
                for fc in range(ncg):
                    for tg in range(TP // 512):
                        pg, pgres = acc()
                        pu, pures = acc()
                        for k in range(8):
                            P.op("pe", lambda e, k=k, pg=pg, s=s, fc=fc, tg=tg: e.matmul(
                                pg[:], lhsT=s[:, k, fc * 128:(fc + 1) * 128], rhs=zT[:, k, tg * 512:(tg + 1) * 512],
                                start=(k == 0), stop=(k == 7)), reads=zT_all[tg * 4:tg * 4 + 4] + [sres], writes=[pgres])
                        for k in range(8):
                            P.op("pe", lambda e, k=k, pu=pu, s=s, fc=fc, tg=tg: e.matmul(
                                pu[:], lhsT=s[:, k, 256 + fc * 128:256 + (fc + 1) * 128],
                                rhs=zT[:, k, tg * 512:(tg + 1) * 512],
                                start=(k == 0), stop=(k == 7)), reads=zT_all[tg * 4:tg * 4 + 4] + [sres], writes=[pures])
                        sb_ = nacc[0] % 2
                        P.op("act", lambda e, pg=pg, sb_=sb_: e.activation(out=sgl[sb_][:], in_=pg[:], func=AF.Silu),
                             reads=[pgres], writes=[f"sgl{sb_}"])
                        av = aT[:, ci + fc, tg * 512:(tg + 1) * 512]
                        P.op("dve", lambda e, pu=pu, sb_=sb_, av=av: e.tensor_tensor(out=av, in0=sgl[sb_][:], in1=pu[:],
                                                                                     op=ALU.mult),
                             reads=[pures, f"sgl{sb_}"], writes=[("aT", ci + fc, tg)])
                ci += ncg
            for cg in range(2):
                wi = (half * 2 + cg) % 2
                wl.load(w2, c0 * 128, 11, cg * 512, 512, w2s[wi], f"w2s{wi}")
                for t in range(NT):
                    pa, pres = acc()
                    for c in range(11):
                        P.op("pe", lambda e, c=c, pa=pa, wi=wi: e.matmul(pa[:], lhsT=aT[:, c, t * 128:(t + 1) * 128],
                                                                        rhs=w2s[wi][:, c, :], start=(c == 0), stop=(c == 10)),
                             reads=[("aT", c, t // 4) for c in range(11)] + [f"w2s{wi}"], writes=[pres])
                    hv = h[:, t, cg * 512:(cg + 1) * 512]
                    P.op("dve", lambda e, pa=pa, hv=hv: e.tensor_tensor(out=hv, in0=hv, in1=pa[:], op=ALU.add),
                         reads=[pres, ("h", t)], writes=[("h", t)])
        for t in range(NT):
            if final:
                g = G(t)
                stats(h[:, t, :], ("h", t), "o", g + 1000 * 0)
                finish("o", g)
                rs, rr = X("rso", g)
                ot, otr = X("ot", g)
                P.op("dve", lambda e, rs=rs, ot=ot: e.scalar_tensor_tensor(out=ot[:], in0=h[:, t, :], scalar=rs[:, 0:1],
                                                                           in1=gB[:, 3, :], op0=ALU.mult, op1=ALU.mult),
                     reads=[("h", t), rr, "gB"], writes=[otr])
                P.dma("sp", hout[t0 + t * 128:t0 + (t + 1) * 128, :], ot[:], reads=[otr], writes=[("hout", ps_, t)])
            else:
                P.dma("sp", hout[t0 + t * 128:t0 + (t + 1) * 128, :], h[:, t, :], reads=[("h", t)],
                      writes=[("hout", ps_, t)])
    return P.finish()


_CACHE = {}


def _get(name, builder):
    if name not in _CACHE:
        _CACHE[name] = builder()
    return _CACHE[name]


def run_tl(mode, final, hin, oin, gains, w_out, w13, w2, w_og=None):
    nc = _get(("tl", mode, final), lambda: build_tl(mode, final))
    maps = []
    for c in range(8):
        m = {"hin": np.ascontiguousarray(hin[c * 2048:(c + 1) * 2048]),
             "oin": np.ascontiguousarray(oin[c * 2048:(c + 1) * 2048]),
             "gains": gains, "w_out": w_out, "w13": w13, "w2": w2}
        if mode == "hgrn":
            m["w_og"] = w_og
        maps.append(m)
    res = run_bass_kernel_spmd(nc, maps, core_ids=list(range(8)))
    return np.concatenate([res.results[c]["hout"] for c in range(8)], axis=0)


def build_hgrn(NTILE=SEQ // 128):
    P = Prog()
    x = P.dram("x", [SEQ, D], F32, "ExternalInput")
    gcol = P.dram("gcol", [128, 8], F32, "ExternalInput")
    wq = P.dram("wq", [D, 256], F32, "ExternalInput")
    wf = P.dram("wf", [D, 256], F32, "ExternalInput")
    wi = P.dram("wi", [D, 256], F32, "ExternalInput")
    lbl = P.dram("lbl", [128, 4], F32, "ExternalInput")
    oout = P.dram("o", [SEQ, 256], F32, "ExternalOutput")

    ident = make_ident(P)
    nm = Normer(P, ident, nbuf=5, npt=2, lnexp=True)
    wqf_b = P.sb("wqf_b", [128, 8, 512], BF16)
    wi_b = P.sb("wi_b", [128, 8, 256], BF16)
    gc = P.sb("gc", [128, 8], F32)
    P.dma("sp", gc[:], gcol[:, :], writes=["gc"])
    wst = [P.sb(f"wst{i}", [128, 8, 256], F32) for i in range(2)]
    for j, (wsrc, dst, dres, dc0) in enumerate(((wq, wqf_b, "wqf_b", 0), (wf, wqf_b, "wqf_b", 256), (wi, wi_b, "wi_b", 0))):
        st = wst[j % 2]
        P.dma("sp", st[:], wsrc[:, :].rearrange("(k p) c -> p k c", p=128), writes=[f"wst{j % 2}"])
        for k in range(8):
            P.op("dve", lambda e, k=k, st=st, dst=dst, dc0=dc0: e.tensor_scalar(
                out=dst[:, k, dc0:dc0 + 256], in0=st[:, k, :], scalar1=gc[:, k:k + 1], scalar2=None, op0=ALU.mult),
                reads=[f"wst{j % 2}", "gc"], writes=[dres])

    lbt = P.sb("lbt", [128, 4], F32)
    lb = P.sb("lb", [128, 2], F32)
    oml = P.sb("oml", [128, 2], F32)
    P.dma("sp", lbt[:], lbl[:, :], writes=["lbt"])
    P.op("dve", lambda e: e.tensor_tensor(out=lb[:], in0=lbt[:, 2:4], in1=lbt[:, 0:2], op=ALU.subtract),
         reads=["lbt"], writes=["lb"])
    P.op("act", lambda e: e.activation(out=lb[:], in_=lb[:], func=AF.Exp), reads=["lb"], writes=["lb"])
    P.op("dve", lambda e: e.tensor_scalar(out=lb[:], in0=lb[:], scalar1=1.0, scalar2=None, op0=ALU.add),
         reads=["lb"], writes=["lb"])
    P.op("dve", lambda e: e.reciprocal(out=lb[:], in_=lb[:]), reads=["lb"], writes=["lb"])
    P.op("dve", lambda e: e.tensor_scalar(out=oml[:], in0=lb[:], scalar1=-1.0, scalar2=1.0, op0=ALU.mult, op1=ALU.add),
         reads=["lb"], writes=["oml"])

    onesf = P.sb("onesf", [128, 2, 128], F32)
    mask = P.sb("mask", [128, 2, 128], F32)
    P.op("pool", lambda e: e.memset(onesf[:], 1.0), writes=["onesf"])
    P.op("pool", lambda e: e.affine_select(out=mask[:], in_=onesf[:], pattern=[[0, 2], [1, 128]],
                                            compare_op=ALU.is_ge, fill=0.0, base=0, channel_multiplier=-1),
         reads=["onesf"], writes=["mask"])

    state = P.sb("state", [128, 2, 128], F32)
    state_b = P.sb("state_b", [128, 2, 128], BF16)
    P.op("dve", lambda e: e.memset(state[:], 0.0), writes=[("state", 0), ("state", 1)])
    P.op("dve", lambda e: e.memset(state_b[:], 0.0), writes=["state_b"])

    rings = {}

    def ring(name, n, shape, dt):
        rings[name] = [P.sb(f"{name}{i}", shape, dt) for i in range(n)]

    def X(name, t):
        n = len(rings[name])
        return rings[name][t % n], f"{name}{t % n}"

    ring("xt", 4, [128, D], F32)
    ring("yT", 3, [128, 8, 128], BF16)
    ring("EE", 4, [128, 4, 128], F32)
    ring("qr", 4, [128, 2, 128], F32)
    ring("ib", 12, [128, 256], BF16)
    ring("qs", 7, [128, 2, 128], F32)
    ring("fg", 3, [128, 2, 128], F32)
    ring("kk", 7, [128, 2, 128], F32)
    ring("lf", 3, [128, 2, 128], F32)
    ring("AA", 4, [128, 4, 2, 128], F32)
    ring("ex", 7, [128, 4, 2, 128], F32)
    ring("qc", 3, [128, 2, 128], BF16)
    ring("kc", 3, [128, 2, 128], BF16)
    ring("qd", 5, [128, 2, 128], BF16)
    ring("kh", 3, [128, 2, 128], BF16)
    ring("khT", 3, [128, 2, 128], BF16)
    ring("scm", 3, [128, 2, 128], BF16)
    ring("osb", 3, [128, 256], F32)
    pQF = [P.ps(f"pQF{i}", [128, 4, 128], F32) for i in range(2)]
    pI = P.ps("pI", [128, 512], F32)[:, 0:256]
    pKS = P.ps("pKS", [128, 512], F32)
    pK = pKS[:, 0:128].bitcast(BF16).rearrange("p (a b) -> p a b", a=2)
    pS = pKS[:, 256:512].rearrange("p (a b) -> p a b", a=2)
    pO = P.ps("pO", [128, 512], F32)[:, 0:256]
    pU = P.ps("pU", [128, 512], F32)[:, 0:256].rearrange("p (a b) -> p a b", a=2)
    NMB = nm.nbuf

    def s_load(t):
        xt, xr = X("xt", t)
        P.dma("sp", xt[:], x[t * 128:(t + 1) * 128, :], writes=[xr])
        nm.stats(xt[:], xr, t % NMB)

    def s_rs(t):
        nm.finish(t % NMB)

    def s_scale(t):
        xt, xr = X("xt", t)
        i = t % NMB
        P.op("act", lambda e: e.activation(out=nm.yb[i][:], in_=xt[:], func=AF.Copy, scale=nm.rs[i][:, 0:1]),
             reads=[xr, f"nm_rs{i}"], writes=[f"nm_yb{i}"])

    def s_tr(t):
        i = t % NMB
        j = t % 2
        for k in range(8):
            P.op("pe", lambda e, k=k: e.transpose(out=nm.pT[j][:, k, :], in_=nm.yb[i][:, k * 128:(k + 1) * 128], identity=ident[:]),
                 reads=[f"nm_yb{i}", "ident"], writes=[f"nm_pT{j}"])

    def s_trc(t):
        yT, yr = X("yT", t)
        j = t % 2
        P.op("act", lambda e: e.copy(out=yT[:], in_=nm.pT[j][:]), reads=[f"nm_pT{j}"], writes=[yr])

    def s_proj(t):
        yT, yr = X("yT", t)
        pq, pres = pQF[t % 2], f"pQF{t % 2}"
        for c in range(4):
            for k in range(8):
                P.op("pe", lambda e, k=k, c=c: e.matmul(pq[:, c, :], lhsT=wqf_b[:, k, c * 128:(c + 1) * 128], rhs=yT[:, k, :],
                                                        start=(k == 0), stop=(k == 7)), reads=[yr, "wqf_b"], writes=[pres])
        for k in range(8):
            P.op("pe", lambda e, k=k: e.matmul(pI, lhsT=yT[:, k, :], rhs=wi_b[:, k, :],
                                               start=(k == 0), stop=(k == 7)), reads=[yr, "wi_b"], writes=["pI"])

    def s_evac(t):
        pq, pres = pQF[t % 2], f"pQF{t % 2}"
        EE, er = X("EE", t)
        qr, qrr = X("qr", t)
        ib, ibr = X("ib", t)
        P.op("act", lambda e: e.activation(out=EE[:], in_=pq[:], func=AF.Exp, scale=-1.0), reads=[pres], writes=[er])
        P.op("act", lambda e: e.copy(out=qr[:], in_=pq[:, 0:2, :]), reads=[pres], writes=[qrr])
        P.op("dve", lambda e: e.tensor_copy(out=ib[:], in_=pI), reads=["pI"], writes=[ibr])

    def s_sig(t):
        EE, er = X("EE", t)
        P.op("act", lambda e: e.activation(out=EE[:], in_=EE[:], func=AF.Ln, bias=1.0, scale=1.0), reads=[er], writes=[er])
        P.op("act", lambda e: e.activation(out=EE[:], in_=EE[:], func=AF.Exp, scale=-1.0), reads=[er], writes=[er])

    def s_gate(t):
        EE, er = X("EE", t)
        qr, qrr = X("qr", t)
        qs, qsr = X("qs", t)
        fg, fr = X("fg", t)
        kk, kr = X("kk", t)
        P.op("pool", lambda e: e.tensor_tensor(out=qs[:], in0=qr[:], in1=EE[:, 0:2, :], op=ALU.mult), reads=[qrr, er], writes=[qsr])
        for hh in range(2):
            P.op("dve", lambda e, hh=hh: e.tensor_scalar(out=fg[:, hh, :], in0=EE[:, 2 + hh, :], scalar1=oml[:, hh:hh + 1],
                                                         scalar2=lb[:, hh:hh + 1], op0=ALU.mult, op1=ALU.add),
                 reads=[er, "oml", "lb"], writes=[(fr, hh)])
        P.op("pool", lambda e: e.tensor_scalar(out=kk[:], in0=fg[:], scalar1=-1.0, scalar2=1.0, op0=ALU.mult, op1=ALU.add),
             reads=[(fr, 0), (fr, 1)], writes=[kr])

    def s_lf(t):
        fg, fr = X("fg", t)
        lf, lr = X("lf", t)
        P.op("act", lambda e: e.activation(out=lf[:], in_=fg[:], func=AF.Ln), reads=[(fr, 0), (fr, 1)], writes=[lr])

    def s_scan(t):
        lf, lr = X("lf", t)
        AA, ar = X("AA", t)
        for hh in range(2):
            P.op("dve", lambda e, hh=hh: e.tensor_tensor_scan(out=AA[:, 2, hh, :], data0=onesf[:, 0, :], data1=lf[:, hh, :],
                                                              initial=0.0, op0=ALU.mult, op1=ALU.add),
                 reads=[lr, "onesf"], writes=[(ar, 2, hh)])

    def s_aa(t):
        AA, ar = X("AA", t)
        for hh in range(2):
            eng = "dve" if hh == 0 else "pool"
            cum_ = AA[:, 2, hh, :]
            cenB = AA[:, 2, hh, 63:64].to_broadcast([128, 128])
            lastB = AA[:, 2, hh, 127:128].to_broadcast([128, 128])
            P.op(eng, lambda e, hh=hh: e.tensor_tensor(out=AA[:, 0, hh, :], in0=cum_, in1=cenB, op=ALU.subtract),
                 reads=[(ar, 2, hh)], writes=[(ar, 0, hh)])
            P.op(eng, lambda e, hh=hh: e.tensor_tensor(out=AA[:, 1, hh, :], in0=cenB, in1=cum_, op=ALU.subtract),
                 reads=[(ar, 2, hh)], writes=[(ar, 1, hh)])
            P.op(eng, lambda e, hh=hh: e.tensor_tensor(out=AA[:, 3, hh, :], in0=lastB, in1=cum_, op=ALU.subtract),
                 reads=[(ar, 2, hh)], writes=[(ar, 3, hh)])

    def s_exp(t):
        AA, ar = X("AA", t)
        ex, exr = X("ex", t)
        P.op("act", lambda e: e.activation(out=ex[:], in_=AA[:], func=AF.Exp),
             reads=[(ar, i, hh) for i in range(4) for hh in range(2)], writes=[exr])

    def s_prod(t):
        ex, exr = X("ex", t)
        qs, qsr = X("qs", t)
        kk, kr = X("kk", t)
        qc, qcr = X("qc", t)
        kc, kcr = X("kc", t)
        qd, qdr = X("qd", t)
        kh, khr = X("kh", t)
        P.op("dve", lambda e: e.tensor_tensor(out=qc[:], in0=qs[:], in1=ex[:, 0, :, :], op=ALU.mult), reads=[qsr, exr], writes=[qcr])
        P.op("pool", lambda e: e.tensor_tensor(out=qd[:], in0=qs[:], in1=ex[:, 2, :, :], op=ALU.mult), reads=[qsr, exr], writes=[qdr])
        P.op("dve", lambda e: e.tensor_tensor(out=kc[:], in0=kk[:], in1=ex[:, 1, :, :], op=ALU.mult), reads=[kr, exr], writes=[kcr])
        P.op("pool", lambda e: e.tensor_tensor(out=kh[:], in0=kk[:], in1=ex[:, 3, :, :], op=ALU.mult), reads=[kr, exr], writes=[khr])

    def s_pe5(t):
        qc, qcr = X("qc", t)
        kc, kcr = X("kc", t)
        kh, khr = X("kh", t)
        for hh in range(2):
            P.op("pe", lambda e, hh=hh: e.transpose(out=pK[:, hh, :], in_=kh[:, hh, :], identity=ident[:]),
                 reads=[khr, "ident"], writes=["pKS"])
        for hh in range(2):
            P.op("pe", lambda e, hh=hh: e.matmul(pS[:, hh, :], lhsT=kc[:, hh, :], rhs=qc[:, hh, :], start=True, stop=True),
                 reads=[kcr, qcr], writes=["pKS"])

    def s_ev5(t):
        khT, ktr = X("khT", t)
        scm, scr_ = X("scm", t)
        P.op("dve", lambda e: e.tensor_copy(out=khT[:], in_=pK), reads=["pKS"], writes=[ktr])
        P.op("dve", lambda e: e.tensor_tensor(out=scm[:], in0=pS, in1=mask[:], op=ALU.mult), reads=["pKS", "mask"], writes=[scr_])

    def s_pe6(t):
        khT, ktr = X("khT", t)
        scm, scr_ = X("scm", t)
        ib, ibr = X("ib", t)
        qd, qdr = X("qd", t)
        for hh in range(2):
            P.op("pe", lambda e, hh=hh: e.matmul(pO[:, hh * 128:(hh + 1) * 128], lhsT=scm[:, hh, :],
                                                 rhs=ib[:, hh * 128:(hh + 1) * 128], start=True, stop=False),
                 reads=[scr_, ibr], writes=["pO"])
            P.op("pe", lambda e, hh=hh: e.matmul(pO[:, hh * 128:(hh + 1) * 128], lhsT=qd[:, hh, :],
                                                 rhs=state_b[:, hh, :], start=False, stop=True),
                 reads=[qdr, "state_b"], writes=["pO"])
        for hh in range(2):
            P.op("pe", lambda e, hh=hh: e.matmul(pU[:, hh, :], lhsT=khT[:, hh, :], rhs=ib[:, hh * 128:(hh + 1) * 128],
                                                 start=True, stop=True),
                 reads=[ktr, ibr], writes=["pU"])

    def s_fin(t):
        ex, exr = X("ex", t)
        osb, osr = X("osb", t)
        for hh in range(2):
            P.op("dve", lambda e, hh=hh: e.scalar_tensor_tensor(out=state[:, hh, :], in0=state[:, hh, :],
                                                                scalar=ex[:, 2, hh, 127:128], in1=pU[:, hh, :],
                                                                op0=ALU.mult, op1=ALU.add),
                 reads=[("state", hh), exr, "pU"], writes=[("state", hh)])
        P.op("pool", lambda e: e.tensor_copy(out=state_b[:], in_=state[:]), reads=[("state", 0), ("state", 1)], writes=["state_b"])
        P.op("act", lambda e: e.copy(out=osb[:], in_=pO), reads=["pO"], writes=[osr])
        P.dma("sp", oout[t * 128:(t + 1) * 128, :], osb[:], reads=[osr], writes=[("o", t)])

    stages = [s_load, s_rs, s_scale, s_tr, s_trc, s_proj, s_evac, s_sig, s_gate, s_lf, s_scan, s_aa, s_exp, s_prod,
              s_pe5, s_ev5, s_pe6, s_fin]
    NS = len(stages)
    order = [NS - 1] + [i for i in range(NS - 3, -1, -1)] + [NS - 2]
    for step in range(NTILE + NS - 1):
        for si in order:
            t = step - si
            if 0 <= t < NTILE:
                stages[si](t)
    return P.finish()


def run_hgrn(x, g_mix, w_in, lb_logits):
    nc = _get("hgrn", build_hgrn)
    maps = []
    for c in range(8):
        b, hp = c // 4, c % 4
        cs = slice(hp * 256, (hp + 1) * 256)
        lbl = lb_logits[:, cs].reshape(2, 2, 128).transpose(2, 0, 1).reshape(128, 4)
        maps.append({"x": np.ascontiguousarray(x[b]), "gcol": np.ascontiguousarray(g_mix.reshape(8, 128).T),
                     "wq": np.ascontiguousarray(w_in[:, cs]),
                     "wf": np.ascontiguousarray(w_in[:, 1024 + hp * 256:1024 + (hp + 1) * 256]),
                     "wi": np.ascontiguousarray(w_in[:, 2048 + hp * 256:2048 + (hp + 1) * 256]),
                     "lbl": np.ascontiguousarray(lbl)})
    res = run_bass_kernel_spmd(nc, maps, core_ids=list(range(8)))
    o = np.empty((2, SEQ, D), np.float32)
    for c in range(8):
        b, hp = c // 4, c % 4
        o[b, :, hp * 256:(hp + 1) * 256] = res.results[c]["o"]
    return o.reshape(2 * SEQ, D)


GL = 1536
MNEG = -30000.0


def t5_bucket_np(dist):
    n = np.maximum(dist, 0)
    nf = np.maximum(n, 16).astype(np.float32)
    large = 16 + (np.log(nf / np.float32(16)) / np.float32(np.log(64.0)) * np.float32(16)).astype(np.int32)
    large = np.minimum(large, 31)
    return np.where(n < 16, n, large)


def moba_consts():
    i = np.arange(GL)
    bk = t5_bucket_np(i - 255)
    oh = np.zeros((32, GL), np.float32)
    valid = i >= 255
    oh[bk[valid], i[valid]] = 1.0
    neg = np.zeros((128, 256), np.float32)
    neg[:, :255] = MNEG
    return oh, neg


def build_moba(S=SEQ, stop=99, skip=()):
    P = Prog()
    nc = P.nc
    NTILE = S // 128
    NBLK = S // 256
    SCALE = 128 ** -0.5
    x = P.dram("x", [S, D], F32, "ExternalInput")
    gains = P.dram("gains", [1, D], F32, "ExternalInput")
    wq = P.dram("wq", [D, 256], F32, "ExternalInput")
    wk = P.dram("wk", [D, 256], F32, "ExternalInput")
    wv = P.dram("wv", [D, 256], F32, "ExternalInput")
    tab = P.dram("tab", [32, 2], F32, "ExternalInput")
    oh = P.dram("oh", [32, GL], F32, "ExternalInput")
    negrow = P.dram("negrow", [128, 256], F32, "ExternalInput")
    oout = P.dram("o", [S, 256], F32, "ExternalOutput")
    scr_t = [nc.dram_tensor(f"scr{h}", [128, GL], F32, kind="Internal") for h in range(2)]

    ident = make_ident(P)
    gB = load_gains(P, gains, 1)
    nm = Normer(P, ident, nbuf=4, npt=2)
    wl = WLoader(P, direct=True)
    wq_b = P.sb("wq_b", [128, 8, 256], BF16)
    wk_b = P.sb("wk_b", [128, 8, 256], BF16)
    wv_b = P.sb("wv_b", [128, 8, 256], BF16)
    wl.load(wq, 0, 8, 0, 256, wq_b, "wq_b")
    wl.load(wk, 0, 8, 0, 256, wk_b, "wk_b")
    wl.load(wv, 0, 8, 0, 256, wv_b, "wv_b")

    QT = P.sb("QT", [128, 2, S], BF16)
    KT = P.sb("KT", [128, 2, S], BF16)
    VA = P.sb("VA", [128, NTILE, 2, 132], BF16)
    sel = P.sb("sel", [128, NTILE, 2, 32], F32)
    BT = P.sb("BT", [128, 2, 5, 2, 256], BF16)
    c31 = P.sb("c31", [128, 2], F32)
    ksum = P.sb("ksum", [128, 2, NTILE], F32)
    kmT = P.sb("kmT", [128, 2, 32], F32)
    kmT_b = P.sb("kmT_b", [128, 2, 32], BF16)
    maskadd = P.sb("maskadd", [128, 32, 32], F32)
    gm = P.sb("gm", [128, 2, 32], F32)
    top8 = P.sb("top8", [128, 2, 8], F32)
    xt = [P.sb(f"xt{i}", [128, D], F32) for i in range(4)]
    yT = [P.sb(f"yT{i}", [128, 8, 128], BF16) for i in range(3)]
    acc = [P.sb(f"acc{i}", [128, 2, 132], F32) for i in range(2)]
    osb = [P.sb(f"osb{i}", [128, 2, 128], F32) for i in range(2)]
    rec = [P.sb(f"rec{i}", [128, 2], F32) for i in range(2)]
    bk = [P.ps(f"bk{i}", [128, 512], F32) for i in range(6)]

    ohs = P.sb("ohs", [32, GL], F32)
    tabs = P.sb("tabs", [32, 2], F32)
    tabc = P.sb("tabc", [32, 2, 128], F32)
    gp = P.sb("gp", [128, GL], F32)
    ngs = P.sb("ngs", [128, 256], F32)
    P.dma("sp", ohs[:], oh[:, :], writes=["ohs"])
    P.dma("sp", tabs[:], tab[:, :], writes=["tabs"])
    P.dma("sp", ngs[:], negrow[:, :], writes=["ngs"])
    for h in range(2):
        if "bc" in skip:
            continue
        P.op("dve", lambda e, h=h: e.tensor_copy(out=tabc[:, h, :], in_=tabs[:, h:h + 1].to_broadcast([32, 128])),
             reads=["tabs"], writes=["tabc"])
    for h in range(2):
        if "bias" in skip:
            continue
        for c in range(GL // 512):
            P.op("pe", lambda e, h=h, c=c: e.matmul(bk[c][:], lhsT=tabc[:, h, :], rhs=ohs[:, c * 512:(c + 1) * 512],
                                                    start=True, stop=True), reads=["tabc", "ohs"], writes=[f"bk{c}"])
            P.op("dve", lambda e, c=c: e.tensor_copy(out=gp[:, c * 512:(c + 1) * 512], in_=bk[c][:]),
                 reads=[f"bk{c}"], writes=["gp"])
        P.op("dve", lambda e: e.tensor_tensor(out=gp[:, 0:256], in0=gp[:, 0:256], in1=ngs[:], op=ALU.add),
             reads=["gp", "ngs"], writes=["gp"])
        P.op("act", lambda e, h=h: e.copy(out=c31[:, h:h + 1], in_=gp[:, GL - 1:GL]), reads=["gp"], writes=["c31"])
        P.op("dve", lambda e: e.tensor_scalar(out=gp[:], in0=gp[:], scalar1=1.0 / SCALE, scalar2=None, op0=ALU.mult),
             reads=["gp", "c31"], writes=["gp"])
        P.dma("sp", scr_t[h].ap(), gp[:], reads=["gp"], writes=[f"scr{h}"])
        for dJ in range(5):
            if "skew" in skip:
                continue
            for half in range(2):
                off = dJ * 256 - half * 128 + 255
                src = bass.AP(tensor=scr_t[h], offset=off, ap=[[GL - 1, 128], [1, 256]])
                P.dma("pool", BT[:, h, dJ, half, :], src, reads=[f"scr{h}"], writes=["BT"])
    P.op("pool", lambda e: e.memset(maskadd[:], 0.0), writes=["maskadd"])
    if "maskadd" not in skip:
      P.op("pool", lambda e: e.affine_select(out=maskadd[:], in_=maskadd[:], pattern=[[1, 32], [-1, 32]],
                                            compare_op=ALU.is_ge, fill=NEG, base=-1, channel_multiplier=0),
         reads=["maskadd"], writes=["maskadd"])
    if "vaones" not in skip:
        P.op("pool", lambda e: e.memset(VA[:, :, :, 128:129], 1.0), writes=["VAones"])

    if stop <= 0:
        return P.finish()
    pQK = bk[3][:].rearrange("p (a b) -> p a b", a=4)
    pV = bk[4][:, 0:256]
    XR, YR = 4, 3

    def p1a(t):
        P.dma("sp", xt[t % XR][:], x[t * 128:(t + 1) * 128, :], writes=[f"xt{t % XR}"])
        nm.stats(xt[t % XR][:], f"xt{t % XR}", t % XR)

    def p1b(t):
        nm.finish(t % XR)
        nm.scale(xt[t % XR][:], f"xt{t % XR}", gB[:, 0, :], t % XR)

    def p1c(t):
        nm.transpose(nm.yb[t % XR], f"nm_yb{t % XR}", yT[t % YR][:], f"yT{t % YR}", t)

    def p1d(t):
        b = t % YR
        for j, w in enumerate((wq_b, wk_b)):
            wres = "wq_b" if j == 0 else "wk_b"
            for hh in range(2):
                for k in range(8):
                    P.op("pe", lambda e, k=k, w=w, hh=hh, j=j: e.matmul(
                        pQK[:, j * 2 + hh, :], lhsT=w[:, k, hh * 128:(hh + 1) * 128], rhs=yT[b][:, k, :],
                        start=(k == 0), stop=(k == 7)), reads=[f"yT{b}", wres], writes=["bk3"])
        for k in range(8):
            P.op("pe", lambda e, k=k: e.matmul(pV, lhsT=yT[b][:, k, :], rhs=wv_b[:, k, :],
                                               start=(k == 0), stop=(k == 7)), reads=[f"yT{b}", "wv_b"], writes=["bk4"])
        P.op("act", lambda e: e.copy(out=QT[:, :, t * 128:(t + 1) * 128], in_=pQK[:, 0:2, :]), reads=["bk3"], writes=[("QT", t)])
        P.op("act", lambda e: e.copy(out=KT[:, :, t * 128:(t + 1) * 128], in_=pQK[:, 2:4, :]), reads=["bk3"], writes=[("KT", t)])
        P.op("dve", lambda e: e.tensor_reduce(out=ksum[:, :, t], in_=KT[:, :, t * 128:(t + 1) * 128], axis=AX.X, op=ALU.add),
             reads=[("KT", t)], writes=["ksum"])
        P.op("act", lambda e: e.copy(out=VA[:, t, :, 0:128], in_=pV.rearrange("p (a b) -> p a b", a=2)),
             reads=["bk4"], writes=[("VA", t)])

    pG = bk[5][:, 0:64].rearrange("p (a b) -> p a b", a=2)
    P.op("dve", lambda e: e.memset(kmT_b[:], 0.0), writes=["kmT_b"])

    def p1e(t):
        if t % 2 == 1:
            n = t // 2
            P.op("dve", lambda e: e.tensor_tensor(out=kmT[:, :, n], in0=ksum[:, :, t - 1], in1=ksum[:, :, t], op=ALU.add),
                 reads=["ksum"], writes=["kmT"])
            P.op("dve", lambda e: e.tensor_scalar(out=kmT_b[:, :, n], in0=kmT[:, :, n], scalar1=1.0 / 256, scalar2=None, op0=ALU.mult),
                 reads=["kmT"], writes=["kmT_b"])

    def p1f(t):
        for hh in range(2):
            P.op("pe", lambda e, hh=hh: e.matmul(pG[:, hh, :], lhsT=QT[:, hh, t * 128:(t + 1) * 128], rhs=kmT_b[:, hh, :],
                                                 start=True, stop=True), reads=[("QT", t), "kmT_b"], writes=["bk5"])

    def p1g(t):
        j = t // 2
        for hh in range(2):
            P.op("dve", lambda e, hh=hh: e.tensor_tensor(out=gm[:, hh, :], in0=pG[:, hh, :], in1=maskadd[:, j, :], op=ALU.add),
                 reads=["bk5", "maskadd"], writes=[("gm", hh)])
        for hh in range(2):
            P.op("dve", lambda e, hh=hh: e.max(out=top8[:, hh, :], in_=gm[:, hh, :]), reads=[("gm", hh)], writes=[("top8", hh)])
        for hh in range(2):
            P.op("dve", lambda e, hh=hh: e.tensor_scalar(out=sel[:, t, hh, :], in0=gm[:, hh, :], scalar1=top8[:, hh, 2:3],
                                                         scalar2=None, op0=ALU.is_ge),
                 reads=[("gm", hh), ("top8", hh)], writes=[("sel", t)])

    p1stages = (p1a, p1b, p1c, p1d, p1e, p1f, p1g)
    for step in range(NTILE + len(p1stages) - 1):
        for s_ in range(len(p1stages) - 1, -1, -1):
            t = step - s_
            if 0 <= t < NTILE:
                p1stages[s_](t)
    if stop <= 2:
        return P.finish()
    tmpO = [gp[:, i * 512:i * 512 + 264].rearrange("p (a b) -> p a b", a=2) for i in range(3)]
    PT = [nm.yb[i][:, 0:512].rearrange("p (a b) -> p a b", a=2) for i in range(3)]
    iters = [(hh, J, n) for hh in range(2) for J in range(NBLK) for n in range(J, -1, -1)]
    NI = len(iters)

    def views(k):
        r = k % 3
        pS = bk[r][:].rearrange("p (a b) -> p a b", a=2)
        pO = bk[3 + r][:, 0:264].rearrange("p (a b) -> p a b", a=2)
        return r, pS, pO

    def stA(k):
        hh, J, n = iters[k]
        r, pS, pO = views(k)
        near = (J - n) <= 4
        for half in range(2):
            P.op("pe", lambda e, half=half: e.matmul(
                pS[:, half, :], lhsT=KT[:, hh, n * 256 + half * 128:n * 256 + (half + 1) * 128],
                rhs=QT[:, hh, J * 256:(J + 1) * 256], start=True, stop=not near),
                reads=[("QT", 2 * J), ("QT", 2 * J + 1), ("KT", 2 * n + half)],
                writes=[f"bk{r}"])
            if near:
                P.op("pe", lambda e, half=half: e.matmul(pS[:, half, :], lhsT=ident[:], rhs=BT[:, hh, J - n, half, :],
                                                         start=False, stop=True),
                     reads=["ident", "BT"], writes=[f"bk{r}"])

    def stB(k):
        hh, J, n = iters[k]
        r, pS, pO = views(k)
        if J - n <= 4:
            P.op("act", lambda e: e.activation(out=PT[r], in_=pS, func=AF.Exp, scale=SCALE),
                 reads=[f"bk{r}"], writes=[f"PT{r}"])
        else:
            P.op("act", lambda e: e.activation(out=PT[r], in_=pS, func=AF.Exp, bias=c31[:, hh:hh + 1], scale=SCALE),
                 reads=[f"bk{r}", "c31"], writes=[f"PT{r}"])

    def stC(k):
        hh, J, n = iters[k]
        r, pS, pO = views(k)
        for qt in range(2):
            for half in range(2):
                P.op("pe", lambda e, qt=qt, half=half: e.matmul(
                    pO[:, qt, 0:129], lhsT=PT[r][:, half, qt * 128:(qt + 1) * 128],
                    rhs=VA[:, 2 * n + half, hh, 0:129], start=(half == 0), stop=(half == 1)),
                    reads=[f"PT{r}", ("VA", 2 * n + half), "VAones"],
                    writes=[f"bk{3 + r}"])

    def stD(k):
        hh, J, n = iters[k]
        r, pS, pO = views(k)
        dJ = J - n
        ab = J % 2
        ares = f"acc{ab}"
        if dJ == 0:
            P.op("dve", lambda e: e.tensor_copy(out=acc[ab][:, :, 0:129], in_=pO[:, :, 0:129]),
                 reads=[f"bk{3 + r}"], writes=[(ares, 0), (ares, 1)])
        else:
            for qt in range(2):
                P.op("dve", lambda e, qt=qt: e.scalar_tensor_tensor(
                    out=acc[ab][:, qt, 0:129], in0=pO[:, qt, 0:129], scalar=sel[:, 2 * J + qt, hh, n:n + 1],
                    in1=acc[ab][:, qt, 0:129], op0=ALU.mult, op1=ALU.add),
                    reads=[f"bk{3 + r}", ("sel", 2 * J + qt), (ares, qt)], writes=[(ares, qt)])
        if n == 0:
            P.op("dve", lambda e: e.reciprocal(out=rec[ab][:], in_=acc[ab][:, :, 128]), reads=[(ares, 0), (ares, 1)], writes=[f"rec{ab}"])
            for qt in range(2):
                P.op("dve", lambda e, qt=qt: e.tensor_scalar(out=osb[ab][:, qt, :], in0=acc[ab][:, qt, 0:128],
                                                             scalar1=rec[ab][:, qt:qt + 1], scalar2=None, op0=ALU.mult),
                     reads=[(ares, qt), f"rec{ab}"], writes=[(f"osb{ab}", qt)])
            P.dma("sp", oout[J * 256:(J + 1) * 256, hh * 128:(hh + 1) * 128].rearrange("(q p) d -> p q d", p=128),
                  osb[ab][:], reads=[(f"osb{ab}", 0), (f"osb{ab}", 1)], writes=[("o", hh, J)])

    for step in range(NI + 3):
        for s_, f_ in enumerate((stA, stB, stC, stD)):
            k = step - s_
            if 0 <= k < NI:
                f_(k)
    return P.finish()


def run_moba(h1, g_mix, w_in, rel_table):
    nc = _get("moba", build_moba)
    oh, neg = moba_consts()
    maps = []
    for c in range(8):
        b, hp = c // 4, c % 4
        cs = slice(hp * 256, (hp + 1) * 256)
        maps.append({"x": np.ascontiguousarray(h1[b * SEQ:(b + 1) * SEQ]), "gains": np.ascontiguousarray(g_mix[None, :]),
                     "wq": np.ascontiguousarray(w_in[:, cs]),
                     "wk": np.ascontiguousarray(w_in[:, 1024 + hp * 256:1024 + (hp + 1) * 256]),
                     "wv": np.ascontiguousarray(w_in[:, 2048 + hp * 256:2048 + (hp + 1) * 256]),
                     "tab": np.ascontiguousarray(rel_table[:, hp * 2:hp * 2 + 2]), "oh": oh, "negrow": neg})
    res = run_bass_kernel_spmd(nc, maps, core_ids=list(range(8)))
    o = np.empty((2, SEQ, D), np.float32)
    for c in range(8):
        b, hp = c // 4, c % 4
        o[b, :, hp * 256:(hp + 1) * 256] = res.results[c]["o"]
    return o.reshape(2 * SEQ, D)


def kernel(x, norm_mix, norm_ffn, hgrn_w_in, hgrn_lb_logits, hgrn_out_norm, hgrn_w_out,
           moba_w_in, moba_w_out, rel_bias_table, ffn_w13, ffn_w2, final_norm):
    f = lambda a: np.ascontiguousarray(np.asarray(a, dtype=np.float32))
    x = f(x)
    norm_mix, norm_ffn, final_norm = f(norm_mix), f(norm_ffn), f(final_norm)
    hgrn_w_in, hgrn_lb_logits, hgrn_out_norm, hgrn_w_out = f(hgrn_w_in), f(hgrn_lb_logits), f(hgrn_out_norm), f(hgrn_w_out)
    moba_w_in, moba_w_out, rel_bias_table = f(moba_w_in), f(moba_w_out), f(rel_bias_table)
    ffn_w13, ffn_w2 = f(ffn_w13), f(ffn_w2)
    xf = x.reshape(2 * SEQ, D)
    o0 = run_hgrn(x, norm_mix[0], hgrn_w_in[0], hgrn_lb_logits)
    gains0 = f(np.stack([norm_mix[0], hgrn_out_norm[0], norm_ffn[0], final_norm]))
    h1 = run_tl("hgrn", False, xf, o0, gains0, hgrn_w_out[0], ffn_w13[0], ffn_w2[0],
                w_og=f(hgrn_w_in[0][:, 3072:4096]))
    o1 = run_moba(h1, norm_mix[1], moba_w_in[0], rel_bias_table)
    gains1 = f(np.stack([norm_mix[1], hgrn_out_norm[0], norm_ffn[1], final_norm]))
    out = run_tl("moba", True, h1, o1, gains1, moba_w_out[0], ffn_w13[1], ffn_w2[1])
    return out.reshape(2, SEQ, D)
```

```python
from contextlib import ExitStack

import numpy as np
import concourse.bass as bass
import concourse.mybir as mybir
from concourse.bass_utils import run_bass_kernel_spmd

F32 = mybir.dt.float32
BF16 = mybir.dt.bfloat16
ALU = mybir.AluOpType
AF = mybir.ActivationFunctionType
AX = mybir.AxisListType

D = 1024
DFF = 2816
SEQ = 8192
EPS = 1e-6
NEG = -1.0e30


class Prog:
    def __init__(self):
        self.nc = bass.Bass("TRN2", target_bir_lowering=False)
        nc = self.nc
        self.es = ExitStack()
        self.E = {"pe": nc.tensor, "act": nc.scalar, "dve": nc.vector, "pool": nc.gpsimd, "sp": nc.sync}
        self.semh = {}
        self.cnt = {}
        self.NCH = 8
        self.dn = {"sp": 0, "pool": 0}
        for k in ["pe", "act", "dve", "pool"] + [f"dq_{q}_{i}" for q in ("sp", "pool") for i in range(self.NCH)]:
            self.semh[k] = self.es.enter_context(nc.semaphore(k))
            self.cnt[k] = 0
        self.lastw = {}
        self.readers = {}
        self.waited = {}
        self._uid = 0

    def sb(self, name, shape, dt):
        return self.es.enter_context(self.nc.sbuf_tensor(name, list(shape), dt))

    def ps(self, name, shape, dt):
        return self.es.enter_context(self.nc.psum_tensor(name, list(shape), dt))

    def dram(self, name, shape, dt, kind):
        return self.nc.dram_tensor(name, list(shape), dt, kind=kind).ap()

    def _deps(self, reads, writes):
        deps = {}

        def add(k, v):
            if deps.get(k, 0) < v:
                deps[k] = v

        for r in reads:
            if r in self.lastw:
                add(*self.lastw[r])
        for w in writes:
            if w in self.lastw:
                add(*self.lastw[w])
            for k, v in self.readers.get(w, {}).items():
                add(k, v)
        return deps

    def _wait(self, e, deps, skip=None):
        eng = self.E[e]
        for k, v in deps.items():
            if k == skip:
                continue
            if self.waited.get((e, k), 0) >= v:
                continue
            eng.wait_ge(self.semh[k], v)
            self.waited[(e, k)] = v

    def _commit(self, key, val, reads, writes):
        for r in reads:
            d = self.readers.setdefault(r, {})
            if d.get(key, 0) < val:
                d[key] = val
        for w in writes:
            self.lastw[w] = (key, val)
            self.readers[w] = {}

    def op(self, e, fn, reads=(), writes=()):
        deps = self._deps(reads, writes)
        self._wait(e, deps, skip=("pe" if e == "pe" else None))
        ins = fn(self.E[e])
        self.cnt[e] += 1
        ins.then_inc(self.semh[e], 1)
        self._commit(e, self.cnt[e], reads, writes)

    def dma(self, q, out, in_, reads=(), writes=()):
        key = f"dq_{q}_{self.dn[q] % self.NCH}"
        self.dn[q] += 1
        deps = self._deps(reads, writes)
        if self.cnt[key] > deps.get(key, 0):
            deps[key] = self.cnt[key]
        self._wait(q, deps)
        ins = self.E[q].dma_start(out=out, in_=in_)
        self.cnt[key] += 16
        ins.then_inc(self.semh[key], 16)
        self._commit(key, self.cnt[key], reads, writes)

    def finish(self):
        for k, v in self.cnt.items():
            if v > 0:
                self.E["sp"].wait_ge(self.semh[k], v)
        self.es.close()
        return self.nc


def make_ident(P, name="ident"):
    ones = P.sb(name + "_ones", [128, 128], BF16)
    ident = P.sb(name, [128, 128], BF16)
    P.op("pool", lambda e: e.memset(ones[:], 1.0), writes=[name + "_ones"])
    P.op("pool", lambda e: e.affine_select(out=ident[:], in_=ones[:], pattern=[[-1, 128]],
                                            compare_op=ALU.is_equal, fill=0.0, base=0,
                                            channel_multiplier=1),
         reads=[name + "_ones"], writes=[name])
    return ident


class Normer:
    def __init__(self, P, ident, nbuf=2, npt=2, lnexp=False):
        self.P = P
        self.ident = ident
        self.lnexp = lnexp
        self.junk = P.sb("nm_junk", [128, D], BF16)
        self.ss = [P.sb(f"nm_ss{i}", [128, 1], F32) for i in range(nbuf)]
        self.rs = [P.sb(f"nm_rs{i}", [128, 1], F32) for i in range(nbuf)]
        self.yb = [P.sb(f"nm_yb{i}", [128, D], BF16) for i in range(nbuf)]
        self.pT = [P.ps(f"nm_pT{i}", [128, 8, 128], BF16) for i in range(npt)]
        self.n = 0
        self.nbuf = nbuf
        self.npt = npt

    def stats(self, x_ap, xres, i):
        P = self.P
        ss = self.ss[i]
        P.op("dve", lambda e: e.memset(ss[:], 0.0), writes=[f"nm_ss{i}"])
        P.op("act", lambda e: e.activation(out=self.junk[:], in_=x_ap, func=AF.Square, accum_out=ss[:]),
             reads=[xres, f"nm_ss{i}"], writes=["nm_junk", f"nm_ss{i}"])

    def finish(self, i):
        P = self.P
        ss, rs = self.ss[i], self.rs[i]
        P.op("dve", lambda e: e.tensor_scalar(out=ss[:], in0=ss[:], scalar1=1.0 / D, scalar2=EPS,
                                              op0=ALU.mult, op1=ALU.add),
             reads=[f"nm_ss{i}"], writes=[f"nm_ss{i}"])
        if self.lnexp:
            P.op("act", lambda e: e.activation(out=ss[:], in_=ss[:], func=AF.Ln),
                 reads=[f"nm_ss{i}"], writes=[f"nm_ss{i}"])
            P.op("act", lambda e: e.activation(out=rs[:], in_=ss[:], func=AF.Exp, scale=-0.5),
                 reads=[f"nm_ss{i}"], writes=[f"nm_rs{i}"])
        else:
            P.op("act", lambda e: e.activation(out=ss[:], in_=ss[:], func=AF.Sqrt),
                 reads=[f"nm_ss{i}"], writes=[f"nm_ss{i}"])
            P.op("dve", lambda e: e.reciprocal(out=rs[:], in_=ss[:]),
                 reads=[f"nm_ss{i}"], writes=[f"nm_rs{i}"])
        return rs

    def rstd(self, x_ap, xres, i):
        self.stats(x_ap, xres, i)
        return self.finish(i)

    def scale(self, x_ap, xres, gB_ap, i):
        P = self.P
        yb, rs = self.yb[i], self.rs[i]
        P.op("dve", lambda e: e.scalar_tensor_tensor(out=yb[:], in0=x_ap, scalar=rs[:, 0:1], in1=gB_ap,
                                                     op0=ALU.mult, op1=ALU.mult),
             reads=[xres, f"nm_rs{i}", "gB"], writes=[f"nm_yb{i}"])
        return yb

    def transpose(self, src_bf, srcres, dst_ap, dstres, i, evac="act"):
        P = self.P
        j = i % self.npt
        pT = self.pT[j]
        for k in range(8):
            P.op("pe", lambda e, k=k: e.transpose(out=pT[:, k, :], in_=src_bf[:, k * 128:(k + 1) * 128],
                                                   identity=self.ident[:]),
                 reads=[srcres, "ident"], writes=[f"nm_pT{j}"])
        if evac == "act":
            P.op("act", lambda e: e.copy(out=dst_ap, in_=pT[:]), reads=[f"nm_pT{j}"], writes=[dstres])
        else:
            P.op(evac, lambda e: e.tensor_copy(out=dst_ap, in_=pT[:]), reads=[f"nm_pT{j}"], writes=[dstres])

    def norm_T(self, x_ap, xres, gB_ap, dst_ap, dstres, evac="act"):
        i = self.n % self.nbuf
        self.n += 1
        self.rstd(x_ap, xres, i)
        yb = self.scale(x_ap, xres, gB_ap, i)
        self.transpose(yb, f"nm_yb{i}", dst_ap, dstres, i, evac)


class WLoader:
    def __init__(self, P, nstage=2, stage_elems=2048, cast_engines=("pool", "dve"), direct=False):
        self.P = P
        self.direct = direct
        self.stage = [] if direct else [P.sb(f"wst{i}", [128, stage_elems], F32) for i in range(nstage)]
        self.n = 0
        self.stage_elems = stage_elems
        self.cast_engines = cast_engines
        self.queues = ("sp", "pool")

    def load(self, w_ap, r0, nk, c0, ncols, dst, dstres, dk0=0, dc0=0):
        P = self.P
        if self.direct:
            src = w_ap[r0:r0 + nk * 128, c0:c0 + ncols].rearrange("(k p) c -> p k c", p=128)
            P.dma("pool", dst[:, dk0:dk0 + nk, dc0:dc0 + ncols], src, writes=[dstres])
            return
        kmax = max(1, self.stage_elems // ncols)
        k = 0
        while k < nk:
            kk = min(kmax, nk - k)
            i = self.n % len(self.stage)
            q = self.queues[self.n % len(self.queues)]
            ce = self.cast_engines[self.n % len(self.cast_engines)]
            self.n += 1
            st = self.stage[i]
            stv = st[:, 0:kk * ncols].rearrange("p (k c) -> p k c", k=kk)
            src = w_ap[r0 + k * 128:r0 + (k + kk) * 128, c0:c0 + ncols].rearrange("(k p) c -> p k c", p=128)
            P.dma(q, stv, src, writes=[f"wst{i}"])
            dv = dst[:, dk0 + k:dk0 + k + kk, dc0:dc0 + ncols]
            P.op(ce, lambda e, dv=dv, stv=stv: e.tensor_copy(out=dv, in_=stv), reads=[f"wst{i}"], writes=[dstres])
            k += kk


def load_gains(P, gains_ap, n):
    gB = P.sb("gB", [128, n, D], F32)
    for i in range(n):
        P.dma("sp", gB[:, i:i + 1, :], gains_ap[i:i + 1, :].partition_broadcast(128), writes=["gB"])
    return gB


def build_tl(mode, final):
    P = Prog()
    hg = mode == "hgrn"
    T = 2048
    NP = 2
    TP = T // NP
    NT = TP // 128
    hin = P.dram("hin", [T, D], F32, "ExternalInput")
    oin = P.dram("oin", [T, D], F32, "ExternalInput")
    gains = P.dram("gains", [4, D], F32, "ExternalInput")
    w_out = P.dram("w_out", [D, D], F32, "ExternalInput")
    w13 = P.dram("w13", [D, 2 * DFF], F32, "ExternalInput")
    w2 = P.dram("w2", [DFF, D], F32, "ExternalInput")
    if hg:
        w_og = P.dram("w_og", [D, D], F32, "ExternalInput")
    hout = P.dram("hout", [T, D], F32, "ExternalOutput")

    ident = make_ident(P)
    need = ([0, 1] if hg else []) + [2] + ([3] if final else [])
    gBt = P.sb("gB", [128, len(need), D], F32)
    for i_, r_ in enumerate(need):
        P.dma("sp", gBt[:, i_:i_ + 1, :], gains[r_:r_ + 1, :].partition_broadcast(128), writes=["gB"])

    class _G:
        def __getitem__(self, key):
            return gBt[key[0], need.index(key[1]), key[2]]
    gB = _G()
    nm = Normer(P, ident, nbuf=3, npt=2)
    wl = WLoader(P, direct=True)

    h = P.sb("h", [128, NT, D], F32)
    zT = P.sb("zT", [128, 8, TP], BF16)
    aT = P.sb("aT", [128, 11, TP], BF16)
    NSL = 2 if hg else 3
    wsl = [P.sb(f"wsl{i}", [128, 8, 512], BF16) for i in range(NSL)]
    w2s = [P.sb(f"w2s{i}", [128, 11, 512], BF16) for i in range(2)]
    wout_b = P.sb("wout_b", [128, 8, D], BF16)
    sgl = [P.sb(f"sgl{i}", [128, 512], F32) for i in range(2)]
    pacc = [P.ps(f"pacc{i}", [128, 512], F32) for i in range(6)]
    nacc = [0]

    def acc():
        i = nacc[0] % 6
        nacc[0] += 1
        return pacc[i], f"pacc{i}"

    nsl = [0]

    def next_wsl():
        i = nsl[0] % NSL
        nsl[0] += 1
        return wsl[i], f"wsl{i}"

    rings = {}

    def ring(name, n, shape, dt):
        rings[name] = [P.sb(f"{name}{i}", shape, dt) for i in range(n)]

    def X(name, t):
        n = len(rings[name])
        return rings[name][t % n], f"{name}{t % n}"

    ring("ot", 5 if hg else 3, [128, D], F32)
    ring("onb", 3, [128, D], BF16)
    ring("oT", 3, [128, 8, 128], BF16)
    ring("zb", 3, [128, D], BF16)
    for fam in ("o", "z"):
        ring("ss" + fam, 4, [128, 1], F32)
        ring("rs" + fam, 4, [128, 1], F32)
    if hg:
        ring("yb", 3, [128, D], BF16)
        ring("yT", 3, [128, 8, 128], BF16)
        ring("sg", 2, [128, D], F32)
        ring("ssy", 4, [128, 1], F32)
        ring("rsy", 4, [128, 1], F32)
        wog_b = aT[:, 0:8, :]
    aT_keys = [("aT", c, tg) for c in range(11) for tg in range(TP // 512)]
    wl.load(w_out, 0, 8, 0, D, wout_b, "wout_b")

    def stats(x_ap, xres, fam, g):
        ss, sr = X("ss" + fam, g)
        P.op("dve", lambda e: e.memset(ss[:], 0.0), writes=[sr])
        P.op("act", lambda e: e.activation(out=nm.junk[:], in_=x_ap, func=AF.Square, accum_out=ss[:]),
             reads=[xres, sr], writes=["nm_junk", sr])

    def finish(fam, g):
        ss, sr = X("ss" + fam, g)
        rs, rr = X("rs" + fam, g)
        P.op("dve", lambda e: e.tensor_scalar(out=ss[:], in0=ss[:], scalar1=1.0 / D, scalar2=EPS, op0=ALU.mult, op1=ALU.add),
             reads=[sr], writes=[sr])
        P.op("act", lambda e: e.activation(out=ss[:], in_=ss[:], func=AF.Sqrt), reads=[sr], writes=[sr])
        P.op("dve", lambda e: e.reciprocal(out=rs[:], in_=ss[:]), reads=[sr], writes=[rr])

    def transposes(src, sres, g):
        j = g % 2
        for k in range(8):
            P.op("pe", lambda e, k=k: e.transpose(out=nm.pT[j][:, k, :], in_=src[:, k * 128:(k + 1) * 128], identity=ident[:]),
                 reads=[sres, "ident"], writes=[f"nm_pT{j}"])

    for ps_ in range(NP):
        t0 = ps_ * TP
        G = lambda t: ps_ * NT + t

        def f_load(t):
            g = G(t)
            P.dma("sp", h[:, t, :], hin[t0 + t * 128:t0 + (t + 1) * 128, :], writes=[("h", t)])
            if hg:
                stats(h[:, t, :], ("h", t), "y", g)
            else:
                ot, otr = X("ot", g)
                P.dma("sp", ot[:], oin[t0 + t * 128:t0 + (t + 1) * 128, :], writes=[otr])

        def f_rs(t):
            finish("y", G(t))

        def f_scale(t):
            g = G(t)
            if hg:
                yb, ybr = X("yb", g)
                rs, rr = X("rsy", g)
                P.op("dve", lambda e: e.scalar_tensor_tensor(out=yb[:], in0=h[:, t, :], scalar=rs[:, 0:1], in1=gB[:, 0, :],
                                                             op0=ALU.mult, op1=ALU.mult), reads=[("h", t), rr, "gB"], writes=[ybr])
            else:
                ot, otr = X("ot", g)
                onb, onr = X("onb", g)
                P.op("act", lambda e: e.copy(out=onb[:], in_=ot[:]), reads=[otr], writes=[onr])

        def f_tr1(t):
            if hg:
                yb, ybr = X("yb", G(t))
                transposes(yb, ybr, 2 * G(t))

        def f_ev1(t):
            g = G(t)
            yT, ytr = X("yT", g)
            j = (2 * g) % 2
            P.op("act", lambda e: e.copy(out=yT[:], in_=nm.pT[j][:]), reads=[f"nm_pT{j}"], writes=[ytr])
            ot, otr = X("ot", g)
            P.dma("sp", ot[:], oin[t0 + t * 128:t0 + (t + 1) * 128, :], writes=[otr])
            stats(ot[:], otr, "o", g)

        og_banks = {}

        def f_og(t):
            if hg:
                finish("o", G(t))
                yT, ytr = X("yT", G(t))
                og_banks[t] = []
                for cg in range(2):
                    pa, pres = acc()
                    og_banks[t].append((pa, pres))
                    for k in range(8):
                        P.op("pe", lambda e, k=k, pa=pa, cg=cg: e.matmul(pa[:], lhsT=yT[:, k, :], rhs=wog_b[:, k, cg * 512:(cg + 1) * 512],
                                                                         start=(k == 0), stop=(k == 7)),
                             reads=[ytr] + aT_keys, writes=[pres])

        def f_sg(t):
            if hg:
                ot, otr = X("ot", G(t))
                rso, rro = X("rso", G(t))
                P.op("dve", lambda e: e.scalar_tensor_tensor(out=ot[:], in0=ot[:], scalar=rso[:, 0:1], in1=gB[:, 1, :],
                                                             op0=ALU.mult, op1=ALU.mult), reads=[otr, rro, "gB"], writes=[otr])
                sg, sgr = X("sg", G(t))
                for cg in range(2):
                    pa, pres = og_banks[t][cg]
                    P.op("act", lambda e, pa=pa, cg=cg: e.activation(out=sg[:, cg * 512:(cg + 1) * 512], in_=pa[:], func=AF.Sigmoid),
                         reads=[pres], writes=[(sgr, cg)])

        def f_gate(t):
            if hg:
                g = G(t)
                ot, otr = X("ot", g)
                sg, sgr = X("sg", g)
                onb, onr = X("onb", g)
                P.op("dve", lambda e: e.tensor_tensor(out=onb[:], in0=ot[:], in1=sg[:], op=ALU.mult),
                     reads=[otr, (sgr, 0), (sgr, 1)], writes=[onr])

        def f_tr2(t):
            onb, onr = X("onb", G(t))
            transposes(onb, onr, 2 * G(t) + 1)

        def f_ev2(t):
            oT, otr_ = X("oT", G(t))
            j = (2 * G(t) + 1) % 2
            P.op("act", lambda e: e.copy(out=oT[:], in_=nm.pT[j][:]), reads=[f"nm_pT{j}"], writes=[otr_])

        wo_banks = {}

        def f_wo(t):
            oT, otr_ = X("oT", G(t))
            wo_banks[t] = []
            for cg in range(2):
                pa, pres = acc()
                wo_banks[t].append((pa, pres))
                for k in range(8):
                    P.op("pe", lambda e, k=k, pa=pa, cg=cg: e.matmul(pa[:], lhsT=oT[:, k, :], rhs=wout_b[:, k, cg * 512:(cg + 1) * 512],
                                                                     start=(k == 0), stop=(k == 7)),
                         reads=[otr_, "wout_b"], writes=[pres])

        def f_res(t):
            for cg in range(2):
                pa, pres = wo_banks[t][cg]
                hv = h[:, t, cg * 512:(cg + 1) * 512]
                P.op("dve", lambda e, pa=pa, hv=hv: e.tensor_tensor(out=hv, in0=hv, in1=pa[:], op=ALU.add),
                     reads=[pres, ("h", t)], writes=[("h", t)])
            stats(h[:, t, :], ("h", t), "z", G(t))

        def f_rs3(t):
            finish("z", G(t))

        def f_sc3(t):
            zb, zbr = X("zb", G(t))
            rs, rr = X("rsz", G(t))
            P.op("dve", lambda e: e.scalar_tensor_tensor(out=zb[:], in0=h[:, t, :], scalar=rs[:, 0:1], in1=gB[:, 2, :],
                                                         op0=ALU.mult, op1=ALU.mult), reads=[("h", t), rr, "gB"], writes=[zbr])

        def f_tr3(t):
            zb, zbr = X("zb", G(t))
            transposes(zb, zbr, 2 * G(t))

        def f_ev3(t):
            j = (2 * G(t)) % 2
            P.op("act", lambda e: e.copy(out=zT[:, :, t * 128:(t + 1) * 128], in_=nm.pT[j][:]), reads=[f"nm_pT{j}"], writes=[("zT", t)])

        if hg:
            P.dma("pool", wog_b, w_og[:, :].rearrange("(k p) c -> p k c", p=128), writes=aT_keys)
        if hg:
            stages = [f_load, f_rs, f_scale, f_tr1, f_ev1, f_og, f_sg, f_gate, f_tr2, f_ev2, f_wo, f_res, f_rs3, f_sc3, f_tr3, f_ev3]
        else:
            stages = [f_load, f_scale, f_tr2, f_ev2, f_wo, f_res, f_rs3, f_sc3, f_tr3, f_ev3]
        NS = len(stages)
        for step in range(NT + NS - 1):
            for si in range(NS - 1, -1, -1):
                t = step - si
                if 0 <= t < NT:
                    stages[si](t)

        zT_all = [("zT", t) for t in range(NT)]
        for half in range(2):
            c0 = half * 11
            ci = 0
            while ci < 11:
                ncg = min(2, 11 - ci)
                s, sres = next_wsl()
                wl.load(w13, 0, 8, (c0 + ci) * 128, ncg * 128, s, sres, dc0=0)
                wl.load(w13, 0, 8, DFF + (c0 + ci) * 128, ncg * 128, s, sres, dc0=256)
                for fc in range(ncg):
                    for tg in range(TP // 512):
                        pg, pgres = acc()
                        pu, pures = acc()
                        for k in range(8):
                            P.op("pe", lambda e, k=k, pg=pg, s=s, fc=fc, tg=tg: e.matmul(
                                pg[:], lhsT=s[:, k, fc * 128:(fc + 1) * 128], rhs=zT[:, k, tg * 512:(tg + 1) * 512],
                                start=(k == 0), stop=(k == 7)), reads=zT_all[tg * 4:tg * 4 + 4] + [sres], writes=[pgres])
                        for k in range(8):
                            P.op("pe", lambda e, k=k, pu=pu, s=s, fc=fc, tg=tg: e.matmul(
                                pu[:], lhsT=s[:, k, 256 + fc * 128:256 + (fc + 1) * 128],
                                rhs=zT[:, k, tg * 512:(tg + 1) * 512],
                                start=(k == 0), stop=(k == 7)), reads=zT_all[tg * 4:tg * 4 + 4] + [sres], writes=[pures])
                        sb_ = nacc[0] % 2
                        P.op("act", lambda e, pg=pg, sb_=sb_: e.activation(out=sgl[sb_][:], in_=pg[:], func=AF.Silu),
                             reads=[pgres], writes=[f"sgl{sb_}"])
                        av = aT[:, ci + fc, tg * 512:(tg + 1) * 512]
                        P.op("dve", lambda e, pu=pu, sb_=sb_, av=av: e.tensor_tensor(out=av, in0=sgl[sb_][:], in1=pu[:],
                                                                                     op=ALU.mult),
                             reads=[pures, f"sgl{sb_}"], writes=[("aT", ci + fc, tg)])
                ci += ncg
            for cg in range(2):
                wi = (half * 2 + cg) % 2
                wl.load(w2, c0 * 128, 11, cg * 512, 512, w2s[wi], f"w2s{wi}")
                for t in range(NT):
                    pa, pres = acc()
                    for c in range(11):
                        P.op("pe", lambda e, c=c, pa=pa, wi=wi: e.matmul(pa[:], lhsT=aT[:, c, t * 128:(t + 1) * 128],
                                                                        rhs=w2s[wi][:, c, :], start=(c == 0), stop=(c == 10)),
                             reads=[("aT", c, t // 4) for c in range(11)] + [f"w2s{wi}"], writes=[pres])
                    hv = h[:, t, cg * 512:(cg + 1) * 512]
                    P.op("dve", lambda e, pa=pa, hv=hv: e.tensor_tensor(out=hv, in0=hv, in1=pa[:], op=ALU.add),
                         reads=[pres, ("h", t)], writes=[("h", t)])
        for t in range(NT):
            if final:
                g = G(t)
                stats(h[:, t, :], ("h", t), "o", g + 1000 * 0)
                finish("o", g)
                rs, rr = X("rso", g)
                ot, otr = X("ot", g)
                P.op("dve", lambda e, rs=rs, ot=ot: e.scalar_tensor_tensor(out=ot[:], in0=h[:, t, :], scalar=rs[:, 0:1],
                                                                           in1=gB[:, 3, :], op0=ALU.mult, op1=ALU.mult),
                     reads=[("h", t), rr, "gB"], writes=[otr])
                P.dma("sp", hout[t0 + t * 128:t0 + (t + 1) * 128, :], ot[:], reads=[otr], writes=[("hout", ps_, t)])
            else:
                P.dma("sp", hout[t0 + t * 128:t0 + (t + 1) * 128, :], h[:, t, :], reads=[("h", t)],
                      writes=[("hout", ps_, t)])
    return P.finish()


_CACHE = {}


def _get(name, builder):
    if name not in _CACHE:
        _CACHE[name] = builder()
    return _CACHE[name]


def run_tl(mode, final, hin, oin, gains, w_out, w13, w2, w_og=None):
    nc = _get(("tl", mode, final), lambda: build_tl(mode, final))
    maps = []
    for c in range(8):
        m = {"hin": np.ascontiguousarray(hin[c * 2048:(c + 1) * 2048]),
             "oin": np.ascontiguousarray(oin[c * 2048:(c + 1) * 2048]),
             "gains": gains, "w_out": w_out, "w13": w13, "w2": w2}
        if mode == "hgrn":
            m["w_og"] = w_og
        maps.append(m)
    res = run_bass_kernel_spmd(nc, maps, core_ids=list(range(8)))
    return np.concatenate([res.results[c]["hout"] for c in range(8)], axis=0)


def build_hgrn(NTILE=SEQ // 128):
    P = Prog()
    x = P.dram("x", [SEQ, D], F32, "ExternalInput")
    gcol = P.dram("gcol", [128, 8], F32, "ExternalInput")
    wq = P.dram("wq", [D, 256], F32, "ExternalInput")
    wf = P.dram("wf", [D, 256], F32, "ExternalInput")
    wi = P.dram("wi", [D, 256], F32, "ExternalInput")
    lbl = P.dram("lbl", [128, 4], F32, "ExternalInput")
    oout = P.dram("o", [SEQ, 256], F32, "ExternalOutput")

    ident = make_ident(P)
    nm = Normer(P, ident, nbuf=5, npt=2, lnexp=True)
    wqf_b = P.sb("wqf_b", [128, 8, 512], BF16)
    wi_b = P.sb("wi_b", [128, 8, 256], BF16)
    gc = P.sb("gc", [128, 8], F32)
    P.dma("sp", gc[:], gcol[:, :], writes=["gc"])
    wst = [P.sb(f"wst{i}", [128, 8, 256], F32) for i in range(2)]
    for j, (wsrc, dst, dres, dc0) in enumerate(((wq, wqf_b, "wqf_b", 0), (wf, wqf_b, "wqf_b", 256), (wi, wi_b, "wi_b", 0))):
        st = wst[j % 2]
        P.dma("sp", st[:], wsrc[:, :].rearrange("(k p) c -> p k c", p=128), writes=[f"wst{j % 2}"])
        for k in range(8):
            P.op("dve", lambda e, k=k, st=st, dst=dst, dc0=dc0: e.tensor_scalar(
                out=dst[:, k, dc0:dc0 + 256], in0=st[:, k, :], scalar1=gc[:, k:k + 1], scalar2=None, op0=ALU.mult),
                reads=[f"wst{j % 2}", "gc"], writes=[dres])

    lbt = P.sb("lbt", [128, 4], F32)
    lb = P.sb("lb", [128, 2], F32)
    oml = P.sb("oml", [128, 2], F32)
    P.dma("sp", lbt[:], lbl[:, :], writes=["lbt"])
    P.op("dve", lambda e: e.tensor_tensor(out=lb[:], in0=lbt[:, 2:4], in1=lbt[:, 0:2], op=ALU.subtract),
         reads=["lbt"], writes=["lb"])
    P.op("act", lambda e: e.activation(out=lb[:], in_=lb[:], func=AF.Exp), reads=["lb"], writes=["lb"])
    P.op("dve", lambda e: e.tensor_scalar(out=lb[:], in0=lb[:], scalar1=1.0, scalar2=None, op0=ALU.add),
         reads=["lb"], writes=["lb"])
    P.op("dve", lambda e: e.reciprocal(out=lb[:], in_=lb[:]), reads=["lb"], writes=["lb"])
    P.op("dve", lambda e: e.tensor_scalar(out=oml[:], in0=lb[:], scalar1=-1.0, scalar2=1.0, op0=ALU.mult, op1=ALU.add),
         reads=["lb"], writes=["oml"])

    onesf = P.sb("onesf", [128, 2, 128], F32)
    mask = P.sb("mask", [128, 2, 128], F32)
    P.op("pool", lambda e: e.memset(onesf[:], 1.0), writes=["onesf"])
    P.op("pool", lambda e: e.affine_select(out=mask[:], in_=onesf[:], pattern=[[0, 2], [1, 128]],
                                            compare_op=ALU.is_ge, fill=0.0, base=0, channel_multiplier=-1),
         reads=["onesf"], writes=["mask"])

    state = P.sb("state", [128, 2, 128], F32)
    state_b = P.sb("state_b", [128, 2, 128], BF16)
    P.op("dve", lambda e: e.memset(state[:], 0.0), writes=[("state", 0), ("state", 1)])
    P.op("dve", lambda e: e.memset(state_b[:], 0.0), writes=["state_b"])

    rings = {}

    def ring(name, n, shape, dt):
        rings[name] = [P.sb(f"{name}{i}", shape, dt) for i in range(n)]

    def X(name, t):
        n = len(rings[name])
        return rings[name][t % n], f"{name}{t % n}"

    ring("xt", 4, [128, D], F32)
    ring("yT", 3, [128, 8, 128], BF16)
    ring("EE", 4, [128, 4, 128], F32)
    ring("qr", 4, [128, 2, 128], F32)
    ring("ib", 12, [128, 256], BF16)
    ring("qs", 7, [128, 2, 128], F32)
    ring("fg", 3, [128, 2, 128], F32)
    ring("kk", 7, [128, 2, 128], F32)
    ring("lf", 3, [128, 2, 128], F32)
    ring("AA", 4, [128, 4, 2, 128], F32)
    ring("ex", 7, [128, 4, 2, 128], F32)
    ring("qc", 3, [128, 2, 128], BF16)
    ring("kc", 3, [128, 2, 128], BF16)
    ring("qd", 5, [128, 2, 128], BF16)
    ring("kh", 3, [128, 2, 128], BF16)
    ring("khT", 3, [128, 2, 128], BF16)
    ring("scm", 3, [128, 2, 128], BF16)
    ring("osb", 3, [128, 256], F32)
    pQF = [P.ps(f"pQF{i}", [128, 4, 128], F32) for i in range(2)]
    pI = P.ps("pI", [128, 512], F32)[:, 0:256]
    pKS = P.ps("pKS", [128, 512], F32)
    pK = pKS[:, 0:128].bitcast(BF16).rearrange("p (a b) -> p a b", a=2)
    pS = pKS[:, 256:512].rearrange("p (a b) -> p a b", a=2)
    pO = P.ps("pO", [128, 512], F32)[:, 0:256]
    pU = P.ps("pU", [128, 512], F32)[:, 0:256].rearrange("p (a b) -> p a b", a=2)
    NMB = nm.nbuf

    def s_load(t):
        xt, xr = X("xt", t)
        P.dma("sp", xt[:], x[t * 128:(t + 1) * 128, :], writes=[xr])
        nm.stats(xt[:], xr, t % NMB)

    def s_rs(t):
        nm.finish(t % NMB)

    def s_scale(t):
        xt, xr = X("xt", t)
        i = t % NMB
        P.op("act", lambda e: e.activation(out=nm.yb[i][:], in_=xt[:], func=AF.Copy, scale=nm.rs[i][:, 0:1]),
             reads=[xr, f"nm_rs{i}"], writes=[f"nm_yb{i}"])

    def s_tr(t):
        i = t % NMB
        j = t % 2
        for k in range(8):
            P.op("pe", lambda e, k=k: e.transpose(out=nm.pT[j][:, k, :], in_=nm.yb[i][:, k * 128:(k + 1) * 128], identity=ident[:]),
                 reads=[f"nm_yb{i}", "ident"], writes=[f"nm_pT{j}"])

    def s_trc(t):
        yT, yr = X("yT", t)
        j = t % 2
        P.op("act", lambda e: e.copy(out=yT[:], in_=nm.pT[j][:]), reads=[f"nm_pT{j}"], writes=[yr])

    def s_proj(t):
        yT, yr = X("yT", t)
        pq, pres = pQF[t % 2], f"pQF{t % 2}"
        for c in range(4):
            for k in range(8):
                P.op("pe", lambda e, k=k, c=c: e.matmul(pq[:, c, :], lhsT=wqf_b[:, k, c * 128:(c + 1) * 128], rhs=yT[:, k, :],
                                                        start=(k == 0), stop=(k == 7)), reads=[yr, "wqf_b"], writes=[pres])
        for k in range(8):
            P.op("pe", lambda e, k=k: e.matmul(pI, lhsT=yT[:, k, :], rhs=wi_b[:, k, :],
                                               start=(k == 0), stop=(k == 7)), reads=[yr, "wi_b"], writes=["pI"])

    def s_evac(t):
        pq, pres = pQF[t % 2], f"pQF{t % 2}"
        EE, er = X("EE", t)
        qr, qrr = X("qr", t)
        ib, ibr = X("ib", t)
        P.op("act", lambda e: e.activation(out=EE[:], in_=pq[:], func=AF.Exp, scale=-1.0), reads=[pres], writes=[er])
        P.op("act", lambda e: e.copy(out=qr[:], in_=pq[:, 0:2, :]), reads=[pres], writes=[qrr])
        P.op("dve", lambda e: e.tensor_copy(out=ib[:], in_=pI), reads=["pI"], writes=[ibr])

    def s_sig(t):
        EE, er = X("EE", t)
        P.op("act", lambda e: e.activation(out=EE[:], in_=EE[:], func=AF.Ln, bias=1.0, scale=1.0), reads=[er], writes=[er])
        P.op("act", lambda e: e.activation(out=EE[:], in_=EE[:], func=AF.Exp, scale=-1.0), reads=[er], writes=[er])

    def s_gate(t):
        EE, er = X("EE", t)
        qr, qrr = X("qr", t)
        qs, qsr = X("qs", t)
        fg, fr = X("fg", t)
        kk, kr = X("kk", t)
        P.op("pool", lambda e: e.tensor_tensor(out=qs[:], in0=qr[:], in1=EE[:, 0:2, :], op=ALU.mult), reads=[qrr, er], writes=[qsr])
        for hh in range(2):
            P.op("dve", lambda e, hh=hh: e.tensor_scalar(out=fg[:, hh, :], in0=EE[:, 2 + hh, :], scalar1=oml[:, hh:hh + 1],
                                                         scalar2=lb[:, hh:hh + 1], op0=ALU.mult, op1=ALU.add),
                 reads=[er, "oml", "lb"], writes=[(fr, hh)])
        P.op("pool", lambda e: e.tensor_scalar(out=kk[:], in0=fg[:], scalar1=-1.0, scalar2=1.0, op0=ALU.mult, op1=ALU.add),
             reads=[(fr, 0), (fr, 1)], writes=[kr])

    def s_lf(t):
        fg, fr = X("fg", t)
        lf, lr = X("lf", t)
        P.op("act", lambda e: e.activation(out=lf[:], in_=fg[:], func=AF.Ln), reads=[(fr, 0), (fr, 1)], writes=[lr])

    def s_scan(t):
        lf, lr = X("lf", t)
        AA, ar = X("AA", t)
        for hh in range(2):
            P.op("dve", lambda e, hh=hh: e.tensor_tensor_scan(out=AA[:, 2, hh, :], data0=onesf[:, 0, :], data1=lf[:, hh, :],
                                                              initial=0.0, op0=ALU.mult, op1=ALU.add),
                 reads=[lr, "onesf"], writes=[(ar, 2, hh)])

    def s_aa(t):
        AA, ar = X("AA", t)
        for hh in range(2):
            eng = "dve" if hh == 0 else "pool"
            cum_ = AA[:, 2, hh, :]
            cenB = AA[:, 2, hh, 63:64].to_broadcast([128, 128])
            lastB = AA[:, 2, hh, 127:128].to_broadcast([128, 128])
            P.op(eng, lambda e, hh=hh: e.tensor_tensor(out=AA[:, 0, hh, :], in0=cum_, in1=cenB, op=ALU.subtract),
                 reads=[(ar, 2, hh)], writes=[(ar, 0, hh)])
            P.op(eng, lambda e, hh=hh: e.tensor_tensor(out=AA[:, 1, hh, :], in0=cenB, in1=cum_, op=ALU.subtract),
                 reads=[(ar, 2, hh)], writes=[(ar, 1, hh)])
            P.op(eng, lambda e, hh=hh: e.tensor_tensor(out=AA[:, 3, hh, :], in0=lastB, in1=cum_, op=ALU.subtract),
                 reads=[(ar, 2, hh)], writes=[(ar, 3, hh)])

    def s_exp(t):
        AA, ar = X("AA", t)
        ex, exr = X("ex", t)
        P.op("act", lambda e: e.activation(out=ex[:], in_=AA[:], func=AF.Exp),
             reads=[(ar, i, hh) for i in range(4) for hh in range(2)], writes=[exr])

    def s_prod(t):
        ex, exr = X("ex", t)
        qs, qsr = X("qs", t)
        kk, kr = X("kk", t)
        qc, qcr = X("qc", t)
        kc, kcr = X("kc", t)
        qd, qdr = X("qd", t)
        kh, khr = X("kh", t)
        P.op("dve", lambda e: e.tensor_tensor(out=qc[:], in0=qs[:], in1=ex[:, 0, :, :], op=ALU.mult), reads=[qsr, exr], writes=[qcr])
        P.op("pool", lambda e: e.tensor_tensor(out=qd[:], in0=qs[:], in1=ex[:, 2, :, :], op=ALU.mult), reads=[qsr, exr], writes=[qdr])
        P.op("dve", lambda e: e.tensor_tensor(out=kc[:], in0=kk[:], in1=ex[:, 1, :, :], op=ALU.mult), reads=[kr, exr], writes=[kcr])
        P.op("pool", lambda e: e.tensor_tensor(out=kh[:], in0=kk[:], in1=ex[:, 3, :, :], op=ALU.mult), reads=[kr, exr], writes=[khr])

    def s_pe5(t):
        qc, qcr = X("qc", t)
        kc, kcr = X("kc", t)
        kh, khr = X("kh", t)
        for hh in range(2):
            P.op("pe", lambda e, hh=hh: e.transpose(out=pK[:, hh, :], in_=kh[:, hh, :], identity=ident[:]),
                 reads=[khr, "ident"], writes=["pKS"])
        for hh in range(2):
            P.op("pe", lambda e, hh=hh: e.matmul(pS[:, hh, :], lhsT=kc[:, hh, :], rhs=qc[:, hh, :], start=True, stop=True),
                 reads=[kcr, qcr], writes=["pKS"])

    def s_ev5(t):
        khT, ktr = X("khT", t)
        scm, scr_ = X("scm", t)
        P.op("dve", lambda e: e.tensor_copy(out=khT[:], in_=pK), reads=["pKS"], writes=[ktr])
        P.op("dve", lambda e: e.tensor_tensor(out=scm[:], in0=pS, in1=mask[:], op=ALU.mult), reads=["pKS", "mask"], writes=[scr_])

    def s_pe6(t):
        khT, ktr = X("khT", t)
        scm, scr_ = X("scm", t)
        ib, ibr = X("ib", t)
        qd, qdr = X("qd", t)
        for hh in range(2):
            P.op("pe", lambda e, hh=hh: e.matmul(pO[:, hh * 128:(hh + 1) * 128], lhsT=scm[:, hh, :],
                                                 rhs=ib[:, hh * 128:(hh + 1) * 128], start=True, stop=False),
                 reads=[scr_, ibr], writes=["pO"])
            P.op("pe", lambda e, hh=hh: e.matmul(pO[:, hh * 128:(hh + 1) * 128], lhsT=qd[:, hh, :],
                                                 rhs=state_b[:, hh, :], start=False, stop=True),
                 reads=[qdr, "state_b"], writes=["pO"])
        for hh in range(2):
            P.op("pe", lambda e, hh=hh: e.matmul(pU[:, hh, :], lhsT=khT[:, hh, :], rhs=ib[:, hh * 128:(hh + 1) * 128],
                                                 start=True, stop=True),
                 reads=[ktr, ibr], writes=["pU"])

    def s_fin(t):
        ex, exr = X("ex", t)
        osb, osr = X("osb", t)
        for hh in range(2):
            P.op("dve", lambda e, hh=hh: e.scalar_tensor_tensor(out=state[:, hh, :], in0=state[:, hh, :],
                                                                scalar=ex[:, 2, hh, 127:128], in1=pU[:, hh, :],
                                                                op0=ALU.mult, op1=ALU.add),
                 reads=[("state", hh), exr, "pU"], writes=[("state", hh)])
        P.op("pool", lambda e: e.tensor_copy(out=state_b[:], in_=state[:]), reads=[("state", 0), ("state", 1)], writes=["state_b"])
        P.op("act", lambda e: e.copy(out=osb[:], in_=pO), reads=["pO"], writes=[osr])
        P.dma("sp", oout[t * 128:(t + 1) * 128, :], osb[:], reads=[osr], writes=[("o", t)])

    stages = [s_load, s_rs, s_scale, s_tr, s_trc, s_proj, s_evac, s_sig, s_gate, s_lf, s_scan, s_aa, s_exp, s_prod,
              s_pe5, s_ev5, s_pe6, s_fin]
    NS = len(stages)
    order = [NS - 1] + [i for i in range(NS - 3, -1, -1)] + [NS - 2]
    for step in range(NTILE + NS - 1):
        for si in order:
            t = step - si
            if 0 <= t < NTILE:
                stages[si](t)
    return P.finish()


def run_hgrn(x, g_mix, w_in, lb_logits):
    nc = _get("hgrn", build_hgrn)
    maps = []
    for c in range(8):
        b, hp = c // 4, c % 4
        cs = slice(hp * 256, (hp + 1) * 256)
        lbl = lb_logits[:, cs].reshape(2, 2, 128).transpose(2, 0, 1).reshape(128, 4)
        maps.append({"x": np.ascontiguousarray(x[b]), "gcol": np.ascontiguousarray(g_mix.reshape(8, 128).T),
                     "wq": np.ascontiguousarray(w_in[:, cs]),
                     "wf": np.ascontiguousarray(w_in[:, 1024 + hp * 256:1024 + (hp + 1) * 256]),
                     "wi": np.ascontiguousarray(w_in[:, 2048 + hp * 256:2048 + (hp + 1) * 256]),
                     "lbl": np.ascontiguousarray(lbl)})
    res = run_bass_kernel_spmd(nc, maps, core_ids=list(range(8)))
    o = np.empty((2, SEQ, D), np.float32)
    for c in range(8):
        b, hp = c // 4, c % 4
        o[b, :, hp * 256:(hp + 1) * 256] = res.results[c]["o"]
    return o.reshape(2 * SEQ, D)


GL = 1536
MNEG = -30000.0


def t5_bucket_np(dist):
    n = np.maximum(dist, 0)
    nf = np.maximum(n, 16).astype(np.float32)
    large = 16 + (np.log(nf / np.float32(16)) / np.float32(np.log(64.0)) * np.float32(16)).astype(np.int32)
    large = np.minimum(large, 31)
    return np.where(n < 16, n, large)


def moba_consts():
    i = np.arange(GL)
    bk = t5_bucket_np(i - 255)
    oh = np.zeros((32, GL), np.float32)
    valid = i >= 255
    oh[bk[valid], i[valid]] = 1.0
    neg = np.zeros((128, 256), np.float32)
    neg[:, :255] = MNEG
    return oh, neg


def build_moba(S=SEQ, stop=99, skip=()):
    P = Prog()
    nc = P.nc
    NTILE = S // 128
    NBLK = S // 256
    SCALE = 128 ** -0.5
    x = P.dram("x", [S, D], F32, "ExternalInput")
    gains = P.dram("gains", [1, D], F32, "ExternalInput")
    wq = P.dram("wq", [D, 256], F32, "ExternalInput")
    wk = P.dram("wk", [D, 256], F32, "ExternalInput")
    wv = P.dram("wv", [D, 256], F32, "ExternalInput")
    tab = P.dram("tab", [32, 2], F32, "ExternalInput")
    oh = P.dram("oh", [32, GL], F32, "ExternalInput")
    negrow = P.dram("negrow", [128, 256], F32, "ExternalInput")
    oout = P.dram("o", [S, 256], F32, "ExternalOutput")
    scr_t = [nc.dram_tensor(f"scr{h}", [128, GL], F32, kind="Internal") for h in range(2)]

    ident = make_ident(P)
    gB = load_gains(P, gains, 1)
    nm = Normer(P, ident, nbuf=4, npt=2)
    wl = WLoader(P, direct=True)
    wq_b = P.sb("wq_b", [128, 8, 256], BF16)
    wk_b = P.sb("wk_b", [128, 8, 256], BF16)
    wv_b = P.sb("wv_b", [128, 8, 256], BF16)
    wl.load(wq, 0, 8, 0, 256, wq_b, "wq_b")
    wl.load(wk, 0, 8, 0, 256, wk_b, "wk_b")
    wl.load(wv, 0, 8, 0, 256, wv_b, "wv_b")

    QT = P.sb("QT", [128, 2, S], BF16)
    KT = P.sb("KT", [128, 2, S], BF16)
    VA = P.sb("VA", [128, NTILE, 2, 132], BF16)
    sel = P.sb("sel", [128, NTILE, 2, 32], F32)
    BT = P.sb("BT", [128, 2, 5, 2, 256], BF16)
    c31 = P.sb("c31", [128, 2], F32)
    ksum = P.sb("ksum", [128, 2, NTILE], F32)
    kmT = P.sb("kmT", [128, 2, 32], F32)
    kmT_b = P.sb("kmT_b", [128, 2, 32], BF16)
    maskadd = P.sb("maskadd", [128, 32, 32], F32)
    gm = P.sb("gm", [128, 2, 32], F32)
    top8 = P.sb("top8", [128, 2, 8], F32)
    xt = [P.sb(f"xt{i}", [128, D], F32) for i in range(4)]
    yT = [P.sb(f"yT{i}", [128, 8, 128], BF16) for i in range(3)]
    acc = [P.sb(f"acc{i}", [128, 2, 132], F32) for i in range(2)]
    osb = [P.sb(f"osb{i}", [128, 2, 128], F32) for i in range(2)]
    rec = [P.sb(f"rec{i}", [128, 2], F32) for i in range(2)]
    bk = [P.ps(f"bk{i}", [128, 512], F32) for i in range(6)]

    ohs = P.sb("ohs", [32, GL], F32)
    tabs = P.sb("tabs", [32, 2], F32)
    tabc = P.sb("tabc", [32, 2, 128], F32)
    gp = P.sb("gp", [128, GL], F32)
    ngs = P.sb("ngs", [128, 256], F32)
    P.dma("sp", ohs[:], oh[:, :], writes=["ohs"])
    P.dma("sp", tabs[:], tab[:, :], writes=["tabs"])
    P.dma("sp", ngs[:], negrow[:, :], writes=["ngs"])
    for h in range(2):
        if "bc" in skip:
            continue
        P.op("dve", lambda e, h=h: e.tensor_copy(out=tabc[:, h, :], in_=tabs[:, h:h + 1].to_broadcast([32, 128])),
             reads=["tabs"], writes=["tabc"])
    for h in range(2):
        if "bias" in skip:
            continue
        for c in range(GL // 512):
            P.op("pe", lambda e, h=h, c=c: e.matmul(bk[c][:], lhsT=tabc[:, h, :], rhs=ohs[:, c * 512:(c + 1) * 512],
                                                    start=True, stop=True), reads=["tabc", "ohs"], writes=[f"bk{c}"])
            P.op("dve", lambda e, c=c: e.tensor_copy(out=gp[:, c * 512:(c + 1) * 512], in_=bk[c][:]),
                 reads=[f"bk{c}"], writes=["gp"])
        P.op("dve", lambda e: e.tensor_tensor(out=gp[:, 0:256], in0=gp[:, 0:256], in1=ngs[:], op=ALU.add),
             reads=["gp", "ngs"], writes=["gp"])
        P.op("act", lambda e, h=h: e.copy(out=c31[:, h:h + 1], in_=gp[:, GL - 1:GL]), reads=["gp"], writes=["c31"])
        P.op("dve", lambda e: e.tensor_scalar(out=gp[:], in0=gp[:], scalar1=1.0 / SCALE, scalar2=None, op0=ALU.mult),
             reads=["gp", "c31"], writes=["gp"])
        P.dma("sp", scr_t[h].ap(), gp[:], reads=["gp"], writes=[f"scr{h}"])
        for dJ in range(5):
            if "skew" in skip:
                continue
            for half in range(2):
                off = dJ * 256 - half * 128 + 255
                src = bass.AP(tensor=scr_t[h], offset=off, ap=[[GL - 1, 128], [1, 256]])
                P.dma("pool", BT[:, h, dJ, half, :], src, reads=[f"scr{h}"], writes=["BT"])
    P.op("pool", lambda e: e.memset(maskadd[:], 0.0), writes=["maskadd"])
    if "maskadd" not in skip:
      P.op("pool", lambda e: e.affine_select(out=maskadd[:], in_=maskadd[:], pattern=[[1, 32], [-1, 32]],
                                            compare_op=ALU.is_ge, fill=NEG, base=-1, channel_multiplier=0),
         reads=["maskadd"], writes=["maskadd"])
    if "vaones" not in skip:
        P.op("pool", lambda e: e.memset(VA[:, :, :, 128:129], 1.0), writes=["VAones"])

    if stop <= 0:
        return P.finish()
    pQKs = [(bk[3][:].rearrange("p (a b) -> p a b", a=4), "bk3"), (bk[2][:].rearrange("p (a b) -> p a b", a=4), "bk2")]
    pVs = [(bk[4][:, 0:256], "bk4"), (bk[5][:, 0:256], "bk5")]
    XR, YR = 4, 3

    def p1a(t):
        P.dma("sp", xt[t % XR][:], x[t * 128:(t + 1) * 128, :], writes=[f"xt{t % XR}"])
        nm.stats(xt[t % XR][:], f"xt{t % XR}", t % XR)

    def p1b(t):
        nm.finish(t % XR)
        nm.scale(xt[t % XR][:], f"xt{t % XR}", gB[:, 0, :], t % XR)

    def p1c(t):
        nm.transpose(nm.yb[t % XR], f"nm_yb{t % XR}", yT[t % YR][:], f"yT{t % YR}", t)

    def p1d(t):
        b = t % YR
        pQK, qkres = pQKs[t % 2]
        pV, vres = pVs[t % 2]
        for j, w in enumerate((wq_b, wk_b)):
            wres = "wq_b" if j == 0 else "wk_b"
            for hh in range(2):
                for k in range(8):
                    P.op("pe", lambda e, k=k, w=w, hh=hh, j=j: e.matmul(
                        pQK[:, j * 2 + hh, :], lhsT=w[:, k, hh * 128:(hh + 1) * 128], rhs=yT[b][:, k, :],
                        start=(k == 0), stop=(k == 7)), reads=[f"yT{b}", wres], writes=[qkres])
        for k in range(8):
            P.op("pe", lambda e, k=k: e.matmul(pV, lhsT=yT[b][:, k, :], rhs=wv_b[:, k, :],
                                               start=(k == 0), stop=(k == 7)), reads=[f"yT{b}", "wv_b"], writes=[vres])
        P.op("act", lambda e: e.copy(out=QT[:, :, t * 128:(t + 1) * 128], in_=pQK[:, 0:2, :]), reads=[qkres], writes=[("QT", t)])
        P.op("act", lambda e: e.copy(out=KT[:, :, t * 128:(t + 1) * 128], in_=pQK[:, 2:4, :]), reads=[qkres], writes=[("KT", t)])
        P.op("dve", lambda e: e.tensor_reduce(out=ksum[:, :, t], in_=KT[:, :, t * 128:(t + 1) * 128], axis=AX.X, op=ALU.add),
             reads=[("KT", t)], writes=["ksum"])
        P.op("act", lambda e: e.copy(out=VA[:, t, :, 0:128], in_=pV.rearrange("p (a b) -> p a b", a=2)),
             reads=[vres], writes=[("VA", t)])

    pGs = [bk[i][:, 0:64].rearrange("p (a b) -> p a b", a=2) for i in range(2)]
    P.op("dve", lambda e: e.memset(kmT_b[:], 0.0), writes=[("kmT_b", n) for n in range(32)])
    P.op("dve", lambda e: e.memset(gm[:], NEG), writes=["gm"])

    def p1e(t):
        if t % 2 == 1:
            n = t // 2
            P.op("dve", lambda e: e.tensor_tensor(out=kmT[:, :, n], in0=ksum[:, :, t - 1], in1=ksum[:, :, t], op=ALU.add),
                 reads=["ksum"], writes=["kmT"])
            P.op("dve", lambda e: e.tensor_scalar(out=kmT_b[:, :, n], in0=kmT[:, :, n], scalar1=1.0 / 256, scalar2=None, op0=ALU.mult),
                 reads=["kmT"], writes=[("kmT_b", n)])

    def p1f(t):
        j = t // 2
        if j == 0:
            return
        pG = pGs[t % 2]
        for hh in range(2):
            P.op("pe", lambda e, hh=hh: e.matmul(pG[:, hh, 0:j], lhsT=QT[:, hh, t * 128:(t + 1) * 128], rhs=kmT_b[:, hh, 0:j],
                                                 start=True, stop=True),
                 reads=[("QT", t)] + [("kmT_b", n) for n in range(j)], writes=[f"bk{t % 2}"])

    def p1g(t):
        j = t // 2
        if j == 0:
            return
        pG = pGs[t % 2]
        P.op("dve", lambda e: e.tensor_copy(out=gm[:, :, 0:j], in_=pG[:, :, 0:j]), reads=[f"bk{t % 2}", "gm"], writes=["gm"])
        for hh in range(2):
            P.op("dve", lambda e, hh=hh: e.max(out=top8[:, hh, :], in_=gm[:, hh, :]), reads=["gm"], writes=[("top8", hh)])
        for hh in range(2):
            P.op("dve", lambda e, hh=hh: e.tensor_scalar(out=sel[:, t, hh, :], in0=gm[:, hh, :], scalar1=top8[:, hh, 2:3],
                                                         scalar2=None, op0=ALU.is_ge),
                 reads=["gm", ("top8", hh)], writes=[("sel", t)])

    p1stages = (p1a, p1b, p1c, p1d, p1e, p1f, p1g)
    for step in range(NTILE + len(p1stages) - 1):
        for s_ in (6, 4, 3, 2, 1, 0, 5):
            t = step - s_
            if 0 <= t < NTILE:
                p1stages[s_](t)
    if stop <= 2:
        return P.finish()
    tmpO = [gp[:, i * 512:i * 512 + 264].rearrange("p (a b) -> p a b", a=2) for i in range(3)]
    PT = [nm.yb[i][:, 0:512].rearrange("p (a b) -> p a b", a=2) for i in range(3)]
    iters = [(hh, J, n) for hh in range(2) for J in range(NBLK) for n in range(J, -1, -1)]
    NI = len(iters)

    def views(k):
        r = k % 3
        pS = bk[r][:].rearrange("p (a b) -> p a b", a=2)
        pO = bk[3 + r][:, 0:264].rearrange("p (a b) -> p a b", a=2)
        return r, pS, pO

    def stA(k):
        hh, J, n = iters[k]
        r, pS, pO = views(k)
        near = (J - n) <= 4
        for half in range(2):
            P.op("pe", lambda e, half=half: e.matmul(
                pS[:, half, :], lhsT=KT[:, hh, n * 256 + half * 128:n * 256 + (half + 1) * 128],
                rhs=QT[:, hh, J * 256:(J + 1) * 256], start=True, stop=not near),
                reads=[("QT", 2 * J), ("QT", 2 * J + 1), ("KT", 2 * n + half)],
                writes=[f"bk{r}"])
            if near:
                P.op("pe", lambda e, half=half: e.matmul(pS[:, half, :], lhsT=ident[:], rhs=BT[:, hh, J - n, half, :],
                                                         start=False, stop=True),
                     reads=["ident", "BT"], writes=[f"bk{r}"])

    def stB(k):
        hh, J, n = iters[k]
        r, pS, pO = views(k)
        if J - n <= 4:
            P.op("act", lambda e: e.activation(out=PT[r], in_=pS, func=AF.Exp, scale=SCALE),
                 reads=[f"bk{r}"], writes=[f"PT{r}"])
        else:
            P.op("act", lambda e: e.activation(out=PT[r], in_=pS, func=AF.Exp, bias=c31[:, hh:hh + 1], scale=SCALE),
                 reads=[f"bk{r}", "c31"], writes=[f"PT{r}"])

    def stC(k):
        hh, J, n = iters[k]
        r, pS, pO = views(k)
        for qt in range(2):
            for half in range(2):
                P.op("pe", lambda e, qt=qt, half=half: e.matmul(
                    pO[:, qt, 0:129], lhsT=PT[r][:, half, qt * 128:(qt + 1) * 128],
                    rhs=VA[:, 2 * n + half, hh, 0:129], start=(half == 0), stop=(half == 1)),
                    reads=[f"PT{r}", ("VA", 2 * n + half), "VAones"],
                    writes=[f"bk{3 + r}"])

    def stD(k):
        hh, J, n = iters[k]
        r, pS, pO = views(k)
        dJ = J - n
        ab = J % 2
        ares = f"acc{ab}"
        if dJ == 0:
            P.op("dve", lambda e: e.tensor_copy(out=acc[ab][:, :, 0:129], in_=pO[:, :, 0:129]),
                 reads=[f"bk{3 + r}"], writes=[(ares, 0), (ares, 1)])
        else:
            for qt in range(2):
                P.op("dve", lambda e, qt=qt: e.scalar_tensor_tensor(
                    out=acc[ab][:, qt, 0:129], in0=pO[:, qt, 0:129], scalar=sel[:, 2 * J + qt, hh, n:n + 1],
                    in1=acc[ab][:, qt, 0:129], op0=ALU.mult, op1=ALU.add),
                    reads=[f"bk{3 + r}", ("sel", 2 * J + qt), (ares, qt)], writes=[(ares, qt)])
        if n == 0:
            P.op("dve", lambda e: e.reciprocal(out=rec[ab][:], in_=acc[ab][:, :, 128]), reads=[(ares, 0), (ares, 1)], writes=[f"rec{ab}"])
            for qt in range(2):
                P.op("dve", lambda e, qt=qt: e.tensor_scalar(out=osb[ab][:, qt, :], in0=acc[ab][:, qt, 0:128],
                                                             scalar1=rec[ab][:, qt:qt + 1], scalar2=None, op0=ALU.mult),
                     reads=[(ares, qt), f"rec{ab}"], writes=[(f"osb{ab}", qt)])
            P.dma("sp", oout[J * 256:(J + 1) * 256, hh * 128:(hh + 1) * 128].rearrange("(q p) d -> p q d", p=128),
                  osb[ab][:], reads=[(f"osb{ab}", 0), (f"osb{ab}", 1)], writes=[("o", hh, J)])

    for step in range(NI + 3):
        for s_, f_ in enumerate((stA, stB, stC, stD)):
            k = step - s_
            if 0 <= k < NI:
                f_(k)
    return P.finish()


def run_moba(h1, g_mix, w_in, rel_table):
    nc = _get("moba", build_moba)
    oh, neg = moba_consts()
    maps = []
    for c in range(8):
        b, hp = c // 4, c % 4
        cs = slice(hp * 256, (hp + 1) * 256)
        maps.append({"x": np.ascontiguousarray(h1[b * SEQ:(b + 1) * SEQ]), "gains": np.ascontiguousarray(g_mix[None, :]),
                     "wq": np.ascontiguousarray(w_in[:, cs]),
                     "wk": np.ascontiguousarray(w_in[:, 1024 + hp * 256:1024 + (hp + 1) * 256]),
                     "wv": np.ascontiguousarray(w_in[:, 2048 + hp * 256:2048 + (hp + 1) * 256]),
                     "tab": np.ascontiguousarray(rel_table[:, hp * 2:hp * 2 + 2]), "oh": oh, "negrow": neg})
    res = run_bass_kernel_spmd(nc, maps, core_ids=list(range(8)))
    o = np.empty((2, SEQ, D), np.float32)
    for c in range(8):
        b, hp = c // 4, c % 4
        o[b, :, hp * 256:(hp + 1) * 256] = res.results[c]["o"]
    return o.reshape(2 * SEQ, D)


def kernel(x, norm_mix, norm_ffn, hgrn_w_in, hgrn_lb_logits, hgrn_out_norm, hgrn_w_out,
           moba_w_in, moba_w_out, rel_bias_table, ffn_w13, ffn_w2, final_norm):
    f = lambda a: np.ascontiguousarray(np.asarray(a, dtype=np.float32))
    x = f(x)
    norm_mix, norm_ffn, final_norm = f(norm_mix), f(norm_ffn), f(final_norm)
    hgrn_w_in, hgrn_lb_logits, hgrn_out_norm, hgrn_w_out = f(hgrn_w_in), f(hgrn_lb_logits), f(hgrn_out_norm), f(hgrn_w_out)
    moba_w_in, moba_w_out, rel_bias_table = f(moba_w_in), f(moba_w_out), f(rel_bias_table)
    ffn_w13, ffn_w2 = f(ffn_w13), f(ffn_w2)
    xf = x.reshape(2 * SEQ, D)
    o0 = run_hgrn(x, norm_mix[0], hgrn_w_in[0], hgrn_lb_logits)
    gains0 = f(np.stack([norm_mix[0], hgrn_out_norm[0], norm_ffn[0], final_norm]))
    h1 = run_tl("hgrn", False, xf, o0, gains0, hgrn_w_out[0], ffn_w13[0], ffn_w2[0],
                w_og=f(hgrn_w_in[0][:, 3072:4096]))
    o1 = run_moba(h1, norm_mix[1], moba_w_in[0], rel_bias_table)
    gains1 = f(np.stack([norm_mix[1], hgrn_out_norm[0], norm_ffn[1], final_norm]))
    out = run_tl("moba", True, h1, o1, gains1, moba_w_out[0], ffn_w13[1], ffn_w2[1])
    return out.reshape(2, SEQ, D)
```

```python
from contextlib import ExitStack

import numpy as np
import concourse.bass as bass
import concourse.mybir as mybir
from concourse.bass_utils import run_bass_kernel_spmd

F32 = mybir.dt.float32
BF16 = mybir.dt.bfloat16
ALU = mybir.AluOpType
AF = mybir.ActivationFunctionType
AX = mybir.AxisListType

D = 1024
DFF = 2816
SEQ = 8192
EPS = 1e-6
NEG = -1.0e30


class Prog:
    def __init__(self):
        self.nc = bass.Bass("TRN2", target_bir_lowering=False)
        nc = self.nc
        self.es = ExitStack()
        self.E = {"pe": nc.tensor, "act": nc.scalar, "dve": nc.vector, "pool": nc.gpsimd, "sp": nc.sync}
        self.semh = {}
        self.cnt = {}
        self.NCH = 8
        self.dn = {"sp": 0, "pool": 0}
        for k in ["pe", "act", "dve", "pool"] + [f"dq_{q}_{i}" for q in ("sp", "pool") for i in range(self.NCH)]:
            self.semh[k] = self.es.enter_context(nc.semaphore(k))
            self.cnt[k] = 0
        self.lastw = {}
        self.readers = {}
        self.waited = {}
        self._uid = 0

    def sb(self, name, shape, dt):
        return self.es.enter_context(self.nc.sbuf_tensor(name, list(shape), dt))

    def ps(self, name, shape, dt):
        return self.es.enter_context(self.nc.psum_tensor(name, list(shape), dt))

    def dram(self, name, shape, dt, kind):
        return self.nc.dram_tensor(name, list(shape), dt, kind=kind).ap()

    def _deps(self, reads, writes):
        deps = {}

        def add(k, v):
            if deps.get(k, 0) < v:
                deps[k] = v

        for r in reads:
            if r in self.lastw:
                add(*self.lastw[r])
        for w in writes:
            if w in self.lastw:
                add(*self.lastw[w])
            for k, v in self.readers.get(w, {}).items():
                add(k, v)
        return deps

    def _wait(self, e, deps, skip=None):
        eng = self.E[e]
        for k, v in deps.items():
            if k == skip:
                continue
            if self.waited.get((e, k), 0) >= v:
                continue
            eng.wait_ge(self.semh[k], v)
            self.waited[(e, k)] = v

    def _commit(self, key, val, reads, writes):
        for r in reads:
            d = self.readers.setdefault(r, {})
            if d.get(key, 0) < val:
                d[key] = val
        for w in writes:
            self.lastw[w] = (key, val)
            self.readers[w] = {}

    def op(self, e, fn, reads=(), writes=()):
        deps = self._deps(reads, writes)
        self._wait(e, deps, skip=("pe" if e == "pe" else None))
        ins = fn(self.E[e])
        self.cnt[e] += 1
        ins.then_inc(self.semh[e], 1)
        self._commit(e, self.cnt[e], reads, writes)

    def dma(self, q, out, in_, reads=(), writes=()):
        key = f"dq_{q}_{self.dn[q] % self.NCH}"
        self.dn[q] += 1
        deps = self._deps(reads, writes)
        if self.cnt[key] > deps.get(key, 0):
            deps[key] = self.cnt[key]
        self._wait(q, deps)
        ins = self.E[q].dma_start(out=out, in_=in_)
        self.cnt[key] += 16
        ins.then_inc(self.semh[key], 16)
        self._commit(key, self.cnt[key], reads, writes)

    def finish(self):
        for k, v in self.cnt.items():
            if v > 0:
                self.E["sp"].wait_ge(self.semh[k], v)
        self.es.close()
        return self.nc


def make_ident(P, name="ident"):
    ones = P.sb(name + "_ones", [128, 128], BF16)
    ident = P.sb(name, [128, 128], BF16)
    P.op("pool", lambda e: e.memset(ones[:], 1.0), writes=[name + "_ones"])
    P.op("pool", lambda e: e.affine_select(out=ident[:], in_=ones[:], pattern=[[-1, 128]],
                                            compare_op=ALU.is_equal, fill=0.0, base=0,
                                            channel_multiplier=1),
         reads=[name + "_ones"], writes=[name])
    return ident


class Normer:
    def __init__(self, P, ident, nbuf=2, npt=2, lnexp=False):
        self.P = P
        self.ident = ident
        self.lnexp = lnexp
        self.junk = P.sb("nm_junk", [128, D], BF16)
        self.ss = [P.sb(f"nm_ss{i}", [128, 1], F32) for i in range(nbuf)]
        self.rs = [P.sb(f"nm_rs{i}", [128, 1], F32) for i in range(nbuf)]
        self.yb = [P.sb(f"nm_yb{i}", [128, D], BF16) for i in range(nbuf)]
        self.pT = [P.ps(f"nm_pT{i}", [128, 8, 128], BF16) for i in range(npt)]
        self.n = 0
        self.nbuf = nbuf
        self.npt = npt

    def stats(self, x_ap, xres, i):
        P = self.P
        ss = self.ss[i]
        P.op("dve", lambda e: e.memset(ss[:], 0.0), writes=[f"nm_ss{i}"])
        P.op("act", lambda e: e.activation(out=self.junk[:], in_=x_ap, func=AF.Square, accum_out=ss[:]),
             reads=[xres, f"nm_ss{i}"], writes=["nm_junk", f"nm_ss{i}"])

    def finish(self, i):
        P = self.P
        ss, rs = self.ss[i], self.rs[i]
        P.op("dve", lambda e: e.tensor_scalar(out=ss[:], in0=ss[:], scalar1=1.0 / D, scalar2=EPS,
                                              op0=ALU.mult, op1=ALU.add),
             reads=[f"nm_ss{i}"], writes=[f"nm_ss{i}"])
        if self.lnexp:
            P.op("act", lambda e: e.activation(out=ss[:], in_=ss[:], func=AF.Ln),
                 reads=[f"nm_ss{i}"], writes=[f"nm_ss{i}"])
            P.op("act", lambda e: e.activation(out=rs[:], in_=ss[:], func=AF.Exp, scale=-0.5),
                 reads=[f"nm_ss{i}"], writes=[f"nm_rs{i}"])
        else:
            P.op("act", lambda e: e.activation(out=ss[:], in_=ss[:], func=AF.Sqrt),
                 reads=[f"nm_ss{i}"], writes=[f"nm_ss{i}"])
            P.op("dve", lambda e: e.reciprocal(out=rs[:], in_=ss[:]),
                 reads=[f"nm_ss{i}"], writes=[f"nm_rs{i}"])
        return rs

    def rstd(self, x_ap, xres, i):
        self.stats(x_ap, xres, i)
        return self.finish(i)

    def scale(self, x_ap, xres, gB_ap, i):
        P = self.P
        yb, rs = self.yb[i], self.rs[i]
        P.op("dve", lambda e: e.scalar_tensor_tensor(out=yb[:], in0=x_ap, scalar=rs[:, 0:1], in1=gB_ap,
                                                     op0=ALU.mult, op1=ALU.mult),
             reads=[xres, f"nm_rs{i}", "gB"], writes=[f"nm_yb{i}"])
        return yb

    def transpose(self, src_bf, srcres, dst_ap, dstres, i, evac="act"):
        P = self.P
        j = i % self.npt
        pT = self.pT[j]
        for k in range(8):
            P.op("pe", lambda e, k=k: e.transpose(out=pT[:, k, :], in_=src_bf[:, k * 128:(k + 1) * 128],
                                                   identity=self.ident[:]),
                 reads=[srcres, "ident"], writes=[f"nm_pT{j}"])
        if evac == "act":
            P.op("act", lambda e: e.copy(out=dst_ap, in_=pT[:]), reads=[f"nm_pT{j}"], writes=[dstres])
        else:
            P.op(evac, lambda e: e.tensor_copy(out=dst_ap, in_=pT[:]), reads=[f"nm_pT{j}"], writes=[dstres])

    def norm_T(self, x_ap, xres, gB_ap, dst_ap, dstres, evac="act"):
        i = self.n % self.nbuf
        self.n += 1
        self.rstd(x_ap, xres, i)
        yb = self.scale(x_ap, xres, gB_ap, i)
        self.transpose(yb, f"nm_yb{i}", dst_ap, dstres, i, evac)


class WLoader:
    def __init__(self, P, nstage=2, stage_elems=2048, cast_engines=("pool", "dve"), direct=False):
        self.P = P
        self.direct = direct
        self.stage = [] if direct else [P.sb(f"wst{i}", [128, stage_elems], F32) for i in range(nstage)]
        self.n = 0
        self.stage_elems = stage_elems
        self.cast_engines = cast_engines
        self.queues = ("sp", "pool")

    def load(self, w_ap, r0, nk, c0, ncols, dst, dstres, dk0=0, dc0=0):
        P = self.P
        if self.direct:
            src = w_ap[r0:r0 + nk * 128, c0:c0 + ncols].rearrange("(k p) c -> p k c", p=128)
            P.dma("pool", dst[:, dk0:dk0 + nk, dc0:dc0 + ncols], src, writes=[dstres])
            return
        kmax = max(1, self.stage_elems // ncols)
        k = 0
        while k < nk:
            kk = min(kmax, nk - k)
            i = self.n % len(self.stage)
            q = self.queues[self.n % len(self.queues)]
            ce = self.cast_engines[self.n % len(self.cast_engines)]
            self.n += 1
            st = self.stage[i]
            stv = st[:, 0:kk * ncols].rearrange("p (k c) -> p k c", k=kk)
            src = w_ap[r0 + k * 128:r0 + (k + kk) * 128, c0:c0 + ncols].rearrange("(k p) c -> p k c", p=128)
            P.dma(q, stv, src, writes=[f"wst{i}"])
            dv = dst[:, dk0 + k:dk0 + k + kk, dc0:dc0 + ncols]
            P.op(ce, lambda e, dv=dv, stv=stv: e.tensor_copy(out=dv, in_=stv), reads=[f"wst{i}"], writes=[dstres])
            k += kk


def load_gains(P, gains_ap, n):
    gB = P.sb("gB", [128, n, D], F32)
    for i in range(n):
        P.dma("sp", gB[:, i:i + 1, :], gains_ap[i:i + 1, :].partition_broadcast(128), writes=["gB"])
    return gB


def build_tl(mode, final):
    P = Prog()
    hg = mode == "hgrn"
    T = 2048
    NP = 2
    TP = T // NP
    NT = TP // 128
    hin = P.dram("hin", [T, D], F32, "ExternalInput")
    oin = P.dram("oin", [T, D], F32, "ExternalInput")
    gains = P.dram("gains", [4, D], F32, "ExternalInput")
    w_out = P.dram("w_out", [D, D], F32, "ExternalInput")
    w13 = P.dram("w13", [D, 2 * DFF], F32, "ExternalInput")
    w2 = P.dram("w2", [DFF, D], F32, "ExternalInput")
    if hg:
        w_og = P.dram("w_og", [D, D], F32, "ExternalInput")
    hout = P.dram("hout", [T, D], F32, "ExternalOutput")

    ident = make_ident(P)
    need = ([0, 1] if hg else []) + [2] + ([3] if final else [])
    gBt = P.sb("gB", [128, len(need), D], F32)
    for i_, r_ in enumerate(need):
        P.dma("sp", gBt[:, i_:i_ + 1, :], gains[r_:r_ + 1, :].partition_broadcast(128), writes=["gB"])

    class _G:
        def __getitem__(self, key):
            return gBt[key[0], need.index(key[1]), key[2]]
    gB = _G()
    nm = Normer(P, ident, nbuf=3, npt=2)
    wl = WLoader(P, direct=True)

    h = P.sb("h", [128, NT, D], F32)
    zT = P.sb("zT", [128, 8, TP], BF16)
    aT = P.sb("aT", [128, 11, TP], BF16)
    NSL = 2 if hg else 3
    wsl = [P.sb(f"wsl{i}", [128, 8, 512], BF16) for i in range(NSL)]
    w2s = [P.sb(f"w2s{i}", [128, 11, 512], BF16) for i in range(2)]
    wout_b = P.sb("wout_b", [128, 8, D], BF16)
    sgl = [P.sb(f"sgl{i}", [128, 512], F32) for i in range(2)]
    pacc = [P.ps(f"pacc{i}", [128, 512], F32) for i in range(6)]
    nacc = [0]

    def acc():
        i = nacc[0] % 6
        nacc[0] += 1
        return pacc[i], f"pacc{i}"

    nsl = [0]

    def next_wsl():
        i = nsl[0] % NSL
        nsl[0] += 1
        return wsl[i], f"wsl{i}"

    rings = {}

    def ring(name, n, shape, dt):
        rings[name] = [P.sb(f"{name}{i}", shape, dt) for i in range(n)]

    def X(name, t):
        n = len(rings[name])
        return rings[name][t % n], f"{name}{t % n}"

    ring("ot", 5 if hg else 3, [128, D], F32)
    ring("onb", 3, [128, D], BF16)
    ring("oT", 3, [128, 8, 128], BF16)
    ring("zb", 3, [128, D], BF16)
    for fam in ("o", "z"):
        ring("ss" + fam, 4, [128, 1], F32)
        ring("rs" + fam, 4, [128, 1], F32)
    if hg:
        ring("yb", 3, [128, D], BF16)
        ring("yT", 3, [128, 8, 128], BF16)
        ring("sg", 2, [128, D], F32)
        ring("ssy", 4, [128, 1], F32)
        ring("rsy", 4, [128, 1], F32)
        wog_b = aT[:, 0:8, :]
    aT_keys = [("aT", c, tg) for c in range(11) for tg in range(TP // 512)]
    wl.load(w_out, 0, 8, 0, D, wout_b, "wout_b")

    def stats(x_ap, xres, fam, g):
        ss, sr = X("ss" + fam, g)
        P.op("dve", lambda e: e.memset(ss[:], 0.0), writes=[sr])
        P.op("act", lambda e: e.activation(out=nm.junk[:], in_=x_ap, func=AF.Square, accum_out=ss[:]),
             reads=[xres, sr], writes=["nm_junk", sr])

    def finish(fam, g):
        ss, sr = X("ss" + fam, g)
        rs, rr = X("rs" + fam, g)
        P.op("dve", lambda e: e.tensor_scalar(out=ss[:], in0=ss[:], scalar1=1.0 / D, scalar2=EPS, op0=ALU.mult, op1=ALU.add),
             reads=[sr], writes=[sr])
        P.op("act", lambda e: e.activation(out=ss[:], in_=ss[:], func=AF.Sqrt), reads=[sr], writes=[sr])
        P.op("dve", lambda e: e.reciprocal(out=rs[:], in_=ss[:]), reads=[sr], writes=[rr])

    def transposes(src, sres, g):
        j = g % 2
        for k in range(8):
            P.op("pe", lambda e, k=k: e.transpose(out=nm.pT[j][:, k, :], in_=src[:, k * 128:(k + 1) * 128], identity=ident[:]),
                 reads=[sres, "ident"], writes=[f"nm_pT{j}"])

    for ps_ in range(NP):
        t0 = ps_ * TP
        G = lambda t: ps_ * NT + t

        def f_load(t):
            g = G(t)
            P.dma("sp", h[:, t, :], hin[t0 + t * 128:t0 + (t + 1) * 128, :], writes=[("h", t)])
            if hg:
                stats(h[:, t, :], ("h", t), "y", g)
            else:
                ot, otr = X("ot", g)
                P.dma("sp", ot[:], oin[t0 + t * 128:t0 + (t + 1) * 128, :], writes=[otr])

        def f_rs(t):
            finish("y", G(t))

        def f_scale(t):
            g = G(t)
            if hg:
                yb, ybr = X("yb", g)
                rs, rr = X("rsy", g)
                P.op("dve", lambda e: e.scalar_tensor_tensor(out=yb[:], in0=h[:, t, :], scalar=rs[:, 0:1], in1=gB[:, 0, :],
                                                             op0=ALU.mult, op1=ALU.mult), reads=[("h", t), rr, "gB"], writes=[ybr])
            else:
                ot, otr = X("ot", g)
                onb, onr = X("onb", g)
                P.op("act", lambda e: e.copy(out=onb[:], in_=ot[:]), reads=[otr], writes=[onr])

        def f_tr1(t):
            if hg:
                yb, ybr = X("yb", G(t))
                transposes(yb, ybr, 2 * G(t))

        def f_ev1(t):
            g = G(t)
            yT, ytr = X("yT", g)
            j = (2 * g) % 2
            P.op("act", lambda e: e.copy(out=yT[:], in_=nm.pT[j][:]), reads=[f"nm_pT{j}"], writes=[ytr])
            ot, otr = X("ot", g)
            P.dma("sp", ot[:], oin[t0 + t * 128:t0 + (t + 1) * 128, :], writes=[otr])
            stats(ot[:], otr, "o", g)

        og_banks = {}

        def f_og(t):
            if hg:
                finish("o", G(t))
                yT, ytr = X("yT", G(t))
                og_banks[t] = []
                for cg in range(2):
                    pa, pres = acc()
                    og_banks[t].append((pa, pres))
                    for k in range(8):
                        P.op("pe", lambda e, k=k, pa=pa, cg=cg: e.matmul(pa[:], lhsT=yT[:, k, :], rhs=wog_b[:, k, cg * 512:(cg + 1) * 512],
                                                                         start=(k == 0), stop=(k == 7)),
                             reads=[ytr] + aT_keys, writes=[pres])

        def f_sg(t):
            if hg:
                ot, otr = X("ot", G(t))
                rso, rro = X("rso", G(t))
                P.op("dve", lambda e: e.scalar_tensor_tensor(out=ot[:], in0=ot[:], scalar=rso[:, 0:1], in1=gB[:, 1, :],
                                                             op0=ALU.mult, op1=ALU.mult), reads=[otr, rro, "gB"], writes=[otr])
                sg, sgr = X("sg", G(t))
                for cg in range(2):
                    pa, pres = og_banks[t][cg]
                    P.op("act", lambda e, pa=pa, cg=cg: e.activation(out=sg[:, cg * 512:(cg + 1) * 512], in_=pa[:], func=AF.Sigmoid),
                         reads=[pres], writes=[(sgr, cg)])

        def f_gate(t):
            if hg:
                g = G(t)
                ot, otr = X("ot", g)
                sg, sgr = X("sg", g)
                onb, onr = X("onb", g)
                P.op("dve", lambda e: e.tensor_tensor(out=onb[:], in0=ot[:], in1=sg[:], op=ALU.mult),
                     reads=[otr, (sgr, 0), (sgr, 1)], writes=[onr])

        def f_tr2(t):
            onb, onr = X("onb", G(t))
            transposes(onb, onr, 2 * G(t) + 1)

        def f_ev2(t):
            oT, otr_ = X("oT", G(t))
            j = (2 * G(t) + 1) % 2
            P.op("act", lambda e: e.copy(out=oT[:], in_=nm.pT[j][:]), reads=[f"nm_pT{j}"], writes=[otr_])

        wo_banks = {}

        def f_wo(t):
            oT, otr_ = X("oT", G(t))
            wo_banks[t] = []
            for cg in range(2):
                pa, pres = acc()
                wo_banks[t].append((pa, pres))
                for k in range(8):
                    P.op("pe", lambda e, k=k, pa=pa, cg=cg: e.matmul(pa[:], lhsT=oT[:, k, :], rhs=wout_b[:, k, cg * 512:(cg + 1) * 512],
                                                                     start=(k == 0), stop=(k == 7)),
                         reads=[otr_, "wout_b"], writes=[pres])

        def f_res(t):
            for cg in range(2):
                pa, pres = wo_banks[t][cg]
                hv = h[:, t, cg * 512:(cg + 1) * 512]
                P.op("dve", lambda e, pa=pa, hv=hv: e.tensor_tensor(out=hv, in0=hv, in1=pa[:], op=ALU.add),
                     reads=[pres, ("h", t)], writes=[("h", t)])
            stats(h[:, t, :], ("h", t), "z", G(t))

        def f_rs3(t):
            finish("z", G(t))

        def f_sc3(t):
            zb, zbr = X("zb", G(t))
            rs, rr = X("rsz", G(t))
            P.op("dve", lambda e: e.scalar_tensor_tensor(out=zb[:], in0=h[:, t, :], scalar=rs[:, 0:1], in1=gB[:, 2, :],
                                                         op0=ALU.mult, op1=ALU.mult), reads=[("h", t), rr, "gB"], writes=[zbr])

        def f_tr3(t):
            zb, zbr = X("zb", G(t))
            transposes(zb, zbr, 2 * G(t))

        def f_ev3(t):
            j = (2 * G(t)) % 2
            P.op("act", lambda e: e.copy(out=zT[:, :, t * 128:(t + 1) * 128], in_=nm.pT[j][:]), reads=[f"nm_pT{j}"], writes=[("zT", t)])

        if hg:
            P.dma("pool", wog_b, w_og[:, :].rearrange("(k p) c -> p k c", p=128), writes=aT_keys)
        if hg:
            stages = [f_load, f_rs, f_scale, f_tr1, f_ev1, f_og, f_sg, f_gate, f_tr2, f_ev2, f_wo, f_res, f_rs3, f_sc3, f_tr3, f_ev3]
        else:
            stages = [f_load, f_scale, f_tr2, f_ev2, f_wo, f_res, f_rs3, f_sc3, f_tr3, f_ev3]
        NS = len(stages)
        for step in range(NT + NS - 1):
            for si in range(NS - 1, -1, -1):
                t = step - si
                if 0 <= t < NT:
                    stages[si](t)

        zT_all = [("zT", t) for t in range(NT)]
        for half in range(2):
            c0 = half * 11
            ci = 0
            while ci < 11:
                ncg = min(2, 11 - ci)
                s, sres = next_wsl()
                wl.load(w13, 0, 8, (c0 + ci) * 128, ncg * 128, s, sres, dc0=0)
                wl.load(w13, 0, 8, DFF + (c0 + ci) * 128, ncg * 128, s, sres, dc0=256)
                for fc in range(ncg):
                    for tg in range(TP // 512):
                        pg, pgres = acc()
                        pu, pures = acc()
                        for k in range(8):
                            P.op("pe", lambda e, k=k, pg=pg, s=s, fc=fc, tg=tg: e.matmul(
                                pg[:], lhsT=s[:, k, fc * 128:(fc + 1) * 128], rhs=zT[:, k, tg * 512:(tg + 1) * 512],
                                start=(k == 0), stop=(k == 7)), reads=zT_all[tg * 4:tg * 4 + 4] + [sres], writes=[pgres])
                        for k in range(8):
                            P.op("pe", lambda e, k=k, pu=pu, s=s, fc=fc, tg=tg: e.matmul(
                                pu[:], lhsT=s[:, k, 256 + fc * 128:256 + (fc + 1) * 128],
                                rhs=zT[:, k, tg * 512:(tg + 1) * 512],
                                start=(k == 0), stop=(k == 7)), reads=zT_all[tg * 4:tg * 4 + 4] + [sres], writes=[pures])
                        sb_ = nacc[0] % 2
                        P.op("act", lambda e, pg=pg, sb_=sb_: e.activation(out=sgl[sb_][:], in_=pg[:], func=AF.Silu),
                             reads=[pgres], writes=[f"sgl{sb_}"])
                        av = aT[:, ci + fc, tg * 512:(tg + 1) * 512]
                        P.op("dve", lambda e, pu=pu, sb_=sb_, av=av: e.tensor_tensor(out=av, in0=sgl[sb_][:], in1=pu[:],
                                                                                     op=ALU.mult),
                             reads=[pures, f"sgl{sb_}"], writes=[("aT", ci + fc, tg)])
                ci += ncg
            for cg in range(2):
                wi = (half * 2 + cg) % 2
                wl.load(w2, c0 * 128, 11, cg * 512, 512, w2s[wi], f"w2s{wi}")
                for t in range(NT):
                    pa, pres = acc()
                    for c in range(11):
                        P.op("pe", lambda e, c=c, pa=pa, wi=wi: e.matmul(pa[:], lhsT=aT[:, c, t * 128:(t + 1) * 128],
                                                                        rhs=w2s[wi][:, c, :], start=(c == 0), stop=(c == 10)),
                             reads=[("aT", c, t // 4) for c in range(11)] + [f"w2s{wi}"], writes=[pres])
                    hv = h[:, t, cg * 512:(cg + 1) * 512]
                    P.op("dve", lambda e, pa=pa, hv=hv: e.tensor_tensor(out=hv, in0=hv, in1=pa[:], op=ALU.add),
                         reads=[pres, ("h", t)], writes=[("h", t)])
        for t in range(NT):
            if final:
                g = G(t)
                stats(h[:, t, :], ("h", t), "o", g + 1000 * 0)
                finish("o", g)
                rs, rr = X("rso", g)
                ot, otr = X("ot", g)
                P.op("dve", lambda e, rs=rs, ot=ot: e.scalar_tensor_tensor(out=ot[:], in0=h[:, t, :], scalar=rs[:, 0:1],
                                                                           in1=gB[:, 3, :], op0=ALU.mult, op1=ALU.mult),
                     reads=[("h", t), rr, "gB"], writes=[otr])
                P.dma("sp", hout[t0 + t * 128:t0 + (t + 1) * 128, :], ot[:], reads=[otr], writes=[("hout", ps_, t)])
            else:
                P.dma("sp", hout[t0 + t * 128:t0 + (t + 1) * 128, :], h[:, t, :], reads=[("h", t)],
                      writes=[("hout", ps_, t)])
    return P.finish()


_CACHE = {}


def _get(name, builder):
    if name not in _CACHE:
        _CACHE[name] = builder()
    return _CACHE[name]


def run_tl(mode, final, hin, oin, gains, w_out, w13, w2, w_og=None):
    nc = _get(("tl", mode, final), lambda: build_tl(mode, final))
    maps = []
    for c in range(8):
        m = {"hin": np.ascontiguousarray(hin[c * 2048:(c + 1) * 2048]),
             "oin": np.ascontiguousarray(oin[c * 2048:(c + 1) * 2048]),
             "gains": gains, "w_out": w_out, "w13": w13, "w2": w2}
        if mode == "hgrn":
            m["w_og"] = w_og
        maps.append(m)
    res = run_bass_kernel_spmd(nc, maps, core_ids=list(range(8)))
    return np.concatenate([res.results[c]["hout"] for c in range(8)], axis=0)


def build_hgrn(NTILE=SEQ // 128):
    P = Prog()
    x = P.dram("x", [SEQ, D], F32, "ExternalInput")
    gcol = P.dram("gcol", [128, 8], F32, "ExternalInput")
    wq = P.dram("wq", [D, 256], F32, "ExternalInput")
    wf = P.dram("wf", [D, 256], F32, "ExternalInput")
    wi = P.dram("wi", [D, 256], F32, "ExternalInput")
    lbl = P.dram("lbl", [128, 4], F32, "ExternalInput")
    oout = P.dram("o", [SEQ, 256], F32, "ExternalOutput")

    ident = make_ident(P)
    nm = Normer(P, ident, nbuf=5, npt=2, lnexp=True)
    wqf_b = P.sb("wqf_b", [128, 8, 512], BF16)
    wi_b = P.sb("wi_b", [128, 8, 256], BF16)
    gc = P.sb("gc", [128, 8], F32)
    P.dma("sp", gc[:], gcol[:, :], writes=["gc"])
    wst = [P.sb(f"wst{i}", [128, 8, 256], F32) for i in range(2)]
    for j, (wsrc, dst, dres, dc0) in enumerate(((wq, wqf_b, "wqf_b", 0), (wf, wqf_b, "wqf_b", 256), (wi, wi_b, "wi_b", 0))):
        st = wst[j % 2]
        P.dma("sp", st[:], wsrc[:, :].rearrange("(k p) c -> p k c", p=128), writes=[f"wst{j % 2}"])
        for k in range(8):
            P.op("dve", lambda e, k=k, st=st, dst=dst, dc0=dc0: e.tensor_scalar(
                out=dst[:, k, dc0:dc0 + 256], in0=st[:, k, :], scalar1=gc[:, k:k + 1], scalar2=None, op0=ALU.mult),
                reads=[f"wst{j % 2}", "gc"], writes=[dres])

    lbt = P.sb("lbt", [128, 4], F32)
    lb = P.sb("lb", [128, 2], F32)
    oml = P.sb("oml", [128, 2], F32)
    P.dma("sp", lbt[:], lbl[:, :], writes=["lbt"])
    P.op("dve", lambda e: e.tensor_tensor(out=lb[:], in0=lbt[:, 2:4], in1=lbt[:, 0:2], op=ALU.subtract),
         reads=["lbt"], writes=["lb"])
    P.op("act", lambda e: e.activation(out=lb[:], in_=lb[:], func=AF.Exp), reads=["lb"], writes=["lb"])
    P.op("dve", lambda e: e.tensor_scalar(out=lb[:], in0=lb[:], scalar1=1.0, scalar2=None, op0=ALU.add),
         reads=["lb"], writes=["lb"])
    P.op("dve", lambda e: e.reciprocal(out=lb[:], in_=lb[:]), reads=["lb"], writes=["lb"])
    P.op("dve", lambda e: e.tensor_scalar(out=oml[:], in0=lb[:], scalar1=-1.0, scalar2=1.0, op0=ALU.mult, op1=ALU.add),
         reads=["lb"], writes=["oml"])

    onesf = P.sb("onesf", [128, 2, 128], F32)
    mask = P.sb("mask", [128, 2, 128], F32)
    P.op("pool", lambda e: e.memset(onesf[:], 1.0), writes=["onesf"])
    P.op("pool", lambda e: e.affine_select(out=mask[:], in_=onesf[:], pattern=[[0, 2], [1, 128]],
                                            compare_op=ALU.is_ge, fill=0.0, base=0, channel_multiplier=-1),
         reads=["onesf"], writes=["mask"])

    state = P.sb("state", [128, 2, 128], F32)
    state_b = P.sb("state_b", [128, 2, 128], BF16)
    P.op("dve", lambda e: e.memset(state[:], 0.0), writes=[("state", 0), ("state", 1)])
    P.op("dve", lambda e: e.memset(state_b[:], 0.0), writes=["state_b"])

    rings = {}

    def ring(name, n, shape, dt):
        rings[name] = [P.sb(f"{name}{i}", shape, dt) for i in range(n)]

    def X(name, t):
        n = len(rings[name])
        return rings[name][t % n], f"{name}{t % n}"

    ring("xt", 4, [128, D], F32)
    ring("yT", 3, [128, 8, 128], BF16)
    ring("EE", 4, [128, 4, 128], F32)
    ring("qr", 4, [128, 2, 128], F32)
    ring("ib", 12, [128, 256], BF16)
    ring("qs", 7, [128, 2, 128], F32)
    ring("fg", 3, [128, 2, 128], F32)
    ring("kk", 7, [128, 2, 128], F32)
    ring("lf", 3, [128, 2, 128], F32)
    ring("AA", 4, [128, 4, 2, 128], F32)
    ring("ex", 7, [128, 4, 2, 128], F32)
    ring("qc", 3, [128, 2, 128], BF16)
    ring("kc", 3, [128, 2, 128], BF16)
    ring("qd", 5, [128, 2, 128], BF16)
    ring("kh", 3, [128, 2, 128], BF16)
    ring("khT", 3, [128, 2, 128], BF16)
    ring("scm", 3, [128, 2, 128], BF16)
    ring("osb", 3, [128, 256], F32)
    pQF = [P.ps(f"pQF{i}", [128, 4, 128], F32) for i in range(2)]
    pI = P.ps("pI", [128, 512], F32)[:, 0:256]
    pKS = P.ps("pKS", [128, 512], F32)
    pK = pKS[:, 0:128].bitcast(BF16).rearrange("p (a b) -> p a b", a=2)
    pS = pKS[:, 256:512].rearrange("p (a b) -> p a b", a=2)
    pO = P.ps("pO", [128, 512], F32)[:, 0:256]
    pU = P.ps("pU", [128, 512], F32)[:, 0:256].rearrange("p (a b) -> p a b", a=2)
    NMB = nm.nbuf

    def s_load(t):
        xt, xr = X("xt", t)
        P.dma("sp", xt[:], x[t * 128:(t + 1) * 128, :], writes=[xr])
        nm.stats(xt[:], xr, t % NMB)

    def s_rs(t):
        nm.finish(t % NMB)

    def s_scale(t):
        xt, xr = X("xt", t)
        i = t % NMB
        P.op("act", lambda e: e.activation(out=nm.yb[i][:], in_=xt[:], func=AF.Copy, scale=nm.rs[i][:, 0:1]),
             reads=[xr, f"nm_rs{i}"], writes=[f"nm_yb{i}"])

    def s_tr(t):
        i = t % NMB
        j = t % 2
        for k in range(8):
            P.op("pe", lambda e, k=k: e.transpose(out=nm.pT[j][:, k, :], in_=nm.yb[i][:, k * 128:(k + 1) * 128], identity=ident[:]),
                 reads=[f"nm_yb{i}", "ident"], writes=[f"nm_pT{j}"])

    def s_trc(t):
        yT, yr = X("yT", t)
        j = t % 2
        P.op("act", lambda e: e.copy(out=yT[:], in_=nm.pT[j][:]), reads=[f"nm_pT{j}"], writes=[yr])

    def s_proj(t):
        yT, yr = X("yT", t)
        pq, pres = pQF[t % 2], f"pQF{t % 2}"
        for c in range(4):
            for k in range(8):
                P.op("pe", lambda e, k=k, c=c: e.matmul(pq[:, c, :], lhsT=wqf_b[:, k, c * 128:(c + 1) * 128], rhs=yT[:, k, :],
                                                        start=(k == 0), stop=(k == 7)), reads=[yr, "wqf_b"], writes=[pres])
        for k in range(8):
            P.op("pe", lambda e, k=k: e.matmul(pI, lhsT=yT[:, k, :], rhs=wi_b[:, k, :],
                                               start=(k == 0), stop=(k == 7)), reads=[yr, "wi_b"], writes=["pI"])

    def s_evac(t):
        pq, pres = pQF[t % 2], f"pQF{t % 2}"
        EE, er = X("EE", t)
        qr, qrr = X("qr", t)
        P.op("act", lambda e: e.activation(out=EE[:], in_=pq[:], func=AF.Exp, scale=-1.0), reads=[pres], writes=[er])
        P.op("act", lambda e: e.copy(out=qr[:], in_=pq[:, 0:2, :]), reads=[pres], writes=[qrr])

    def s_ib(t):
        ib, ibr = X("ib", t)
        P.op("dve", lambda e: e.tensor_copy(out=ib[:], in_=pI), reads=["pI"], writes=[ibr])

    def s_sig(t):
        EE, er = X("EE", t)
        P.op("act", lambda e: e.activation(out=EE[:], in_=EE[:], func=AF.Ln, bias=1.0, scale=1.0), reads=[er], writes=[er])
        P.op("act", lambda e: e.activation(out=EE[:], in_=EE[:], func=AF.Exp, scale=-1.0), reads=[er], writes=[er])

    def s_gate(t):
        EE, er = X("EE", t)
        qr, qrr = X("qr", t)
        qs, qsr = X("qs", t)
        fg, fr = X("fg", t)
        kk, kr = X("kk", t)
        P.op("pool", lambda e: e.tensor_tensor(out=qs[:], in0=qr[:], in1=EE[:, 0:2, :], op=ALU.mult), reads=[qrr, er], writes=[qsr])
        for hh in range(2):
            P.op("dve", lambda e, hh=hh: e.tensor_scalar(out=fg[:, hh, :], in0=EE[:, 2 + hh, :], scalar1=oml[:, hh:hh + 1],
                                                         scalar2=lb[:, hh:hh + 1], op0=ALU.mult, op1=ALU.add),
                 reads=[er, "oml", "lb"], writes=[(fr, hh)])
        P.op("pool", lambda e: e.tensor_scalar(out=kk[:], in0=fg[:], scalar1=-1.0, scalar2=1.0, op0=ALU.mult, op1=ALU.add),
             reads=[(fr, 0), (fr, 1)], writes=[kr])

    def s_lf(t):
        fg, fr = X("fg", t)
        lf, lr = X("lf", t)
        P.op("act", lambda e: e.activation(out=lf[:], in_=fg[:], func=AF.Ln), reads=[(fr, 0), (fr, 1)], writes=[lr])

    def s_scan(t):
        lf, lr = X("lf", t)
        AA, ar = X("AA", t)
        for hh in range(2):
            P.op("dve", lambda e, hh=hh: e.tensor_tensor_scan(out=AA[:, 2, hh, :], data0=onesf[:, 0, :], data1=lf[:, hh, :],
                                                              initial=0.0, op0=ALU.mult, op1=ALU.add),
                 reads=[lr, "onesf"], writes=[(ar, 2, hh)])

    def s_aa(t):
        AA, ar = X("AA", t)
        for hh in range(2):
            eng = "dve" if hh == 0 else "pool"
            cum_ = AA[:, 2, hh, :]
            cenB = AA[:, 2, hh, 63:64].to_broadcast([128, 128])
            lastB = AA[:, 2, hh, 127:128].to_broadcast([128, 128])
            P.op(eng, lambda e, hh=hh: e.tensor_tensor(out=AA[:, 0, hh, :], in0=cum_, in1=cenB, op=ALU.subtract),
                 reads=[(ar, 2, hh)], writes=[(ar, 0, hh)])
            P.op(eng, lambda e, hh=hh: e.tensor_tensor(out=AA[:, 1, hh, :], in0=cenB, in1=cum_, op=ALU.subtract),
                 reads=[(ar, 2, hh)], writes=[(ar, 1, hh)])
            P.op(eng, lambda e, hh=hh: e.tensor_tensor(out=AA[:, 3, hh, :], in0=lastB, in1=cum_, op=ALU.subtract),
                 reads=[(ar, 2, hh)], writes=[(ar, 3, hh)])

    def s_exp(t):
        AA, ar = X("AA", t)
        ex, exr = X("ex", t)
        P.op("act", lambda e: e.activation(out=ex[:], in_=AA[:], func=AF.Exp),
             reads=[(ar, i, hh) for i in range(4) for hh in range(2)], writes=[exr])

    def s_prod(t):
        ex, exr = X("ex", t)
        qs, qsr = X("qs", t)
        kk, kr = X("kk", t)
        qc, qcr = X("qc", t)
        kc, kcr = X("kc", t)
        qd, qdr = X("qd", t)
        kh, khr = X("kh", t)
        P.op("dve", lambda e: e.tensor_tensor(out=qc[:], in0=qs[:], in1=ex[:, 0, :, :], op=ALU.mult), reads=[qsr, exr], writes=[qcr])
        P.op("pool", lambda e: e.tensor_tensor(out=qd[:], in0=qs[:], in1=ex[:, 2, :, :], op=ALU.mult), reads=[qsr, exr], writes=[qdr])
        P.op("dve", lambda e: e.tensor_tensor(out=kc[:], in0=kk[:], in1=ex[:, 1, :, :], op=ALU.mult), reads=[kr, exr], writes=[kcr])
        P.op("pool", lambda e: e.tensor_tensor(out=kh[:], in0=kk[:], in1=ex[:, 3, :, :], op=ALU.mult), reads=[kr, exr], writes=[khr])

    def s_pe5(t):
        qc, qcr = X("qc", t)
        kc, kcr = X("kc", t)
        kh, khr = X("kh", t)
        for hh in range(2):
            P.op("pe", lambda e, hh=hh: e.transpose(out=pK[:, hh, :], in_=kh[:, hh, :], identity=ident[:]),
                 reads=[khr, "ident"], writes=["pKS"])
        for hh in range(2):
            P.op("pe", lambda e, hh=hh: e.matmul(pS[:, hh, :], lhsT=kc[:, hh, :], rhs=qc[:, hh, :], start=True, stop=True),
                 reads=[kcr, qcr], writes=["pKS"])

    def s_ev5(t):
        khT, ktr = X("khT", t)
        scm, scr_ = X("scm", t)
        P.op("dve", lambda e: e.tensor_copy(out=khT[:], in_=pK), reads=["pKS"], writes=[ktr])
        P.op("dve", lambda e: e.tensor_tensor(out=scm[:], in0=pS, in1=mask[:], op=ALU.mult), reads=["pKS", "mask"], writes=[scr_])

    def s_pe6(t):
        khT, ktr = X("khT", t)
        scm, scr_ = X("scm", t)
        ib, ibr = X("ib", t)
        qd, qdr = X("qd", t)
        for hh in range(2):
            P.op("pe", lambda e, hh=hh: e.matmul(pO[:, hh * 128:(hh + 1) * 128], lhsT=scm[:, hh, :],
                                                 rhs=ib[:, hh * 128:(hh + 1) * 128], start=True, stop=False),
                 reads=[scr_, ibr], writes=["pO"])
            P.op("pe", lambda e, hh=hh: e.matmul(pO[:, hh * 128:(hh + 1) * 128], lhsT=qd[:, hh, :],
                                                 rhs=state_b[:, hh, :], start=False, stop=True),
                 reads=[qdr, "state_b"], writes=["pO"])
        for hh in range(2):
            P.op("pe", lambda e, hh=hh: e.matmul(pU[:, hh, :], lhsT=khT[:, hh, :], rhs=ib[:, hh * 128:(hh + 1) * 128],
                                                 start=True, stop=True),
                 reads=[ktr, ibr], writes=["pU"])

    def s_fin(t):
        ex, exr = X("ex", t)
        osb, osr = X("osb", t)
        for hh in range(2):
            P.op("dve", lambda e, hh=hh: e.scalar_tensor_tensor(out=state[:, hh, :], in0=state[:, hh, :],
                                                                scalar=ex[:, 2, hh, 127:128], in1=pU[:, hh, :],
                                                                op0=ALU.mult, op1=ALU.add),
                 reads=[("state", hh), exr, "pU"], writes=[("state", hh)])
        P.op("pool", lambda e: e.tensor_copy(out=state_b[:], in_=state[:]), reads=[("state", 0), ("state", 1)], writes=["state_b"])
        P.op("act", lambda e: e.copy(out=osb[:], in_=pO), reads=["pO"], writes=[osr])
        P.dma("sp", oout[t * 128:(t + 1) * 128, :], osb[:], reads=[osr], writes=[("o", t)])

    stages = [s_load, s_rs, s_scale, s_tr, s_trc, s_proj, s_evac, s_sig, s_gate, s_lf, s_scan, s_aa, s_exp, s_prod,
              s_pe5, s_ev5, s_pe6, s_fin]
    NS = len(stages)
    order = [(s_ib, 6), (s_fin, NS - 1)] + [(stages[i], i) for i in range(NS - 3, -1, -1)] + [(s_pe6, NS - 2)]
    for step in range(NTILE + NS - 1):
        for f_, si in order:
            t = step - si
            if 0 <= t < NTILE:
                f_(t)
    return P.finish()


def run_hgrn(x, g_mix, w_in, lb_logits):
    nc = _get("hgrn", build_hgrn)
    maps = []
    for c in range(8):
        b, hp = c // 4, c % 4
        cs = slice(hp * 256, (hp + 1) * 256)
        lbl = lb_logits[:, cs].reshape(2, 2, 128).transpose(2, 0, 1).reshape(128, 4)
        maps.append({"x": np.ascontiguousarray(x[b]), "gcol": np.ascontiguousarray(g_mix.reshape(8, 128).T),
                     "wq": np.ascontiguousarray(w_in[:, cs]),
                     "wf": np.ascontiguousarray(w_in[:, 1024 + hp * 256:1024 + (hp + 1) * 256]),
                     "wi": np.ascontiguousarray(w_in[:, 2048 + hp * 256:2048 + (hp + 1) * 256]),
                     "lbl": np.ascontiguousarray(lbl)})
    res = run_bass_kernel_spmd(nc, maps, core_ids=list(range(8)))
    o = np.empty((2, SEQ, D), np.float32)
    for c in range(8):
        b, hp = c // 4, c % 4
        o[b, :, hp * 256:(hp + 1) * 256] = res.results[c]["o"]
    return o.reshape(2 * SEQ, D)


GL = 1536
MNEG = -30000.0


def t5_bucket_np(dist):
    n = np.maximum(dist, 0)
    nf = np.maximum(n, 16).astype(np.float32)
    large = 16 + (np.log(nf / np.float32(16)) / np.float32(np.log(64.0)) * np.float32(16)).astype(np.int32)
    large = np.minimum(large, 31)
    return np.where(n < 16, n, large)


def moba_consts():
    i = np.arange(GL)
    bk = t5_bucket_np(i - 255)
    oh = np.zeros((32, GL), np.float32)
    valid = i >= 255
    oh[bk[valid], i[valid]] = 1.0
    neg = np.zeros((128, 256), np.float32)
    neg[:, :255] = MNEG
    return oh, neg


def build_moba(S=SEQ, stop=99, skip=()):
    P = Prog()
    nc = P.nc
    NTILE = S // 128
    NBLK = S // 256
    SCALE = 128 ** -0.5
    x = P.dram("x", [S, D], F32, "ExternalInput")
    gains = P.dram("gains", [1, D], F32, "ExternalInput")
    wq = P.dram("wq", [D, 256], F32, "ExternalInput")
    wk = P.dram("wk", [D, 256], F32, "ExternalInput")
    wv = P.dram("wv", [D, 256], F32, "ExternalInput")
    tab = P.dram("tab", [32, 2], F32, "ExternalInput")
    oh = P.dram("oh", [32, GL], F32, "ExternalInput")
    negrow = P.dram("negrow", [128, 256], F32, "ExternalInput")
    oout = P.dram("o", [S, 256], F32, "ExternalOutput")
    scr_t = [nc.dram_tensor(f"scr{h}", [128, GL], F32, kind="Internal") for h in range(2)]

    ident = make_ident(P)
    gB = load_gains(P, gains, 1)
    nm = Normer(P, ident, nbuf=4, npt=2)
    wl = WLoader(P, direct=True)
    wq_b = P.sb("wq_b", [128, 8, 256], BF16)
    wk_b = P.sb("wk_b", [128, 8, 256], BF16)
    wv_b = P.sb("wv_b", [128, 8, 256], BF16)
    wl.load(wq, 0, 8, 0, 256, wq_b, "wq_b")
    wl.load(wk, 0, 8, 0, 256, wk_b, "wk_b")
    wl.load(wv, 0, 8, 0, 256, wv_b, "wv_b")

    QT = P.sb("QT", [128, 2, S], BF16)
    KT = P.sb("KT", [128, 2, S], BF16)
    VA = P.sb("VA", [128, NTILE, 2, 132], BF16)
    sel = P.sb("sel", [128, NTILE, 2, 32], F32)
    BT = P.sb("BT", [128, 2, 5, 2, 256], BF16)
    c31 = P.sb("c31", [128, 2], F32)
    ksum = P.sb("ksum", [128, 2, NTILE], F32)
    kmT = P.sb("kmT", [128, 2, 32], F32)
    kmT_b = P.sb("kmT_b", [128, 2, 32], BF16)
    maskadd = P.sb("maskadd", [128, 32, 32], F32)
    gm = P.sb("gm", [128, 2, 32], F32)
    top8 = P.sb("top8", [128, 2, 8], F32)
    xt = [P.sb(f"xt{i}", [128, D], F32) for i in range(4)]
    yT = [P.sb(f"yT{i}", [128, 8, 128], BF16) for i in range(3)]
    acc = [P.sb(f"acc{i}", [128, 2, 132], F32) for i in range(2)]
    osb = [P.sb(f"osb{i}", [128, 2, 128], F32) for i in range(2)]
    rec = [P.sb(f"rec{i}", [128, 2], F32) for i in range(2)]
    bk = [P.ps(f"bk{i}", [128, 512], F32) for i in range(6)]

    ohs = P.sb("ohs", [32, GL], F32)
    tabs = P.sb("tabs", [32, 2], F32)
    tabc = P.sb("tabc", [32, 2, 128], F32)
    gp = P.sb("gp", [128, GL], F32)
    ngs = P.sb("ngs", [128, 256], F32)
    P.dma("sp", ohs[:], oh[:, :], writes=["ohs"])
    P.dma("sp", tabs[:], tab[:, :], writes=["tabs"])
    P.dma("sp", ngs[:], negrow[:, :], writes=["ngs"])
    for h in range(2):
        if "bc" in skip:
            continue
        P.op("dve", lambda e, h=h: e.tensor_copy(out=tabc[:, h, :], in_=tabs[:, h:h + 1].to_broadcast([32, 128])),
             reads=["tabs"], writes=["tabc"])
    for h in range(2):
        if "bias" in skip:
            continue
        for c in range(GL // 512):
            P.op("pe", lambda e, h=h, c=c: e.matmul(bk[c][:], lhsT=tabc[:, h, :], rhs=ohs[:, c * 512:(c + 1) * 512],
                                                    start=True, stop=True), reads=["tabc", "ohs"], writes=[f"bk{c}"])
            P.op("dve", lambda e, c=c: e.tensor_copy(out=gp[:, c * 512:(c + 1) * 512], in_=bk[c][:]),
                 reads=[f"bk{c}"], writes=["gp"])
        P.op("dve", lambda e: e.tensor_tensor(out=gp[:, 0:256], in0=gp[:, 0:256], in1=ngs[:], op=ALU.add),
             reads=["gp", "ngs"], writes=["gp"])
        P.op("act", lambda e, h=h: e.copy(out=c31[:, h:h + 1], in_=gp[:, GL - 1:GL]), reads=["gp"], writes=["c31"])
        P.op("dve", lambda e: e.tensor_scalar(out=gp[:], in0=gp[:], scalar1=1.0 / SCALE, scalar2=None, op0=ALU.mult),
             reads=["gp", "c31"], writes=["gp"])
        P.dma("sp", scr_t[h].ap(), gp[:], reads=["gp"], writes=[f"scr{h}"])
        for dJ in range(5):
            if "skew" in skip:
                continue
            for half in range(2):
                off = dJ * 256 - half * 128 + 255
                src = bass.AP(tensor=scr_t[h], offset=off, ap=[[GL - 1, 128], [1, 256]])
                P.dma("pool", BT[:, h, dJ, half, :], src, reads=[f"scr{h}"], writes=["BT"])
    P.op("pool", lambda e: e.memset(maskadd[:], 0.0), writes=["maskadd"])
    if "maskadd" not in skip:
      P.op("pool", lambda e: e.affine_select(out=maskadd[:], in_=maskadd[:], pattern=[[1, 32], [-1, 32]],
                                            compare_op=ALU.is_ge, fill=NEG, base=-1, channel_multiplier=0),
         reads=["maskadd"], writes=["maskadd"])
    if "vaones" not in skip:
        P.op("pool", lambda e: e.memset(VA[:, :, :, 128:129], 1.0), writes=["VAones"])

    if stop <= 0:
        return P.finish()
    pQKs = [(bk[3][:].rearrange("p (a b) -> p a b", a=4), "bk3"), (bk[2][:].rearrange("p (a b) -> p a b", a=4), "bk2")]
    pVs = [(bk[4][:, 0:256], "bk4"), (bk[5][:, 0:256], "bk5")]
    XR, YR = 4, 3

    def p1a(t):
        P.dma("sp", xt[t % XR][:], x[t * 128:(t + 1) * 128, :], writes=[f"xt{t % XR}"])
        nm.stats(xt[t % XR][:], f"xt{t % XR}", t % XR)

    def p1b(t):
        nm.finish(t % XR)
        nm.scale(xt[t % XR][:], f"xt{t % XR}", gB[:, 0, :], t % XR)

    def p1c(t):
        nm.transpose(nm.yb[t % XR], f"nm_yb{t % XR}", yT[t % YR][:], f"yT{t % YR}", t)

    def p1d(t):
        b = t % YR
        pQK, qkres = pQKs[t % 2]
        pV, vres = pVs[t % 2]
        for j, w in enumerate((wq_b, wk_b)):
            wres = "wq_b" if j == 0 else "wk_b"
            for hh in range(2):
                for k in range(8):
                    P.op("pe", lambda e, k=k, w=w, hh=hh, j=j: e.matmul(
                        pQK[:, j * 2 + hh, :], lhsT=w[:, k, hh * 128:(hh + 1) * 128], rhs=yT[b][:, k, :],
                        start=(k == 0), stop=(k == 7)), reads=[f"yT{b}", wres], writes=[qkres])
        for k in range(8):
            P.op("pe", lambda e, k=k: e.matmul(pV, lhsT=yT[b][:, k, :], rhs=wv_b[:, k, :],
                                               start=(k == 0), stop=(k == 7)), reads=[f"yT{b}", "wv_b"], writes=[vres])
        P.op("act", lambda e: e.copy(out=QT[:, :, t * 128:(t + 1) * 128], in_=pQK[:, 0:2, :]), reads=[qkres], writes=[("QT", t)])
        P.op("act", lambda e: e.copy(out=KT[:, :, t * 128:(t + 1) * 128], in_=pQK[:, 2:4, :]), reads=[qkres], writes=[("KT", t)])
        P.op("dve", lambda e: e.tensor_reduce(out=ksum[:, :, t], in_=KT[:, :, t * 128:(t + 1) * 128], axis=AX.X, op=ALU.add),
             reads=[("KT", t)], writes=["ksum"])
        P.op("act", lambda e: e.copy(out=VA[:, t, :, 0:128], in_=pV.rearrange("p (a b) -> p a b", a=2)),
             reads=[vres], writes=[("VA", t)])

    pGs = [bk[i][:, 0:64].rearrange("p (a b) -> p a b", a=2) for i in range(2)]
    P.op("dve", lambda e: e.memset(kmT_b[:], 0.0), writes=[("kmT_b", n) for n in range(32)])
    P.op("dve", lambda e: e.memset(gm[:], NEG), writes=["gm"])

    def p1e(t):
        if t % 2 == 1:
            n = t // 2
            P.op("dve", lambda e: e.tensor_tensor(out=kmT[:, :, n], in0=ksum[:, :, t - 1], in1=ksum[:, :, t], op=ALU.add),
                 reads=["ksum"], writes=["kmT"])
            P.op("dve", lambda e: e.tensor_scalar(out=kmT_b[:, :, n], in0=kmT[:, :, n], scalar1=1.0 / 256, scalar2=None, op0=ALU.mult),
                 reads=["kmT"], writes=[("kmT_b", n)])

    def p1f(t):
        j = t // 2
        if j == 0:
            return
        pG = pGs[t % 2]
        for hh in range(2):
            P.op("pe", lambda e, hh=hh: e.matmul(pG[:, hh, 0:j], lhsT=QT[:, hh, t * 128:(t + 1) * 128], rhs=kmT_b[:, hh, 0:j],
                                                 start=True, stop=True),
                 reads=[("QT", t)] + [("kmT_b", n) for n in range(j)], writes=[f"bk{t % 2}"])

    def p1g(t):
        j = t // 2
        if j == 0:
            return
        pG = pGs[t % 2]
        P.op("dve", lambda e: e.tensor_copy(out=gm[:, :, 0:j], in_=pG[:, :, 0:j]), reads=[f"bk{t % 2}", "gm"], writes=["gm"])
        for hh in range(2):
            P.op("dve", lambda e, hh=hh: e.max(out=top8[:, hh, :], in_=gm[:, hh, :]), reads=["gm"], writes=[("top8", hh)])
        for hh in range(2):
            P.op("dve", lambda e, hh=hh: e.tensor_scalar(out=sel[:, t, hh, :], in0=gm[:, hh, :], scalar1=top8[:, hh, 2:3],
                                                         scalar2=None, op0=ALU.is_ge),
                 reads=["gm", ("top8", hh)], writes=[("sel", t)])

    p1stages = (p1a, p1b, p1c, p1d, p1e, p1f, p1g)
    for step in range(NTILE + len(p1stages) - 1):
        for s_ in (6, 4, 3, 2, 1, 0, 5):
            t = step - s_
            if 0 <= t < NTILE:
                p1stages[s_](t)
    if stop <= 2:
        return P.finish()
    tmpO = [gp[:, i * 512:i * 512 + 264].rearrange("p (a b) -> p a b", a=2) for i in range(3)]
    PT = [nm.yb[i][:, 0:512].rearrange("p (a b) -> p a b", a=2) for i in range(3)]
    iters = [(hh, J, n) for hh in range(2) for J in range(NBLK) for n in range(J, -1, -1)]
    NI = len(iters)

    def views(k):
        r = k % 3
        pS = bk[r][:].rearrange("p (a b) -> p a b", a=2)
        pO = bk[3 + r][:, 0:264].rearrange("p (a b) -> p a b", a=2)
        return r, pS, pO

    def stA(k):
        hh, J, n = iters[k]
        r, pS, pO = views(k)
        near = (J - n) <= 4
        for half in range(2):
            P.op("pe", lambda e, half=half: e.matmul(
                pS[:, half, :], lhsT=KT[:, hh, n * 256 + half * 128:n * 256 + (half + 1) * 128],
                rhs=QT[:, hh, J * 256:(J + 1) * 256], start=True, stop=not near),
                reads=[("QT", 2 * J), ("QT", 2 * J + 1), ("KT", 2 * n + half)],
                writes=[f"bk{r}"])
            if near:
                P.op("pe", lambda e, half=half: e.matmul(pS[:, half, :], lhsT=ident[:], rhs=BT[:, hh, J - n, half, :],
                                                         start=False, stop=True),
                     reads=["ident", "BT"], writes=[f"bk{r}"])

    def stB(k):
        hh, J, n = iters[k]
        r, pS, pO = views(k)
        if J - n <= 4:
            P.op("act", lambda e: e.activation(out=PT[r], in_=pS, func=AF.Exp, scale=SCALE),
                 reads=[f"bk{r}"], writes=[f"PT{r}"])
        else:
            P.op("act", lambda e: e.activation(out=PT[r], in_=pS, func=AF.Exp, bias=c31[:, hh:hh + 1], scale=SCALE),
                 reads=[f"bk{r}", "c31"], writes=[f"PT{r}"])

    def stC(k):
        hh, J, n = iters[k]
        r, pS, pO = views(k)
        for qt in range(2):
            for half in range(2):
                P.op("pe", lambda e, qt=qt, half=half: e.matmul(
                    pO[:, qt, 0:129], lhsT=PT[r][:, half, qt * 128:(qt + 1) * 128],
                    rhs=VA[:, 2 * n + half, hh, 0:129], start=(half == 0), stop=(half == 1)),
                    reads=[f"PT{r}", ("VA", 2 * n + half), "VAones"],
                    writes=[f"bk{3 + r}"])

    def stD(k):
        hh, J, n = iters[k]
        r, pS, pO = views(k)
        dJ = J - n
        ab = J % 2
        ares = f"acc{ab}"
        if dJ == 0:
            P.op("dve", lambda e: e.tensor_copy(out=acc[ab][:, :, 0:129], in_=pO[:, :, 0:129]),
                 reads=[f"bk{3 + r}"], writes=[(ares, 0), (ares, 1)])
        else:
            for qt in range(2):
                P.op("dve", lambda e, qt=qt: e.scalar_tensor_tensor(
                    out=acc[ab][:, qt, 0:129], in0=pO[:, qt, 0:129], scalar=sel[:, 2 * J + qt, hh, n:n + 1],
                    in1=acc[ab][:, qt, 0:129], op0=ALU.mult, op1=ALU.add),
                    reads=[f"bk{3 + r}", ("sel", 2 * J + qt), (ares, qt)], writes=[(ares, qt)])
        if n == 0:
            P.op("dve", lambda e: e.reciprocal(out=rec[ab][:], in_=acc[ab][:, :, 128]), reads=[(ares, 0), (ares, 1)], writes=[f"rec{ab}"])
            for qt in range(2):
                P.op("dve", lambda e, qt=qt: e.tensor_scalar(out=osb[ab][:, qt, :], in0=acc[ab][:, qt, 0:128],
                                                             scalar1=rec[ab][:, qt:qt + 1], scalar2=None, op0=ALU.mult),
                     reads=[(ares, qt), f"rec{ab}"], writes=[(f"osb{ab}", qt)])
            P.dma("sp", oout[J * 256:(J + 1) * 256, hh * 128:(hh + 1) * 128].rearrange("(q p) d -> p q d", p=128),
                  osb[ab][:], reads=[(f"osb{ab}", 0), (f"osb{ab}", 1)], writes=[("o", hh, J)])

    for step in range(NI + 3):
        for s_, f_ in enumerate((stA, stB, stC, stD)):
            k = step - s_
            if 0 <= k < NI:
                f_(k)
    return P.finish()


def run_moba(h1, g_mix, w_in, rel_table):
    nc = _get("moba", build_moba)
    oh, neg = moba_consts()
    maps = []
    for c in range(8):
        b, hp = c // 4, c % 4
        cs = slice(hp * 256, (hp + 1) * 256)
        maps.append({"x": np.ascontiguousarray(h1[b * SEQ:(b + 1) * SEQ]), "gains": np.ascontiguousarray(g_mix[None, :]),
                     "wq": np.ascontiguousarray(w_in[:, cs]),
                     "wk": np.ascontiguousarray(w_in[:, 1024 + hp * 256:1024 + (hp + 1) * 256]),
                     "wv": np.ascontiguousarray(w_in[:, 2048 + hp * 256:2048 + (hp + 1) * 256]),
                     "tab": np.ascontiguousarray(rel_table[:, hp * 2:hp * 2 + 2]), "oh": oh, "negrow": neg})
    res = run_bass_kernel_spmd(nc, maps, core_ids=list(range(8)))
    o = np.empty((2, SEQ, D), np.float32)
    for c in range(8):
        b, hp = c // 4, c % 4
        o[b, :, hp * 256:(hp + 1) * 256] = res.results[c]["o"]
    return o.reshape(2 * SEQ, D)


def kernel(x, norm_mix, norm_ffn, hgrn_w_in, hgrn_lb_logits, hgrn_out_norm, hgrn_w_out,
           moba_w_in, moba_w_out, rel_bias_table, ffn_w13, ffn_w2, final_norm):
    f = lambda a: np.ascontiguousarray(np.asarray(a, dtype=np.float32))
    x = f(x)
    norm_mix, norm_ffn, final_norm = f(norm_mix), f(norm_ffn), f(final_norm)
    hgrn_w_in, hgrn_lb_logits, hgrn_out_norm, hgrn_w_out = f(hgrn_w_in), f(hgrn_lb_logits), f(hgrn_out_norm), f(hgrn_w_out)
    moba_w_in, moba_w_out, rel_bias_table = f(moba_w_in), f(moba_w_out), f(rel_bias_table)
    ffn_w13, ffn_w2 = f(ffn_w13), f(ffn_w2)
    xf = x.reshape(2 * SEQ, D)
    o0 = run_hgrn(x, norm_mix[0], hgrn_w_in[0], hgrn_lb_logits)
    gains0 = f(np.stack([norm_mix[0], hgrn_out_norm[0], norm_ffn[0], final_norm]))
    h1 = run_tl("hgrn", False, xf, o0, gains0, hgrn_w_out[0], ffn_w13[0], ffn_w2[0],
                w_og=f(hgrn_w_in[0][:, 3072:4096]))
    o1 = run_moba(h1, norm_mix[1], moba_w_in[0], rel_bias_table)
    gains1 = f(np.stack([norm_mix[1], hgrn_out_norm[0], norm_ffn[1], final_norm]))
    out = run_tl("moba", True, h1, o1, gains1, moba_w_out[0], ffn_w13[1], ffn_w2[1])
    return out.reshape(2, SEQ, D)
```

```python
from contextlib import ExitStack

import numpy as np
import concourse.bass as bass
import concourse.mybir as mybir
from concourse.bass_utils import run_bass_kernel_spmd

F32 = mybir.dt.float32
BF16 = mybir.dt.bfloat16
ALU = mybir.AluOpType
AF = mybir.ActivationFunctionType
AX = mybir.AxisListType

D = 1024
DFF = 2816
SEQ = 8192
EPS = 1e-6
NEG = -1.0e30


class Prog:
    def __init__(self):
        self.nc = bass.Bass("TRN2", target_bir_lowering=False)
        nc = self.nc
        self.es = ExitStack()
        self.E = {"pe": nc.tensor, "act": nc.scalar, "dve": nc.vector, "pool": nc.gpsimd, "sp": nc.sync}
        self.semh = {}
        self.cnt = {}
        self.NCH = 8
        self.dn = {"sp": 0, "pool": 0}
        for k in ["pe", "act", "dve", "pool"] + [f"dq_{q}_{i}" for q in ("sp", "pool") for i in range(self.NCH)]:
            self.semh[k] = self.es.enter_context(nc.semaphore(k))
            self.cnt[k] = 0
        self.lastw = {}
        self.readers = {}
        self.waited = {}
        self._uid = 0

    def sb(self, name, shape, dt):
        return self.es.enter_context(self.nc.sbuf_tensor(name, list(shape), dt))

    def ps(self, name, shape, dt):
        return self.es.enter_context(self.nc.psum_tensor(name, list(shape), dt))

    def dram(self, name, shape, dt, kind):
        return self.nc.dram_tensor(name, list(shape), dt, kind=kind).ap()

    def _deps(self, reads, writes):
        deps = {}

        def add(k, v):
            if deps.get(k, 0) < v:
                deps[k] = v

        for r in reads:
            if r in self.lastw:
                add(*self.lastw[r])
        for w in writes:
            if w in self.lastw:
                add(*self.lastw[w])
            for k, v in self.readers.get(w, {}).items():
                add(k, v)
        return deps

    def _wait(self, e, deps, skip=None):
        eng = self.E[e]
        for k, v in deps.items():
            if k == skip:
                continue
            if self.waited.get((e, k), 0) >= v:
                continue
            eng.wait_ge(self.semh[k], v)
            self.waited[(e, k)] = v

    def _commit(self, key, val, reads, writes):
        for r in reads:
            d = self.readers.setdefault(r, {})
            if d.get(key, 0) < val:
                d[key] = val
        for w in writes:
            self.lastw[w] = (key, val)
            self.readers[w] = {}

    def op(self, e, fn, reads=(), writes=()):
        deps = self._deps(reads, writes)
        self._wait(e, deps, skip=("pe" if e == "pe" else None))
        ins = fn(self.E[e])
        self.cnt[e] += 1
        ins.then_inc(self.semh[e], 1)
        self._commit(e, self.cnt[e], reads, writes)

    def dma(self, q, out, in_, reads=(), writes=()):
        key = f"dq_{q}_{self.dn[q] % self.NCH}"
        self.dn[q] += 1
        deps = self._deps(reads, writes)
        if self.cnt[key] > deps.get(key, 0):
            deps[key] = self.cnt[key]
        self._wait(q, deps)
        ins = self.E[q].dma_start(out=out, in_=in_)
        self.cnt[key] += 16
        ins.then_inc(self.semh[key], 16)
        self._commit(key, self.cnt[key], reads, writes)

    def finish(self):
        for k, v in self.cnt.items():
            if v > 0:
                self.E["sp"].wait_ge(self.semh[k], v)
        self.es.close()
        return self.nc


def make_ident(P, name="ident"):
    ones = P.sb(name + "_ones", [128, 128], BF16)
    ident = P.sb(name, [128, 128], BF16)
    P.op("pool", lambda e: e.memset(ones[:], 1.0), writes=[name + "_ones"])
    P.op("pool", lambda e: e.affine_select(out=ident[:], in_=ones[:], pattern=[[-1, 128]],
                                            compare_op=ALU.is_equal, fill=0.0, base=0,
                                            channel_multiplier=1),
         reads=[name + "_ones"], writes=[name])
    return ident


class Normer:
    def __init__(self, P, ident, nbuf=2, npt=2, lnexp=False):
        self.P = P
        self.ident = ident
        self.lnexp = lnexp
        self.junk = P.sb("nm_junk", [128, D], BF16)
        self.ss = [P.sb(f"nm_ss{i}", [128, 1], F32) for i in range(nbuf)]
        self.rs = [P.sb(f"nm_rs{i}", [128, 1], F32) for i in range(nbuf)]
        self.yb = [P.sb(f"nm_yb{i}", [128, D], BF16) for i in range(nbuf)]
        self.pT = [P.ps(f"nm_pT{i}", [128, 8, 128], BF16) for i in range(npt)]
        self.n = 0
        self.nbuf = nbuf
        self.npt = npt

    def stats(self, x_ap, xres, i):
        P = self.P
        ss = self.ss[i]
        P.op("dve", lambda e: e.memset(ss[:], 0.0), writes=[f"nm_ss{i}"])
        P.op("act", lambda e: e.activation(out=self.junk[:], in_=x_ap, func=AF.Square, accum_out=ss[:]),
             reads=[xres, f"nm_ss{i}"], writes=["nm_junk", f"nm_ss{i}"])

    def finish(self, i):
        P = self.P
        ss, rs = self.ss[i], self.rs[i]
        P.op("dve", lambda e: e.tensor_scalar(out=ss[:], in0=ss[:], scalar1=1.0 / D, scalar2=EPS,
                                              op0=ALU.mult, op1=ALU.add),
             reads=[f"nm_ss{i}"], writes=[f"nm_ss{i}"])
        if self.lnexp:
            P.op("act", lambda e: e.activation(out=ss[:], in_=ss[:], func=AF.Ln),
                 reads=[f"nm_ss{i}"], writes=[f"nm_ss{i}"])
            P.op("act", lambda e: e.activation(out=rs[:], in_=ss[:], func=AF.Exp, scale=-0.5),
                 reads=[f"nm_ss{i}"], writes=[f"nm_rs{i}"])
        else:
            P.op("act", lambda e: e.activation(out=ss[:], in_=ss[:], func=AF.Sqrt),
                 reads=[f"nm_ss{i}"], writes=[f"nm_ss{i}"])
            P.op("dve", lambda e: e.reciprocal(out=rs[:], in_=ss[:]),
                 reads=[f"nm_ss{i}"], writes=[f"nm_rs{i}"])
        return rs

    def rstd(self, x_ap, xres, i):
        self.stats(x_ap, xres, i)
        return self.finish(i)

    def scale(self, x_ap, xres, gB_ap, i):
        P = self.P
        yb, rs = self.yb[i], self.rs[i]
        P.op("dve", lambda e: e.scalar_tensor_tensor(out=yb[:], in0=x_ap, scalar=rs[:, 0:1], in1=gB_ap,
                                                     op0=ALU.mult, op1=ALU.mult),
             reads=[xres, f"nm_rs{i}", "gB"], writes=[f"nm_yb{i}"])
        return yb

    def transpose(self, src_bf, srcres, dst_ap, dstres, i, evac="act"):
        P = self.P
        j = i % self.npt
        pT = self.pT[j]
        for k in range(8):
            P.op("pe", lambda e, k=k: e.transpose(out=pT[:, k, :], in_=src_bf[:, k * 128:(k + 1) * 128],
                                                   identity=self.ident[:]),
                 reads=[srcres, "ident"], writes=[f"nm_pT{j}"])
        if evac == "act":
            P.op("act", lambda e: e.copy(out=dst_ap, in_=pT[:]), reads=[f"nm_pT{j}"], writes=[dstres])
        else:
            P.op(evac, lambda e: e.tensor_copy(out=dst_ap, in_=pT[:]), reads=[f"nm_pT{j}"], writes=[dstres])

    def norm_T(self, x_ap, xres, gB_ap, dst_ap, dstres, evac="act"):
        i = self.n % self.nbuf
        self.n += 1
        self.rstd(x_ap, xres, i)
        yb = self.scale(x_ap, xres, gB_ap, i)
        self.transpose(yb, f"nm_yb{i}", dst_ap, dstres, i, evac)


class WLoader:
    def __init__(self, P, nstage=2, stage_elems=2048, cast_engines=("pool", "dve"), direct=False):
        self.P = P
        self.direct = direct
        self.stage = [] if direct else [P.sb(f"wst{i}", [128, stage_elems], F32) for i in range(nstage)]
        self.n = 0
        self.stage_elems = stage_elems
        self.cast_engines = cast_engines
        self.queues = ("sp", "pool")

    def load(self, w_ap, r0, nk, c0, ncols, dst, dstres, dk0=0, dc0=0):
        P = self.P
        if self.direct:
            src = w_ap[r0:r0 + nk * 128, c0:c0 + ncols].rearrange("(k p) c -> p k c", p=128)
            P.dma("pool", dst[:, dk0:dk0 + nk, dc0:dc0 + ncols], src, writes=[dstres])
            return
        kmax = max(1, self.stage_elems // ncols)
        k = 0
        while k < nk:
            kk = min(kmax, nk - k)
            i = self.n % len(self.stage)
            q = self.queues[self.n % len(self.queues)]
            ce = self.cast_engines[self.n % len(self.cast_engines)]
            self.n += 1
            st = self.stage[i]
            stv = st[:, 0:kk * ncols].rearrange("p (k c) -> p k c", k=kk)
            src = w_ap[r0 + k * 128:r0 + (k + kk) * 128, c0:c0 + ncols].rearrange("(k p) c -> p k c", p=128)
            P.dma(q, stv, src, writes=[f"wst{i}"])
            dv = dst[:, dk0 + k:dk0 + k + kk, dc0:dc0 + ncols]
            P.op(ce, lambda e, dv=dv, stv=stv: e.tensor_copy(out=dv, in_=stv), reads=[f"wst{i}"], writes=[dstres])
            k += kk


def load_gains(P, gains_ap, n):
    gB = P.sb("gB", [128, n, D], F32)
    for i in range(n):
        P.dma("sp", gB[:, i:i + 1, :], gains_ap[i:i + 1, :].partition_broadcast(128), writes=["gB"])
    return gB


def build_tl(mode, final):
    P = Prog()
    hg = mode == "hgrn"
    T = 2048
    NP = 2
    TP = T // NP
    NT = TP // 128
    hin = P.dram("hin", [T, D], F32, "ExternalInput")
    oin = P.dram("oin", [T, D], F32, "ExternalInput")
    gains = P.dram("gains", [4, D], F32, "ExternalInput")
    w_out = P.dram("w_out", [D, D], F32, "ExternalInput")
    w13 = P.dram("w13", [D, 2 * DFF], F32, "ExternalInput")
    w2 = P.dram("w2", [DFF, D], F32, "ExternalInput")
    if hg:
        w_og = P.dram("w_og", [D, D], F32, "ExternalInput")
    hout = P.dram("hout", [T, D], F32, "ExternalOutput")

    ident = make_ident(P)
    need = ([0, 1] if hg else []) + [2] + ([3] if final else [])
    gBt = P.sb("gB", [128, len(need), D], F32)
    for i_, r_ in enumerate(need):
        P.dma("pool", gBt[:, i_:i_ + 1, :], gains[r_:r_ + 1, :].partition_broadcast(128), writes=[("gB", r_)])

    class _G:
        def __getitem__(self, key):
            return gBt[key[0], need.index(key[1]), key[2]]
    gB = _G()
    nm = Normer(P, ident, nbuf=3, npt=2)
    wl = WLoader(P, direct=True)

    h = P.sb("h", [128, NT, D], F32)
    zT = P.sb("zT", [128, 8, TP], BF16)
    aT = P.sb("aT", [128, 11, TP], BF16)
    NSL = 2 if hg else 3
    wsl = [P.sb(f"wsl{i}", [128, 8, 512], BF16) for i in range(NSL)]
    w2s = [P.sb(f"w2s{i}", [128, 11, 512], BF16) for i in range(2)]
    wout_b = P.sb("wout_b", [128, 8, D], BF16)
    sgl = [P.sb(f"sgl{i}", [128, 512], F32) for i in range(2)]
    pacc = [P.ps(f"pacc{i}", [128, 512], F32) for i in range(6)]
    nacc = [0]

    def acc():
        i = nacc[0] % 6
        nacc[0] += 1
        return pacc[i], f"pacc{i}"

    nsl = [0]

    def next_wsl():
        i = nsl[0] % NSL
        nsl[0] += 1
        return wsl[i], f"wsl{i}"

    rings = {}

    def ring(name, n, shape, dt):
        rings[name] = [P.sb(f"{name}{i}", shape, dt) for i in range(n)]

    def X(name, t):
        n = len(rings[name])
        return rings[name][t % n], f"{name}{t % n}"

    ring("ot", 5 if hg else 3, [128, D], F32)
    ring("onb", 3, [128, D], BF16)
    ring("oT", 3, [128, 8, 128], BF16)
    ring("zb", 3, [128, D], BF16)
    for fam in ("o", "z"):
        ring("ss" + fam, 4, [128, 1], F32)
        ring("rs" + fam, 4, [128, 1], F32)
    if hg:
        ring("yb", 3, [128, D], BF16)
        ring("yT", 3, [128, 8, 128], BF16)
        ring("sg", 2, [128, D], F32)
        ring("ssy", 4, [128, 1], F32)
        ring("rsy", 4, [128, 1], F32)
        wog_b = aT[:, 0:8, :]
    aT_keys = [("aT", c, tg) for c in range(11) for tg in range(TP // 512)]
    wl.load(w_out, 0, 8, 0, D, wout_b, "wout_b")

    def stats(x_ap, xres, fam, g):
        ss, sr = X("ss" + fam, g)
        P.op("dve", lambda e: e.memset(ss[:], 0.0), writes=[sr])
        P.op("act", lambda e: e.activation(out=nm.junk[:], in_=x_ap, func=AF.Square, accum_out=ss[:]),
             reads=[xres, sr], writes=["nm_junk", sr])

    def finish(fam, g):
        ss, sr = X("ss" + fam, g)
        rs, rr = X("rs" + fam, g)
        P.op("dve", lambda e: e.tensor_scalar(out=ss[:], in0=ss[:], scalar1=1.0 / D, scalar2=EPS, op0=ALU.mult, op1=ALU.add),
             reads=[sr], writes=[sr])
        P.op("act", lambda e: e.activation(out=ss[:], in_=ss[:], func=AF.Sqrt), reads=[sr], writes=[sr])
        P.op("dve", lambda e: e.reciprocal(out=rs[:], in_=ss[:]), reads=[sr], writes=[rr])

    def transposes(src, sres, g):
        j = g % 2
        for k in range(8):
            P.op("pe", lambda e, k=k: e.transpose(out=nm.pT[j][:, k, :], in_=src[:, k * 128:(k + 1) * 128], identity=ident[:]),
                 reads=[sres, "ident"], writes=[f"nm_pT{j}"])

    for ps_ in range(NP):
        t0 = ps_ * TP
        G = lambda t: ps_ * NT + t

        def f_load(t):
            g = G(t)
            P.dma("sp", h[:, t, :], hin[t0 + t * 128:t0 + (t + 1) * 128, :], writes=[("h", t)])
            if hg:
                stats(h[:, t, :], ("h", t), "y", g)
            else:
                ot, otr = X("ot", g)
                P.dma("sp", ot[:], oin[t0 + t * 128:t0 + (t + 1) * 128, :], writes=[otr])

        def f_rs(t):
            finish("y", G(t))

        def f_scale(t):
            g = G(t)
            if hg:
                yb, ybr = X("yb", g)
                rs, rr = X("rsy", g)
                P.op("dve", lambda e: e.scalar_tensor_tensor(out=yb[:], in0=h[:, t, :], scalar=rs[:, 0:1], in1=gB[:, 0, :],
                                                             op0=ALU.mult, op1=ALU.mult), reads=[("h", t), rr, ("gB", 0)], writes=[ybr])
            else:
                ot, otr = X("ot", g)
                onb, onr = X("onb", g)
                P.op("act", lambda e: e.copy(out=onb[:], in_=ot[:]), reads=[otr], writes=[onr])

        def f_tr1(t):
            if hg:
                yb, ybr = X("yb", G(t))
                transposes(yb, ybr, 2 * G(t))

        def f_ev1(t):
            g = G(t)
            yT, ytr = X("yT", g)
            j = (2 * g) % 2
            P.op("act", lambda e: e.copy(out=yT[:], in_=nm.pT[j][:]), reads=[f"nm_pT{j}"], writes=[ytr])
            ot, otr = X("ot", g)
            P.dma("sp", ot[:], oin[t0 + t * 128:t0 + (t + 1) * 128, :], writes=[otr])
            stats(ot[:], otr, "o", g)

        og_banks = {}

        def f_og(t):
            if hg:
                finish("o", G(t))
                yT, ytr = X("yT", G(t))
                og_banks[t] = []
                for cg in range(2):
                    pa, pres = acc()
                    og_banks[t].append((pa, pres))
                    for k in range(8):
                        P.op("pe", lambda e, k=k, pa=pa, cg=cg: e.matmul(pa[:], lhsT=yT[:, k, :], rhs=wog_b[:, k, cg * 512:(cg + 1) * 512],
                                                                         start=(k == 0), stop=(k == 7)),
                             reads=[ytr] + aT_keys, writes=[pres])

        def f_sg(t):
            if hg:
                ot, otr = X("ot", G(t))
                rso, rro = X("rso", G(t))
                P.op("dve", lambda e: e.scalar_tensor_tensor(out=ot[:], in0=ot[:], scalar=rso[:, 0:1], in1=gB[:, 1, :],
                                                             op0=ALU.mult, op1=ALU.mult), reads=[otr, rro, ("gB", 1)], writes=[otr])
                sg, sgr = X("sg", G(t))
                for cg in range(2):
                    pa, pres = og_banks[t][cg]
                    P.op("act", lambda e, pa=pa, cg=cg: e.activation(out=sg[:, cg * 512:(cg + 1) * 512], in_=pa[:], func=AF.Sigmoid),
                         reads=[pres], writes=[(sgr, cg)])

        def f_gate(t):
            if hg:
                g = G(t)
                ot, otr = X("ot", g)
                sg, sgr = X("sg", g)
                onb, onr = X("onb", g)
                P.op("dve", lambda e: e.tensor_tensor(out=onb[:], in0=ot[:], in1=sg[:], op=ALU.mult),
                     reads=[otr, (sgr, 0), (sgr, 1)], writes=[onr])

        def f_tr2(t):
            onb, onr = X("onb", G(t))
            transposes(onb, onr, 2 * G(t) + 1)

        def f_ev2(t):
            oT, otr_ = X("oT", G(t))
            j = (2 * G(t) + 1) % 2
            P.op("act", lambda e: e.copy(out=oT[:], in_=nm.pT[j][:]), reads=[f"nm_pT{j}"], writes=[otr_])

        wo_banks = {}

        def f_wo(t):
            oT, otr_ = X("oT", G(t))
            wo_banks[t] = []
            for cg in range(2):
                pa, pres = acc()
                wo_banks[t].append((pa, pres))
                for k in range(8):
                    P.op("pe", lambda e, k=k, pa=pa, cg=cg: e.matmul(pa[:], lhsT=oT[:, k, :], rhs=wout_b[:, k, cg * 512:(cg + 1) * 512],
                                                                     start=(k == 0), stop=(k == 7)),
                         reads=[otr_, "wout_b"], writes=[pres])

        def f_res(t):
            for cg in range(2):
                pa, pres = wo_banks[t][cg]
                hv = h[:, t, cg * 512:(cg + 1) * 512]
                P.op("dve", lambda e, pa=pa, hv=hv: e.tensor_tensor(out=hv, in0=hv, in1=pa[:], op=ALU.add),
                     reads=[pres, ("h", t)], writes=[("h", t)])
            stats(h[:, t, :], ("h", t), "z", G(t))

        def f_rs3(t):
            finish("z", G(t))

        def f_sc3(t):
            zb, zbr = X("zb", G(t))
            rs, rr = X("rsz", G(t))
            P.op("dve", lambda e: e.scalar_tensor_tensor(out=zb[:], in0=h[:, t, :], scalar=rs[:, 0:1], in1=gB[:, 2, :],
                                                         op0=ALU.mult, op1=ALU.mult), reads=[("h", t), rr, ("gB", 2)], writes=[zbr])

        def f_tr3(t):
            zb, zbr = X("zb", G(t))
            transposes(zb, zbr, 2 * G(t))

        def f_ev3(t):
            j = (2 * G(t)) % 2
            P.op("act", lambda e: e.copy(out=zT[:, :, t * 128:(t + 1) * 128], in_=nm.pT[j][:]), reads=[f"nm_pT{j}"], writes=[("zT", t)])

        if hg:
            P.dma("pool", wog_b, w_og[:, :].rearrange("(k p) c -> p k c", p=128), writes=aT_keys)
        if hg:
            stages = [f_load, f_rs, f_scale, f_tr1, f_ev1, f_og, f_sg, f_gate, f_tr2, f_ev2, f_wo, f_res, f_rs3, f_sc3, f_tr3, f_ev3]
        else:
            stages = [f_load, f_scale, f_tr2, f_ev2, f_wo, f_res, f_rs3, f_sc3, f_tr3, f_ev3]
        NS = len(stages)
        for step in range(NT + NS - 1):
            for si in range(NS - 1, -1, -1):
                t = step - si
                if 0 <= t < NT:
                    stages[si](t)

        zT_all = [("zT", t) for t in range(NT)]
        for half in range(2):
            c0 = half * 11
            ci = 0
            while ci < 11:
                ncg = min(2, 11 - ci)
                s, sres = next_wsl()
                wl.load(w13, 0, 8, (c0 + ci) * 128, ncg * 128, s, sres, dc0=0)
                wl.load(w13, 0, 8, DFF + (c0 + ci) * 128, ncg * 128, s, sres, dc0=256)
                for fc in range(ncg):
                    for tg in range(TP // 512):
                        pg, pgres = acc()
                        pu, pures = acc()
                        for k in range(8):
                            P.op("pe", lambda e, k=k, pg=pg, s=s, fc=fc, tg=tg: e.matmul(
                                pg[:], lhsT=s[:, k, fc * 128:(fc + 1) * 128], rhs=zT[:, k, tg * 512:(tg + 1) * 512],
                                start=(k == 0), stop=(k == 7)), reads=zT_all[tg * 4:tg * 4 + 4] + [sres], writes=[pgres])
                        for k in range(8):
                            P.op("pe", lambda e, k=k, pu=pu, s=s, fc=fc, tg=tg: e.matmul(
                                pu[:], lhsT=s[:, k, 256 + fc * 128:256 + (fc + 1) * 128],
                                rhs=zT[:, k, tg * 512:(tg + 1) * 512],
                                start=(k == 0), stop=(k == 7)), reads=zT_all[tg * 4:tg * 4 + 4] + [sres], writes=[pures])
                        sb_ = nacc[0] % 2
                        P.op("act", lambda e, pg=pg, sb_=sb_: e.activation(out=sgl[sb_][:], in_=pg[:], func=AF.Silu),
                             reads=[pgres], writes=[f"sgl{sb_}"])
                        av = aT[:, ci + fc, tg * 512:(tg + 1) * 512]
                        P.op("dve", lambda e, pu=pu, sb_=sb_, av=av: e.tensor_tensor(out=av, in0=sgl[sb_][:], in1=pu[:],
                                                                                     op=ALU.mult),
                             reads=[pures, f"sgl{sb_}"], writes=[("aT", ci + fc, tg)])
                ci += ncg
            for cg in range(2):
                wi = (half * 2 + cg) % 2
                wl.load(w2, c0 * 128, 11, cg * 512, 512, w2s[wi], f"w2s{wi}")
                for t in range(NT):
                    pa, pres = acc()
                    for c in range(11):
                        P.op("pe", lambda e, c=c, pa=pa, wi=wi: e.matmul(pa[:], lhsT=aT[:, c, t * 128:(t + 1) * 128],
                                                                        rhs=w2s[wi][:, c, :], start=(c == 0), stop=(c == 10)),
                             reads=[("aT", c, t // 4) for c in range(11)] + [f"w2s{wi}"], writes=[pres])
                    hv = h[:, t, cg * 512:(cg + 1) * 512]
                    P.op("dve", lambda e, pa=pa, hv=hv: e.tensor_tensor(out=hv, in0=hv, in1=pa[:], op=ALU.add),
                         reads=[pres, ("h", t)], writes=[("h", t)])
        for t in range(NT):
            if final:
                g = G(t)
                stats(h[:, t, :], ("h", t), "o", g + 1000 * 0)
                finish("o", g)
                rs, rr = X("rso", g)
                ot, otr = X("ot", g)
                P.op("dve", lambda e, rs=rs, ot=ot: e.scalar_tensor_tensor(out=ot[:], in0=h[:, t, :], scalar=rs[:, 0:1],
                                                                           in1=gB[:, 3, :], op0=ALU.mult, op1=ALU.mult),
                     reads=[("h", t), rr, ("gB", 3)], writes=[otr])
                P.dma("sp", hout[t0 + t * 128:t0 + (t + 1) * 128, :], ot[:], reads=[otr], writes=[("hout", ps_, t)])
            else:
                P.dma("sp", hout[t0 + t * 128:t0 + (t + 1) * 128, :], h[:, t, :], reads=[("h", t)],
                      writes=[("hout", ps_, t)])
    return P.finish()


_CACHE = {}


def _get(name, builder):
    if name not in _CACHE:
        _CACHE[name] = builder()
    return _CACHE[name]


def run_tl(mode, final, hin, oin, gains, w_out, w13, w2, w_og=None):
    nc = _get(("tl", mode, final), lambda: build_tl(mode, final))
    maps = []
    for c in range(8):
        m = {"hin": np.ascontiguousarray(hin[c * 2048:(c + 1) * 2048]),
             "oin": np.ascontiguousarray(oin[c * 2048:(c + 1) * 2048]),
             "gains": gains, "w_out": w_out, "w13": w13, "w2": w2}
        if mode == "hgrn":
            m["w_og"] = w_og
        maps.append(m)
    res = run_bass_kernel_spmd(nc, maps, core_ids=list(range(8)))
    return np.concatenate([res.results[c]["hout"] for c in range(8)], axis=0)


def build_hgrn(NTILE=SEQ // 128):
    P = Prog()
    x = P.dram("x", [SEQ, D], F32, "ExternalInput")
    gcol = P.dram("gcol", [128, 8], F32, "ExternalInput")
    wq = P.dram("wq", [D, 256], F32, "ExternalInput")
    wf = P.dram("wf", [D, 256], F32, "ExternalInput")
    wi = P.dram("wi", [D, 256], F32, "ExternalInput")
    lbl = P.dram("lbl", [128, 4], F32, "ExternalInput")
    oout = P.dram("o", [SEQ, 256], F32, "ExternalOutput")

    ident = make_ident(P)
    nm = Normer(P, ident, nbuf=5, npt=2, lnexp=True)
    wqf_b = P.sb("wqf_b", [128, 8, 512], BF16)
    wi_b = P.sb("wi_b", [128, 8, 256], BF16)
    gc = P.sb("gc", [128, 8], F32)
    P.dma("sp", gc[:], gcol[:, :], writes=["gc"])
    wst = [P.sb(f"wst{i}", [128, 8, 256], F32) for i in range(2)]
    for j, (wsrc, dst, dres, dc0) in enumerate(((wq, wqf_b, "wqf_b", 0), (wf, wqf_b, "wqf_b", 256), (wi, wi_b, "wi_b", 0))):
        st = wst[j % 2]
        P.dma("sp", st[:], wsrc[:, :].rearrange("(k p) c -> p k c", p=128), writes=[f"wst{j % 2}"])
        for k in range(8):
            P.op("dve", lambda e, k=k, st=st, dst=dst, dc0=dc0: e.tensor_scalar(
                out=dst[:, k, dc0:dc0 + 256], in0=st[:, k, :], scalar1=gc[:, k:k + 1], scalar2=None, op0=ALU.mult),
                reads=[f"wst{j % 2}", "gc"], writes=[dres])

    lbt = P.sb("lbt", [128, 4], F32)
    lb = P.sb("lb", [128, 2], F32)
    oml = P.sb("oml", [128, 2], F32)
    P.dma("sp", lbt[:], lbl[:, :], writes=["lbt"])
    P.op("dve", lambda e: e.tensor_tensor(out=lb[:], in0=lbt[:, 2:4], in1=lbt[:, 0:2], op=ALU.subtract),
         reads=["lbt"], writes=["lb"])
    P.op("act", lambda e: e.activation(out=lb[:], in_=lb[:], func=AF.Exp), reads=["lb"], writes=["lb"])
    P.op("dve", lambda e: e.tensor_scalar(out=lb[:], in0=lb[:], scalar1=1.0, scalar2=None, op0=ALU.add),
         reads=["lb"], writes=["lb"])
    P.op("dve", lambda e: e.reciprocal(out=lb[:], in_=lb[:]), reads=["lb"], writes=["lb"])
    P.op("dve", lambda e: e.tensor_scalar(out=oml[:], in0=lb[:], scalar1=-1.0, scalar2=1.0, op0=ALU.mult, op1=ALU.add),
         reads=["lb"], writes=["oml"])

    onesf = P.sb("onesf", [128, 2, 128], F32)
    mask = P.sb("mask", [128, 2, 128], F32)
    P.op("pool", lambda e: e.memset(onesf[:], 1.0), writes=["onesf"])
    P.op("pool", lambda e: e.affine_select(out=mask[:], in_=onesf[:], pattern=[[0, 2], [1, 128]],
                                            compare_op=ALU.is_ge, fill=0.0, base=0, channel_multiplier=-1),
         reads=["onesf"], writes=["mask"])

    state = P.sb("state", [128, 2, 128], F32)
    state_b = P.sb("state_b", [128, 2, 128], BF16)
    P.op("dve", lambda e: e.memset(state[:], 0.0), writes=[("state", 0), ("state", 1)])
    P.op("dve", lambda e: e.memset(state_b[:], 0.0), writes=["state_b"])

    rings = {}

    def ring(name, n, shape, dt):
        rings[name] = [P.sb(f"{name}{i}", shape, dt) for i in range(n)]

    def X(name, t):
        n = len(rings[name])
        return rings[name][t % n], f"{name}{t % n}"

    ring("xt", 4, [128, D], F32)
    ring("yT", 3, [128, 8, 128], BF16)
    ring("EE", 4, [128, 4, 128], F32)
    ring("qr", 4, [128, 2, 128], F32)
    ring("ib", 12, [128, 256], BF16)
    ring("qs", 7, [128, 2, 128], F32)
    ring("fg", 3, [128, 2, 128], F32)
    ring("kk", 7, [128, 2, 128], F32)
    ring("lf", 3, [128, 2, 128], F32)
    ring("AA", 4, [128, 4, 2, 128], F32)
    ring("ex", 7, [128, 4, 2, 128], F32)
    ring("qc", 3, [128, 2, 128], BF16)
    ring("kc", 3, [128, 2, 128], BF16)
    ring("qd", 5, [128, 2, 128], BF16)
    ring("kh", 3, [128, 2, 128], BF16)
    ring("khT", 3, [128, 2, 128], BF16)
    ring("scm", 3, [128, 2, 128], BF16)
    ring("osb", 3, [128, 256], F32)
    pQF = [P.ps(f"pQF{i}", [128, 4, 128], F32) for i in range(2)]
    pI = P.ps("pI", [128, 512], F32)[:, 0:256]
    pKS = P.ps("pKS", [128, 512], F32)
    pK = pKS[:, 0:128].bitcast(BF16).rearrange("p (a b) -> p a b", a=2)
    pS = pKS[:, 256:512].rearrange("p (a b) -> p a b", a=2)
    pO = P.ps("pO", [128, 512], F32)[:, 0:256]
    pU = P.ps("pU", [128, 512], F32)[:, 0:256].rearrange("p (a b) -> p a b", a=2)
    NMB = nm.nbuf

    def s_load(t):
        xt, xr = X("xt", t)
        P.dma("sp", xt[:], x[t * 128:(t + 1) * 128, :], writes=[xr])
        nm.stats(xt[:], xr, t % NMB)

    def s_rs(t):
        nm.finish(t % NMB)

    def s_scale(t):
        xt, xr = X("xt", t)
        i = t % NMB
        P.op("act", lambda e: e.activation(out=nm.yb[i][:], in_=xt[:], func=AF.Copy, scale=nm.rs[i][:, 0:1]),
             reads=[xr, f"nm_rs{i}"], writes=[f"nm_yb{i}"])

    def s_tr(t):
        i = t % NMB
        j = t % 2
        for k in range(8):
            P.op("pe", lambda e, k=k: e.transpose(out=nm.pT[j][:, k, :], in_=nm.yb[i][:, k * 128:(k + 1) * 128], identity=ident[:]),
                 reads=[f"nm_yb{i}", "ident"], writes=[f"nm_pT{j}"])

    def s_trc(t):
        yT, yr = X("yT", t)
        j = t % 2
        P.op("act", lambda e: e.copy(out=yT[:], in_=nm.pT[j][:]), reads=[f"nm_pT{j}"], writes=[yr])

    def s_proj(t):
        yT, yr = X("yT", t)
        pq, pres = pQF[t % 2], f"pQF{t % 2}"
        for c in range(4):
            for k in range(8):
                P.op("pe", lambda e, k=k, c=c: e.matmul(pq[:, c, :], lhsT=wqf_b[:, k, c * 128:(c + 1) * 128], rhs=yT[:, k, :],
                                                        start=(k == 0), stop=(k == 7)), reads=[yr, "wqf_b"], writes=[pres])
        for k in range(8):
            P.op("pe", lambda e, k=k: e.matmul(pI, lhsT=yT[:, k, :], rhs=wi_b[:, k, :],
                                               start=(k == 0), stop=(k == 7)), reads=[yr, "wi_b"], writes=["pI"])

    def s_evac(t):
        pq, pres = pQF[t % 2], f"pQF{t % 2}"
        EE, er = X("EE", t)
        qr, qrr = X("qr", t)
        P.op("act", lambda e: e.activation(out=EE[:], in_=pq[:], func=AF.Exp, scale=-1.0), reads=[pres], writes=[er])
        P.op("act", lambda e: e.copy(out=qr[:], in_=pq[:, 0:2, :]), reads=[pres], writes=[qrr])

    def s_ib(t):
        ib, ibr = X("ib", t)
        P.op("dve", lambda e: e.tensor_copy(out=ib[:], in_=pI), reads=["pI"], writes=[ibr])

    def s_sig(t):
        EE, er = X("EE", t)
        P.op("act", lambda e: e.activation(out=EE[:], in_=EE[:], func=AF.Ln, bias=1.0, scale=1.0), reads=[er], writes=[er])
        P.op("act", lambda e: e.activation(out=EE[:], in_=EE[:], func=AF.Exp, scale=-1.0), reads=[er], writes=[er])

    def s_gate(t):
        EE, er = X("EE", t)
        qr, qrr = X("qr", t)
        qs, qsr = X("qs", t)
        fg, fr = X("fg", t)
        kk, kr = X("kk", t)
        P.op("pool", lambda e: e.tensor_tensor(out=qs[:], in0=qr[:], in1=EE[:, 0:2, :], op=ALU.mult), reads=[qrr, er], writes=[qsr])
        for hh in range(2):
            P.op("dve", lambda e, hh=hh: e.tensor_scalar(out=fg[:, hh, :], in0=EE[:, 2 + hh, :], scalar1=oml[:, hh:hh + 1],
                                                         scalar2=lb[:, hh:hh + 1], op0=ALU.mult, op1=ALU.add),
                 reads=[er, "oml", "lb"], writes=[(fr, hh)])
        P.op("pool", lambda e: e.tensor_scalar(out=kk[:], in0=fg[:], scalar1=-1.0, scalar2=1.0, op0=ALU.mult, op1=ALU.add),
             reads=[(fr, 0), (fr, 1)], writes=[kr])

    def s_lf(t):
        fg, fr = X("fg", t)
        lf, lr = X("lf", t)
        P.op("act", lambda e: e.activation(out=lf[:], in_=fg[:], func=AF.Ln), reads=[(fr, 0), (fr, 1)], writes=[lr])

    def s_scan(t):
        lf, lr = X("lf", t)
        AA, ar = X("AA", t)
        for hh in range(2):
            P.op("dve", lambda e, hh=hh: e.tensor_tensor_scan(out=AA[:, 2, hh, :], data0=onesf[:, 0, :], data1=lf[:, hh, :],
                                                              initial=0.0, op0=ALU.mult, op1=ALU.add),
                 reads=[lr, "onesf"], writes=[(ar, 2, hh)])

    def s_aa(t):
        AA, ar = X("AA", t)
        for hh in range(2):
            eng = "dve" if hh == 0 else "pool"
            cum_ = AA[:, 2, hh, :]
            cenB = AA[:, 2, hh, 63:64].to_broadcast([128, 128])
            lastB = AA[:, 2, hh, 127:128].to_broadcast([128, 128])
            P.op(eng, lambda e, hh=hh: e.tensor_tensor(out=AA[:, 0, hh, :], in0=cum_, in1=cenB, op=ALU.subtract),
                 reads=[(ar, 2, hh)], writes=[(ar, 0, hh)])
            P.op(eng, lambda e, hh=hh: e.tensor_tensor(out=AA[:, 1, hh, :], in0=cenB, in1=cum_, op=ALU.subtract),
                 reads=[(ar, 2, hh)], writes=[(ar, 1, hh)])
            P.op(eng, lambda e, hh=hh: e.tensor_tensor(out=AA[:, 3, hh, :], in0=lastB, in1=cum_, op=ALU.subtract),
                 reads=[(ar, 2, hh)], writes=[(ar, 3, hh)])

    def s_exp(t):
        AA, ar = X("AA", t)
        ex, exr = X("ex", t)
        P.op("act", lambda e: e.activation(out=ex[:], in_=AA[:], func=AF.Exp),
             reads=[(ar, i, hh) for i in range(4) for hh in range(2)], writes=[exr])

    def s_prod(t):
        ex, exr = X("ex", t)
        qs, qsr = X("qs", t)
        kk, kr = X("kk", t)
        qc, qcr = X("qc", t)
        kc, kcr = X("kc", t)
        qd, qdr = X("qd", t)
        kh, khr = X("kh", t)
        P.op("dve", lambda e: e.tensor_tensor(out=qc[:], in0=qs[:], in1=ex[:, 0, :, :], op=ALU.mult), reads=[qsr, exr], writes=[qcr])
        P.op("pool", lambda e: e.tensor_tensor(out=qd[:], in0=qs[:], in1=ex[:, 2, :, :], op=ALU.mult), reads=[qsr, exr], writes=[qdr])
        P.op("dve", lambda e: e.tensor_tensor(out=kc[:], in0=kk[:], in1=ex[:, 1, :, :], op=ALU.mult), reads=[kr, exr], writes=[kcr])
        P.op("pool", lambda e: e.tensor_tensor(out=kh[:], in0=kk[:], in1=ex[:, 3, :, :], op=ALU.mult), reads=[kr, exr], writes=[khr])

    def s_pe5(t):
        qc, qcr = X("qc", t)
        kc, kcr = X("kc", t)
        kh, khr = X("kh", t)
        for hh in range(2):
            P.op("pe", lambda e, hh=hh: e.transpose(out=pK[:, hh, :], in_=kh[:, hh, :], identity=ident[:]),
                 reads=[khr, "ident"], writes=["pKS"])
        for hh in range(2):
            P.op("pe", lambda e, hh=hh: e.matmul(pS[:, hh, :], lhsT=kc[:, hh, :], rhs=qc[:, hh, :], start=True, stop=True),
                 reads=[kcr, qcr], writes=["pKS"])

    def s_ev5(t):
        khT, ktr = X("khT", t)
        scm, scr_ = X("scm", t)
        P.op("dve", lambda e: e.tensor_copy(out=khT[:], in_=pK), reads=["pKS"], writes=[ktr])
        P.op("dve", lambda e: e.tensor_tensor(out=scm[:], in0=pS, in1=mask[:], op=ALU.mult), reads=["pKS", "mask"], writes=[scr_])

    def s_pe6(t):
        khT, ktr = X("khT", t)
        scm, scr_ = X("scm", t)
        ib, ibr = X("ib", t)
        qd, qdr = X("qd", t)
        for hh in range(2):
            P.op("pe", lambda e, hh=hh: e.matmul(pO[:, hh * 128:(hh + 1) * 128], lhsT=scm[:, hh, :],
                                                 rhs=ib[:, hh * 128:(hh + 1) * 128], start=True, stop=False),
                 reads=[scr_, ibr], writes=["pO"])
            P.op("pe", lambda e, hh=hh: e.matmul(pO[:, hh * 128:(hh + 1) * 128], lhsT=qd[:, hh, :],
                                                 rhs=state_b[:, hh, :], start=False, stop=True),
                 reads=[qdr, "state_b"], writes=["pO"])
        for hh in range(2):
            P.op("pe", lambda e, hh=hh: e.matmul(pU[:, hh, :], lhsT=khT[:, hh, :], rhs=ib[:, hh * 128:(hh + 1) * 128],
                                                 start=True, stop=True),
                 reads=[ktr, ibr], writes=["pU"])

    def s_fin(t):
        ex, exr = X("ex", t)
        osb, osr = X("osb", t)
        for hh in range(2):
            P.op("dve", lambda e, hh=hh: e.scalar_tensor_tensor(out=state[:, hh, :], in0=state[:, hh, :],
                                                                scalar=ex[:, 2, hh, 127:128], in1=pU[:, hh, :],
                                                                op0=ALU.mult, op1=ALU.add),
                 reads=[("state", hh), exr, "pU"], writes=[("state", hh)])
        P.op("pool", lambda e: e.tensor_copy(out=state_b[:], in_=state[:]), reads=[("state", 0), ("state", 1)], writes=["state_b"])
        P.op("act", lambda e: e.copy(out=osb[:], in_=pO), reads=["pO"], writes=[osr])
        P.dma("sp", oout[t * 128:(t + 1) * 128, :], osb[:], reads=[osr], writes=[("o", t)])

    stages = [s_load, s_rs, s_scale, s_tr, s_trc, s_proj, s_evac, s_sig, s_gate, s_lf, s_scan, s_aa, s_exp, s_prod,
              s_pe5, s_ev5, s_pe6, s_fin]
    NS = len(stages)
    order = [(s_ib, 6), (s_fin, NS - 1)] + [(stages[i], i) for i in range(NS - 3, -1, -1)] + [(s_pe6, NS - 2)]
    for step in range(NTILE + NS - 1):
        for f_, si in order:
            t = step - si
            if 0 <= t < NTILE:
                f_(t)
    return P.finish()


def run_hgrn(x, g_mix, w_in, lb_logits):
    nc = _get("hgrn", build_hgrn)
    maps = []
    for c in range(8):
        b, hp = c // 4, c % 4
        cs = slice(hp * 256, (hp + 1) * 256)
        lbl = lb_logits[:, cs].reshape(2, 2, 128).transpose(2, 0, 1).reshape(128, 4)
        maps.append({"x": np.ascontiguousarray(x[b]), "gcol": np.ascontiguousarray(g_mix.reshape(8, 128).T),
                     "wq": np.ascontiguousarray(w_in[:, cs]),
                     "wf": np.ascontiguousarray(w_in[:, 1024 + hp * 256:1024 + (hp + 1) * 256]),
                     "wi": np.ascontiguousarray(w_in[:, 2048 + hp * 256:2048 + (hp + 1) * 256]),
                     "lbl": np.ascontiguousarray(lbl)})
    res = run_bass_kernel_spmd(nc, maps, core_ids=list(range(8)))
    o = np.empty((2, SEQ, D), np.float32)
    for c in range(8):
        b, hp = c // 4, c % 4
        o[b, :, hp * 256:(hp + 1) * 256] = res.results[c]["o"]
    return o.reshape(2 * SEQ, D)


GL = 1536
MNEG = -30000.0


def t5_bucket_np(dist):
    n = np.maximum(dist, 0)
    nf = np.maximum(n, 16).astype(np.float32)
    large = 16 + (np.log(nf / np.float32(16)) / np.float32(np.log(64.0)) * np.float32(16)).astype(np.int32)
    large = np.minimum(large, 31)
    return np.where(n < 16, n, large)


def moba_consts():
    i = np.arange(GL)
    bk = t5_bucket_np(i - 255)
    oh = np.zeros((32, GL), np.float32)
    valid = i >= 255
    oh[bk[valid], i[valid]] = 1.0
    neg = np.zeros((128, 256), np.float32)
    neg[:, :255] = MNEG
    return oh, neg


def build_moba(S=SEQ, stop=99, skip=()):
    P = Prog()
    nc = P.nc
    NTILE = S // 128
    NBLK = S // 256
    SCALE = 128 ** -0.5
    x = P.dram("x", [S, D], F32, "ExternalInput")
    gains = P.dram("gains", [1, D], F32, "ExternalInput")
    wq = P.dram("wq", [D, 256], F32, "ExternalInput")
    wk = P.dram("wk", [D, 256], F32, "ExternalInput")
    wv = P.dram("wv", [D, 256], F32, "ExternalInput")
    tab = P.dram("tab", [32, 2], F32, "ExternalInput")
    oh = P.dram("oh", [32, GL], F32, "ExternalInput")
    negrow = P.dram("negrow", [128, 256], F32, "ExternalInput")
    oout = P.dram("o", [S, 256], F32, "ExternalOutput")
    scr_t = [nc.dram_tensor(f"scr{h}", [128, GL], F32, kind="Internal") for h in range(2)]

    ident = make_ident(P)
    gB = load_gains(P, gains, 1)
    nm = Normer(P, ident, nbuf=5, npt=2)
    wl = WLoader(P, direct=True)
    wq_b = P.sb("wq_b", [128, 8, 256], BF16)
    wk_b = P.sb("wk_b", [128, 8, 256], BF16)
    wv_b = P.sb("wv_b", [128, 8, 256], BF16)
    wl.load(wq, 0, 8, 0, 256, wq_b, "wq_b")
    wl.load(wk, 0, 8, 0, 256, wk_b, "wk_b")
    wl.load(wv, 0, 8, 0, 256, wv_b, "wv_b")

    QT = P.sb("QT", [128, 2, S], BF16)
    KT = P.sb("KT", [128, 2, S], BF16)
    VA = P.sb("VA", [128, NTILE, 2, 132], BF16)
    sel = P.sb("sel", [128, NTILE, 2, 32], F32)
    BT = P.sb("BT", [128, 2, 5, 2, 256], BF16)
    c31 = P.sb("c31", [128, 2], F32)
    ksum = P.sb("ksum", [128, 2, NTILE], F32)
    kmT = P.sb("kmT", [128, 2, 32], F32)
    kmT_b = P.sb("kmT_b", [128, 2, 32], BF16)
    maskadd = P.sb("maskadd", [128, 32, 32], F32)
    gm = P.sb("gm", [128, 2, 32], F32)
    top8 = P.sb("top8", [128, 2, 8], F32)
    xt = [P.sb(f"xt{i}", [128, D], F32) for i in range(5)]
    yT = [P.sb(f"yT{i}", [128, 8, 128], BF16) for i in range(3)]
    acc = [P.sb(f"acc{i}", [128, 2, 132], F32) for i in range(2)]
    osb = [P.sb(f"osb{i}", [128, 2, 128], F32) for i in range(2)]
    rec = [P.sb(f"rec{i}", [128, 2], F32) for i in range(2)]
    bk = [P.ps(f"bk{i}", [128, 512], F32) for i in range(6)]

    ohs = P.sb("ohs", [32, GL], F32)
    tabs = P.sb("tabs", [32, 2], F32)
    tabc = P.sb("tabc", [32, 2, 128], F32)
    gp = P.sb("gp", [128, GL], F32)
    ngs = P.sb("ngs", [128, 256], F32)
    P.dma("sp", ohs[:], oh[:, :], writes=["ohs"])
    P.dma("sp", tabs[:], tab[:, :], writes=["tabs"])
    P.dma("sp", ngs[:], negrow[:, :], writes=["ngs"])
    for h in range(2):
        if "bc" in skip:
            continue
        P.op("dve", lambda e, h=h: e.tensor_copy(out=tabc[:, h, :], in_=tabs[:, h:h + 1].to_broadcast([32, 128])),
             reads=["tabs"], writes=["tabc"])
    for h in range(2):
        if "bias" in skip:
            continue
        for c in range(GL // 512):
            P.op("pe", lambda e, h=h, c=c: e.matmul(bk[c][:], lhsT=tabc[:, h, :], rhs=ohs[:, c * 512:(c + 1) * 512],
                                                    start=True, stop=True), reads=["tabc", "ohs"], writes=[f"bk{c}"])
            P.op("dve", lambda e, c=c: e.tensor_copy(out=gp[:, c * 512:(c + 1) * 512], in_=bk[c][:]),
                 reads=[f"bk{c}"], writes=["gp"])
        P.op("dve", lambda e: e.tensor_tensor(out=gp[:, 0:256], in0=gp[:, 0:256], in1=ngs[:], op=ALU.add),
             reads=["gp", "ngs"], writes=["gp"])
        P.op("act", lambda e, h=h: e.copy(out=c31[:, h:h + 1], in_=gp[:, GL - 1:GL]), reads=["gp"], writes=["c31"])
        P.op("dve", lambda e: e.tensor_scalar(out=gp[:], in0=gp[:], scalar1=1.0 / SCALE, scalar2=None, op0=ALU.mult),
             reads=["gp", "c31"], writes=["gp"])
        P.dma("sp", scr_t[h].ap(), gp[:], reads=["gp"], writes=[f"scr{h}"])
        for dJ in range(5):
            if "skew" in skip:
                continue
            for half in range(2):
                off = dJ * 256 - half * 128 + 255
                src = bass.AP(tensor=scr_t[h], offset=off, ap=[[GL - 1, 128], [1, 256]])
                P.dma("pool", BT[:, h, dJ, half, :], src, reads=[f"scr{h}"], writes=["BT"])
    P.op("pool", lambda e: e.memset(maskadd[:], 0.0), writes=["maskadd"])
    if "maskadd" not in skip:
      P.op("pool", lambda e: e.affine_select(out=maskadd[:], in_=maskadd[:], pattern=[[1, 32], [-1, 32]],
                                            compare_op=ALU.is_ge, fill=NEG, base=-1, channel_multiplier=0),
         reads=["maskadd"], writes=["maskadd"])
    if "vaones" not in skip:
        P.op("pool", lambda e: e.memset(VA[:, :, :, 128:129], 1.0), writes=["VAones"])

    if stop <= 0:
        return P.finish()
    pQKs = [(bk[3][:].rearrange("p (a b) -> p a b", a=4), "bk3"), (bk[2][:].rearrange("p (a b) -> p a b", a=4), "bk2")]
    pVs = [(bk[4][:, 0:256], "bk4"), (bk[5][:, 0:256], "bk5")]
    XR, YR = 5, 3

    def p1a(t):
        P.dma("sp", xt[t % XR][:], x[t * 128:(t + 1) * 128, :], writes=[f"xt{t % XR}"])
        nm.stats(xt[t % XR][:], f"xt{t % XR}", t % XR)

    def p1b(t):
        nm.finish(t % XR)

    def p1b2(t):
        nm.scale(xt[t % XR][:], f"xt{t % XR}", gB[:, 0, :], t % XR)

    def p1c_pe(t):
        i = t % XR
        j = t % 2
        for k in range(8):
            P.op("pe", lambda e, k=k: e.transpose(out=nm.pT[j][:, k, :], in_=nm.yb[i][:, k * 128:(k + 1) * 128], identity=ident[:]),
                 reads=[f"nm_yb{i}", "ident"], writes=[f"nm_pT{j}"])

    def p1c_act(t):
        j = t % 2
        P.op("act", lambda e: e.copy(out=yT[t % YR][:], in_=nm.pT[j][:]), reads=[f"nm_pT{j}"], writes=[f"yT{t % YR}"])

    def p1d_pe(t):
        b = t % YR
        pQK, qkres = pQKs[t % 2]
        pV, vres = pVs[t % 2]
        for j, w in enumerate((wq_b, wk_b)):
            wres = "wq_b" if j == 0 else "wk_b"
            for hh in range(2):
                for k in range(8):
                    P.op("pe", lambda e, k=k, w=w, hh=hh, j=j: e.matmul(
                        pQK[:, j * 2 + hh, :], lhsT=w[:, k, hh * 128:(hh + 1) * 128], rhs=yT[b][:, k, :],
                        start=(k == 0), stop=(k == 7)), reads=[f"yT{b}", wres], writes=[qkres])
        for k in range(8):
            P.op("pe", lambda e, k=k: e.matmul(pV, lhsT=yT[b][:, k, :], rhs=wv_b[:, k, :],
                                               start=(k == 0), stop=(k == 7)), reads=[f"yT{b}", "wv_b"], writes=[vres])

    def p1d_act(t):
        pQK, qkres = pQKs[t % 2]
        pV, vres = pVs[t % 2]
        P.op("act", lambda e: e.copy(out=QT[:, :, t * 128:(t + 1) * 128], in_=pQK[:, 0:2, :]), reads=[qkres], writes=[("QT", t)])
        P.op("act", lambda e: e.copy(out=KT[:, :, t * 128:(t + 1) * 128], in_=pQK[:, 2:4, :]), reads=[qkres], writes=[("KT", t)])
        P.op("act", lambda e: e.copy(out=VA[:, t, :, 0:128], in_=pV.rearrange("p (a b) -> p a b", a=2)),
             reads=[vres], writes=[("VA", t)])

    def p1d_dve(t):
        P.op("dve", lambda e: e.tensor_reduce(out=ksum[:, :, t], in_=KT[:, :, t * 128:(t + 1) * 128], axis=AX.X, op=ALU.add),
             reads=[("KT", t)], writes=["ksum"])

    pGs = [bk[i][:, 0:64].rearrange("p (a b) -> p a b", a=2) for i in range(2)]
    P.op("dve", lambda e: e.memset(kmT_b[:], 0.0), writes=[("kmT_b", n) for n in range(32)])
    P.op("dve", lambda e: e.memset(gm[:], NEG), writes=["gm"])

    def p1e(t):
        if t % 2 == 1:
            n = t // 2
            P.op("dve", lambda e: e.tensor_tensor(out=kmT[:, :, n], in0=ksum[:, :, t - 1], in1=ksum[:, :, t], op=ALU.add),
                 reads=["ksum"], writes=["kmT"])
            P.op("dve", lambda e: e.tensor_scalar(out=kmT_b[:, :, n], in0=kmT[:, :, n], scalar1=1.0 / 256, scalar2=None, op0=ALU.mult),
                 reads=["kmT"], writes=[("kmT_b", n)])

    def p1f(t):
        j = t // 2
        if j == 0:
            return
        pG = pGs[t % 2]
        for hh in range(2):
            P.op("pe", lambda e, hh=hh: e.matmul(pG[:, hh, 0:j], lhsT=QT[:, hh, t * 128:(t + 1) * 128], rhs=kmT_b[:, hh, 0:j],
                                                 start=True, stop=True),
                 reads=[("QT", t)] + [("kmT_b", n) for n in range(j)], writes=[f"bk{t % 2}"])

    def p1g(t):
        j = t // 2
        if j == 0:
            return
        pG = pGs[t % 2]
        P.op("dve", lambda e: e.tensor_copy(out=gm[:, :, 0:j], in_=pG[:, :, 0:j]), reads=[f"bk{t % 2}", "gm"], writes=["gm"])
        for hh in range(2):
            P.op("dve", lambda e, hh=hh: e.max(out=top8[:, hh, :], in_=gm[:, hh, :]), reads=["gm"], writes=[("top8", hh)])
        for hh in range(2):
            P.op("dve", lambda e, hh=hh: e.tensor_scalar(out=sel[:, t, hh, :], in0=gm[:, hh, :], scalar1=top8[:, hh, 2:3],
                                                         scalar2=None, op0=ALU.is_ge),
                 reads=["gm", ("top8", hh)], writes=[("sel", t)])

    p1stages = [p1a, p1b, p1b2, p1c_pe, p1c_act, p1d_pe, p1d_act, p1d_dve, p1e, p1f, p1g]
    NS1 = len(p1stages)
    first = [1, 0, 6, 4, 10]
    order1 = first + [i for i in range(NS1 - 1, -1, -1) if i not in first and i != 9] + [9]
    for step in range(NTILE + NS1 - 1):
        for s_ in order1:
            t = step - s_
            if 0 <= t < NTILE:
                p1stages[s_](t)
    if stop <= 2:
        return P.finish()
    tmpO = [gp[:, i * 512:i * 512 + 264].rearrange("p (a b) -> p a b", a=2) for i in range(3)]
    PT = [nm.yb[i][:, 0:512].rearrange("p (a b) -> p a b", a=2) for i in range(3)]
    iters = [(hh, J, n) for hh in range(2) for J in range(NBLK) for n in range(J, -1, -1)]
    NI = len(iters)

    def views(k):
        r = k % 3
        pS = bk[r][:].rearrange("p (a b) -> p a b", a=2)
        pO = bk[3 + r][:, 0:264].rearrange("p (a b) -> p a b", a=2)
        return r, pS, pO

    def stA(k):
        hh, J, n = iters[k]
        r, pS, pO = views(k)
        near = (J - n) <= 4
        for half in range(2):
            P.op("pe", lambda e, half=half: e.matmul(
                pS[:, half, :], lhsT=KT[:, hh, n * 256 + half * 128:n * 256 + (half + 1) * 128],
                rhs=QT[:, hh, J * 256:(J + 1) * 256], start=True, stop=not near),
                reads=[("QT", 2 * J), ("QT", 2 * J + 1), ("KT", 2 * n + half)],
                writes=[f"bk{r}"])
            if near:
                P.op("pe", lambda e, half=half: e.matmul(pS[:, half, :], lhsT=ident[:], rhs=BT[:, hh, J - n, half, :],
                                                         start=False, stop=True),
                     reads=["ident", "BT"], writes=[f"bk{r}"])

    def stB(k):
        hh, J, n = iters[k]
        r, pS, pO = views(k)
        if J - n <= 4:
            P.op("act", lambda e: e.activation(out=PT[r], in_=pS, func=AF.Exp, scale=SCALE),
                 reads=[f"bk{r}"], writes=[f"PT{r}"])
        else:
            P.op("act", lambda e: e.activation(out=PT[r], in_=pS, func=AF.Exp, bias=c31[:, hh:hh + 1], scale=SCALE),
                 reads=[f"bk{r}", "c31"], writes=[f"PT{r}"])

    def stC(k):
        hh, J, n = iters[k]
        r, pS, pO = views(k)
        for qt in range(2):
            for half in range(2):
                P.op("pe", lambda e, qt=qt, half=half: e.matmul(
                    pO[:, qt, 0:129], lhsT=PT[r][:, half, qt * 128:(qt + 1) * 128],
                    rhs=VA[:, 2 * n + half, hh, 0:129], start=(half == 0), stop=(half == 1)),
                    reads=[f"PT{r}", ("VA", 2 * n + half), "VAones"],
                    writes=[f"bk{3 + r}"])

    def stD(k):
        hh, J, n = iters[k]
        r, pS, pO = views(k)
        dJ = J - n
        ab = J % 2
        ares = f"acc{ab}"
        if dJ == 0:
            P.op("dve", lambda e: e.tensor_copy(out=acc[ab][:, :, 0:129], in_=pO[:, :, 0:129]),
                 reads=[f"bk{3 + r}"], writes=[(ares, 0), (ares, 1)])
        else:
            for qt in range(2):
                P.op("dve", lambda e, qt=qt: e.scalar_tensor_tensor(
                    out=acc[ab][:, qt, 0:129], in0=pO[:, qt, 0:129], scalar=sel[:, 2 * J + qt, hh, n:n + 1],
                    in1=acc[ab][:, qt, 0:129], op0=ALU.mult, op1=ALU.add),
                    reads=[f"bk{3 + r}", ("sel", 2 * J + qt), (ares, qt)], writes=[(ares, qt)])
        if n == 0:
            P.op("dve", lambda e: e.reciprocal(out=rec[ab][:], in_=acc[ab][:, :, 128]), reads=[(ares, 0), (ares, 1)], writes=[f"rec{ab}"])
            for qt in range(2):
                P.op("dve", lambda e, qt=qt: e.tensor_scalar(out=osb[ab][:, qt, :], in0=acc[ab][:, qt, 0:128],
                                                             scalar1=rec[ab][:, qt:qt + 1], scalar2=None, op0=ALU.mult),
                     reads=[(ares, qt), f"rec{ab}"], writes=[(f"osb{ab}", qt)])
            P.dma("sp", oout[J * 256:(J + 1) * 256, hh * 128:(hh + 1) * 128].rearrange("(q p) d -> p q d", p=128),
                  osb[ab][:], reads=[(f"osb{ab}", 0), (f"osb{ab}", 1)], writes=[("o", hh, J)])

    for step in range(NI + 3):
        for s_, f_ in enumerate((stA, stB, stC, stD)):
            k = step - s_
            if 0 <= k < NI:
                f_(k)
    return P.finish()


def run_moba(h1, g_mix, w_in, rel_table):
    nc = _get("moba", build_moba)
    oh, neg = moba_consts()
    maps = []
    for c in range(8):
        b, hp = c // 4, c % 4
        cs = slice(hp * 256, (hp + 1) * 256)
        maps.append({"x": np.ascontiguousarray(h1[b * SEQ:(b + 1) * SEQ]), "gains": np.ascontiguousarray(g_mix[None, :]),
                     "wq": np.ascontiguousarray(w_in[:, cs]),
                     "wk": np.ascontiguousarray(w_in[:, 1024 + hp * 256:1024 + (hp + 1) * 256]),
                     "wv": np.ascontiguousarray(w_in[:, 2048 + hp * 256:2048 + (hp + 1) * 256]),
                     "tab": np.ascontiguousarray(rel_table[:, hp * 2:hp * 2 + 2]), "oh": oh, "negrow": neg})
    res = run_bass_kernel_spmd(nc, maps, core_ids=list(range(8)))
    o = np.empty((2, SEQ, D), np.float32)
    for c in range(8):
        b, hp = c // 4, c % 4
        o[b, :, hp * 256:(hp + 1) * 256] = res.results[c]["o"]
    return o.reshape(2 * SEQ, D)


def kernel(x, norm_mix, norm_ffn, hgrn_w_in, hgrn_lb_logits, hgrn_out_norm, hgrn_w_out,
           moba_w_in, moba_w_out, rel_bias_table, ffn_w13, ffn_w2, final_norm):
    f = lambda a: np.ascontiguousarray(np.asarray(a, dtype=np.float32))
    x = f(x)
    norm_mix, norm_ffn, final_norm = f(norm_mix), f(norm_ffn), f(final_norm)
    hgrn_w_in, hgrn_lb_logits, hgrn_out_norm, hgrn_w_out = f(hgrn_w_in), f(hgrn_lb_logits), f(hgrn_out_norm), f(hgrn_w_out)
    moba_w_in, moba_w_out, rel_bias_table = f(moba_w_in), f(moba_w_out), f(rel_bias_table)
    ffn_w13, ffn_w2 = f(ffn_w13), f(ffn_w2)
    xf = x.reshape(2 * SEQ, D)
    o0 = run_hgrn(x, norm_mix[0], hgrn_w_in[0], hgrn_lb_logits)
    gains0 = f(np.stack([norm_mix[0], hgrn_out_norm[0], norm_ffn[0], final_norm]))
    h1 = run_tl("hgrn", False, xf, o0, gains0, hgrn_w_out[0], ffn_w13[0], ffn_w2[0],
                w_og=f(hgrn_w_in[0][:, 3072:4096]))
    o1 = run_moba(h1, norm_mix[1], moba_w_in[0], rel_bias_table)
    gains1 = f(np.stack([norm_mix[1], hgrn_out_norm[0], norm_ffn[1], final_norm]))
    out = run_tl("moba", True, h1, o1, gains1, moba_w_out[0], ffn_w13[1], ffn_w2[1])
    return out.reshape(2, SEQ, D)
```
